# Optimizing a Trainium2 kernel written in Bass

```python
import math
import jax, jax.numpy as jnp
from jax import lax
import numpy as np

D_MODEL = 1024
BATCH = 8
SEQ = 2048
DEPTH = 2
DEC_BATCH = 128
DEC_SEQ = 8
PAST_LEN = 16384
PAGE_SIZE = 128

N_MIXERS = 2
N_SSM_LAYERS = (DEPTH + 1) // 2
N_POOL_LAYERS = DEPTH // 2
EXPAND = 2
D_INNER = EXPAND * D_MODEL
SSM_HEADDIM = 64
SSM_HEADS = D_INNER // SSM_HEADDIM
SSM_GROUPS = 8
SSM_HEADS_PER_GROUP = SSM_HEADS // SSM_GROUPS
D_STATE = 128
D_CONV = 4
CONV_DIM = D_INNER + 2 * SSM_GROUPS * D_STATE
IN_PROJ_DIM = D_INNER + CONV_DIM + SSM_HEADS
SSD_CHUNK = 128
POOL_WINDOWS = (2, 4, 8, 16)
POOL_GROUPS = len(POOL_WINDOWS)
POOL_GROUP_DIM = D_MODEL // POOL_GROUPS
POOL_BUF = max(POOL_WINDOWS) - 1
N_MEM = 256
XATTN_HEADS = 4
XATTN_HEAD_DIM = D_MODEL // XATTN_HEADS
D_FF = 4 * D_MODEL
EPS = 1e-5

kernel_name = 'hybrid_ssd_pool_memory_decoder_step'


def rmsnorm(x, g):
    xf = x.astype(jnp.float32)
    y = xf * lax.rsqrt(jnp.mean(xf * xf, axis=-1, keepdims=True) + EPS)
    return (y * g.astype(jnp.float32)).astype(x.dtype)


def ssd_scan(xs, dt, a, bm, cm, h0):
    f32 = jnp.float32
    b, L = xs.shape[0], xs.shape[1]
    q = min(SSD_CHUNK, L)
    nc = -(-L // q)
    pad = nc * q - L
    xs, dt, bm, cm = xs.astype(f32), dt.astype(f32), bm.astype(f32), cm.astype(f32)
    if pad:
        pw = lambda t: jnp.pad(t, [(0, 0), (0, pad)] + [(0, 0)] * (t.ndim - 2))
        xs, dt, bm, cm = pw(xs), pw(dt), pw(bm), pw(cm)
    G, R = SSM_GROUPS, SSM_HEADS_PER_GROUP
    x = xs.reshape(b, nc, q, G, R, SSM_HEADDIM)
    d = dt.reshape(b, nc, q, G, R)
    B = bm.reshape(b, nc, q, G, D_STATE)
    C = cm.reshape(b, nc, q, G, D_STATE)
    acs = jnp.cumsum(d * a.astype(f32).reshape(G, R), axis=2)
    causal = jnp.tril(jnp.ones((q, q), dtype=bool))
    seg = acs[:, :, :, None] - acs[:, :, None, :]
    decay = jnp.exp(jnp.where(causal[:, :, None, None], seg, -jnp.inf))
    xdt = x * d[..., None]
    cb = jnp.einsum('bclgn,bcsgn->bclsg', C, B)
    y_diag = jnp.einsum('bclsgr,bcsgrp->bclgrp', cb[..., None] * decay, xdt)
    decay_end = jnp.exp(acs[:, :, -1:] - acs)
    chunk_states = jnp.einsum('bcsgn,bcsgr,bcsgrp->bcgrpn', B, decay_end, xdt)
    chunk_decay = jnp.exp(acs[:, :, -1])

    def step(h, inp):
        st, dec = inp
        return h * dec[..., None, None] + st, h

    h_init = h0.astype(f32).reshape(b, G, R, SSM_HEADDIM, D_STATE)
    h_final, h_prev = lax.scan(step, h_init,
                               (jnp.moveaxis(chunk_states, 1, 0), jnp.moveaxis(chunk_decay, 1, 0)))
    h_prev = jnp.moveaxis(h_prev, 0, 1)
    y_off = jnp.einsum('bclgn,bcgrpn,bclgr->bclgrp', C, h_prev, jnp.exp(acs))
    y = (y_diag + y_off).reshape(b, nc * q, SSM_HEADS, SSM_HEADDIM)[:, :L]
    return y, h_final.reshape(b, SSM_HEADS, SSM_HEADDIM, D_STATE)


def mamba_mixer(u, conv_buf, ssm_state, w_in, conv_w, conv_b, dt_bias, a_log, d_skip, norm_gated, w_out):
    b, L, _ = u.shape
    zxbcdt = u @ w_in
    z = zxbcdt[..., :D_INNER]
    xbc = zxbcdt[..., D_INNER:D_INNER + CONV_DIM]
    dt_raw = zxbcdt[..., D_INNER + CONV_DIM:]
    xpad = jnp.concatenate([conv_buf.astype(xbc.dtype), xbc], axis=1)
    conv = conv_b + sum(xpad[:, k:k + L] * conv_w[k] for k in range(D_CONV))
    xbc_act = jax.nn.silu(conv)
    new_conv = xpad[:, L:]
    xs = xbc_act[..., :D_INNER].reshape(b, L, SSM_HEADS, SSM_HEADDIM)
    bm = xbc_act[..., D_INNER:D_INNER + SSM_GROUPS * D_STATE].reshape(b, L, SSM_GROUPS, D_STATE)
    cm = xbc_act[..., D_INNER + SSM_GROUPS * D_STATE:].reshape(b, L, SSM_GROUPS, D_STATE)
    dt = jax.nn.softplus(dt_raw.astype(jnp.float32) + dt_bias.astype(jnp.float32))
    a = -jnp.exp(a_log.astype(jnp.float32))
    y, new_state = ssd_scan(xs, dt, a, bm, cm, ssm_state)
    y = y + d_skip.astype(jnp.float32)[:, None] * xs.astype(jnp.float32)
    yg = y.reshape(b, L, D_INNER) * jax.nn.silu(z.astype(jnp.float32))
    yg = yg.reshape(b, L, SSM_GROUPS, D_INNER // SSM_GROUPS)
    yg = yg * lax.rsqrt(jnp.mean(yg * yg, axis=-1, keepdims=True) + EPS)
    yg = yg.reshape(b, L, D_INNER) * norm_gated.astype(jnp.float32)
    out = yg.astype(u.dtype) @ w_out
    return out, new_conv, new_state.astype(ssm_state.dtype)


def pool_mixer(u, buf, pos0, w_pool, pool_scale):
    b, L, D = u.shape
    uf = u.astype(jnp.float32)
    ext = jnp.concatenate([buf.astype(jnp.float32), uf], axis=1)
    cs = jnp.concatenate([jnp.zeros((b, 1, D), jnp.float32), jnp.cumsum(ext, axis=1)], axis=1)
    pos = pos0 + jnp.arange(L)
    W = POOL_BUF + 1
    outs = []
    for g, w in enumerate(POOL_WINDOWS):
        sl = slice(g * POOL_GROUP_DIM, (g + 1) * POOL_GROUP_DIM)
        win_sum = cs[:, W:W + L, sl] - cs[:, W - w:W - w + L, sl]
        cnt = jnp.minimum(pos + 1, w).astype(jnp.float32)[None, :, None]
        outs.append(win_sum / cnt)
    pooled = jnp.concatenate(outs, axis=-1) - uf
    mixed = jnp.einsum('blgc,gcd->blgd', pooled.reshape(b, L, POOL_GROUPS, POOL_GROUP_DIM),
                       w_pool.astype(jnp.float32)).reshape(b, L, D)
    out = (mixed * pool_scale.astype(jnp.float32)).astype(u.dtype)
    return out, ext[:, L:].astype(u.dtype)


def mem_kv(mem, g_mem, w_k, w_v):
    b, m, _ = mem.shape
    mn = rmsnorm(mem, g_mem)
    k = (mn @ w_k).reshape(b, m, XATTN_HEADS, XATTN_HEAD_DIM)
    v = (mn @ w_v).reshape(b, m, XATTN_HEADS, XATTN_HEAD_DIM)
    return k, v


def cross_attn(h, k, v, w_q, w_o):
    b, L, _ = h.shape
    q = (h @ w_q).reshape(b, L, XATTN_HEADS, XATTN_HEAD_DIM)
    s = jnp.einsum('blhd,bmhd->bhlm', q.astype(jnp.float32), k.astype(jnp.float32)) * (XATTN_HEAD_DIM ** -0.5)
    p = jax.nn.softmax(s, axis=-1)
    o = jnp.einsum('bhlm,bmhd->blhd', p, v.astype(jnp.float32)).reshape(b, L, D_MODEL)
    return o.astype(h.dtype) @ w_o


def sq_relu_mlp(h, w_up, w_down):
    a = jax.nn.relu(h @ w_up)
    return (a * a) @ w_down


def setup_inputs(seed: int = 0) -> dict:
    key = jax.random.key(seed)
    ks = jax.random.split(key, 32)
    f32 = jnp.float32
    nrm = lambda k, shape, s: jax.random.normal(k, shape, f32) * s
    gain = lambda k, shape: 1.0 + 0.1 * jax.random.normal(k, shape, f32)
    kv_shape = (DEPTH, DEC_BATCH, N_MEM, XATTN_HEADS, XATTN_HEAD_DIM)
    dt0 = jnp.exp(jax.random.uniform(ks[16], (N_SSM_LAYERS, SSM_HEADS), f32, math.log(1e-3), math.log(1e-1)))
    return {
        'x_prompt': nrm(ks[0], (BATCH, SEQ, D_MODEL), 1.0),
        'x_sample': nrm(ks[1], (DEC_BATCH, DEC_SEQ, D_MODEL), 1.0),
        'cache_mem_k': nrm(ks[2], kv_shape, 1.0),
        'cache_mem_v': nrm(ks[3], kv_shape, 1.0),
        'state_ssm': nrm(ks[4], (N_SSM_LAYERS, DEC_BATCH, SSM_HEADS, SSM_HEADDIM, D_STATE), 0.1),
        'state_conv': nrm(ks[5], (N_SSM_LAYERS, DEC_BATCH, D_CONV - 1, CONV_DIM), 1.0),
        'state_pool': nrm(ks[6], (N_POOL_LAYERS, DEC_BATCH, POOL_BUF, D_MODEL), 1.0),
        'mem_prompt': nrm(ks[7], (BATCH, N_MEM, D_MODEL), 1.0),
        'norm_mix': gain(ks[8], (DEPTH, D_MODEL)),
        'norm_xattn': gain(ks[9], (DEPTH, D_MODEL)),
        'norm_mem': gain(ks[10], (DEPTH, D_MODEL)),
        'norm_mlp': gain(ks[11], (DEPTH, D_MODEL)),
        'norm_final': gain(ks[12], (D_MODEL,)),
        'w_in': nrm(ks[13], (N_SSM_LAYERS, D_MODEL, IN_PROJ_DIM), D_MODEL ** -0.5),
        'conv_w': nrm(ks[14], (N_SSM_LAYERS, D_CONV, CONV_DIM), D_CONV ** -0.5),
        'conv_b': nrm(ks[15], (N_SSM_LAYERS, CONV_DIM), 0.01),
        'dt_bias': dt0 + jnp.log(-jnp.expm1(-dt0)),
        'a_log': jnp.log(jax.random.uniform(ks[17], (N_SSM_LAYERS, SSM_HEADS), f32, 1.0, 16.0)),
        'd_skip': gain(ks[18], (N_SSM_LAYERS, SSM_HEADS)),
        'norm_gated': gain(ks[19], (N_SSM_LAYERS, D_INNER)),
        'w_out': nrm(ks[20], (N_SSM_LAYERS, D_INNER, D_MODEL), D_INNER ** -0.5),
        'w_pool': nrm(ks[21], (N_POOL_LAYERS, POOL_GROUPS, POOL_GROUP_DIM, POOL_GROUP_DIM), POOL_GROUP_DIM ** -0.5),
        'pool_scale': gain(ks[22], (N_POOL_LAYERS, D_MODEL)),
        'w_xq': nrm(ks[23], (DEPTH, D_MODEL, D_MODEL), D_MODEL ** -0.5),
        'w_xk': nrm(ks[24], (DEPTH, D_MODEL, D_MODEL), D_MODEL ** -0.5),
        'w_xv': nrm(ks[25], (DEPTH, D_MODEL, D_MODEL), D_MODEL ** -0.5),
        'w_xo': nrm(ks[26], (DEPTH, D_MODEL, D_MODEL), D_MODEL ** -0.5),
        'w_up': nrm(ks[27], (DEPTH, D_MODEL, D_FF), D_MODEL ** -0.5),
        'w_down': nrm(ks[28], (DEPTH, D_FF, D_MODEL), D_FF ** -0.5),
    }


def reference(x_prompt, x_sample, cache_mem_k, cache_mem_v, state_ssm, state_conv, state_pool, mem_prompt,
              norm_mix, norm_xattn, norm_mem, norm_mlp, norm_final,
              w_in, conv_w, conv_b, dt_bias, a_log, d_skip, norm_gated, w_out,
              w_pool, pool_scale, w_xq, w_xk, w_xv, w_xo, w_up, w_down):
    xp, xs = x_prompt, x_sample
    bp = x_prompt.shape[0]
    mk_p, mv_p = [], []
    ssm_p, conv_p, pool_p = [], [], []
    ssm_s, conv_s, pool_s = [], [], []
    for i in range(DEPTH):
        j = i // N_MIXERS
        hp = rmsnorm(xp, norm_mix[i])
        hs = rmsnorm(xs, norm_mix[i])
        if i % N_MIXERS == 0:
            prm = (w_in[j], conv_w[j], conv_b[j], dt_bias[j], a_log[j], d_skip[j], norm_gated[j], w_out[j])
            zc = jnp.zeros((bp, D_CONV - 1, CONV_DIM), x_prompt.dtype)
            zs = jnp.zeros((bp, SSM_HEADS, SSM_HEADDIM, D_STATE), state_ssm.dtype)
            op, cp, sp = mamba_mixer(hp, zc, zs, *prm)
            osm, csm, ssm = mamba_mixer(hs, state_conv[j], state_ssm[j], *prm)
            conv_p.append(cp); ssm_p.append(sp)
            conv_s.append(csm); ssm_s.append(ssm)
        else:
            zb = jnp.zeros((bp, POOL_BUF, D_MODEL), x_prompt.dtype)
            op, pbp = pool_mixer(hp, zb, 0, w_pool[j], pool_scale[j])
            osm, pbs = pool_mixer(hs, state_pool[j], PAST_LEN, w_pool[j], pool_scale[j])
            pool_p.append(pbp); pool_s.append(pbs)
        xp = xp + op
        xs = xs + osm
        k_p, v_p = mem_kv(mem_prompt, norm_mem[i], w_xk[i], w_xv[i])
        mk_p.append(k_p); mv_p.append(v_p)
        xp = xp + cross_attn(rmsnorm(xp, norm_xattn[i]), k_p, v_p, w_xq[i], w_xo[i])
        xs = xs + cross_attn(rmsnorm(xs, norm_xattn[i]), cache_mem_k[i], cache_mem_v[i], w_xq[i], w_xo[i])
        xp = xp + sq_relu_mlp(rmsnorm(xp, norm_mlp[i]), w_up[i], w_down[i])
        xs = xs + sq_relu_mlp(rmsnorm(xs, norm_mlp[i]), w_up[i], w_down[i])
    y_prompt = rmsnorm(xp, norm_final)
    y_sample = rmsnorm(xs, norm_final)
    new_mem_k_p = jnp.stack(mk_p)
    new_mem_v_p = jnp.stack(mv_p)
    new_ssm_p = jnp.stack(ssm_p)
    new_conv_p = jnp.stack(conv_p)
    new_pool_p = jnp.stack(pool_p)
    new_ssm_s = jnp.stack(ssm_s)
    new_conv_s = jnp.stack(conv_s)
    new_pool_s = jnp.stack(pool_s)
    return (y_prompt, y_sample, new_mem_k_p, new_mem_v_p, new_ssm_p, new_conv_p, new_pool_p, new_ssm_s, new_conv_s, new_pool_s)
```

```python
import numpy as np
from contextlib import ExitStack
import concourse.bass as bass
import concourse.mybir as mybir
from concourse.bass_utils import run_bass_kernel_spmd

F32 = mybir.dt.float32
BF16 = mybir.dt.bfloat16
AF = mybir.ActivationFunctionType
ALU = mybir.AluOpType
AX = mybir.AxisListType

NCORES = 8
D = 1024
DC = 8
SEQ = 2048
NSB = 16
DSEQ = 8
TS = NSB * DSEQ
T = SEQ + TS
NT = T // 128
TBS = [(0, 512), (512, 512), (1024, 512), (1536, 512), (2048, 128)]
DI = 2048
NH = 32
HP = 64
NG = 8
NS = 128
CONV_DIM = 4096
IN_PROJ = 6176
NMEM = 256
XH = 4
XD = 256
DFF = 4096
EPS = 1e-5
POOL_W = (2, 4, 8, 16)

SAME_ENGINE_SYNC = True
NDS = 32


class Buf:
    __slots__ = ("w", "r", "excl")

    def __init__(self):
        self.w = None
        self.r = {}
        self.excl = False


class Tile:
    def __init__(self, t):
        self.t = t
        self.bufs = {}

    def b(self, key=None):
        v = self.bufs.get(key)
        if v is None:
            v = self.bufs[key] = Buf()
        return v

    def __getitem__(self, idx):
        return self.t[idx]


class _Eng:
    def __init__(self, h, sem):
        self.h = h
        self.sem = sem
        self.cnt = 0
        self.waited = {}


class Sched:
    def __init__(self, nc, es):
        self.nc = nc
        self.es = es
        self.eng = {}
        self.sems = {}
        for name, h in [("pe", nc.tensor), ("act", nc.scalar), ("dve", nc.vector),
                        ("pool", nc.gpsimd), ("sp", nc.sync)]:
            sem = es.enter_context(nc.semaphore("s_" + name))
            self.eng[name] = _Eng(h, sem)
            self.sems[name] = sem
        self.dcnt = [0] * NDS
        self.dnext = 0
        self.dnext_sw = 0
        for i in range(NDS):
            self.sems[("d", i)] = es.enter_context(nc.semaphore("sd%d" % i))
        self.nwaits = 0
        self.nops = 0

    def _wait(self, e, key, val):
        if e.waited.get(key, 0) >= val:
            return
        e.h.wait_ge(self.sems[key], val)
        e.waited[key] = val
        self.nwaits += 1

    @staticmethod
    def _deps(reads, writes, en=None):
        deps = {}
        for b in reads:
            if b.w is not None:
                k, v = b.w
                if deps.get(k, 0) < v:
                    deps[k] = v
            if b.excl:
                for k, v in b.r.items():
                    if k != en and deps.get(k, 0) < v:
                        deps[k] = v
        for b in writes:
            if b.w is not None:
                k, v = b.w
                if deps.get(k, 0) < v:
                    deps[k] = v
            for k, v in b.r.items():
                if deps.get(k, 0) < v:
                    deps[k] = v
        return deps

    @staticmethod
    def _mark(ev, reads, writes):
        k, v = ev
        for b in reads:
            b.r[k] = v
        for b in writes:
            b.w = ev
            b.r = {}

    def op(self, en, fn, reads=(), writes=()):
        e = self.eng[en]
        deps = self._deps(reads, writes, en)
        raw_self = 0
        if en != "pe":
            for b in reads:
                if b.w is not None and b.w[0] == en and b.w[1] > raw_self:
                    raw_self = b.w[1]
        for k, v in deps.items():
            if k == en:
                if en == "pe":
                    continue
                if not SAME_ENGINE_SYNC:
                    v = raw_self
                    if v == 0:
                        continue
            self._wait(e, k, v)
        ins = fn(e.h)
        e.cnt += 1
        ins.then_inc(e.sem, 1)
        self._mark((en, e.cnt), reads, writes)
        self.nops += 1
        return ins

    def dma(self, qn, out, in_, reads=(), writes=(), **kw):
        e = self.eng[qn]
        deps = self._deps(reads, writes)
        for k, v in deps.items():
            self._wait(e, k, v)
        half = NDS // 2
        if qn == "pool":
            i = half + self.dnext_sw
            self.dnext_sw = (self.dnext_sw + 1) % (NDS - half)
        else:
            i = self.dnext
            self.dnext = (i + 1) % half
        if self.dcnt[i] > 0:
            self._wait(e, ("d", i), 16 * self.dcnt[i])
        self.dcnt[i] += 1
        ins = e.h.dma_start(out=out, in_=in_, **kw)
        ins.then_inc(self.sems[("d", i)], 16)
        self._mark((("d", i), 16 * self.dcnt[i]), reads, writes)
        self.nops += 1
        return ins

    def barrier(self, engines=("pe", "act", "dve", "pool", "sp")):
        for en in engines:
            e = self.eng[en]
            for on, o in self.eng.items():
                if on != en and o.cnt > 0:
                    self._wait(e, on, o.cnt)
            for i in range(NDS):
                if self.dcnt[i]:
                    self._wait(e, ("d", i), 16 * self.dcnt[i])


class K:
    def __init__(self):
        self.nc = bass.Bass("TRN2", target_bir_lowering=False)
        self.es = ExitStack()
        self.S = Sched(self.nc, self.es)
        self.bank_rr = 0

    def sb(self, name, shape, dt, es=None):
        es = es or self.es
        self.uid = getattr(self, "uid", 0) + 1
        return Tile(es.enter_context(self.nc.sbuf_tensor("%s_u%d" % (name, self.uid), list(shape), dt)))

    def dram_in(self, name, shape, dt=F32):
        return self.nc.dram_tensor(name, list(shape), dt, kind="ExternalInput").ap()

    def dram_out(self, name, shape, dt=F32):
        return self.nc.dram_tensor(name, list(shape), dt, kind="ExternalOutput").ap()

    def bank(self, excl=()):
        i = self.bank_rr
        pe_ = getattr(self, "perm_excl", ())
        while i in excl or i in pe_:
            i = (i + 1) % 8
        self.bank_rr = (i + 1) % 8
        return i

    def mm(self, out, lhsT, rhs, start, stop, reads, writes):
        return self.S.op("pe", lambda h: h.matmul(out, lhsT=lhsT, rhs=rhs, start=start, stop=stop,
                                                  skip_group_check=True), reads, writes)

    def tr(self, out, in_, ident, reads, writes):
        return self.S.op("pe", lambda h: h.transpose(out, in_, ident), reads, writes)

    def act(self, out, in_, func, reads, writes, bias=None, scale=None, accum_out=None):
        kw = {}
        if bias is not None:
            kw["bias"] = bias
        if scale is not None:
            kw["scale"] = scale
        if accum_out is not None:
            kw["accum_out"] = accum_out
        return self.S.op("act", lambda h: h.activation(out, in_, func, **kw), reads, writes)

    def ts(self, en, out, in0, s1, op0, reads, writes, s2=None, op1=None):
        if op1 is None:
            if en == "pool" and op0 == ALU.mult:
                return self.S.op(en, lambda h: h.tensor_scalar(out, in0, s1, 0.0, ALU.mult, ALU.add), reads, writes)
            return self.S.op(en, lambda h: h.tensor_scalar(out, in0, s1, None, op0), reads, writes)
        return self.S.op(en, lambda h: h.tensor_scalar(out, in0, s1, s2, op0, op1), reads, writes)

    def stt(self, out, in0, scalar, in1, op0, op1, reads, writes):
        return self.S.op("dve", lambda h: h.scalar_tensor_tensor(out, in0, scalar, in1, op0, op1), reads, writes)

    def tt(self, en, out, in0, in1, op, reads, writes):
        return self.S.op(en, lambda h: h.tensor_tensor(out, in0, in1, op), reads, writes)

    def cp(self, en, out, in_, reads, writes):
        if en == "act":
            return self.S.op("act", lambda h: h.copy(out, in_), reads, writes)
        return self.S.op(en, lambda h: h.tensor_copy(out, in_), reads, writes)

    def memset(self, en, ap, val, writes):
        return self.S.op(en, lambda h: h.memset(ap, val), (), writes)


CFG = {"mamba": True, "pool": True, "attn": True, "mlp": True}
MBS = 9


def build_program():
    k = K()
    nc, S, es = k.nc, k.S, k.es
    cfg = CFG

    xp = k.dram_in("xp", [SEQ, D])
    xs = k.dram_in("xs", [TS, D])
    ck = k.dram_in("ck", [2, NSB, NMEM, D])
    cv = k.dram_in("cv", [2, NSB, NMEM, D])
    ssm = k.dram_in("ssm", [NSB, DI, NS])
    sconv = k.dram_in("sconv", [NSB * 3, CONV_DIM])
    spool = k.dram_in("spool", [NSB * 15, D])
    mem = k.dram_in("mem", [NMEM, D])
    norm_mix = k.dram_in("norm_mix", [2, D])
    norm_xattn = k.dram_in("norm_xattn", [2, D])
    norm_mem = k.dram_in("norm_mem", [2, D])
    norm_mlp = k.dram_in("norm_mlp", [2, D])
    norm_final = k.dram_in("norm_final", [1, D])
    w_in = k.dram_in("w_in", [D, IN_PROJ])
    conv_w = k.dram_in("conv_w", [4, CONV_DIM])
    conv_b = k.dram_in("conv_b", [1, CONV_DIM])
    dt_bias = k.dram_in("dt_bias", [1, NH])
    a_log = k.dram_in("a_log", [1, NH])
    d_skip = k.dram_in("d_skip", [1, NH])
    norm_gated = k.dram_in("norm_gated", [1, DI])
    w_out = k.dram_in("w_out", [DI, D])
    w_pool = k.dram_in("w_pool", [4, 256, 256])
    pool_scale = k.dram_in("pool_scale", [1, D])
    w_xq = k.dram_in("w_xq", [2, D, D])
    w_xk = k.dram_in("w_xk", [2, D, D])
    w_xv = k.dram_in("w_xv", [2, D, D])
    w_xo = k.dram_in("w_xo", [2, D, D])
    w_up = k.dram_in("w_up", [2, D, DFF])
    w_down = k.dram_in("w_down", [2, DFF, D])

    y_p = k.dram_out("y_p", [SEQ, D])
    y_s = k.dram_out("y_s", [TS, D])
    mk_p = k.dram_out("mk_p", [2, NMEM, D])
    mv_p = k.dram_out("mv_p", [2, NMEM, D])
    ssm_p = k.dram_out("ssm_p", [DI, NS])
    conv_p = k.dram_out("conv_p", [3, CONV_DIM])
    pool_p = k.dram_out("pool_p", [15, D])
    ssm_s = k.dram_out("ssm_s", [NSB, DI, NS])
    conv_s = k.dram_out("conv_s", [NSB * 3, CONV_DIM])
    pool_s = k.dram_out("pool_s", [NSB * 15, D])

    xres = k.sb("xres", [128, DC, T], F32)
    ident_f = k.sb("ident_f", [128, 128], F32)
    ident_b = k.sb("ident_b", [128, 128], BF16)
    ones_f = k.sb("ones_f", [128, 128], F32)
    ones_b = k.sb("ones_b", [128, 128], BF16)
    zeros_b = k.sb("zeros_b", [128, 512], BF16)
    colv = k.sb("colv", [128, 32, 12], F32)
    k.wsl = []
    psum = Tile(es.enter_context(nc.psum_tensor("psum", [128, 8, 512], F32)))
    k.slab_rr = 0

    def pbuf(i):
        b_ = psum.b(i)
        b_.excl = True
        return b_

    def next_slab():
        k.slab_rr = (k.slab_rr + 1) % len(k.wsl)
        return k.wsl[k.slab_rr]

    def bank2(excl=()):
        if k.bank_rr % 2:
            k.bank_rr = (k.bank_rr + 1) % 8
        b0 = k.bank_rr
        pe_ = getattr(k, "perm_excl", ())
        while b0 in excl or (b0 + 1) in excl or b0 in pe_ or (b0 + 1) in pe_:
            b0 = (b0 + 2) % 8
        k.bank_rr = (b0 + 2) % 8
        return b0

    def xb(c, t0, tn):
        return [xres.b((c, tt)) for tt in range(t0 // 128, (t0 + tn) // 128)]

    k.memset("pool", ones_f[:], 1.0, [ones_f.b()])
    S.op("pool", lambda h: h.affine_select(ident_f[:], ones_f[:], [[-1, 128]], ALU.is_equal, 0.0,
                                           base=0, channel_multiplier=1),
         [ones_f.b()], [ident_f.b()])
    k.cp("pool", ident_b[:], ident_f[:], [ident_f.b()], [ident_b.b()])
    k.cp("pool", ones_b[:], ones_f[:], [ones_f.b()], [ones_b.b()])
    k.memset("pool", zeros_b[:], 0.0, [zeros_b.b()])
    with ExitStack() as ph:
        vecrows = k.sb("vecrows", [16, 4096], F32, ph)
        k.memset("dve", vecrows[:], 0.0, [vecrows.b()])
        S.dma("sp", vecrows[0:4, :], conv_w[:, :], (), [vecrows.b()])
        S.dma("sp", vecrows[4:5, :], conv_b[:, :], (), [vecrows.b()])
        S.dma("sp", vecrows[5:7, 0:D], norm_mix[:, :], (), [vecrows.b()])
        S.dma("sp", vecrows[7:9, 0:D], norm_xattn[:, :], (), [vecrows.b()])
        S.dma("sp", vecrows[9:11, 0:D], norm_mlp[:, :], (), [vecrows.b()])
        S.dma("sp", vecrows[11:12, 0:D], pool_scale[:, :], (), [vecrows.b()])
        bi = k.bank()
        for c in range(32):
            k.tr(psum[:, bi, c * 12:(c + 1) * 12], vecrows[0:12, c * 128:(c + 1) * 128], ident_f[0:12, 0:12],
                 [vecrows.b(), ident_f.b()], [pbuf(bi)])
        k.cp("dve", colv[:], psum[:, bi, 0:384].rearrange("p (c r) -> p c r", r=12), [pbuf(bi)], [colv.b()])
        S.barrier()
    CV_CONVW, CV_CONVB, CV_MIX, CV_XATTN, CV_MLP, CV_PSCALE = 0, 4, 5, 7, 9, 11

    with ExitStack() as ph:
        xin = [k.sb("xin%d" % i, [128, D], F32, ph) for i in range(6)]
        for t in range(NT):
            xt = xin[t % 6]
            src = xp[t * 128:(t + 1) * 128, :] if t < 16 else xs[:, :]
            S.dma("sp", xt[:], src, (), [xt.b()])
            for half in range(2):
                bi = k.bank()
                for c4 in range(4):
                    c = half * 4 + c4
                    k.tr(psum[:, bi, c4 * 128:(c4 + 1) * 128], xt[:, c * 128:(c + 1) * 128], ident_f[:],
                         [xt.b(), ident_f.b()], [pbuf(bi)])
                eng = "dve" if half == 0 else "act"
                k.cp(eng, xres[:, half * 4:half * 4 + 4, t * 128:(t + 1) * 128],
                     psum[:, bi, :].rearrange("p (c n) -> p c n", c=4),
                     [pbuf(bi)], [xres.b((c, t)) for c in range(half * 4, half * 4 + 4)])
        S.barrier()

    def rmsnorm_fm(ph_tiles, gidx, out_fn, tbs):
        sqt, rst = ph_tiles
        for tbi, (t0, tn) in tbs:
            bi = k.bank()
            for c in range(DC):
                sq = sqt[c % 2]
                k.act(sq[:, 0:tn], xres[:, c, t0:t0 + tn], AF.Square, xb(c, t0, tn), [sq.b()])
                k.mm(psum[:, bi, 0:tn], ones_b[:], sq[:, 0:tn], c == 0, c == DC - 1,
                     [ones_b.b(), sq.b()], [pbuf(bi)])
            rs = rst[tbi % 2]
            k.act(rs[:, 0:tn], psum[:, bi, 0:tn], AF.Ln, [pbuf(bi)], [rs.b()], bias=EPS, scale=1.0 / D)
            k.act(rs[:, 0:tn], rs[:, 0:tn], AF.Exp, [rs.b()], [rs.b()], scale=-0.5)
            for c in range(DC):
                o_ap, o_bufs = out_fn(c, tbi, t0, tn)
                k.stt(o_ap, xres[:, c, t0:t0 + tn], colv[:, c, gidx:gidx + 1], rs[:, 0:tn], ALU.mult, ALU.mult,
                      xb(c, t0, tn) + [colv.b(), rs.b()], o_bufs)

    def linear_fm(W, KC, c0, ncols, rhs_fn, evac_fn, tbs):
        NW = 4096 // KC
        for s0 in range(0, ncols, NW):
            nw = min(NW, ncols - s0)
            slab = next_slab()
            view = slab[:, 0:KC * nw].rearrange("p (k n) -> p k n", k=KC)
            S.dma("pool", view, W[0:KC * 128, c0 + s0:c0 + s0 + nw].rearrange("(k p) n -> p k n", p=128),
                  (), [slab.b()])
            for m in range(nw // 128):
                for tbi, (t0, tn) in tbs:
                    bi = k.bank()
                    for kc in range(KC):
                        rhs, rreads = rhs_fn(kc, tbi, t0, tn)
                        k.mm(psum[:, bi, 0:tn], view[:, kc, m * 128:(m + 1) * 128], rhs, kc == 0, kc == KC - 1,
                             [slab.b()] + rreads, [pbuf(bi)])
                    evac_fn((s0 // 128) + m, tbi, t0, tn, psum[:, bi, 0:tn], pbuf(bi))

    ALL_TBS = list(enumerate(TBS))

    def evac_add_xres(m, tbi, t0, tn, ps, pb):
        k.tt("dve", xres[:, m, t0:t0 + tn], ps, xres[:, m, t0:t0 + tn], ALU.add,
             [pb] + xb(m, t0, tn), xb(m, t0, tn))

    def mlp_layer(li):
        with ExitStack() as ph:
            k.wsl = [k.sb("wsl%d" % i, [128, 4096], BF16, ph) for i in range(3)]
            h = k.sb("mlp_h", [128, DC, T], BF16, ph)
            a = k.sb("mlp_a", [128, DC, T], BF16, ph)
            sqt = [k.sb("mlp_sq%d" % i, [128, 512], BF16, ph) for i in range(2)]
            rst = [k.sb("mlp_rs%d" % i, [128, 512], F32, ph) for i in range(2)]
            rl = [k.sb("mlp_rl%d" % i, [128, 512], F32, ph) for i in range(3)]
            k.rl_rr = 0
            rmsnorm_fm((sqt, rst), CV_MLP + li,
                       lambda c, tbi, t0, tn: (h[:, c, t0:t0 + tn], [h.b((c, tbi))]), ALL_TBS)
            for j in range(4):
                def ev_up(m, tbi, t0, tn, ps, pb):
                    r = rl[k.rl_rr]
                    k.rl_rr = (k.rl_rr + 1) % 3
                    k.act(r[:, 0:tn], ps, AF.Relu, [pb], [r.b()])
                    k.tt("pool", a[:, m, t0:t0 + tn], r[:, 0:tn], r[:, 0:tn], ALU.mult, [r.b()], [a.b((m, tbi))])
                linear_fm(w_up[li], DC, j * 1024, 1024,
                          lambda kc, tbi, t0, tn: (h[:, kc, t0:t0 + tn], [h.b((kc, tbi))]), ev_up, ALL_TBS)
                linear_fm(w_down[li][j * 1024:(j + 1) * 1024, :], DC, 0, 1024,
                          lambda kc, tbi, t0, tn: (a[:, kc, t0:t0 + tn], [a.b((kc, tbi))]), evac_add_xres, ALL_TBS)
            S.barrier()

    def attn_layer(li):
        scale = float(XD) ** -0.5
        with ExitStack() as ph:
            wq = k.sb("at_wq", [128, DC, D], BF16, ph)
            wo = k.sb("at_wo", [128, DC, D], BF16, ph)
            sqt = [k.sb("at_sq%d" % i, [128, 512], BF16, ph) for i in range(2)]
            rst = [k.sb("at_rs%d" % i, [128, 512], F32, ph) for i in range(2)]
            hn = [k.sb("at_hn0", [128, DC, 512], BF16, ph)] * 2
            qt = [k.sb("at_q0", [128, DC, 512], BF16, ph)] * 2
            ot = hn
            kT = k.sb("at_kT", [128, DC, NMEM], BF16, ph)
            Vp = k.sb("at_V", [128, 2, D], BF16, ph)
            Pt = [k.sb("at_P%d" % i, [128, XH, NMEM], BF16, ph) for i in range(2)]
            Pn = Pt
            PT = [k.sb("at_PT%d" % i, [128, XH * 2, 128], BF16, ph) for i in range(2)]
            sst = [k.sb("at_st%d" % i, [128, 16], F32, ph) for i in range(2)]

            with ExitStack() as ph2:
                k.wsl = [k.sb("wsl%d" % i, [128, 4096], BF16, ph2) for i in range(2)]
                grow = k.sb("at_grow", [128, D], F32, ph2)
                memt = k.sb("at_mem", [128, D], F32, ph2)
                mn = k.sb("at_mn", [128, D], BF16, ph2)
                mnT = k.sb("at_mnT", [128, DC, NMEM], BF16, ph2)
                ktok = k.sb("at_ktok", [128, 2, D], F32, ph2)
                vtok = ktok
                sq = k.sb("at_sqscr", [128, D], BF16, ph2)
                st = k.sb("at_mst", [128, 4], F32, ph2)
                S.dma("sp", grow[:], norm_mem[li:li + 1, :].to_broadcast([128, D]), (), [grow.b()])
                for mt in range(2):
                    S.dma("sp", memt[:], mem[mt * 128:(mt + 1) * 128, :], (), [memt.b()])
                    k.act(sq[:], memt[:], AF.Square, [memt.b()], [sq.b(), st.b()], accum_out=st[:, 0:1])
                    k.act(st[:, 1:2], st[:, 0:1], AF.Ln, [st.b()], [st.b()], bias=EPS, scale=1.0 / D)
                    k.act(st[:, 2:3], st[:, 1:2], AF.Exp, [st.b()], [st.b()], scale=-0.5)
                    k.stt(mn[:], memt[:], st[:, 2:3], grow[:], ALU.mult, ALU.mult,
                          [memt.b(), st.b(), grow.b()], [mn.b()])
                    bi = k.bank()
                    pv = psum[:, bi, :].bitcast(BF16)
                    for c in range(DC):
                        k.tr(pv[:, c * 128:(c + 1) * 128], mn[:, c * 128:(c + 1) * 128], ident_b[:],
                             [mn.b(), ident_b.b()], [pbuf(bi)])
                    k.cp("dve", mnT[:, :, mt * 128:(mt + 1) * 128], pv.rearrange("p (c n) -> p c n", c=DC),
                         [pbuf(bi)], [mnT.b()])
                for which, (W, tok, outd) in enumerate(((w_xk[li], ktok, mk_p[li]), (w_xv[li], vtok, mv_p[li]))):
                    for ch in range(2):
                        slab = next_slab()
                        view = slab[:, 0:DC * 512].rearrange("p (k n) -> p k n", k=DC)
                        S.dma("pool", view, W[:, ch * 512:(ch + 1) * 512].rearrange("(k p) n -> p k n", p=128),
                              (), [slab.b()])
                        for mt in range(2):
                            bi = k.bank()
                            for kc in range(DC):
                                k.mm(psum[:, bi, :], mnT[:, kc, mt * 128:(mt + 1) * 128], view[:, kc, :],
                                     kc == 0, kc == DC - 1, [mnT.b(), slab.b()], [pbuf(bi)])
                            k.cp("act", tok[:, mt, ch * 512:(ch + 1) * 512], psum[:, bi, :], [pbuf(bi)], [tok.b()])
                    S.dma("sp", outd.rearrange("(a p) n -> p a n", p=128), tok[:], [tok.b()], ())
                    if which == 0:
                        for mt in range(2):
                            for c4 in range(2):
                                bi = k.bank()
                                for cc in range(4):
                                    c = c4 * 4 + cc
                                    k.tr(psum[:, bi, cc * 128:(cc + 1) * 128], ktok[:, mt, c * 128:(c + 1) * 128], ident_f[:],
                                         [ktok.b(), ident_f.b()], [pbuf(bi)])
                                k.cp("dve", kT[:, c4 * 4:c4 * 4 + 4, mt * 128:(mt + 1) * 128],
                                     psum[:, bi, :].rearrange("p (c n) -> p c n", c=4), [pbuf(bi)], [kT.b()])
                    else:
                        k.cp("act", Vp[:], vtok[:], [vtok.b()], [Vp.b()])
                    if which == 0:
                        S.dma("pool", wq[:], w_xq[li].rearrange("(k p) n -> p k n", p=128), (), [wq.b()])
                        S.dma("pool", wo[:], w_xo[li].rearrange("(k p) n -> p k n", p=128), (), [wo.b()])
                S.barrier()

            Kb = [k.sb("at_Kb%d" % i, [128, 2, D], BF16, ph) for i in range(2)]
            Vb = [k.sb("at_Vb%d" % i, [128, 2, D], BF16, ph) for i in range(2)]
            kTb = [k.sb("at_kTb%d" % i, [128, DC, NMEM], BF16, ph) for i in range(2)]
            Qz = [k.sb("at_Qz%d" % i, [128, DC, 128], BF16, ph) for i in range(2)]
            for i in range(2):
                k.memset("pool", Qz[i][:], 0.0, [Qz[i].b()])

            def softmax_tile(b0, bufs, excl=()):
                P, PTt, st = bufs
                Pnn = P
                sview = psum[:, b0:b0 + 2, :].rearrange("p a (h m) -> p (a h) m", h=2)
                S.op("dve", lambda h: h.tensor_reduce(st[:, 0:4], sview, AX.X, ALU.max),
                     [pbuf(b0), pbuf(b0 + 1)], [st.b()])
                k.ts("dve", st[:, 4:8], st[:, 0:4], -scale, ALU.mult, [st.b()], [st.b()])
                for hd in range(XH):
                    k.act(P[:, hd, :], sview[:, hd, :], AF.Exp, [pbuf(b0), pbuf(b0 + 1), st.b()], [P.b(), st.b()],
                          bias=st[:, 4 + hd:5 + hd], scale=scale, accum_out=st[:, 8 + hd:9 + hd])
                S.op("dve", lambda h: h.reciprocal(st[:, 12:16], st[:, 8:12]), [st.b()], [st.b()])
                k.tt("dve", Pnn[:], P[:], st[:, 12:16].unsqueeze(2).to_broadcast([128, XH, NMEM]), ALU.mult,
                     [P.b(), st.b()], [Pnn.b()])
                bi = k.bank(excl=excl)
                pv = psum[:, bi, :].bitcast(BF16)
                for hd in range(XH):
                    for mc in range(2):
                        j = hd * 2 + mc
                        k.tr(pv[:, j * 128:(j + 1) * 128], Pnn[:, hd, mc * 128:(mc + 1) * 128], ident_b[:],
                             [Pnn.b(), ident_b.b()], [pbuf(bi)])
                k.cp("act", PTt[:], pv.rearrange("p (j n) -> p j n", j=XH * 2), [pbuf(bi)], [PTt.b()])
                return PTt

            qs = k.sb("at_qs", [128, DC, 128], BF16, ph)
            os_ = k.sb("at_os", [128, DC, 128], BF16, ph)
            Ps = k.sb("at_Ps", [128, XH, NMEM], BF16, ph)
            PTs = k.sb("at_PTs", [128, XH * 2, 128], BF16, ph)
            sts = k.sb("at_sts", [128, 16], F32, ph)
            SB0 = 6
            t0s, tns = TBS[4]
            hnt = hn[0]
            rmsnorm_fm((sqt, rst), CV_XATTN + li,
                       lambda c, tbi_, t0_, tn_: (hnt[:, c, 0:tn_], [hnt.b()]), [(4, (t0s, tns))])
            for m in range(DC):
                bi = k.bank()
                for kc in range(DC):
                    k.mm(psum[:, bi, 0:tns], wq[:, kc, m * 128:(m + 1) * 128], hnt[:, kc, 0:tns], kc == 0, kc == DC - 1,
                         [wq.b(), hnt.b()], [pbuf(bi)])
                k.cp("act", qs[:, m, :], psum[:, bi, 0:tns], [pbuf(bi)], [qs.b()])
            k.perm_excl = (SB0, SB0 + 1)
            for bb in range(2):
                k.mm(psum[:, SB0 + bb, :], zeros_b[:, 0:128], zeros_b[:], True, True, [zeros_b.b()], [pbuf(SB0 + bb)])

            def sample_K(b, excl):
                Kt, kTt, Qzt = Kb[b % 2], kTb[b % 2], Qz[b % 2]
                S.dma("pool", Kt[:], ck[li, b].rearrange("(a p) n -> p a n", p=128), (), [Kt.b()])
                for c4 in range(2):
                    bi = k.bank(excl=excl)
                    pv = psum[:, bi, :].bitcast(BF16)
                    for cc in range(4):
                        for mc in range(2):
                            c = c4 * 4 + cc
                            j = cc * 2 + mc
                            k.tr(pv[:, j * 128:(j + 1) * 128], Kt[:, mc, c * 128:(c + 1) * 128], ident_b[:],
                                 [Kt.b(), ident_b.b()], [pbuf(bi)])
                    k.cp("dve" if c4 == 0 else "act", kTt[:, c4 * 4:c4 * 4 + 4, :],
                         pv.rearrange("p (c m) -> p c m", c=4), [pbuf(bi)], [kTt.b()])
                if b >= 2:
                    pb_ = b - 2
                    k.memset("pool", Qzt[:, :, pb_ * 8:pb_ * 8 + 8], 0.0, [Qzt.b()])
                k.cp("pool", Qzt[:, :, b * 8:b * 8 + 8], qs[:, :, b * 8:b * 8 + 8], [qs.b()], [Qzt.b()])
                for hd in range(XH):
                    for dc in range(2):
                        k.mm(psum[:, SB0 + hd // 2, (hd % 2) * 256:(hd % 2) * 256 + 256],
                             Qzt[:, hd * 2 + dc, :], kTt[:, hd * 2 + dc, :], False, (b == NSB - 1 and dc == 1),
                             [Qzt.b(), kTt.b()], [pbuf(SB0 + hd // 2)])

            def sample_V(b):
                Vt = Vb[b % 2]
                S.dma("pool", Vt[:], cv[li, b].rearrange("(a p) n -> p a n", p=128), (), [Vt.b()])
                for d8 in range(DC):
                    hd = d8 // 2
                    for mc in range(2):
                        k.mm(psum[:, SB0 + d8 // 4, (d8 % 4) * 128 + b * 8:(d8 % 4) * 128 + b * 8 + 8],
                             Vt[:, mc, d8 * 128:(d8 + 1) * 128], PTs[:, hd * 2 + mc, b * 8:b * 8 + 8],
                             mc == 0, mc == 1, [Vt.b(), PTs.b()], [pbuf(SB0 + d8 // 4)])

            tile_ctr = 0
            gt = 0
            for tbi, (t0, tn) in ALL_TBS[0:4]:
                hnt, qtt, ott = hn[tbi % 2], qt[tbi % 2], ot[tbi % 2]
                rmsnorm_fm((sqt, rst), CV_XATTN + li,
                           lambda c, tbi_, t0_, tn_: (hnt[:, c, 0:tn_], [hnt.b()]), [(tbi, (t0, tn))])
                for m in range(DC):
                    bi = k.bank()
                    for kc in range(DC):
                        k.mm(psum[:, bi, 0:tn], wq[:, kc, m * 128:(m + 1) * 128], hnt[:, kc, 0:tn], kc == 0, kc == DC - 1,
                             [wq.b(), hnt.b()], [pbuf(bi)])
                    k.cp("act", qtt[:, m, 0:tn], psum[:, bi, 0:tn], [pbuf(bi)], [qtt.b()])

                def scoresA(tt, excl=()):
                    lsl = slice(tt * 128, (tt + 1) * 128)
                    b0 = bank2(excl)
                    for hd in range(XH):
                        for dc in range(2):
                            k.mm(psum[:, b0 + hd // 2, (hd % 2) * 256:(hd % 2) * 256 + 256],
                                 qtt[:, hd * 2 + dc, lsl], kT[:, hd * 2 + dc, :], dc == 0, dc == 1,
                                 [qtt.b(), kT.b()], [pbuf(b0 + hd // 2)])
                    return b0

                def restB(tt, b0, slot, excl=()):
                    lsl = slice(tt * 128, (tt + 1) * 128)
                    PTt = softmax_tile(b0, (Pt[slot], PT[slot], sst[slot]), excl)
                    bo = bank2(excl)
                    for d8 in range(DC):
                        hd = d8 // 2
                        for mc in range(2):
                            k.mm(psum[:, bo + d8 // 4, (d8 % 4) * 128:(d8 % 4 + 1) * 128],
                                 Vp[:, mc, d8 * 128:(d8 + 1) * 128], PTt[:, hd * 2 + mc, :], mc == 0, mc == 1,
                                 [Vp.b(), PTt.b()], [pbuf(bo + d8 // 4)])
                    k.cp("act", ott[:, :, lsl], psum[:, bo:bo + 2, :].rearrange("p a (c n) -> p (a c) n", c=4),
                         [pbuf(bo), pbuf(bo + 1)], [ott.b()])

                ntile = tn // 128
                pend = scoresA(0)
                for tt in range(ntile):
                    nxt = scoresA(tt + 1, excl=(pend, pend + 1)) if tt + 1 < ntile else None
                    ex_ = (nxt, nxt + 1) if nxt is not None else ()
                    restB(tt, pend, tile_ctr % 2, ex_)
                    tile_ctr += 1
                    pend = nxt
                    if gt < 8:
                        for b in (2 * gt, 2 * gt + 1):
                            sample_K(b, ex_)
                        if gt == 7:
                            for i in range(2):
                                k.memset("pool", Qz[i][:], 0.0, [Qz[i].b()])
                            softmax_tile(SB0, (Ps, PTs, sts), ex_)
                    else:
                        for b in (2 * (gt - 8), 2 * (gt - 8) + 1):
                            sample_V(b)
                    gt += 1
                for m in range(DC):
                    bi = k.bank()
                    for kc in range(DC):
                        k.mm(psum[:, bi, 0:tn], wo[:, kc, m * 128:(m + 1) * 128], ott[:, kc, 0:tn], kc == 0, kc == DC - 1,
                             [wo.b(), ott.b()], [pbuf(bi)])
                    evac_add_xres(m, tbi, t0, tn, psum[:, bi, 0:tn], pbuf(bi))
            k.cp("act", os_[:], psum[:, SB0:SB0 + 2, :].rearrange("p a (c n) -> p (a c) n", c=4),
                 [pbuf(SB0), pbuf(SB0 + 1)], [os_.b()])
            k.perm_excl = ()
            for m in range(DC):
                bi = k.bank()
                for kc in range(DC):
                    k.mm(psum[:, bi, 0:tns], wo[:, kc, m * 128:(m + 1) * 128], os_[:, kc, :], kc == 0, kc == DC - 1,
                         [wo.b(), os_.b()], [pbuf(bi)])
                evac_add_xres(m, 4, t0s, tns, psum[:, bi, 0:tns], pbuf(bi))
            S.barrier()

    def pool_layer():
        with ExitStack() as ph:
            HP_ = 16
            up_ = k.sb("pl_up", [128, DC, HP_ + SEQ], BF16, ph)
            us_ = k.sb("pl_us", [128, DC, NSB, 24], BF16, ph)
            pooled_s = k.sb("pl_pooled_s", [128, DC, TS], BF16, ph)
            sqt = [k.sb("pl_sq%d" % i, [128, 512], BF16, ph) for i in range(2)]
            rst = [k.sb("pl_rs%d" % i, [128, 512], F32, ph) for i in range(2)]
            wA = k.sb("pl_wA", [128, 2, 2048], BF16, ph)
            wB = k.sb("pl_wB", [128, 2, 2048], BF16, ph)
            wp = k.sb("pl_wp", [128, 4, 2, 256], BF16, ph)
            ptmp = [k.sb("pl_ptmp%d" % i, [128, 512], F32, ph) for i in range(2)]
            invc = k.sb("pl_invc", [128, 4, 16], F32, ph)
            iot = k.sb("pl_iota", [128, 16], F32, ph)
            ph3 = ExitStack()
            hist = k.sb("pl_hist", [128, 2, D], F32, ph3)

            S.dma("pool", wp[:], w_pool.rearrange("g (k p) n -> p g k n", p=128), (), [wp.b()])
            S.op("pool", lambda h: h.iota(iot[:], [[1, 16]], base=1, channel_multiplier=0, allow_small_or_imprecise_dtypes=True), (), [iot.b()])
            for g, w in enumerate(POOL_W):
                k.ts("dve", invc[:, g, :], iot[:], float(w), ALU.min, [iot.b()], [invc.b()])
            S.op("dve", lambda h: h.reciprocal(invc[:], invc[:]), [invc.b()], [invc.b()])

            k.memset("pool", up_[:, :, 0:HP_], 0.0, [up_.b("hist")])
            k.memset("pool", us_[:, :, :, 0:1], 0.0, [us_.b()])
            S.dma("sp", hist[:, 0, :], spool[0:128, :], (), [hist.b()])
            S.dma("sp", hist[0:112, 1, :], spool[128:240, :], (), [hist.b()])
            usf = us_[:].rearrange("p c b j -> p c (b j)")
            for c in range(DC):
                bi = k.bank()
                k.tr(psum[:, bi, 0:128], hist[:, 0, c * 128:(c + 1) * 128], ident_f[:],
                     [hist.b(), ident_f.b()], [pbuf(bi)])
                k.tr(psum[:, bi, 128:240], hist[0:112, 1, c * 128:(c + 1) * 128], ident_f[0:112, 0:112],
                     [hist.b(), ident_f.b()], [pbuf(bi)])
                k.cp("dve", us_[:, c, :, 1:16], psum[:, bi, 0:240].rearrange("p (b j) -> p b j", j=15),
                     [pbuf(bi)], [us_.b()])

            S.barrier()
            ph3.close()
            outp = k.sb("pl_outp", [128, D], F32, ph)
            outs = k.sb("pl_outs", [128, D], F32, ph)

            def norm_out(c, tbi, t0, tn):
                if tbi < 4:
                    return up_[:, c, HP_ + t0:HP_ + t0 + tn], [up_.b((c, tbi))]
                return us_[:, c, :, 16:24], [us_.b()]
            sq_, rs_ = sqt, rst
            for tbi, (t0, tn) in ALL_TBS:
                bi = k.bank()
                for c in range(DC):
                    sq = sq_[c % 2]
                    k.act(sq[:, 0:tn], xres[:, c, t0:t0 + tn], AF.Square, xb(c, t0, tn), [sq.b()])
                    k.mm(psum[:, bi, 0:tn], ones_b[:], sq[:, 0:tn], c == 0, c == DC - 1, [ones_b.b(), sq.b()], [pbuf(bi)])
                rs = rs_[tbi % 2]
                k.act(rs[:, 0:tn], psum[:, bi, 0:tn], AF.Ln, [pbuf(bi)], [rs.b()], bias=EPS, scale=1.0 / D)
                k.act(rs[:, 0:tn], rs[:, 0:tn], AF.Exp, [rs.b()], [rs.b()], scale=-0.5)
                for c in range(DC):
                    o_ap, o_bufs = norm_out(c, tbi, t0, tn)
                    xin_ = xres[:, c, t0:t0 + tn]
                    rin_ = rs[:, 0:tn]
                    if tbi == 4:
                        xin_ = xin_.rearrange("p (b j) -> p b j", j=8)
                        rin_ = rin_.rearrange("p (b j) -> p b j", j=8)
                    k.stt(o_ap, xin_, colv[:, c, CV_MIX + 1:CV_MIX + 2], rin_, ALU.mult, ALU.mult,
                          xb(c, t0, tn) + [colv.b(), rs.b()], o_bufs)

            b0 = bank2()
            pvb = [psum[:, b0 + i, :].bitcast(BF16) for i in range(2)]
            for c in range(DC):
                k.tr(pvb[0][:, c * 128:(c + 1) * 128], up_[:, c, HP_ + SEQ - 128:HP_ + SEQ], ident_b[:],
                     [up_.b((c, 3)), ident_b.b()], [pbuf(b0)])
            k.cp("dve", outp[:], pvb[0], [pbuf(b0)], [outp.b()])
            S.dma("sp", pool_p[:, :], outp[113:128, :], [outp.b()], ())
            usn = k.sb("pl_usn", [128, DC, 128], BF16, ph)
            k.cp("pool", usn[:].rearrange("p c (b j) -> p c b j", j=8), us_[:, :, :, 16:24], [us_.b()], [usn.b()])
            for c in range(DC):
                k.tr(pvb[1][:, c * 128:(c + 1) * 128], usn[:, c, :], ident_b[:], [usn.b(), ident_b.b()], [pbuf(b0 + 1)])
            k.cp("dve", outs[:], pvb[1], [pbuf(b0 + 1)], [outs.b()])
            for b in range(NSB):
                S.dma("sp", pool_s[b * 15 + 7:b * 15 + 15, :], outs[b * 8:b * 8 + 8, :], [outs.b()], ())
                S.dma("sp", pool_s[b * 15:b * 15 + 7, :], spool[b * 15 + 8:b * 15 + 15, :], (), ())

            for g, w in enumerate(POOL_W):
                cs = slice(2 * g, 2 * g + 2)
                nst = g + 1
                L = HP_ + SEQ
                src = up_
                src_b = [up_.b((c, tb)) for c in (2 * g, 2 * g + 1) for tb in range(4)] + [up_.b("hist")]
                cur = None
                sh = 1
                for s in range(nst):
                    dst = wA if s % 2 == 0 else wB
                    if s == 0:
                        k.tt("dve", dst[:, :, 0:SEQ], up_[:, cs, HP_:L], up_[:, cs, HP_ - 1:L - 1], ALU.add,
                             src_b, [dst.b()])
                    else:
                        k.tt("dve", dst[:, :, sh:SEQ], cur[:, :, sh:SEQ], cur[:, :, 0:SEQ - sh], ALU.add,
                             [cur.b()], [dst.b()])
                        k.cp("dve", dst[:, :, 0:sh], cur[:, :, 0:sh], [cur.b()], [dst.b()])
                    cur = dst
                    sh *= 2
                tmp16 = wB if cur is wA else wA
                t16 = k.sb("pl_t16_%d" % g, [128, 2, 16], F32, ph)
                k.tt("dve", tmp16[:, :, 0:16], cur[:, :, 0:16], invc[:, g:g + 1, :].to_broadcast([128, 2, 16]), ALU.mult,
                     [cur.b(), invc.b()], [tmp16.b()])
                k.tt("dve", t16[:], tmp16[:, :, 0:16], up_[:, cs, HP_:HP_ + 16], ALU.subtract,
                     [tmp16.b()] + src_b, [t16.b()])
                k.stt(up_[:, cs, HP_:L], cur[:, :, 0:SEQ], 1.0 / w, up_[:, cs, HP_:L], ALU.mult, ALU.subtract,
                      [cur.b()] + src_b, src_b)
                k.cp("dve", up_[:, cs, HP_:HP_ + 16], t16[:], [t16.b()] + src_b, src_b)
                sA = k.sb("pl_sA%d" % g, [128, 2, NSB, 8], F32, ph)
                k.tt("dve", sA[:], us_[:, cs, :, 16:24], us_[:, cs, :, 15:23], ALU.add, [us_.b()], [sA.b()])
                for j in range(2, w):
                    k.tt("dve", sA[:], sA[:], us_[:, cs, :, 16 - j:24 - j], ALU.add, [us_.b(), sA.b()], [sA.b()])
                k.stt(pooled_s[:, cs, :].rearrange("p c (b j) -> p c b j", j=8), sA[:], 1.0 / w, us_[:, cs, :, 16:24],
                      ALU.mult, ALU.subtract, [sA.b(), us_.b()], [pooled_s.b(g)])
                for mo in range(2):
                    m = 2 * g + mo
                    for tbi, (t0, tn) in ALL_TBS:
                        bi = k.bank()
                        for kc in range(2):
                            if tbi < 4:
                                rhs_ = up_[:, 2 * g + kc, HP_ + t0:HP_ + t0 + tn]
                                rb_ = [up_.b((2 * g + kc, tbi))]
                            else:
                                rhs_ = pooled_s[:, 2 * g + kc, :]
                                rb_ = [pooled_s.b(g)]
                            k.mm(psum[:, bi, 0:tn], wp[:, g, kc, mo * 128:(mo + 1) * 128], rhs_,
                                 kc == 0, kc == 1, [wp.b()] + rb_, [pbuf(bi)])
                        pt_ = ptmp[(mo * 5 + tbi) % 2]
                        k.act(pt_[:, 0:tn], psum[:, bi, 0:tn], AF.Copy, [pbuf(bi), colv.b()], [pt_.b()],
                              scale=colv[:, m, CV_PSCALE:CV_PSCALE + 1])
                        k.tt("pool", xres[:, m, t0:t0 + tn], xres[:, m, t0:t0 + tn], pt_[:, 0:tn], ALU.add,
                             [pt_.b()] + xb(m, t0, tn), xb(m, t0, tn))

            S.barrier()

    def mamba_layer():
        with ExitStack() as ph:
            h = k.sb("mb_h", [128, DC, T], BF16, ph)
            with ExitStack() as ph0:
                sqt = [k.sb("mb_sq%d" % i, [128, 512], BF16, ph0) for i in range(2)]
                rst = [k.sb("mb_rs%d" % i, [128, 512], F32, ph0) for i in range(2)]
                rmsnorm_fm((sqt, rst), CV_MIX + 0,
                           lambda c, tbi, t0, tn: (h[:, c, t0:t0 + tn], [h.b((c, tbi))]), ALL_TBS)
                S.barrier()

            Umat = k.sb("mb_U", [128, 128], F32, ph)
            SameB = k.sb("mb_SB", [128, 128], F32, ph)
            Ublk = k.sb("mb_Ub", [128, 128], F32, ph)
            S.op("pool", lambda hh: hh.affine_select(Umat[:], ones_f[:], [[1, 128]], ALU.is_ge, 0.0,
                                                     base=0, channel_multiplier=-1), [ones_f.b()], [Umat.b()])
            S.op("pool", lambda hh: hh.affine_select(SameB[:].rearrange("p (b j) -> p b j", j=8),
                                                     ones_f[:].rearrange("p (b j) -> p b j", j=8),
                                                     [[8, 16], [0, 8]], ALU.is_ge, 0.0, base=7, channel_multiplier=-1),
                 [ones_f.b()], [SameB.b()])
            S.op("pool", lambda hh: hh.affine_select(SameB[:].rearrange("p (b j) -> p b j", j=8),
                                                     SameB[:].rearrange("p (b j) -> p b j", j=8),
                                                     [[-8, 16], [0, 8]], ALU.is_ge, 0.0, base=0, channel_multiplier=1),
                 [SameB.b()], [SameB.b()])
            k.tt("pool", Ublk[:], SameB[:], Umat[:], ALU.mult, [SameB.b(), Umat.b()], [Ublk.b()])
            brow = k.sb("mb_brow", [128, 3, NH], F32, ph)
            S.dma("sp", brow[:, 0, :], dt_bias.to_broadcast([128, NH]), (), [brow.b()])
            S.dma("sp", brow[:, 1, :], a_log.to_broadcast([128, NH]), (), [brow.b()])
            S.dma("sp", brow[:, 2, :], d_skip.to_broadcast([128, NH]), (), [brow.b()])
            k.act(brow[:, 1, :], brow[:, 1, :], AF.Exp, [brow.b()], [brow.b()])
            k.ts("dve", brow[:, 1, :], brow[:, 1, :], -1.0, ALU.mult, [brow.b()], [brow.b()])
            wdt = k.sb("mb_wdt", [128, DC, NH], BF16, ph)
            S.dma("pool", wdt[:], w_in[:, 6144:6176].rearrange("(k p) n -> p k n", p=128), (), [wdt.b()])

            dt_a = k.sb("mb_dt", [128, NT, NH], F32, ph)
            cd_a = k.sb("mb_cd", [128, NT, NH], F32, ph)
            dtd_a = k.sb("mb_dtd", [128, NT, NH], F32, ph)
            eacs_a = k.sb("mb_eacs", [128, NT, NH], F32, ph)
            cdp2 = k.sb("mb_cdp2", [128, NSB, 16], F32, ph)
            nb_a = k.sb("mb_nb", [128, NT, NH], F32, ph)
            dtAh = k.sb("mb_dtAh", [128, NT, NH], BF16, ph)
            dtAl = k.sb("mb_dtAl", [128, NT, NH], BF16, ph)
            phT = ExitStack()
            dtA_a = k.sb("mb_dtA", [128, NT, NH], F32, phT)
            nacs_a = k.sb("mb_nacs", [128, NT, NH], F32, phT)
            tmp32 = k.sb("mb_tmp32", [128, NH], F32, phT)
            Xs = k.sb("mb_Xs", [128, NSB, NH], F32, phT)
            CDB = k.sb("mb_CDB", [128, NSB, NH], F32, phT)
            dtAf = k.sb("mb_dtAf", [128, NT, NH], F32, phT)
            for t in range(NT):
                Um = Umat if t < 16 else Ublk
                Jm = ones_f if t < 16 else SameB
                bi = k.bank()
                for kc in range(DC):
                    k.mm(psum[:, bi, 0:NH], h[:, kc, t * 128:(t + 1) * 128], wdt[:, kc, :], kc == 0, kc == DC - 1,
                         [h.b((kc, min(t // 4, 4))), wdt.b()], [pbuf(bi)])
                k.tt("dve", tmp32[:], psum[:, bi, 0:NH], brow[:, 0, :], ALU.add, [pbuf(bi), brow.b()], [tmp32.b()])
                k.act(tmp32[:], tmp32[:], AF.Exp, [tmp32.b()], [tmp32.b()])
                k.act(dt_a[:, t, :], tmp32[:], AF.Ln, [tmp32.b()], [dt_a.b(t)], bias=1.0, scale=1.0)
                k.tt("dve", dtA_a[:, t, :], dt_a[:, t, :], brow[:, 1, :], ALU.mult, [dt_a.b(t), brow.b()], [dtA_a.b(t)])
                bi = k.bank()
                k.mm(psum[:, bi, 0:NH], Um[:], dtA_a[:, t, :], True, True, [Um.b(), dtA_a.b(t)], [pbuf(bi)])
                k.mm(psum[:, bi, 64:64 + NH], Jm[:], dtA_a[:, t, :], True, True, [Jm.b(), dtA_a.b(t)], [pbuf(bi)])
                k.ts("dve", nacs_a[:, t, :], psum[:, bi, 0:NH], -1.0, ALU.mult, [pbuf(bi)], [nacs_a.b(t)])
                k.act(eacs_a[:, t, :], psum[:, bi, 0:NH], AF.Exp, [pbuf(bi)], [eacs_a.b(t)])
                k.act(cd_a[:, t, :], psum[:, bi, 64:64 + NH], AF.Exp, [pbuf(bi)], [cd_a.b(t)])
                k.tt("dve", tmp32[:], psum[:, bi, 64:64 + NH], nacs_a[:, t, :], ALU.add,
                     [pbuf(bi), nacs_a.b(t)], [tmp32.b()])
                k.act(tmp32[:], tmp32[:], AF.Exp, [tmp32.b()], [tmp32.b()])
                k.tt("dve", dtd_a[:, t, :], tmp32[:], dt_a[:, t, :], ALU.mult, [tmp32.b(), dt_a.b(t)], [dtd_a.b(t)])
                k.act(tmp32[:], dt_a[:, t, :], AF.Ln, [dt_a.b(t)], [tmp32.b()])
                k.tt("dve", nb_a[:, t, :], tmp32[:], nacs_a[:, t, :], ALU.add, [tmp32.b(), nacs_a.b(t)], [nb_a.b(t)])

            k.cp("dve", dtAh[:], dtA_a[:], [dtA_a.b(t_) for t_ in range(NT)], [dtAh.b()])
            k.cp("dve", dtAf[:], dtAh[:], [dtAh.b()], [dtAf.b()])
            k.tt("dve", dtAl[:], dtA_a[:], dtAf[:], ALU.subtract, [dtA_a.b(t_) for t_ in range(NT)] + [dtAf.b()], [dtAl.b()])
            k.tt("dve", Xs[:], dtA_a[:, 16:17, :].to_broadcast([128, NSB, NH]),
                 SameB[:, 0:128:8].unsqueeze(2).to_broadcast([128, NSB, NH]), ALU.mult,
                 [dtA_a.b(16), SameB.b()], [Xs.b()])
            bi = k.bank()
            k.mm(psum[:, bi, :], ones_f[:], Xs[:].rearrange("p b h -> p (b h)"), True, True,
                 [ones_f.b(), Xs.b()], [pbuf(bi)])
            k.act(CDB[:].rearrange("p b h -> p (b h)"), psum[:, bi, :], AF.Exp, [pbuf(bi)], [CDB.b()])
            k.cp("dve", cdp2[0:64, :, :], CDB[0:64, :, 0:NH:2], [CDB.b()], [cdp2.b()])
            k.cp("dve", cdp2[64:128, :, :], CDB[64:128, :, 1:NH:2], [CDB.b()], [cdp2.b()])

            S.barrier()
            phT.close()
            wz = k.sb("mb_wz", [128, DC, 256], BF16, ph)
            wx = [k.sb("mb_wx%d" % i, [128, DC, 512], BF16, ph) for i in range(2)]
            wog = [k.sb("mb_wog0", [128, 2, D], BF16, ph)] * 2
            dgw = k.sb("mb_dgw", [128, 4, 4, 128], BF16, ph)
            rawt = [k.sb("mb_raw%d" % i, [128, 4, 3 + 512], BF16, ph) for i in range(2)]
            carry = k.sb("mb_carry", [128, 4, 3], BF16, ph)
            raws = k.sb("mb_raws", [128, 4, NSB, 11], BF16, ph)
            scv = k.sb("mb_scv", [48, 4, 128], F32, ph)
            ncv = k.sb("mb_ncv", [128, 4, 51], F32, ph)
            ncvo = k.sb("mb_ncvo", [128, 4, 128], F32, ph)
            hout = ncvo
            xact = [k.sb("mb_xact%d" % i, [128, 4, 512], BF16, ph) for i in range(2)]
            tht = [k.sb("mb_th%d" % i, [128, 512], BF16, ph) for i in range(2)]
            vht = [k.sb("mb_vh%d" % i, [128, 512], BF16, ph) for i in range(2)]
            cbh = k.sb("mb_cbh", [128, 32], F32, ph)
            k.ts("dve", cbh[:], colv[:, :, 4], 0.5, ALU.mult, [colv.b()], [cbh.b()])
            ygT = k.sb("mb_ygT", [128, 2, 512], BF16, ph)
            ngrow = [k.sb("mb_ngrow%d" % i, [128, 256], F32, ph) for i in range(2)]
            zs4 = [k.sb("mb_zs4_%d" % i, [128, 4, 256], BF16, ph) for i in range(2)]
            xdts = [k.sb("mb_xdts%d" % i, [128, 4, HP], BF16, ph) for i in range(2)]
            xB = [k.sb("mb_xB%d" % i, [128, 384], BF16, ph) for i in range(2)]
            Dg = k.sb("mb_Dg", [128, 4, 128], BF16, ph)
            cbs = [k.sb("mb_cbs%d" % i, [128, 128], BF16, ph) for i in range(2)]
            U_b = [k.sb("mb_Ubf%d" % i, [128, 128], BF16, ph) for i in range(2)]
            k.cp("pool", U_b[0][:], Umat[:], [Umat.b()], [U_b[0].b()])
            k.cp("pool", U_b[1][:], Ublk[:], [Ublk.b()], [U_b[1].b()])
            dcy = [k.sb("mb_dcy%d" % i, [128, 128], BF16, ph) for i in range(4)]
            MT = [k.sb("mb_MT%d" % i, [128, 128], BF16, ph) for i in range(8)]
            Neg4 = [k.sb("mb_Neg%d" % i, [128, 128], BF16, ph) for i in range(2)]
            for i, Us in enumerate((Umat, Ublk)):
                k.ts("dve", Neg4[i][:], Us[:], -1.0, ALU.add, [Us.b()], [Neg4[i].b()], s2=30000.0, op1=ALU.mult)
            t1 = k.sb("mb_t1", [128, 4, HP], F32, ph)
            yg = k.sb("mb_yg", [128, 256], F32, ph)
            ygn = [k.sb("mb_ygn%d" % i, [128, 256], BF16, ph) for i in range(2)]
            yst = k.sb("mb_yst", [128, 4], F32, ph)
            mhalf = k.sb("mb_mhalf", [128, 1], F32, ph)
            k.memset("pool", mhalf[:], -0.5, [mhalf.b()])
            hTf = k.sb("mb_hTf", [128, 256], F32, ph)
            hTb = k.sb("mb_hTb", [128, 256], BF16, ph)
            h0s = [k.sb("mb_h0s%d" % i, [128, 2, 2, 128], F32, ph) for i in range(4)]
            h0T = k.sb("mb_h0T", [128, 2, 256], BF16, ph)
            CTz = [k.sb("mb_CTz%d" % i, [128, 128], BF16, ph) for i in range(2)]
            Bm = [k.sb("mb_Bm%d" % i, [128, 128], BF16, ph) for i in range(2)]
            for i in range(2):
                k.memset("pool", CTz[i][:], 0.0, [CTz[i].b()])

            def cglob_of(gg):
                return [2 * gg, 2 * gg + 1, 16 + gg, 24 + gg]

            def load_wx(gg):
                for (dst0, src0, n) in ((0, DI + gg * 256, 256), (256, 2 * DI + gg * 128, 128),
                                        (384, 2 * DI + 1024 + gg * 128, 128)):
                    S.dma("pool", wx[gg % 2][:, :, dst0:dst0 + n],
                          w_in[:, src0:src0 + n].rearrange("(k p) n -> p k n", p=128), (), [wx[gg % 2].b()])

            def load_h0(gg, e8):
                for b2 in range(2):
                    S.dma("sp", h0s[e8 % 4][:, b2, :, :], ssm[e8 * 2 + b2, gg * 256:(gg + 1) * 256, :]
                          .rearrange("(a p) n -> p a n", p=128), (), [h0s[e8 % 4].b()])

            def setup_early(gg):
                cg = cglob_of(gg)
                if gg == 0:
                    load_wx(0)
                S.dma("pool", wz[:], w_in[:, gg * 256:(gg + 1) * 256].rearrange("(k p) n -> p k n", p=128), (), [wz.b()])
                if gg + 1 < NG:
                    load_wx(gg + 1)
                S.dma("sp", ngrow[gg % 2][:], norm_gated[:, gg * 256:(gg + 1) * 256].to_broadcast([128, 256]), (),
                      [ngrow[gg % 2].b()])
                for ci in range(4):
                    S.dma("sp", scv[:, ci, :], sconv[:, cg[ci] * 128:(cg[ci] + 1) * 128], (), [scv.b()])
                for ci in range(4):
                    for tap in range(4):
                        k.ts("dve", dgw[:, ci, tap, :], ident_f[:], colv[:, cg[ci], tap:tap + 1], ALU.mult,
                             [ident_f.b(), colv.b()], [dgw.b()])
                k.memset("dve", carry[:], 0.0, [carry.b()])

            def setup_late(gg):
                S.dma("pool", wog[0][:], w_out[gg * 256:(gg + 1) * 256, :].rearrange("(k p) n -> p k n", p=128), (), [wog[0].b()])
                for r in range(4):
                    k.ts("dve", Dg[:, r, :], ident_f[:], brow[:, 2, 4 * gg + r:4 * gg + r + 1], ALU.mult,
                         [ident_f.b(), brow.b()], [Dg.b()])

            def P_units(gg, tbi, bset):
                t0, tn = TBS[tbi]
                cglob = cglob_of(gg)
                wxg = wx[gg % 2]
                rw, xa, zz = rawt[bset], xact[bset], zs4[bset]
                units = []

                def u_hist():
                    bi = k.bank()
                    for ci in range(4):
                        k.tr(psum[:, bi, ci * 48:(ci + 1) * 48], scv[:, ci, :], ident_f[0:48, 0:48],
                             [scv.b(), ident_f.b()], [pbuf(bi)])
                    k.cp("dve", raws[:, :, :, 0:3], psum[:, bi, 0:192].rearrange("p (c b j) -> p c b j", c=4, j=3),
                         [pbuf(bi)], [raws.b()])
                if tbi == 4:
                    units.append(u_hist)

                def u_in(ci):
                    if ci == 0 and tbi < 4:
                        k.cp("dve", rw[:, :, 0:3], carry[:], [carry.b()], [rw.b()])
                    bi = k.bank()
                    for kc in range(DC):
                        k.mm(psum[:, bi, 0:tn], wxg[:, kc, ci * 128:(ci + 1) * 128], h[:, kc, t0:t0 + tn],
                             kc == 0, kc == DC - 1, [wxg.b(), h.b((kc, tbi))], [pbuf(bi)])
                    if tbi < 4:
                        k.cp("act", rw[:, ci, 3:3 + tn], psum[:, bi, 0:tn], [pbuf(bi)], [rw.b()])
                        if tbi == 3:
                            k.cp("dve", ncv[:, ci, 48:51], psum[:, bi, 509:512], [pbuf(bi)], [ncv.b()])
                    else:
                        pvv = psum[:, bi, 0:128].rearrange("p (b j) -> p b j", j=8)
                        k.cp("act", raws[:, ci, :, 3:11], pvv, [pbuf(bi)], [raws.b()])
                        k.cp("dve", ncv[:, ci, 0:48].rearrange("p (b j) -> p b j", j=3), pvv[:, :, 5:8],
                             [pbuf(bi)], [ncv.b()])
                    if ci == 3 and tbi < 3:
                        k.cp("dve", carry[:], rw[:, :, 512:515], [rw.b()], [carry.b()])

                def u_cv(ci):
                    bi = k.bank()
                    for tap in range(4):
                        if tbi < 4:
                            rhs_ = rw[:, ci, tap:tap + tn]
                            rb_ = rw.b()
                        else:
                            rhs_ = raws[:, ci, :, tap:tap + 8]
                            rb_ = raws.b()
                        k.mm(psum[:, bi, 0:tn], dgw[:, ci, tap, :], rhs_, tap == 0, tap == 3,
                             [dgw.b(), rb_], [pbuf(bi)])
                    th_, vh_ = tht[ci % 2], vht[ci % 2]
                    k.act(th_[:, 0:tn], psum[:, bi, 0:tn], AF.Tanh, [pbuf(bi), cbh.b()], [th_.b()],
                          bias=cbh[:, cglob[ci]:cglob[ci] + 1], scale=0.5)
                    k.act(vh_[:, 0:tn], psum[:, bi, 0:tn], AF.Identity, [pbuf(bi), cbh.b()], [vh_.b()],
                          bias=cbh[:, cglob[ci]:cglob[ci] + 1], scale=0.5)
                    k.stt(xa[:, ci, 0:tn], th_[:, 0:tn], 1.0, vh_[:, 0:tn], ALU.add, ALU.mult,
                          [th_.b(), vh_.b()], [xa.b()])

                def u_z(tt):
                    t = t0 // 128 + tt
                    bi = k.bank()
                    for kc in range(DC):
                        k.mm(psum[:, bi, 0:256], h[:, kc, t * 128:(t + 1) * 128], wz[:, kc, :],
                             kc == 0, kc == DC - 1, [h.b((kc, tbi)), wz.b()], [pbuf(bi)])
                    th_, vh_ = tht[tt % 2], vht[tt % 2]
                    k.act(th_[:, 0:256], psum[:, bi, 0:256], AF.Tanh, [pbuf(bi)], [th_.b()], scale=0.5)
                    k.act(vh_[:, 0:256], psum[:, bi, 0:256], AF.Identity, [pbuf(bi)], [vh_.b()], scale=0.5)
                    k.stt(zz[:, tt, :], th_[:, 0:256], 1.0, vh_[:, 0:256], ALU.add, ALU.mult,
                          [th_.b(), vh_.b()], [zz.b(tt)])

                for ci in range(4):
                    units.append(lambda ci=ci: u_in(ci))
                for ci in range(4):
                    units.append(lambda ci=ci: u_cv(ci))
                for tt in range(tn // 128):
                    units.append(lambda tt=tt: u_z(tt))
                return units

            setup_early(0)
            for e8 in range(4):
                load_h0(0, e8)
            for u_ in P_units(0, 0, 0):
                u_()
            for g in range(NG):
                hd0 = 4 * g
                cglob = cglob_of(g)
                wo = wog[0]
                ngrow_c = ngrow[g % 2]
                setup_late(g)
                for tbi, (t0, tn) in ALL_TBS:
                    bidx = g * len(TBS) + tbi
                    xact_c, zs4_c = xact[bidx % 2], zs4[bidx % 2]
                    if tbi + 1 < len(TBS):
                        nxt_units = P_units(g, tbi + 1, (bidx + 1) % 2)
                    elif g + 1 < NG:
                        setup_early(g + 1)
                        nxt_units = P_units(g + 1, 0, (bidx + 1) % 2)
                    else:
                        nxt_units = []

                    def head(tt):
                        t = t0 // 128 + tt
                        sl = t % 2
                        lsl = slice(tt * 128, (tt + 1) * 128)
                        Ub_ = U_b[0] if t < 16 else U_b[1]
                        Ng = Neg4[0] if t < 16 else Neg4[1]
                        bt = k.bank()
                        pv = psum[:, bt, :].bitcast(BF16)
                        for ci in range(3):
                            k.tr(pv[:, ci * 128:(ci + 1) * 128], xact_c[:, ci, lsl], ident_b[:],
                                 [xact_c.b(), ident_b.b()], [pbuf(bt)])
                        xv = pv[:, 0:256].rearrange("p (r q) -> p r q", q=HP)
                        k.tt("dve", xdts[sl][:], xv, dtd_a[:, t, hd0:hd0 + 4].unsqueeze(2).to_broadcast([128, 4, HP]), ALU.mult,
                             [pbuf(bt), dtd_a.b(t)], [xdts[sl].b()])
                        k.cp("act", xB[sl][:], pv[:, 0:384], [pbuf(bt)], [xB[sl].b()])
                        bc = k.bank()
                        k.mm(psum[:, bc, 0:128], xact_c[:, 2, lsl], xact_c[:, 3, lsl], True, True, [xact_c.b()], [pbuf(bc)])
                        k.cp("act", cbs[sl][:], psum[:, bc, 0:128], [pbuf(bc)], [cbs[sl].b()])
                        br = k.bank()
                        for r in range(4):
                            k.mm(psum[:, br, r * 128:(r + 1) * 128], ident_b[:], Ng[:], r == 0, False,
                                 [ident_b.b(), Ng.b()], [pbuf(br)])
                        for r in range(4):
                            k.mm(psum[:, br, r * 128:(r + 1) * 128],
                                 dtAh[:, t, hd0 + r:hd0 + r + 1].to_broadcast([128, 128]), Ub_[:], False, False,
                                 [dtAh.b(), Ub_.b()], [pbuf(br)])
                            k.mm(psum[:, br, r * 128:(r + 1) * 128],
                                 dtAl[:, t, hd0 + r:hd0 + r + 1].to_broadcast([128, 128]), Ub_[:], False, r == 3,
                                 [dtAl.b(), Ub_.b()], [pbuf(br)])
                        for r in range(4):
                            dc_ = dcy[(t * 4 + r) % 4]
                            mt_ = MT[(t % 2) * 4 + r]
                            k.act(dc_[:], psum[:, br, r * 128:(r + 1) * 128], AF.Exp, [pbuf(br), nb_a.b(t)], [dc_.b()],
                                  bias=nb_a[:, t, hd0 + r:hd0 + r + 1], scale=1.0)
                            k.tt("pool", mt_[:], dc_[:], cbs[sl][:], ALU.mult, [dc_.b(), cbs[sl].b()], [mt_.b()])
                        return None

                    def tail(tt, banks):
                        t = t0 // 128 + tt
                        sl = t % 2
                        lsl = slice(tt * 128, (tt + 1) * 128)
                        has_off = True
                        by = k.bank()
                        for r in range(4):
                            mt_ = MT[(t % 2) * 4 + r]
                            k.mm(psum[:, by, r * HP:(r + 1) * HP], mt_[:], xB[sl][:, r * HP:(r + 1) * HP], True, False,
                                 [mt_.b(), xB[sl].b()], [pbuf(by)])
                            k.mm(psum[:, by, r * HP:(r + 1) * HP], Dg[:, r, :], xB[sl][:, r * HP:(r + 1) * HP], False, True,
                                 [Dg.b(), xB[sl].b()], [pbuf(by)])
                        bs_ = None
                        if t < 16:
                            bs_ = k.bank()
                            k.mm(psum[:, bs_, 0:256], xB[sl][:, 256:384], xdts[sl][:].rearrange("p r q -> p (r q)"), True, True,
                                 [xB[sl].b(), xdts[sl].b()], [pbuf(bs_)])
                        bo = k.bank()
                        held = (by, bo)
                        if t < 16:
                            if t == 0:
                                has_off = False
                            else:
                                k.mm(psum[:, bo, 0:256], xact_c[:, 3, lsl], hTb[:], True, True, [xact_c.b(), hTb.b()], [pbuf(bo)])
                        else:
                            k.perm_excl = held
                            for e8 in range(8):
                                hs_ = h0s[e8 % 4]
                                bh = k.bank(excl=held)
                                for b2 in range(2):
                                    for a in range(2):
                                        k.tr(psum[:, bh, (b2 * 2 + a) * 128:(b2 * 2 + a + 1) * 128],
                                             hs_[:, b2, a, :], ident_f[:], [hs_.b(), ident_f.b()], [pbuf(bh)])
                                k.cp("act" if e8 % 2 else "dve", h0T[:],
                                     psum[:, bh, :].rearrange("p (b q) -> p b q", b=2), [pbuf(bh)], [h0T.b()])
                                for b4 in range(2):
                                    b = e8 * 2 + b4
                                    cz = CTz[b % 2]
                                    if b >= 2:
                                        k.memset("pool", cz[:, (b - 2) * 8:(b - 2) * 8 + 8], 0.0, [cz.b()])
                                    k.cp("pool", cz[:, b * 8:b * 8 + 8], xact_c[:, 3, b * 8:b * 8 + 8], [xact_c.b()], [cz.b()])
                                    k.mm(psum[:, bo, 0:256], cz[:], h0T[:, b4, :], b == 0, b == NSB - 1,
                                         [cz.b(), h0T.b()], [pbuf(bo)])
                                    bmt = Bm[b % 2]
                                    k.ts("pool", bmt[:], xB[sl][:, 256:384], SameB[:, b * 8:b * 8 + 1], ALU.mult,
                                         [xB[sl].b(), SameB.b()], [bmt.b()])
                                    for a in range(2):
                                        bn = k.bank(excl=held)
                                        k.mm(psum[:, bn, 0:128], xdts[sl][:, 2 * a:2 * a + 2, :].rearrange("p r q -> p (r q)"),
                                             bmt[:], True, True, [xdts[sl].b(), bmt.b()], [pbuf(bn)])
                                        k.stt(hs_[:, b4, a, :], hs_[:, b4, a, :], cdp2[:, b, 2 * g + a:2 * g + a + 1],
                                              psum[:, bn, 0:128], ALU.mult, ALU.add, [hs_.b(), cdp2.b(), pbuf(bn)], [hs_.b()])
                                for b4 in range(2):
                                    S.dma("sp", ssm_s[e8 * 2 + b4, g * 256:(g + 1) * 256, :]
                                          .rearrange("(a p) n -> p a n", p=128), hs_[:, b4, :, :], [hs_.b()], ())
                                if e8 + 4 < 8:
                                    load_h0(g, e8 + 4)
                                elif g + 1 < NG:
                                    load_h0(g + 1, e8 - 4)
                                for _ in range(2):
                                    if ucur[0] < len(nxt_units):
                                        nxt_units[ucur[0]]()
                                        ucur[0] += 1
                            k.perm_excl = ()
                            for i in range(2):
                                k.memset("pool", CTz[i][:], 0.0, [CTz[i].b()])
                        if t < 16:
                            if t == 0:
                                k.cp("dve", hTf[:], psum[:, bs_, 0:256], [pbuf(bs_)], [hTf.b()])
                            else:
                                hv_ = hTf[:].rearrange("p (r q) -> p r q", q=HP)
                                k.tt("dve", hv_, hv_, cd_a[:, t, hd0:hd0 + 4].unsqueeze(2).to_broadcast([128, 4, HP]), ALU.mult,
                                     [hTf.b(), cd_a.b(t)], [hTf.b()])
                                k.tt("dve", hTf[:], hTf[:], psum[:, bs_, 0:256], ALU.add, [hTf.b(), pbuf(bs_)], [hTf.b()])
                            if t < 15:
                                k.cp("pool", hTb[:], hTf[:], [hTf.b()], [hTb.b()])
                            else:
                                bf_ = k.bank(excl=held)
                                for a in range(2):
                                    k.tr(psum[:, bf_, a * 128:(a + 1) * 128], hTf[:, a * 128:(a + 1) * 128], ident_f[:],
                                         [hTf.b(), ident_f.b()], [pbuf(bf_)])
                                k.cp("dve", hout[:, 0:2, :], psum[:, bf_, 0:256].rearrange("p (a n) -> p a n", a=2), [pbuf(bf_)], [hout.b()])
                                S.dma("sp", ssm_p[g * 256:(g + 1) * 256, :].rearrange("(a p) n -> p a n", p=128), hout[:, 0:2, :],
                                      [hout.b()], ())
                        yv = psum[:, by, 0:256]
                        if has_off:
                            k.tt("dve", t1[:], psum[:, bo, 0:256].rearrange("p (r q) -> p r q", q=HP),
                                 eacs_a[:, t, hd0:hd0 + 4].unsqueeze(2).to_broadcast([128, 4, HP]), ALU.mult,
                                 [pbuf(bo), eacs_a.b(t)], [t1.b()])
                            k.tt("dve", yg[:], yv, t1[:].rearrange("p r q -> p (r q)"), ALU.add, [pbuf(by), t1.b()], [yg.b()])
                            k.tt("dve", yg[:], yg[:], zs4_c[:, tt, :], ALU.mult, [yg.b(), zs4_c.b(tt)], [yg.b()])
                        else:
                            k.tt("dve", yg[:], yv, zs4_c[:, tt, :], ALU.mult, [pbuf(by), zs4_c.b(tt)], [yg.b()])
                        yn_ = ygn[t % 2]
                        k.act(yn_[:], yg[:], AF.Square, [yg.b()], [yn_.b(), yst.b()], accum_out=yst[:, 0:1])
                        k.ts("pool", yst[:, 1:2], yst[:, 0:1], 1.0 / 256, ALU.mult, [yst.b()], [yst.b()], s2=EPS, op1=ALU.add)
                        k.tt("pool", yst[:, 2:3], yst[:, 1:2], mhalf[:, 0:1], ALU.pow, [yst.b(), mhalf.b()], [yst.b()])
                        yn_ = ygn[t % 2]
                        k.stt(yn_[:], yg[:], yst[:, 2:3], ngrow_c[:], ALU.mult, ALU.mult, [yg.b(), yst.b(), ngrow_c.b()], [yn_.b()])

                    def fin(tt):
                        t = t0 // 128 + tt
                        lsl = slice(tt * 128, (tt + 1) * 128)
                        yn_ = ygn[t % 2]
                        bg = k.bank()
                        pg = psum[:, bg, :].bitcast(BF16)
                        for a in range(2):
                            k.tr(pg[:, a * 128:(a + 1) * 128], yn_[:, a * 128:(a + 1) * 128], ident_b[:],
                                 [yn_.b(), ident_b.b()], [pbuf(bg)])
                        k.cp("act", ygT[:, :, lsl], pg[:, 0:256].rearrange("p (a n) -> p a n", a=2), [pbuf(bg)], [ygT.b()])

                    ntile = tn // 128
                    per_step = -(-len(nxt_units) // ntile)
                    ucur = [0]
                    head(0)
                    for tt in range(ntile):
                        if tt + 1 < ntile:
                            head(tt + 1)
                        if tt >= 1:
                            fin(tt - 1)
                        tail(tt, None)
                        lim = min(len(nxt_units), (tt + 1) * per_step)
                        while ucur[0] < lim:
                            nxt_units[ucur[0]]()
                            ucur[0] += 1
                    fin(ntile - 1)
                    for m in range(DC):
                        bi = k.bank()
                        for kc in range(2):
                            k.mm(psum[:, bi, 0:tn], wo[:, kc, m * 128:(m + 1) * 128], ygT[:, kc, 0:tn], kc == 0, kc == 1,
                                 [wo.b(), ygT.b()], [pbuf(bi)])
                        evac_add_xres(m, tbi, t0, tn, psum[:, bi, 0:tn], pbuf(bi))
                bi = k.bank()
                for ci in range(4):
                    k.tr(psum[0:51, bi, ci * 128:(ci + 1) * 128], ncv[:, ci, :], ident_f[:], [ncv.b(), ident_f.b()], [pbuf(bi)])
                k.cp("dve", ncvo[0:51, :, :], psum[0:51, bi, :].rearrange("p (c n) -> p c n", c=4), [pbuf(bi)], [ncvo.b()])
                for ci in range(4):
                    cs_ = slice(cglob[ci] * 128, (cglob[ci] + 1) * 128)
                    S.dma("sp", conv_s[:, cs_], ncvo[0:48, ci, :], [ncvo.b()], ())
                    S.dma("sp", conv_p[:, cs_], ncvo[48:51, ci, :], [ncvo.b()], ())
            S.barrier()

    if cfg["mamba"]:
        mamba_layer()
    if cfg["attn"]:
        attn_layer(0)
    if cfg["mlp"]:
        mlp_layer(0)
    if cfg["pool"]:
        pool_layer()
    if cfg["attn"]:
        attn_layer(1)
    if cfg["mlp"]:
        mlp_layer(1)

    with ExitStack() as ph:
        gfin = k.sb("gfin", [128, D], F32, ph)
        S.dma("sp", gfin[:], norm_final.to_broadcast([128, D]), (), [gfin.b()])
        yt = [k.sb("yt%d" % i, [128, D], F32, ph) for i in range(4)]
        sq = k.sb("sq_scr", [128, D], F32, ph)
        stat = [k.sb("stat%d" % i, [128, 4], F32, ph) for i in range(2)]
        for t in range(NT):
            b0 = bank2()
            for c in range(DC):
                bi = b0 + c // 4
                k.tr(psum[:, bi, (c % 4) * 128:(c % 4 + 1) * 128], xres[:, c, t * 128:(t + 1) * 128], ident_f[:],
                     [xres.b((c, t)), ident_f.b()], [pbuf(bi)])
            st = stat[t % 2]
            y = yt[t % 4]
            pin = psum[:, b0:b0 + 2, :].rearrange("p a n -> p (a n)")
            k.act(sq[:], pin, AF.Square, [pbuf(b0), pbuf(b0 + 1)], [sq.b(), st.b()], accum_out=st[:, 0:1])
            k.act(st[:, 1:2], st[:, 0:1], AF.Ln, [st.b()], [st.b()], bias=EPS, scale=1.0 / D)
            k.act(st[:, 2:3], st[:, 1:2], AF.Exp, [st.b()], [st.b()], scale=-0.5)
            k.stt(y[:], pin, st[:, 2:3], gfin[:], ALU.mult, ALU.mult,
                  [pbuf(b0), pbuf(b0 + 1), st.b(), gfin.b()], [y.b()])
            dst = y_p[t * 128:(t + 1) * 128, :] if t < 16 else y_s[:, :]
            S.dma("sp", dst, y[:], [y.b()], ())
        S.barrier()
    print("ops", S.nops, "waits", S.nwaits)
    return k


_CACHE = {}


def _get_program():
    if "k" not in _CACHE:
        _CACHE["k"] = build_program()
    return _CACHE["k"]


def kernel(**inputs):
    inp = {k_: np.asarray(v) for k_, v in inputs.items()}
    kk = _get_program()
    f = lambda a: np.ascontiguousarray(a, dtype=np.float32)
    shared = {
        "norm_mix": f(inp["norm_mix"]), "norm_xattn": f(inp["norm_xattn"]), "norm_mem": f(inp["norm_mem"]),
        "norm_mlp": f(inp["norm_mlp"]), "norm_final": f(inp["norm_final"].reshape(1, D)),
        "w_in": f(inp["w_in"][0]), "conv_w": f(inp["conv_w"][0]), "conv_b": f(inp["conv_b"].reshape(1, CONV_DIM)),
        "dt_bias": f(inp["dt_bias"].reshape(1, NH)), "a_log": f(inp["a_log"].reshape(1, NH)),
        "d_skip": f(inp["d_skip"].reshape(1, NH)), "norm_gated": f(inp["norm_gated"].reshape(1, DI)),
        "w_out": f(inp["w_out"][0]), "w_pool": f(inp["w_pool"][0]), "pool_scale": f(inp["pool_scale"].reshape(1, D)),
        "w_xq": f(inp["w_xq"]), "w_xk": f(inp["w_xk"]), "w_xv": f(inp["w_xv"]), "w_xo": f(inp["w_xo"]),
        "w_up": f(inp["w_up"]), "w_down": f(inp["w_down"]),
    }
    in_maps = []
    for c in range(NCORES):
        sl = slice(c * NSB, (c + 1) * NSB)
        m = dict(shared)
        m.update({
            "xp": f(inp["x_prompt"][c]),
            "xs": f(inp["x_sample"][sl].reshape(TS, D)),
            "ck": f(inp["cache_mem_k"][:, sl].reshape(2, NSB, NMEM, D)),
            "cv": f(inp["cache_mem_v"][:, sl].reshape(2, NSB, NMEM, D)),
            "ssm": f(inp["state_ssm"][0, sl].reshape(NSB, DI, NS)),
            "sconv": f(inp["state_conv"][0, sl].reshape(NSB * 3, CONV_DIM)),
            "spool": f(inp["state_pool"][0, sl].reshape(NSB * 15, D)),
            "mem": f(inp["mem_prompt"][c]),
        })
        in_maps.append(m)
    res = run_bass_kernel_spmd(kk.nc, in_maps, core_ids=list(range(NCORES)))
    R = res.results
    cat = lambda name, shp: np.concatenate([R[c][name].reshape(shp) for c in range(NCORES)], 0)
    y_prompt = np.stack([R[c]["y_p"] for c in range(NCORES)], 0)
    y_sample = cat("y_s", (NSB, DSEQ, D))
    mk = np.stack([R[c]["mk_p"].reshape(2, NMEM, XH, XD) for c in range(NCORES)], 1)
    mv = np.stack([R[c]["mv_p"].reshape(2, NMEM, XH, XD) for c in range(NCORES)], 1)
    ssm_p = np.stack([R[c]["ssm_p"].reshape(NH, HP, NS) for c in range(NCORES)], 0)[None]
    conv_p = np.stack([R[c]["conv_p"] for c in range(NCORES)], 0)[None]
    pool_p = np.stack([R[c]["pool_p"] for c in range(NCORES)], 0)[None]
    ssm_s = cat("ssm_s", (NSB, NH, HP, NS))[None]
    conv_s = cat("conv_s", (NSB, 3, CONV_DIM))[None]
    pool_s = cat("pool_s", (NSB, 15, D))[None]
    return (y_prompt, y_sample, mk, mv, ssm_p, conv_p, pool_p, ssm_s, conv_s, pool_s)
```

```python
import numpy as np
from contextlib import ExitStack
import concourse.bass as bass
import concourse.mybir as mybir
from concourse.bass_utils import run_bass_kernel_spmd

F32 = mybir.dt.float32
BF16 = mybir.dt.bfloat16
AF = mybir.ActivationFunctionType
ALU = mybir.AluOpType
AX = mybir.AxisListType

NCORES = 8
D = 1024
DC = 8
SEQ = 2048
NSB = 16
DSEQ = 8
TS = NSB * DSEQ
T = SEQ + TS
NT = T // 128
TBS = [(0, 512), (512, 512), (1024, 512), (1536, 512), (2048, 128)]
DI = 2048
NH = 32
HP = 64
NG = 8
NS = 128
CONV_DIM = 4096
IN_PROJ = 6176
NMEM = 256
XH = 4
XD = 256
DFF = 4096
EPS = 1e-5
POOL_W = (2, 4, 8, 16)

SAME_ENGINE_SYNC = False
NDS = 32


class Buf:
    __slots__ = ("w", "r", "excl")

    def __init__(self):
        self.w = None
        self.r = {}
        self.excl = False


class Tile:
    def __init__(self, t):
        self.t = t
        self.bufs = {}

    def b(self, key=None):
        v = self.bufs.get(key)
        if v is None:
            v = self.bufs[key] = Buf()
        return v

    def __getitem__(self, idx):
        return self.t[idx]


class _Eng:
    def __init__(self, h, sem):
        self.h = h
        self.sem = sem
        self.cnt = 0
        self.waited = {}


class Sched:
    def __init__(self, nc, es):
        self.nc = nc
        self.es = es
        self.eng = {}
        self.sems = {}
        for name, h in [("pe", nc.tensor), ("act", nc.scalar), ("dve", nc.vector),
                        ("pool", nc.gpsimd), ("sp", nc.sync)]:
            sem = es.enter_context(nc.semaphore("s_" + name))
            self.eng[name] = _Eng(h, sem)
            self.sems[name] = sem
        self.dcnt = [0] * NDS
        self.dnext = 0
        self.dnext_sw = 0
        for i in range(NDS):
            self.sems[("d", i)] = es.enter_context(nc.semaphore("sd%d" % i))
        self.nwaits = 0
        self.nops = 0

    def _wait(self, e, key, val):
        if e.waited.get(key, 0) >= val:
            return
        e.h.wait_ge(self.sems[key], val)
        e.waited[key] = val
        self.nwaits += 1

    @staticmethod
    def _deps(reads, writes, en=None):
        deps = {}
        for b in reads:
            if b.w is not None:
                k, v = b.w
                if deps.get(k, 0) < v:
                    deps[k] = v
            if b.excl:
                for k, v in b.r.items():
                    if k != en and deps.get(k, 0) < v:
                        deps[k] = v
        for b in writes:
            if b.w is not None:
                k, v = b.w
                if deps.get(k, 0) < v:
                    deps[k] = v
            for k, v in b.r.items():
                if deps.get(k, 0) < v:
                    deps[k] = v
        return deps

    @staticmethod
    def _mark(ev, reads, writes):
        k, v = ev
        for b in reads:
            b.r[k] = v
        for b in writes:
            b.w = ev
            b.r = {}

    def op(self, en, fn, reads=(), writes=()):
        e = self.eng[en]
        deps = self._deps(reads, writes, en)
        raw_self = 0
        if en != "pe":
            for b in reads:
                if b.w is not None and b.w[0] == en and b.w[1] > raw_self:
                    raw_self = b.w[1]
        for k, v in deps.items():
            if k == en:
                if en == "pe":
                    continue
                if not SAME_ENGINE_SYNC:
                    v = raw_self
                    if v == 0:
                        continue
            self._wait(e, k, v)
        ins = fn(e.h)
        e.cnt += 1
        ins.then_inc(e.sem, 1)
        self._mark((en, e.cnt), reads, writes)
        self.nops += 1
        return ins

    def dma(self, qn, out, in_, reads=(), writes=(), **kw):
        e = self.eng[qn]
        deps = self._deps(reads, writes)
        for k, v in deps.items():
            self._wait(e, k, v)
        half = NDS // 2
        if qn == "pool":
            i = half + self.dnext_sw
            self.dnext_sw = (self.dnext_sw + 1) % (NDS - half)
        else:
            i = self.dnext
            self.dnext = (i + 1) % half
        if self.dcnt[i] > 0:
            self._wait(e, ("d", i), 16 * self.dcnt[i])
        self.dcnt[i] += 1
        ins = e.h.dma_start(out=out, in_=in_, **kw)
        ins.then_inc(self.sems[("d", i)], 16)
        self._mark((("d", i), 16 * self.dcnt[i]), reads, writes)
        self.nops += 1
        return ins

    def barrier(self, engines=("pe", "act", "dve", "pool", "sp")):
        for en in engines:
            e = self.eng[en]
            for on, o in self.eng.items():
                if on != en and o.cnt > 0:
                    self._wait(e, on, o.cnt)
            for i in range(NDS):
                if self.dcnt[i]:
                    self._wait(e, ("d", i), 16 * self.dcnt[i])


class K:
    def __init__(self):
        self.nc = bass.Bass("TRN2", target_bir_lowering=False)
        self.es = ExitStack()
        self.S = Sched(self.nc, self.es)
        self.bank_rr = 0

    def sb(self, name, shape, dt, es=None):
        es = es or self.es
        self.uid = getattr(self, "uid", 0) + 1
        return Tile(es.enter_context(self.nc.sbuf_tensor("%s_u%d" % (name, self.uid), list(shape), dt)))

    def dram_in(self, name, shape, dt=F32):
        return self.nc.dram_tensor(name, list(shape), dt, kind="ExternalInput").ap()

    def dram_out(self, name, shape, dt=F32):
        return self.nc.dram_tensor(name, list(shape), dt, kind="ExternalOutput").ap()

    def bank(self, excl=()):
        i = self.bank_rr
        pe_ = getattr(self, "perm_excl", ())
        while i in excl or i in pe_:
            i = (i + 1) % 8
        self.bank_rr = (i + 1) % 8
        return i

    def mm(self, out, lhsT, rhs, start, stop, reads, writes):
        return self.S.op("pe", lambda h: h.matmul(out, lhsT=lhsT, rhs=rhs, start=start, stop=stop,
                                                  skip_group_check=True), reads, writes)

    def tr(self, out, in_, ident, reads, writes):
        return self.S.op("pe", lambda h: h.transpose(out, in_, ident), reads, writes)

    def act(self, out, in_, func, reads, writes, bias=None, scale=None, accum_out=None):
        kw = {}
        if bias is not None:
            kw["bias"] = bias
        if scale is not None:
            kw["scale"] = scale
        if accum_out is not None:
            kw["accum_out"] = accum_out
        return self.S.op("act", lambda h: h.activation(out, in_, func, **kw), reads, writes)

    def ts(self, en, out, in0, s1, op0, reads, writes, s2=None, op1=None):
        if op1 is None:
            if en == "pool" and op0 == ALU.mult:
                return self.S.op(en, lambda h: h.tensor_scalar(out, in0, s1, 0.0, ALU.mult, ALU.add), reads, writes)
            return self.S.op(en, lambda h: h.tensor_scalar(out, in0, s1, None, op0), reads, writes)
        return self.S.op(en, lambda h: h.tensor_scalar(out, in0, s1, s2, op0, op1), reads, writes)

    def stt(self, out, in0, scalar, in1, op0, op1, reads, writes):
        return self.S.op("dve", lambda h: h.scalar_tensor_tensor(out, in0, scalar, in1, op0, op1), reads, writes)

    def tt(self, en, out, in0, in1, op, reads, writes):
        return self.S.op(en, lambda h: h.tensor_tensor(out, in0, in1, op), reads, writes)

    def cp(self, en, out, in_, reads, writes):
        if en == "act":
            return self.S.op("act", lambda h: h.copy(out, in_), reads, writes)
        return self.S.op(en, lambda h: h.tensor_copy(out, in_), reads, writes)

    def memset(self, en, ap, val, writes):
        return self.S.op(en, lambda h: h.memset(ap, val), (), writes)


CFG = {"mamba": True, "pool": True, "attn": True, "mlp": True}
MBS = 9


def build_program():
    k = K()
    nc, S, es = k.nc, k.S, k.es
    cfg = CFG

    xp = k.dram_in("xp", [SEQ, D])
    xs = k.dram_in("xs", [TS, D])
    ck = k.dram_in("ck", [2, NSB, NMEM, D])
    cv = k.dram_in("cv", [2, NSB, NMEM, D])
    ssm = k.dram_in("ssm", [NSB, DI, NS])
    sconv = k.dram_in("sconv", [NSB * 3, CONV_DIM])
    spool = k.dram_in("spool", [NSB * 15, D])
    mem = k.dram_in("mem", [NMEM, D])
    norm_mix = k.dram_in("norm_mix", [2, D])
    norm_xattn = k.dram_in("norm_xattn", [2, D])
    norm_mem = k.dram_in("norm_mem", [2, D])
    norm_mlp = k.dram_in("norm_mlp", [2, D])
    norm_final = k.dram_in("norm_final", [1, D])
    w_in = k.dram_in("w_in", [D, IN_PROJ])
    conv_w = k.dram_in("conv_w", [4, CONV_DIM])
    conv_b = k.dram_in("conv_b", [1, CONV_DIM])
    dt_bias = k.dram_in("dt_bias", [1, NH])
    a_log = k.dram_in("a_log", [1, NH])
    d_skip = k.dram_in("d_skip", [1, NH])
    norm_gated = k.dram_in("norm_gated", [1, DI])
    w_out = k.dram_in("w_out", [DI, D])
    w_pool = k.dram_in("w_pool", [4, 256, 256])
    pool_scale = k.dram_in("pool_scale", [1, D])
    w_xq = k.dram_in("w_xq", [2, D, D])
    w_xk = k.dram_in("w_xk", [2, D, D])
    w_xv = k.dram_in("w_xv", [2, D, D])
    w_xo = k.dram_in("w_xo", [2, D, D])
    w_up = k.dram_in("w_up", [2, D, DFF])
    w_down = k.dram_in("w_down", [2, DFF, D])

    y_p = k.dram_out("y_p", [SEQ, D])
    y_s = k.dram_out("y_s", [TS, D])
    mk_p = k.dram_out("mk_p", [2, NMEM, D])
    mv_p = k.dram_out("mv_p", [2, NMEM, D])
    ssm_p = k.dram_out("ssm_p", [DI, NS])
    conv_p = k.dram_out("conv_p", [3, CONV_DIM])
    pool_p = k.dram_out("pool_p", [15, D])
    ssm_s = k.dram_out("ssm_s", [NSB, DI, NS])
    conv_s = k.dram_out("conv_s", [NSB * 3, CONV_DIM])
    pool_s = k.dram_out("pool_s", [NSB * 15, D])

    xres = k.sb("xres", [128, DC, T], F32)
    ident_f = k.sb("ident_f", [128, 128], F32)
    ident_b = k.sb("ident_b", [128, 128], BF16)
    ones_f = k.sb("ones_f", [128, 128], F32)
    ones_b = k.sb("ones_b", [128, 128], BF16)
    zeros_b = k.sb("zeros_b", [128, 512], BF16)
    colv = k.sb("colv", [128, 32, 12], F32)
    k.wsl = []
    psum = Tile(es.enter_context(nc.psum_tensor("psum", [128, 8, 512], F32)))
    k.slab_rr = 0

    def pbuf(i):
        b_ = psum.b(i)
        b_.excl = True
        return b_

    def next_slab():
        k.slab_rr = (k.slab_rr + 1) % len(k.wsl)
        return k.wsl[k.slab_rr]

    def bank2(excl=()):
        if k.bank_rr % 2:
            k.bank_rr = (k.bank_rr + 1) % 8
        b0 = k.bank_rr
        pe_ = getattr(k, "perm_excl", ())
        while b0 in excl or (b0 + 1) in excl or b0 in pe_ or (b0 + 1) in pe_:
            b0 = (b0 + 2) % 8
        k.bank_rr = (b0 + 2) % 8
        return b0

    def xb(c, t0, tn):
        return [xres.b((c, tt)) for tt in range(t0 // 128, (t0 + tn) // 128)]

    k.memset("pool", ones_f[:], 1.0, [ones_f.b()])
    S.op("pool", lambda h: h.affine_select(ident_f[:], ones_f[:], [[-1, 128]], ALU.is_equal, 0.0,
                                           base=0, channel_multiplier=1),
         [ones_f.b()], [ident_f.b()])
    k.cp("pool", ident_b[:], ident_f[:], [ident_f.b()], [ident_b.b()])
    k.cp("pool", ones_b[:], ones_f[:], [ones_f.b()], [ones_b.b()])
    k.memset("pool", zeros_b[:], 0.0, [zeros_b.b()])
    with ExitStack() as ph:
        vecrows = k.sb("vecrows", [16, 4096], F32, ph)
        k.memset("dve", vecrows[:], 0.0, [vecrows.b()])
        S.dma("sp", vecrows[0:4, :], conv_w[:, :], (), [vecrows.b()])
        S.dma("sp", vecrows[4:5, :], conv_b[:, :], (), [vecrows.b()])
        S.dma("sp", vecrows[5:7, 0:D], norm_mix[:, :], (), [vecrows.b()])
        S.dma("sp", vecrows[7:9, 0:D], norm_xattn[:, :], (), [vecrows.b()])
        S.dma("sp", vecrows[9:11, 0:D], norm_mlp[:, :], (), [vecrows.b()])
        S.dma("sp", vecrows[11:12, 0:D], pool_scale[:, :], (), [vecrows.b()])
        bi = k.bank()
        for c in range(32):
            k.tr(psum[:, bi, c * 12:(c + 1) * 12], vecrows[0:12, c * 128:(c + 1) * 128], ident_f[0:12, 0:12],
                 [vecrows.b(), ident_f.b()], [pbuf(bi)])
        k.cp("dve", colv[:], psum[:, bi, 0:384].rearrange("p (c r) -> p c r", r=12), [pbuf(bi)], [colv.b()])
        S.barrier()
    CV_CONVW, CV_CONVB, CV_MIX, CV_XATTN, CV_MLP, CV_PSCALE = 0, 4, 5, 7, 9, 11

    with ExitStack() as ph:
        xin = [k.sb("xin%d" % i, [128, D], F32, ph) for i in range(6)]
        for t in range(NT):
            xt = xin[t % 6]
            src = xp[t * 128:(t + 1) * 128, :] if t < 16 else xs[:, :]
            S.dma("sp", xt[:], src, (), [xt.b()])
            for half in range(2):
                bi = k.bank()
                for c4 in range(4):
                    c = half * 4 + c4
                    k.tr(psum[:, bi, c4 * 128:(c4 + 1) * 128], xt[:, c * 128:(c + 1) * 128], ident_f[:],
                         [xt.b(), ident_f.b()], [pbuf(bi)])
                eng = "dve" if half == 0 else "act"
                k.cp(eng, xres[:, half * 4:half * 4 + 4, t * 128:(t + 1) * 128],
                     psum[:, bi, :].rearrange("p (c n) -> p c n", c=4),
                     [pbuf(bi)], [xres.b((c, t)) for c in range(half * 4, half * 4 + 4)])
        S.barrier()

    def rmsnorm_fm(ph_tiles, gidx, out_fn, tbs):
        sqt, rst = ph_tiles
        for tbi, (t0, tn) in tbs:
            bi = k.bank()
            for c in range(DC):
                sq = sqt[c % 2]
                k.act(sq[:, 0:tn], xres[:, c, t0:t0 + tn], AF.Square, xb(c, t0, tn), [sq.b()])
                k.mm(psum[:, bi, 0:tn], ones_b[:], sq[:, 0:tn], c == 0, c == DC - 1,
                     [ones_b.b(), sq.b()], [pbuf(bi)])
            rs = rst[tbi % 2]
            k.act(rs[:, 0:tn], psum[:, bi, 0:tn], AF.Ln, [pbuf(bi)], [rs.b()], bias=EPS, scale=1.0 / D)
            k.act(rs[:, 0:tn], rs[:, 0:tn], AF.Exp, [rs.b()], [rs.b()], scale=-0.5)
            for c in range(DC):
                o_ap, o_bufs = out_fn(c, tbi, t0, tn)
                k.stt(o_ap, xres[:, c, t0:t0 + tn], colv[:, c, gidx:gidx + 1], rs[:, 0:tn], ALU.mult, ALU.mult,
                      xb(c, t0, tn) + [colv.b(), rs.b()], o_bufs)

    def linear_fm(W, KC, c0, ncols, rhs_fn, evac_fn, tbs):
        NW = 4096 // KC
        for s0 in range(0, ncols, NW):
            nw = min(NW, ncols - s0)
            slab = next_slab()
            view = slab[:, 0:KC * nw].rearrange("p (k n) -> p k n", k=KC)
            S.dma("pool", view, W[0:KC * 128, c0 + s0:c0 + s0 + nw].rearrange("(k p) n -> p k n", p=128),
                  (), [slab.b()])
            for m in range(nw // 128):
                for tbi, (t0, tn) in tbs:
                    bi = k.bank()
                    for kc in range(KC):
                        rhs, rreads = rhs_fn(kc, tbi, t0, tn)
                        k.mm(psum[:, bi, 0:tn], view[:, kc, m * 128:(m + 1) * 128], rhs, kc == 0, kc == KC - 1,
                             [slab.b()] + rreads, [pbuf(bi)])
                    evac_fn((s0 // 128) + m, tbi, t0, tn, psum[:, bi, 0:tn], pbuf(bi))

    ALL_TBS = list(enumerate(TBS))

    def evac_add_xres(m, tbi, t0, tn, ps, pb):
        k.tt("dve", xres[:, m, t0:t0 + tn], ps, xres[:, m, t0:t0 + tn], ALU.add,
             [pb] + xb(m, t0, tn), xb(m, t0, tn))

    def mlp_layer(li):
        with ExitStack() as ph:
            k.wsl = [k.sb("wsl%d" % i, [128, 4096], BF16, ph) for i in range(4)]
            h = k.sb("mlp_h", [128, DC, T], BF16, ph)
            a = k.sb("mlp_a", [128, DC, T], BF16, ph)
            sqt = [k.sb("mlp_sq%d" % i, [128, 512], BF16, ph) for i in range(2)]
            rst = [k.sb("mlp_rs%d" % i, [128, 512], F32, ph) for i in range(2)]
            rl = [k.sb("mlp_rl%d" % i, [128, 512], F32, ph) for i in range(3)]
            k.rl_rr = 0
            rmsnorm_fm((sqt, rst), CV_MLP + li,
                       lambda c, tbi, t0, tn: (h[:, c, t0:t0 + tn], [h.b((c, tbi))]), ALL_TBS)
            for j in range(4):
                def ev_up(m, tbi, t0, tn, ps, pb):
                    r = rl[k.rl_rr]
                    k.rl_rr = (k.rl_rr + 1) % 3
                    k.act(r[:, 0:tn], ps, AF.Relu, [pb], [r.b()])
                    k.tt("pool", a[:, m, t0:t0 + tn], r[:, 0:tn], r[:, 0:tn], ALU.mult, [r.b()], [a.b((m, tbi))])
                linear_fm(w_up[li], DC, j * 1024, 1024,
                          lambda kc, tbi, t0, tn: (h[:, kc, t0:t0 + tn], [h.b((kc, tbi))]), ev_up, ALL_TBS)
                linear_fm(w_down[li][j * 1024:(j + 1) * 1024, :], DC, 0, 1024,
                          lambda kc, tbi, t0, tn: (a[:, kc, t0:t0 + tn], [a.b((kc, tbi))]), evac_add_xres, ALL_TBS)
            S.barrier()

    def attn_layer(li):
        scale = float(XD) ** -0.5
        with ExitStack() as ph:
            wq = k.sb("at_wq", [128, DC, D], BF16, ph)
            wo = k.sb("at_wo", [128, DC, D], BF16, ph)
            sqt = [k.sb("at_sq%d" % i, [128, 512], BF16, ph) for i in range(2)]
            rst = [k.sb("at_rs%d" % i, [128, 512], F32, ph) for i in range(2)]
            hn = [k.sb("at_hn0", [128, DC, 512], BF16, ph)] * 2
            qt = [k.sb("at_q0", [128, DC, 512], BF16, ph)] * 2
            ot = hn
            kT = k.sb("at_kT", [128, DC, NMEM], BF16, ph)
            Vp = k.sb("at_V", [128, 2, D], BF16, ph)
            Pt = [k.sb("at_P%d" % i, [128, XH, NMEM], BF16, ph) for i in range(2)]
            Pn = Pt
            PT = [k.sb("at_PT%d" % i, [128, XH * 2, 128], BF16, ph) for i in range(2)]
            sst = [k.sb("at_st%d" % i, [128, 16], F32, ph) for i in range(2)]

            with ExitStack() as ph2:
                k.wsl = [k.sb("wsl%d" % i, [128, 4096], BF16, ph2) for i in range(2)]
                grow = k.sb("at_grow", [128, D], F32, ph2)
                memt = k.sb("at_mem", [128, D], F32, ph2)
                mn = k.sb("at_mn", [128, D], BF16, ph2)
                mnT = k.sb("at_mnT", [128, DC, NMEM], BF16, ph2)
                ktok = k.sb("at_ktok", [128, 2, D], F32, ph2)
                vtok = ktok
                sq = k.sb("at_sqscr", [128, D], BF16, ph2)
                st = k.sb("at_mst", [128, 4], F32, ph2)
                S.dma("sp", grow[:], norm_mem[li:li + 1, :].to_broadcast([128, D]), (), [grow.b()])
                for mt in range(2):
                    S.dma("sp", memt[:], mem[mt * 128:(mt + 1) * 128, :], (), [memt.b()])
                    k.act(sq[:], memt[:], AF.Square, [memt.b()], [sq.b(), st.b()], accum_out=st[:, 0:1])
                    k.act(st[:, 1:2], st[:, 0:1], AF.Ln, [st.b()], [st.b()], bias=EPS, scale=1.0 / D)
                    k.act(st[:, 2:3], st[:, 1:2], AF.Exp, [st.b()], [st.b()], scale=-0.5)
                    k.stt(mn[:], memt[:], st[:, 2:3], grow[:], ALU.mult, ALU.mult,
                          [memt.b(), st.b(), grow.b()], [mn.b()])
                    bi = k.bank()
                    pv = psum[:, bi, :].bitcast(BF16)
                    for c in range(DC):
                        k.tr(pv[:, c * 128:(c + 1) * 128], mn[:, c * 128:(c + 1) * 128], ident_b[:],
                             [mn.b(), ident_b.b()], [pbuf(bi)])
                    k.cp("dve", mnT[:, :, mt * 128:(mt + 1) * 128], pv.rearrange("p (c n) -> p c n", c=DC),
                         [pbuf(bi)], [mnT.b()])
                for which, (W, tok, outd) in enumerate(((w_xk[li], ktok, mk_p[li]), (w_xv[li], vtok, mv_p[li]))):
                    for ch in range(2):
                        slab = next_slab()
                        view = slab[:, 0:DC * 512].rearrange("p (k n) -> p k n", k=DC)
                        S.dma("pool", view, W[:, ch * 512:(ch + 1) * 512].rearrange("(k p) n -> p k n", p=128),
                              (), [slab.b()])
                        for mt in range(2):
                            bi = k.bank()
                            for kc in range(DC):
                                k.mm(psum[:, bi, :], mnT[:, kc, mt * 128:(mt + 1) * 128], view[:, kc, :],
                                     kc == 0, kc == DC - 1, [mnT.b(), slab.b()], [pbuf(bi)])
                            k.cp("act", tok[:, mt, ch * 512:(ch + 1) * 512], psum[:, bi, :], [pbuf(bi)], [tok.b()])
                    S.dma("sp", outd.rearrange("(a p) n -> p a n", p=128), tok[:], [tok.b()], ())
                    if which == 0:
                        for mt in range(2):
                            for c4 in range(2):
                                bi = k.bank()
                                for cc in range(4):
                                    c = c4 * 4 + cc
                                    k.tr(psum[:, bi, cc * 128:(cc + 1) * 128], ktok[:, mt, c * 128:(c + 1) * 128], ident_f[:],
                                         [ktok.b(), ident_f.b()], [pbuf(bi)])
                                k.cp("dve", kT[:, c4 * 4:c4 * 4 + 4, mt * 128:(mt + 1) * 128],
                                     psum[:, bi, :].rearrange("p (c n) -> p c n", c=4), [pbuf(bi)], [kT.b()])
                    else:
                        k.cp("act", Vp[:], vtok[:], [vtok.b()], [Vp.b()])
                    if which == 0:
                        S.dma("pool", wq[:], w_xq[li].rearrange("(k p) n -> p k n", p=128), (), [wq.b()])
                        S.dma("pool", wo[:], w_xo[li].rearrange("(k p) n -> p k n", p=128), (), [wo.b()])
                S.barrier()

            Kb = [k.sb("at_Kb%d" % i, [128, 2, D], BF16, ph) for i in range(2)]
            Vb = [k.sb("at_Vb%d" % i, [128, 2, D], BF16, ph) for i in range(2)]
            kTb = [k.sb("at_kTb%d" % i, [128, DC, NMEM], BF16, ph) for i in range(2)]
            Qz = [k.sb("at_Qz%d" % i, [128, DC, 128], BF16, ph) for i in range(2)]
            for i in range(2):
                k.memset("pool", Qz[i][:], 0.0, [Qz[i].b()])

            def softmax_tile(b0, bufs, excl=()):
                P, PTt, st = bufs
                Pnn = P
                sview = psum[:, b0:b0 + 2, :].rearrange("p a (h m) -> p (a h) m", h=2)
                S.op("dve", lambda h: h.tensor_reduce(st[:, 0:4], sview, AX.X, ALU.max),
                     [pbuf(b0), pbuf(b0 + 1)], [st.b()])
                k.ts("dve", st[:, 4:8], st[:, 0:4], -scale, ALU.mult, [st.b()], [st.b()])
                for hd in range(XH):
                    k.act(P[:, hd, :], sview[:, hd, :], AF.Exp, [pbuf(b0), pbuf(b0 + 1), st.b()], [P.b(), st.b()],
                          bias=st[:, 4 + hd:5 + hd], scale=scale, accum_out=st[:, 8 + hd:9 + hd])
                S.op("dve", lambda h: h.reciprocal(st[:, 12:16], st[:, 8:12]), [st.b()], [st.b()])
                k.tt("dve", Pnn[:], P[:], st[:, 12:16].unsqueeze(2).to_broadcast([128, XH, NMEM]), ALU.mult,
                     [P.b(), st.b()], [Pnn.b()])
                bi = k.bank(excl=excl)
                pv = psum[:, bi, :].bitcast(BF16)
                for hd in range(XH):
                    for mc in range(2):
                        j = hd * 2 + mc
                        k.tr(pv[:, j * 128:(j + 1) * 128], Pnn[:, hd, mc * 128:(mc + 1) * 128], ident_b[:],
                             [Pnn.b(), ident_b.b()], [pbuf(bi)])
                k.cp("act", PTt[:], pv.rearrange("p (j n) -> p j n", j=XH * 2), [pbuf(bi)], [PTt.b()])
                return PTt

            qs = k.sb("at_qs", [128, DC, 128], BF16, ph)
            os_ = k.sb("at_os", [128, DC, 128], BF16, ph)
            Ps = k.sb("at_Ps", [128, XH, NMEM], BF16, ph)
            PTs = k.sb("at_PTs", [128, XH * 2, 128], BF16, ph)
            sts = k.sb("at_sts", [128, 16], F32, ph)
            SB0 = 6
            t0s, tns = TBS[4]
            hnt = hn[0]
            rmsnorm_fm((sqt, rst), CV_XATTN + li,
                       lambda c, tbi_, t0_, tn_: (hnt[:, c, 0:tn_], [hnt.b()]), [(4, (t0s, tns))])
            for m in range(DC):
                bi = k.bank()
                for kc in range(DC):
                    k.mm(psum[:, bi, 0:tns], wq[:, kc, m * 128:(m + 1) * 128], hnt[:, kc, 0:tns], kc == 0, kc == DC - 1,
                         [wq.b(), hnt.b()], [pbuf(bi)])
                k.cp("act", qs[:, m, :], psum[:, bi, 0:tns], [pbuf(bi)], [qs.b()])
            k.perm_excl = (SB0, SB0 + 1)
            for bb in range(2):
                k.mm(psum[:, SB0 + bb, :], zeros_b[:, 0:128], zeros_b[:], True, True, [zeros_b.b()], [pbuf(SB0 + bb)])

            def sample_K(b, excl):
                Kt, kTt, Qzt = Kb[b % 2], kTb[b % 2], Qz[b % 2]
                S.dma("pool", Kt[:], ck[li, b].rearrange("(a p) n -> p a n", p=128), (), [Kt.b()])
                for c4 in range(2):
                    bi = k.bank(excl=excl)
                    pv = psum[:, bi, :].bitcast(BF16)
                    for cc in range(4):
                        for mc in range(2):
                            c = c4 * 4 + cc
                            j = cc * 2 + mc
                            k.tr(pv[:, j * 128:(j + 1) * 128], Kt[:, mc, c * 128:(c + 1) * 128], ident_b[:],
                                 [Kt.b(), ident_b.b()], [pbuf(bi)])
                    k.cp("dve" if c4 == 0 else "act", kTt[:, c4 * 4:c4 * 4 + 4, :],
                         pv.rearrange("p (c m) -> p c m", c=4), [pbuf(bi)], [kTt.b()])
                if b >= 2:
                    pb_ = b - 2
                    k.memset("pool", Qzt[:, :, pb_ * 8:pb_ * 8 + 8], 0.0, [Qzt.b()])
                k.cp("pool", Qzt[:, :, b * 8:b * 8 + 8], qs[:, :, b * 8:b * 8 + 8], [qs.b()], [Qzt.b()])
                for hd in range(XH):
                    for dc in range(2):
                        k.mm(psum[:, SB0 + hd // 2, (hd % 2) * 256:(hd % 2) * 256 + 256],
                             Qzt[:, hd * 2 + dc, :], kTt[:, hd * 2 + dc, :], False, (b == NSB - 1 and dc == 1),
                             [Qzt.b(), kTt.b()], [pbuf(SB0 + hd // 2)])

            def sample_V(b):
                Vt = Vb[b % 2]
                S.dma("pool", Vt[:], cv[li, b].rearrange("(a p) n -> p a n", p=128), (), [Vt.b()])
                for d8 in range(DC):
                    hd = d8 // 2
                    for mc in range(2):
                        k.mm(psum[:, SB0 + d8 // 4, (d8 % 4) * 128 + b * 8:(d8 % 4) * 128 + b * 8 + 8],
                             Vt[:, mc, d8 * 128:(d8 + 1) * 128], PTs[:, hd * 2 + mc, b * 8:b * 8 + 8],
                             mc == 0, mc == 1, [Vt.b(), PTs.b()], [pbuf(SB0 + d8 // 4)])

            tile_ctr = 0
            gt = 0
            for tbi, (t0, tn) in ALL_TBS[0:4]:
                hnt, qtt, ott = hn[tbi % 2], qt[tbi % 2], ot[tbi % 2]
                rmsnorm_fm((sqt, rst), CV_XATTN + li,
                           lambda c, tbi_, t0_, tn_: (hnt[:, c, 0:tn_], [hnt.b()]), [(tbi, (t0, tn))])
                for m in range(DC):
                    bi = k.bank()
                    for kc in range(DC):
                        k.mm(psum[:, bi, 0:tn], wq[:, kc, m * 128:(m + 1) * 128], hnt[:, kc, 0:tn], kc == 0, kc == DC - 1,
                             [wq.b(), hnt.b()], [pbuf(bi)])
                    k.cp("act", qtt[:, m, 0:tn], psum[:, bi, 0:tn], [pbuf(bi)], [qtt.b()])

                def scoresA(tt, excl=()):
                    lsl = slice(tt * 128, (tt + 1) * 128)
                    b0 = bank2(excl)
                    for hd in range(XH):
                        for dc in range(2):
                            k.mm(psum[:, b0 + hd // 2, (hd % 2) * 256:(hd % 2) * 256 + 256],
                                 qtt[:, hd * 2 + dc, lsl], kT[:, hd * 2 + dc, :], dc == 0, dc == 1,
                                 [qtt.b(), kT.b()], [pbuf(b0 + hd // 2)])
                    return b0

                def restB(tt, b0, slot, excl=()):
                    lsl = slice(tt * 128, (tt + 1) * 128)
                    PTt = softmax_tile(b0, (Pt[slot], PT[slot], sst[slot]), excl)
                    bo = bank2(excl)
                    for d8 in range(DC):
                        hd = d8 // 2
                        for mc in range(2):
                            k.mm(psum[:, bo + d8 // 4, (d8 % 4) * 128:(d8 % 4 + 1) * 128],
                                 Vp[:, mc, d8 * 128:(d8 + 1) * 128], PTt[:, hd * 2 + mc, :], mc == 0, mc == 1,
                                 [Vp.b(), PTt.b()], [pbuf(bo + d8 // 4)])
                    k.cp("act", ott[:, :, lsl], psum[:, bo:bo + 2, :].rearrange("p a (c n) -> p (a c) n", c=4),
                         [pbuf(bo), pbuf(bo + 1)], [ott.b()])

                ntile = tn // 128
                pend = scoresA(0)
                for tt in range(ntile):
                    nxt = scoresA(tt + 1, excl=(pend, pend + 1)) if tt + 1 < ntile else None
                    ex_ = (nxt, nxt + 1) if nxt is not None else ()
                    restB(tt, pend, tile_ctr % 2, ex_)
                    tile_ctr += 1
                    pend = nxt
                    if gt < 8:
                        for b in (2 * gt, 2 * gt + 1):
                            sample_K(b, ex_)
                        if gt == 7:
                            for i in range(2):
                                k.memset("pool", Qz[i][:], 0.0, [Qz[i].b()])
                            softmax_tile(SB0, (Ps, PTs, sts), ex_)
                    else:
                        for b in (2 * (gt - 8), 2 * (gt - 8) + 1):
                            sample_V(b)
                    gt += 1
                for m in range(DC):
                    bi = k.bank()
                    for kc in range(DC):
                        k.mm(psum[:, bi, 0:tn], wo[:, kc, m * 128:(m + 1) * 128], ott[:, kc, 0:tn], kc == 0, kc == DC - 1,
                             [wo.b(), ott.b()], [pbuf(bi)])
                    evac_add_xres(m, tbi, t0, tn, psum[:, bi, 0:tn], pbuf(bi))
            k.cp("act", os_[:], psum[:, SB0:SB0 + 2, :].rearrange("p a (c n) -> p (a c) n", c=4),
                 [pbuf(SB0), pbuf(SB0 + 1)], [os_.b()])
            k.perm_excl = ()
            for m in range(DC):
                bi = k.bank()
                for kc in range(DC):
                    k.mm(psum[:, bi, 0:tns], wo[:, kc, m * 128:(m + 1) * 128], os_[:, kc, :], kc == 0, kc == DC - 1,
                         [wo.b(), os_.b()], [pbuf(bi)])
                evac_add_xres(m, 4, t0s, tns, psum[:, bi, 0:tns], pbuf(bi))
            S.barrier()

    def pool_layer():
        with ExitStack() as ph:
            HP_ = 16
            up_ = k.sb("pl_up", [128, DC, HP_ + SEQ], BF16, ph)
            us_ = k.sb("pl_us", [128, DC, NSB, 24], BF16, ph)
            pooled_s = k.sb("pl_pooled_s", [128, DC, TS], BF16, ph)
            sqt = [k.sb("pl_sq%d" % i, [128, 512], BF16, ph) for i in range(2)]
            rst = [k.sb("pl_rs%d" % i, [128, 512], F32, ph) for i in range(2)]
            wA = k.sb("pl_wA", [128, 2, 2048], BF16, ph)
            wB = k.sb("pl_wB", [128, 2, 2048], BF16, ph)
            wp = k.sb("pl_wp", [128, 4, 2, 256], BF16, ph)
            ptmp = [k.sb("pl_ptmp%d" % i, [128, 512], F32, ph) for i in range(2)]
            invc = k.sb("pl_invc", [128, 4, 16], F32, ph)
            iot = k.sb("pl_iota", [128, 16], F32, ph)
            ph3 = ExitStack()
            hist = k.sb("pl_hist", [128, 2, D], F32, ph3)

            S.dma("pool", wp[:], w_pool.rearrange("g (k p) n -> p g k n", p=128), (), [wp.b()])
            S.op("pool", lambda h: h.iota(iot[:], [[1, 16]], base=1, channel_multiplier=0, allow_small_or_imprecise_dtypes=True), (), [iot.b()])
            for g, w in enumerate(POOL_W):
                k.ts("dve", invc[:, g, :], iot[:], float(w), ALU.min, [iot.b()], [invc.b()])
            S.op("dve", lambda h: h.reciprocal(invc[:], invc[:]), [invc.b()], [invc.b()])

            k.memset("pool", up_[:, :, 0:HP_], 0.0, [up_.b("hist")])
            k.memset("pool", us_[:, :, :, 0:1], 0.0, [us_.b()])
            S.dma("sp", hist[:, 0, :], spool[0:128, :], (), [hist.b()])
            S.dma("sp", hist[0:112, 1, :], spool[128:240, :], (), [hist.b()])
            usf = us_[:].rearrange("p c b j -> p c (b j)")
            for c in range(DC):
                bi = k.bank()
                k.tr(psum[:, bi, 0:128], hist[:, 0, c * 128:(c + 1) * 128], ident_f[:],
                     [hist.b(), ident_f.b()], [pbuf(bi)])
                k.tr(psum[:, bi, 128:240], hist[0:112, 1, c * 128:(c + 1) * 128], ident_f[0:112, 0:112],
                     [hist.b(), ident_f.b()], [pbuf(bi)])
                k.cp("dve", us_[:, c, :, 1:16], psum[:, bi, 0:240].rearrange("p (b j) -> p b j", j=15),
                     [pbuf(bi)], [us_.b()])

            S.barrier()
            ph3.close()
            outp = k.sb("pl_outp", [128, D], F32, ph)
            outs = k.sb("pl_outs", [128, D], F32, ph)

            def norm_out(c, tbi, t0, tn):
                if tbi < 4:
                    return up_[:, c, HP_ + t0:HP_ + t0 + tn], [up_.b((c, tbi))]
                return us_[:, c, :, 16:24], [us_.b()]
            sq_, rs_ = sqt, rst
            for tbi, (t0, tn) in ALL_TBS:
                bi = k.bank()
                for c in range(DC):
                    sq = sq_[c % 2]
                    k.act(sq[:, 0:tn], xres[:, c, t0:t0 + tn], AF.Square, xb(c, t0, tn), [sq.b()])
                    k.mm(psum[:, bi, 0:tn], ones_b[:], sq[:, 0:tn], c == 0, c == DC - 1, [ones_b.b(), sq.b()], [pbuf(bi)])
                rs = rs_[tbi % 2]
                k.act(rs[:, 0:tn], psum[:, bi, 0:tn], AF.Ln, [pbuf(bi)], [rs.b()], bias=EPS, scale=1.0 / D)
                k.act(rs[:, 0:tn], rs[:, 0:tn], AF.Exp, [rs.b()], [rs.b()], scale=-0.5)
                for c in range(DC):
                    o_ap, o_bufs = norm_out(c, tbi, t0, tn)
                    xin_ = xres[:, c, t0:t0 + tn]
                    rin_ = rs[:, 0:tn]
                    if tbi == 4:
                        xin_ = xin_.rearrange("p (b j) -> p b j", j=8)
                        rin_ = rin_.rearrange("p (b j) -> p b j", j=8)
                    k.stt(o_ap, xin_, colv[:, c, CV_MIX + 1:CV_MIX + 2], rin_, ALU.mult, ALU.mult,
                          xb(c, t0, tn) + [colv.b(), rs.b()], o_bufs)

            b0 = bank2()
            pvb = [psum[:, b0 + i, :].bitcast(BF16) for i in range(2)]
            for c in range(DC):
                k.tr(pvb[0][:, c * 128:(c + 1) * 128], up_[:, c, HP_ + SEQ - 128:HP_ + SEQ], ident_b[:],
                     [up_.b((c, 3)), ident_b.b()], [pbuf(b0)])
            k.cp("dve", outp[:], pvb[0], [pbuf(b0)], [outp.b()])
            S.dma("sp", pool_p[:, :], outp[113:128, :], [outp.b()], ())
            usn = k.sb("pl_usn", [128, DC, 128], BF16, ph)
            k.cp("pool", usn[:].rearrange("p c (b j) -> p c b j", j=8), us_[:, :, :, 16:24], [us_.b()], [usn.b()])
            for c in range(DC):
                k.tr(pvb[1][:, c * 128:(c + 1) * 128], usn[:, c, :], ident_b[:], [usn.b(), ident_b.b()], [pbuf(b0 + 1)])
            k.cp("dve", outs[:], pvb[1], [pbuf(b0 + 1)], [outs.b()])
            for b in range(NSB):
                S.dma("sp", pool_s[b * 15 + 7:b * 15 + 15, :], outs[b * 8:b * 8 + 8, :], [outs.b()], ())
                S.dma("sp", pool_s[b * 15:b * 15 + 7, :], spool[b * 15 + 8:b * 15 + 15, :], (), ())

            for g, w in enumerate(POOL_W):
                cs = slice(2 * g, 2 * g + 2)
                nst = g + 1
                L = HP_ + SEQ
                src = up_
                src_b = [up_.b((c, tb)) for c in (2 * g, 2 * g + 1) for tb in range(4)] + [up_.b("hist")]
                cur = None
                sh = 1
                for s in range(nst):
                    dst = wA if s % 2 == 0 else wB
                    if s == 0:
                        k.tt("dve", dst[:, :, 0:SEQ], up_[:, cs, HP_:L], up_[:, cs, HP_ - 1:L - 1], ALU.add,
                             src_b, [dst.b()])
                    else:
                        k.tt("dve", dst[:, :, sh:SEQ], cur[:, :, sh:SEQ], cur[:, :, 0:SEQ - sh], ALU.add,
                             [cur.b()], [dst.b()])
                        k.cp("dve", dst[:, :, 0:sh], cur[:, :, 0:sh], [cur.b()], [dst.b()])
                    cur = dst
                    sh *= 2
                tmp16 = wB if cur is wA else wA
                t16 = k.sb("pl_t16_%d" % g, [128, 2, 16], F32, ph)
                k.tt("dve", tmp16[:, :, 0:16], cur[:, :, 0:16], invc[:, g:g + 1, :].to_broadcast([128, 2, 16]), ALU.mult,
                     [cur.b(), invc.b()], [tmp16.b()])
                k.tt("dve", t16[:], tmp16[:, :, 0:16], up_[:, cs, HP_:HP_ + 16], ALU.subtract,
                     [tmp16.b()] + src_b, [t16.b()])
                k.stt(up_[:, cs, HP_:L], cur[:, :, 0:SEQ], 1.0 / w, up_[:, cs, HP_:L], ALU.mult, ALU.subtract,
                      [cur.b()] + src_b, src_b)
                k.cp("dve", up_[:, cs, HP_:HP_ + 16], t16[:], [t16.b()] + src_b, src_b)
                sA = k.sb("pl_sA%d" % g, [128, 2, NSB, 8], F32, ph)
                k.tt("dve", sA[:], us_[:, cs, :, 16:24], us_[:, cs, :, 15:23], ALU.add, [us_.b()], [sA.b()])
                for j in range(2, w):
                    k.tt("dve", sA[:], sA[:], us_[:, cs, :, 16 - j:24 - j], ALU.add, [us_.b(), sA.b()], [sA.b()])
                k.stt(pooled_s[:, cs, :].rearrange("p c (b j) -> p c b j", j=8), sA[:], 1.0 / w, us_[:, cs, :, 16:24],
                      ALU.mult, ALU.subtract, [sA.b(), us_.b()], [pooled_s.b(g)])
                for mo in range(2):
                    m = 2 * g + mo
                    for tbi, (t0, tn) in ALL_TBS:
                        bi = k.bank()
                        for kc in range(2):
                            if tbi < 4:
                                rhs_ = up_[:, 2 * g + kc, HP_ + t0:HP_ + t0 + tn]
                                rb_ = [up_.b((2 * g + kc, tbi))]
                            else:
                                rhs_ = pooled_s[:, 2 * g + kc, :]
                                rb_ = [pooled_s.b(g)]
                            k.mm(psum[:, bi, 0:tn], wp[:, g, kc, mo * 128:(mo + 1) * 128], rhs_,
                                 kc == 0, kc == 1, [wp.b()] + rb_, [pbuf(bi)])
                        pt_ = ptmp[(mo * 5 + tbi) % 2]
                        k.act(pt_[:, 0:tn], psum[:, bi, 0:tn], AF.Copy, [pbuf(bi), colv.b()], [pt_.b()],
                              scale=colv[:, m, CV_PSCALE:CV_PSCALE + 1])
                        k.tt("pool", xres[:, m, t0:t0 + tn], xres[:, m, t0:t0 + tn], pt_[:, 0:tn], ALU.add,
                             [pt_.b()] + xb(m, t0, tn), xb(m, t0, tn))

            S.barrier()

    def mamba_layer():
        with ExitStack() as ph:
            h = k.sb("mb_h", [128, DC, T], BF16, ph)
            with ExitStack() as ph0:
                sqt = [k.sb("mb_sq%d" % i, [128, 512], BF16, ph0) for i in range(2)]
                rst = [k.sb("mb_rs%d" % i, [128, 512], F32, ph0) for i in range(2)]
                rmsnorm_fm((sqt, rst), CV_MIX + 0,
                           lambda c, tbi, t0, tn: (h[:, c, t0:t0 + tn], [h.b((c, tbi))]), ALL_TBS)
                S.barrier()

            Umat = k.sb("mb_U", [128, 128], F32, ph)
            SameB = k.sb("mb_SB", [128, 128], F32, ph)
            Ublk = k.sb("mb_Ub", [128, 128], F32, ph)
            S.op("pool", lambda hh: hh.affine_select(Umat[:], ones_f[:], [[1, 128]], ALU.is_ge, 0.0,
                                                     base=0, channel_multiplier=-1), [ones_f.b()], [Umat.b()])
            S.op("pool", lambda hh: hh.affine_select(SameB[:].rearrange("p (b j) -> p b j", j=8),
                                                     ones_f[:].rearrange("p (b j) -> p b j", j=8),
                                                     [[8, 16], [0, 8]], ALU.is_ge, 0.0, base=7, channel_multiplier=-1),
                 [ones_f.b()], [SameB.b()])
            S.op("pool", lambda hh: hh.affine_select(SameB[:].rearrange("p (b j) -> p b j", j=8),
                                                     SameB[:].rearrange("p (b j) -> p b j", j=8),
                                                     [[-8, 16], [0, 8]], ALU.is_ge, 0.0, base=0, channel_multiplier=1),
                 [SameB.b()], [SameB.b()])
            k.tt("pool", Ublk[:], SameB[:], Umat[:], ALU.mult, [SameB.b(), Umat.b()], [Ublk.b()])
            brow = k.sb("mb_brow", [128, 3, NH], F32, ph)
            S.dma("sp", brow[:, 0, :], dt_bias.to_broadcast([128, NH]), (), [brow.b()])
            S.dma("sp", brow[:, 1, :], a_log.to_broadcast([128, NH]), (), [brow.b()])
            S.dma("sp", brow[:, 2, :], d_skip.to_broadcast([128, NH]), (), [brow.b()])
            k.act(brow[:, 1, :], brow[:, 1, :], AF.Exp, [brow.b()], [brow.b()])
            k.ts("dve", brow[:, 1, :], brow[:, 1, :], -1.0, ALU.mult, [brow.b()], [brow.b()])
            wdt = k.sb("mb_wdt", [128, DC, NH], BF16, ph)
            S.dma("pool", wdt[:], w_in[:, 6144:6176].rearrange("(k p) n -> p k n", p=128), (), [wdt.b()])

            dt_a = k.sb("mb_dt", [128, NT, NH], F32, ph)
            cd_a = k.sb("mb_cd", [128, NT, NH], F32, ph)
            dtd_a = k.sb("mb_dtd", [128, NT, NH], F32, ph)
            eacs_a = k.sb("mb_eacs", [128, NT, NH], F32, ph)
            cdp2 = k.sb("mb_cdp2", [128, NSB, 16], F32, ph)
            nb_a = k.sb("mb_nb", [128, NT, NH], F32, ph)
            dtAh = k.sb("mb_dtAh", [128, NT, NH], BF16, ph)
            dtAl = k.sb("mb_dtAl", [128, NT, NH], BF16, ph)
            phT = ExitStack()
            dtA_a = k.sb("mb_dtA", [128, NT, NH], F32, phT)
            nacs_a = k.sb("mb_nacs", [128, NT, NH], F32, phT)
            tmp32 = k.sb("mb_tmp32", [128, NH], F32, phT)
            Xs = k.sb("mb_Xs", [128, NSB, NH], F32, phT)
            CDB = k.sb("mb_CDB", [128, NSB, NH], F32, phT)
            dtAf = k.sb("mb_dtAf", [128, NT, NH], F32, phT)
            for t in range(NT):
                Um = Umat if t < 16 else Ublk
                Jm = ones_f if t < 16 else SameB
                bi = k.bank()
                for kc in range(DC):
                    k.mm(psum[:, bi, 0:NH], h[:, kc, t * 128:(t + 1) * 128], wdt[:, kc, :], kc == 0, kc == DC - 1,
                         [h.b((kc, min(t // 4, 4))), wdt.b()], [pbuf(bi)])
                k.tt("dve", tmp32[:], psum[:, bi, 0:NH], brow[:, 0, :], ALU.add, [pbuf(bi), brow.b()], [tmp32.b()])
                k.act(tmp32[:], tmp32[:], AF.Exp, [tmp32.b()], [tmp32.b()])
                k.act(dt_a[:, t, :], tmp32[:], AF.Ln, [tmp32.b()], [dt_a.b(t)], bias=1.0, scale=1.0)
                k.tt("dve", dtA_a[:, t, :], dt_a[:, t, :], brow[:, 1, :], ALU.mult, [dt_a.b(t), brow.b()], [dtA_a.b(t)])
                bi = k.bank()
                k.mm(psum[:, bi, 0:NH], Um[:], dtA_a[:, t, :], True, True, [Um.b(), dtA_a.b(t)], [pbuf(bi)])
                k.mm(psum[:, bi, 64:64 + NH], Jm[:], dtA_a[:, t, :], True, True, [Jm.b(), dtA_a.b(t)], [pbuf(bi)])
                k.ts("dve", nacs_a[:, t, :], psum[:, bi, 0:NH], -1.0, ALU.mult, [pbuf(bi)], [nacs_a.b(t)])
                k.act(eacs_a[:, t, :], psum[:, bi, 0:NH], AF.Exp, [pbuf(bi)], [eacs_a.b(t)])
                k.act(cd_a[:, t, :], psum[:, bi, 64:64 + NH], AF.Exp, [pbuf(bi)], [cd_a.b(t)])
                k.tt("dve", tmp32[:], psum[:, bi, 64:64 + NH], nacs_a[:, t, :], ALU.add,
                     [pbuf(bi), nacs_a.b(t)], [tmp32.b()])
                k.act(tmp32[:], tmp32[:], AF.Exp, [tmp32.b()], [tmp32.b()])
                k.tt("dve", dtd_a[:, t, :], tmp32[:], dt_a[:, t, :], ALU.mult, [tmp32.b(), dt_a.b(t)], [dtd_a.b(t)])
                k.act(tmp32[:], dt_a[:, t, :], AF.Ln, [dt_a.b(t)], [tmp32.b()])
                k.tt("dve", nb_a[:, t, :], tmp32[:], nacs_a[:, t, :], ALU.add, [tmp32.b(), nacs_a.b(t)], [nb_a.b(t)])

            k.cp("dve", dtAh[:], dtA_a[:], [dtA_a.b(t_) for t_ in range(NT)], [dtAh.b()])
            k.cp("dve", dtAf[:], dtAh[:], [dtAh.b()], [dtAf.b()])
            k.tt("dve", dtAl[:], dtA_a[:], dtAf[:], ALU.subtract, [dtA_a.b(t_) for t_ in range(NT)] + [dtAf.b()], [dtAl.b()])
            k.tt("dve", Xs[:], dtA_a[:, 16:17, :].to_broadcast([128, NSB, NH]),
                 SameB[:, 0:128:8].unsqueeze(2).to_broadcast([128, NSB, NH]), ALU.mult,
                 [dtA_a.b(16), SameB.b()], [Xs.b()])
            bi = k.bank()
            k.mm(psum[:, bi, :], ones_f[:], Xs[:].rearrange("p b h -> p (b h)"), True, True,
                 [ones_f.b(), Xs.b()], [pbuf(bi)])
            k.act(CDB[:].rearrange("p b h -> p (b h)"), psum[:, bi, :], AF.Exp, [pbuf(bi)], [CDB.b()])
            k.cp("dve", cdp2[0:64, :, :], CDB[0:64, :, 0:NH:2], [CDB.b()], [cdp2.b()])
            k.cp("dve", cdp2[64:128, :, :], CDB[64:128, :, 1:NH:2], [CDB.b()], [cdp2.b()])

            S.barrier()
            phT.close()
            wz = k.sb("mb_wz", [128, DC, 256], BF16, ph)
            wx = [k.sb("mb_wx%d" % i, [128, DC, 512], BF16, ph) for i in range(2)]
            wog = [k.sb("mb_wog0", [128, 2, D], BF16, ph)] * 2
            dgw = k.sb("mb_dgw", [128, 4, 4, 128], BF16, ph)
            rawt = [k.sb("mb_raw%d" % i, [128, 4, 3 + 512], BF16, ph) for i in range(2)]
            carry = k.sb("mb_carry", [128, 4, 3], BF16, ph)
            raws = k.sb("mb_raws", [128, 4, NSB, 11], BF16, ph)
            scv = k.sb("mb_scv", [48, 4, 128], F32, ph)
            ncv = k.sb("mb_ncv", [128, 4, 51], F32, ph)
            ncvo = k.sb("mb_ncvo", [128, 4, 128], F32, ph)
            hout = ncvo
            xact = [k.sb("mb_xact%d" % i, [128, 4, 512], BF16, ph) for i in range(2)]
            tht = [k.sb("mb_th%d" % i, [128, 512], BF16, ph) for i in range(2)]
            vht = [k.sb("mb_vh%d" % i, [128, 512], BF16, ph) for i in range(2)]
            cbh = k.sb("mb_cbh", [128, 32], F32, ph)
            k.ts("dve", cbh[:], colv[:, :, 4], 0.5, ALU.mult, [colv.b()], [cbh.b()])
            ygT = k.sb("mb_ygT", [128, 2, 512], BF16, ph)
            ngrow = [k.sb("mb_ngrow%d" % i, [128, 256], F32, ph) for i in range(2)]
            zs4 = [k.sb("mb_zs4_%d" % i, [128, 4, 256], BF16, ph) for i in range(2)]
            xdts = [k.sb("mb_xdts%d" % i, [128, 4, HP], BF16, ph) for i in range(2)]
            xB = [k.sb("mb_xB%d" % i, [128, 384], BF16, ph) for i in range(2)]
            Dg = k.sb("mb_Dg", [128, 4, 128], BF16, ph)
            cbs = [k.sb("mb_cbs%d" % i, [128, 128], BF16, ph) for i in range(2)]
            U_b = [k.sb("mb_Ubf%d" % i, [128, 128], BF16, ph) for i in range(2)]
            k.cp("pool", U_b[0][:], Umat[:], [Umat.b()], [U_b[0].b()])
            k.cp("pool", U_b[1][:], Ublk[:], [Ublk.b()], [U_b[1].b()])
            dcy = [k.sb("mb_dcy%d" % i, [128, 128], BF16, ph) for i in range(4)]
            MT = [k.sb("mb_MT%d" % i, [128, 128], BF16, ph) for i in range(8)]
            Neg4 = [k.sb("mb_Neg%d" % i, [128, 128], BF16, ph) for i in range(2)]
            for i, Us in enumerate((Umat, Ublk)):
                k.ts("dve", Neg4[i][:], Us[:], -1.0, ALU.add, [Us.b()], [Neg4[i].b()], s2=30000.0, op1=ALU.mult)
            t1 = k.sb("mb_t1", [128, 4, HP], F32, ph)
            yg = k.sb("mb_yg", [128, 256], F32, ph)
            ygn = [k.sb("mb_ygn%d" % i, [128, 256], BF16, ph) for i in range(2)]
            yst = k.sb("mb_yst", [128, 4], F32, ph)
            mhalf = k.sb("mb_mhalf", [128, 1], F32, ph)
            k.memset("pool", mhalf[:], -0.5, [mhalf.b()])
            hTf = k.sb("mb_hTf", [128, 256], F32, ph)
            hTb = k.sb("mb_hTb", [128, 256], BF16, ph)
            h0s = [k.sb("mb_h0s%d" % i, [128, 2, 2, 128], F32, ph) for i in range(4)]
            h0T = k.sb("mb_h0T", [128, 2, 256], BF16, ph)
            CTz = [k.sb("mb_CTz%d" % i, [128, 128], BF16, ph) for i in range(2)]
            Bm = [k.sb("mb_Bm%d" % i, [128, 128], BF16, ph) for i in range(2)]
            for i in range(2):
                k.memset("pool", CTz[i][:], 0.0, [CTz[i].b()])

            def cglob_of(gg):
                return [2 * gg, 2 * gg + 1, 16 + gg, 24 + gg]

            def load_wx(gg):
                for (dst0, src0, n) in ((0, DI + gg * 256, 256), (256, 2 * DI + gg * 128, 128),
                                        (384, 2 * DI + 1024 + gg * 128, 128)):
                    S.dma("pool", wx[gg % 2][:, :, dst0:dst0 + n],
                          w_in[:, src0:src0 + n].rearrange("(k p) n -> p k n", p=128), (), [wx[gg % 2].b()])

            def load_h0(gg, e8):
                for b2 in range(2):
                    S.dma("sp", h0s[e8 % 4][:, b2, :, :], ssm[e8 * 2 + b2, gg * 256:(gg + 1) * 256, :]
                          .rearrange("(a p) n -> p a n", p=128), (), [h0s[e8 % 4].b()])

            def setup_early(gg):
                cg = cglob_of(gg)
                if gg == 0:
                    load_wx(0)
                S.dma("pool", wz[:], w_in[:, gg * 256:(gg + 1) * 256].rearrange("(k p) n -> p k n", p=128), (), [wz.b()])
                if gg + 1 < NG:
                    load_wx(gg + 1)
                S.dma("sp", ngrow[gg % 2][:], norm_gated[:, gg * 256:(gg + 1) * 256].to_broadcast([128, 256]), (),
                      [ngrow[gg % 2].b()])
                for ci in range(4):
                    S.dma("sp", scv[:, ci, :], sconv[:, cg[ci] * 128:(cg[ci] + 1) * 128], (), [scv.b()])
                for ci in range(4):
                    for tap in range(4):
                        k.ts("dve", dgw[:, ci, tap, :], ident_f[:], colv[:, cg[ci], tap:tap + 1], ALU.mult,
                             [ident_f.b(), colv.b()], [dgw.b()])
                k.memset("dve", carry[:], 0.0, [carry.b()])

            def setup_late(gg):
                S.dma("pool", wog[0][:], w_out[gg * 256:(gg + 1) * 256, :].rearrange("(k p) n -> p k n", p=128), (), [wog[0].b()])
                for r in range(4):
                    k.ts("dve", Dg[:, r, :], ident_f[:], brow[:, 2, 4 * gg + r:4 * gg + r + 1], ALU.mult,
                         [ident_f.b(), brow.b()], [Dg.b()])

            def P_units(gg, tbi, bset):
                t0, tn = TBS[tbi]
                cglob = cglob_of(gg)
                wxg = wx[gg % 2]
                rw, xa, zz = rawt[bset], xact[bset], zs4[bset]
                units = []

                def u_hist():
                    bi = k.bank()
                    for ci in range(4):
                        k.tr(psum[:, bi, ci * 48:(ci + 1) * 48], scv[:, ci, :], ident_f[0:48, 0:48],
                             [scv.b(), ident_f.b()], [pbuf(bi)])
                    k.cp("dve", raws[:, :, :, 0:3], psum[:, bi, 0:192].rearrange("p (c b j) -> p c b j", c=4, j=3),
                         [pbuf(bi)], [raws.b()])
                if tbi == 4:
                    units.append(u_hist)

                def u_in(ci):
                    if ci == 0 and tbi < 4:
                        k.cp("dve", rw[:, :, 0:3], carry[:], [carry.b()], [rw.b()])
                    bi = k.bank()
                    for kc in range(DC):
                        k.mm(psum[:, bi, 0:tn], wxg[:, kc, ci * 128:(ci + 1) * 128], h[:, kc, t0:t0 + tn],
                             kc == 0, kc == DC - 1, [wxg.b(), h.b((kc, tbi))], [pbuf(bi)])
                    if tbi < 4:
                        k.cp("act", rw[:, ci, 3:3 + tn], psum[:, bi, 0:tn], [pbuf(bi)], [rw.b()])
                        if tbi == 3:
                            k.cp("dve", ncv[:, ci, 48:51], psum[:, bi, 509:512], [pbuf(bi)], [ncv.b()])
                    else:
                        pvv = psum[:, bi, 0:128].rearrange("p (b j) -> p b j", j=8)
                        k.cp("act", raws[:, ci, :, 3:11], pvv, [pbuf(bi)], [raws.b()])
                        k.cp("dve", ncv[:, ci, 0:48].rearrange("p (b j) -> p b j", j=3), pvv[:, :, 5:8],
                             [pbuf(bi)], [ncv.b()])
                    if ci == 3 and tbi < 3:
                        k.cp("dve", carry[:], rw[:, :, 512:515], [rw.b()], [carry.b()])

                def u_cv(ci):
                    bi = k.bank()
                    for tap in range(4):
                        if tbi < 4:
                            rhs_ = rw[:, ci, tap:tap + tn]
                            rb_ = rw.b()
                        else:
                            rhs_ = raws[:, ci, :, tap:tap + 8]
                            rb_ = raws.b()
                        k.mm(psum[:, bi, 0:tn], dgw[:, ci, tap, :], rhs_, tap == 0, tap == 3,
                             [dgw.b(), rb_], [pbuf(bi)])
                    th_, vh_ = tht[ci % 2], vht[ci % 2]
                    k.act(th_[:, 0:tn], psum[:, bi, 0:tn], AF.Tanh, [pbuf(bi), cbh.b()], [th_.b()],
                          bias=cbh[:, cglob[ci]:cglob[ci] + 1], scale=0.5)
                    k.act(vh_[:, 0:tn], psum[:, bi, 0:tn], AF.Identity, [pbuf(bi), cbh.b()], [vh_.b()],
                          bias=cbh[:, cglob[ci]:cglob[ci] + 1], scale=0.5)
                    k.stt(xa[:, ci, 0:tn], th_[:, 0:tn], 1.0, vh_[:, 0:tn], ALU.add, ALU.mult,
                          [th_.b(), vh_.b()], [xa.b()])

                def u_z(tt):
                    t = t0 // 128 + tt
                    bi = k.bank()
                    for kc in range(DC):
                        k.mm(psum[:, bi, 0:256], h[:, kc, t * 128:(t + 1) * 128], wz[:, kc, :],
                             kc == 0, kc == DC - 1, [h.b((kc, tbi)), wz.b()], [pbuf(bi)])
                    th_, vh_ = tht[tt % 2], vht[tt % 2]
                    k.act(th_[:, 0:256], psum[:, bi, 0:256], AF.Tanh, [pbuf(bi)], [th_.b()], scale=0.5)
                    k.act(vh_[:, 0:256], psum[:, bi, 0:256], AF.Identity, [pbuf(bi)], [vh_.b()], scale=0.5)
                    k.stt(zz[:, tt, :], th_[:, 0:256], 1.0, vh_[:, 0:256], ALU.add, ALU.mult,
                          [th_.b(), vh_.b()], [zz.b(tt)])

                for ci in range(4):
                    units.append(lambda ci=ci: u_in(ci))
                for ci in range(4):
                    units.append(lambda ci=ci: u_cv(ci))
                for tt in range(tn // 128):
                    units.append(lambda tt=tt: u_z(tt))
                return units

            setup_early(0)
            for e8 in range(4):
                load_h0(0, e8)
            for u_ in P_units(0, 0, 0):
                u_()
            for g in range(NG):
                hd0 = 4 * g
                cglob = cglob_of(g)
                wo = wog[0]
                ngrow_c = ngrow[g % 2]
                setup_late(g)
                for tbi, (t0, tn) in ALL_TBS:
                    bidx = g * len(TBS) + tbi
                    xact_c, zs4_c = xact[bidx % 2], zs4[bidx % 2]
                    if tbi + 1 < len(TBS):
                        nxt_units = P_units(g, tbi + 1, (bidx + 1) % 2)
                    elif g + 1 < NG:
                        setup_early(g + 1)
                        nxt_units = P_units(g + 1, 0, (bidx + 1) % 2)
                    else:
                        nxt_units = []

                    def head(tt):
                        t = t0 // 128 + tt
                        sl = t % 2
                        lsl = slice(tt * 128, (tt + 1) * 128)
                        Ub_ = U_b[0] if t < 16 else U_b[1]
                        Ng = Neg4[0] if t < 16 else Neg4[1]
                        bt = k.bank()
                        pv = psum[:, bt, :].bitcast(BF16)
                        for ci in range(3):
                            k.tr(pv[:, ci * 128:(ci + 1) * 128], xact_c[:, ci, lsl], ident_b[:],
                                 [xact_c.b(), ident_b.b()], [pbuf(bt)])
                        xv = pv[:, 0:256].rearrange("p (r q) -> p r q", q=HP)
                        k.tt("dve", xdts[sl][:], xv, dtd_a[:, t, hd0:hd0 + 4].unsqueeze(2).to_broadcast([128, 4, HP]), ALU.mult,
                             [pbuf(bt), dtd_a.b(t)], [xdts[sl].b()])
                        k.cp("act", xB[sl][:], pv[:, 0:384], [pbuf(bt)], [xB[sl].b()])
                        bc = k.bank()
                        k.mm(psum[:, bc, 0:128], xact_c[:, 2, lsl], xact_c[:, 3, lsl], True, True, [xact_c.b()], [pbuf(bc)])
                        k.cp("act", cbs[sl][:], psum[:, bc, 0:128], [pbuf(bc)], [cbs[sl].b()])
                        br = k.bank()
                        for r in range(4):
                            k.mm(psum[:, br, r * 128:(r + 1) * 128], ident_b[:], Ng[:], r == 0, False,
                                 [ident_b.b(), Ng.b()], [pbuf(br)])
                        for r in range(4):
                            k.mm(psum[:, br, r * 128:(r + 1) * 128],
                                 dtAh[:, t, hd0 + r:hd0 + r + 1].to_broadcast([128, 128]), Ub_[:], False, False,
                                 [dtAh.b(), Ub_.b()], [pbuf(br)])
                            k.mm(psum[:, br, r * 128:(r + 1) * 128],
                                 dtAl[:, t, hd0 + r:hd0 + r + 1].to_broadcast([128, 128]), Ub_[:], False, r == 3,
                                 [dtAl.b(), Ub_.b()], [pbuf(br)])
                        for r in range(4):
                            dc_ = dcy[(t * 4 + r) % 4]
                            mt_ = MT[(t % 2) * 4 + r]
                            k.act(dc_[:], psum[:, br, r * 128:(r + 1) * 128], AF.Exp, [pbuf(br), nb_a.b(t)], [dc_.b()],
                                  bias=nb_a[:, t, hd0 + r:hd0 + r + 1], scale=1.0)
                            k.tt("pool", mt_[:], dc_[:], cbs[sl][:], ALU.mult, [dc_.b(), cbs[sl].b()], [mt_.b()])
                        return None

                    def tail(tt, banks):
                        t = t0 // 128 + tt
                        sl = t % 2
                        lsl = slice(tt * 128, (tt + 1) * 128)
                        has_off = True
                        by = k.bank()
                        for r in range(4):
                            mt_ = MT[(t % 2) * 4 + r]
                            k.mm(psum[:, by, r * HP:(r + 1) * HP], mt_[:], xB[sl][:, r * HP:(r + 1) * HP], True, False,
                                 [mt_.b(), xB[sl].b()], [pbuf(by)])
                            k.mm(psum[:, by, r * HP:(r + 1) * HP], Dg[:, r, :], xB[sl][:, r * HP:(r + 1) * HP], False, True,
                                 [Dg.b(), xB[sl].b()], [pbuf(by)])
                        bs_ = None
                        if t < 16:
                            bs_ = k.bank()
                            k.mm(psum[:, bs_, 0:256], xB[sl][:, 256:384], xdts[sl][:].rearrange("p r q -> p (r q)"), True, True,
                                 [xB[sl].b(), xdts[sl].b()], [pbuf(bs_)])
                        bo = k.bank()
                        held = (by, bo)
                        if t < 16:
                            if t == 0:
                                has_off = False
                            else:
                                k.mm(psum[:, bo, 0:256], xact_c[:, 3, lsl], hTb[:], True, True, [xact_c.b(), hTb.b()], [pbuf(bo)])
                        else:
                            k.perm_excl = held
                            for e8 in range(8):
                                hs_ = h0s[e8 % 4]
                                bh = k.bank(excl=held)
                                for b2 in range(2):
                                    for a in range(2):
                                        k.tr(psum[:, bh, (b2 * 2 + a) * 128:(b2 * 2 + a + 1) * 128],
                                             hs_[:, b2, a, :], ident_f[:], [hs_.b(), ident_f.b()], [pbuf(bh)])
                                k.cp("act" if e8 % 2 else "dve", h0T[:],
                                     psum[:, bh, :].rearrange("p (b q) -> p b q", b=2), [pbuf(bh)], [h0T.b()])
                                for b4 in range(2):
                                    b = e8 * 2 + b4
                                    cz = CTz[b % 2]
                                    if b >= 2:
                                        k.memset("pool", cz[:, (b - 2) * 8:(b - 2) * 8 + 8], 0.0, [cz.b()])
                                    k.cp("pool", cz[:, b * 8:b * 8 + 8], xact_c[:, 3, b * 8:b * 8 + 8], [xact_c.b()], [cz.b()])
                                    k.mm(psum[:, bo, 0:256], cz[:], h0T[:, b4, :], b == 0, b == NSB - 1,
                                         [cz.b(), h0T.b()], [pbuf(bo)])
                                    bmt = Bm[b % 2]
                                    k.ts("pool", bmt[:], xB[sl][:, 256:384], SameB[:, b * 8:b * 8 + 1], ALU.mult,
                                         [xB[sl].b(), SameB.b()], [bmt.b()])
                                    for a in range(2):
                                        bn = k.bank(excl=held)
                                        k.mm(psum[:, bn, 0:128], xdts[sl][:, 2 * a:2 * a + 2, :].rearrange("p r q -> p (r q)"),
                                             bmt[:], True, True, [xdts[sl].b(), bmt.b()], [pbuf(bn)])
                                        k.stt(hs_[:, b4, a, :], hs_[:, b4, a, :], cdp2[:, b, 2 * g + a:2 * g + a + 1],
                                              psum[:, bn, 0:128], ALU.mult, ALU.add, [hs_.b(), cdp2.b(), pbuf(bn)], [hs_.b()])
                                for b4 in range(2):
                                    S.dma("sp", ssm_s[e8 * 2 + b4, g * 256:(g + 1) * 256, :]
                                          .rearrange("(a p) n -> p a n", p=128), hs_[:, b4, :, :], [hs_.b()], ())
                                if e8 + 4 < 8:
                                    load_h0(g, e8 + 4)
                                elif g + 1 < NG:
                                    load_h0(g + 1, e8 - 4)
                                for _ in range(2):
                                    if ucur[0] < len(nxt_units):
                                        nxt_units[ucur[0]]()
                                        ucur[0] += 1
                            k.perm_excl = ()
                            for i in range(2):
                                k.memset("pool", CTz[i][:], 0.0, [CTz[i].b()])
                        if t < 16:
                            if t == 0:
                                k.cp("dve", hTf[:], psum[:, bs_, 0:256], [pbuf(bs_)], [hTf.b()])
                            else:
                                hv_ = hTf[:].rearrange("p (r q) -> p r q", q=HP)
                                k.tt("dve", hv_, hv_, cd_a[:, t, hd0:hd0 + 4].unsqueeze(2).to_broadcast([128, 4, HP]), ALU.mult,
                                     [hTf.b(), cd_a.b(t)], [hTf.b()])
                                k.tt("dve", hTf[:], hTf[:], psum[:, bs_, 0:256], ALU.add, [hTf.b(), pbuf(bs_)], [hTf.b()])
                            if t < 15:
                                k.cp("pool", hTb[:], hTf[:], [hTf.b()], [hTb.b()])
                            else:
                                bf_ = k.bank(excl=held)
                                for a in range(2):
                                    k.tr(psum[:, bf_, a * 128:(a + 1) * 128], hTf[:, a * 128:(a + 1) * 128], ident_f[:],
                                         [hTf.b(), ident_f.b()], [pbuf(bf_)])
                                k.cp("dve", hout[:, 0:2, :], psum[:, bf_, 0:256].rearrange("p (a n) -> p a n", a=2), [pbuf(bf_)], [hout.b()])
                                S.dma("sp", ssm_p[g * 256:(g + 1) * 256, :].rearrange("(a p) n -> p a n", p=128), hout[:, 0:2, :],
                                      [hout.b()], ())
                        yv = psum[:, by, 0:256]
                        if has_off:
                            k.tt("dve", t1[:], psum[:, bo, 0:256].rearrange("p (r q) -> p r q", q=HP),
                                 eacs_a[:, t, hd0:hd0 + 4].unsqueeze(2).to_broadcast([128, 4, HP]), ALU.mult,
                                 [pbuf(bo), eacs_a.b(t)], [t1.b()])
                            k.tt("dve", yg[:], yv, t1[:].rearrange("p r q -> p (r q)"), ALU.add, [pbuf(by), t1.b()], [yg.b()])
                            k.tt("dve", yg[:], yg[:], zs4_c[:, tt, :], ALU.mult, [yg.b(), zs4_c.b(tt)], [yg.b()])
                        else:
                            k.tt("dve", yg[:], yv, zs4_c[:, tt, :], ALU.mult, [pbuf(by), zs4_c.b(tt)], [yg.b()])
                        yn_ = ygn[t % 2]
                        k.act(yn_[:], yg[:], AF.Square, [yg.b()], [yn_.b(), yst.b()], accum_out=yst[:, 0:1])
                        k.ts("pool", yst[:, 1:2], yst[:, 0:1], 1.0 / 256, ALU.mult, [yst.b()], [yst.b()], s2=EPS, op1=ALU.add)
                        k.tt("pool", yst[:, 2:3], yst[:, 1:2], mhalf[:, 0:1], ALU.pow, [yst.b(), mhalf.b()], [yst.b()])
                        yn_ = ygn[t % 2]
                        k.stt(yn_[:], yg[:], yst[:, 2:3], ngrow_c[:], ALU.mult, ALU.mult, [yg.b(), yst.b(), ngrow_c.b()], [yn_.b()])

                    def fin(tt):
                        t = t0 // 128 + tt
                        lsl = slice(tt * 128, (tt + 1) * 128)
                        yn_ = ygn[t % 2]
                        bg = k.bank()
                        pg = psum[:, bg, :].bitcast(BF16)
                        for a in range(2):
                            k.tr(pg[:, a * 128:(a + 1) * 128], yn_[:, a * 128:(a + 1) * 128], ident_b[:],
                                 [yn_.b(), ident_b.b()], [pbuf(bg)])
                        k.cp("act", ygT[:, :, lsl], pg[:, 0:256].rearrange("p (a n) -> p a n", a=2), [pbuf(bg)], [ygT.b()])

                    ntile = tn // 128
                    per_step = -(-len(nxt_units) // ntile)
                    ucur = [0]
                    head(0)
                    for tt in range(ntile):
                        if tt + 1 < ntile:
                            head(tt + 1)
                        if tt >= 1:
                            fin(tt - 1)
                        tail(tt, None)
                        lim = min(len(nxt_units), (tt + 1) * per_step)
                        while ucur[0] < lim:
                            nxt_units[ucur[0]]()
                            ucur[0] += 1
                    fin(ntile - 1)
                    for m in range(DC):
                        bi = k.bank()
                        for kc in range(2):
                            k.mm(psum[:, bi, 0:tn], wo[:, kc, m * 128:(m + 1) * 128], ygT[:, kc, 0:tn], kc == 0, kc == 1,
                                 [wo.b(), ygT.b()], [pbuf(bi)])
                        evac_add_xres(m, tbi, t0, tn, psum[:, bi, 0:tn], pbuf(bi))
                bi = k.bank()
                for ci in range(4):
                    k.tr(psum[0:51, bi, ci * 128:(ci + 1) * 128], ncv[:, ci, :], ident_f[:], [ncv.b(), ident_f.b()], [pbuf(bi)])
                k.cp("dve", ncvo[0:51, :, :], psum[0:51, bi, :].rearrange("p (c n) -> p c n", c=4), [pbuf(bi)], [ncvo.b()])
                for ci in range(4):
                    cs_ = slice(cglob[ci] * 128, (cglob[ci] + 1) * 128)
                    S.dma("sp", conv_s[:, cs_], ncvo[0:48, ci, :], [ncvo.b()], ())
                    S.dma("sp", conv_p[:, cs_], ncvo[48:51, ci, :], [ncvo.b()], ())
            S.barrier()

    if cfg["mamba"]:
        mamba_layer()
    if cfg["attn"]:
        attn_layer(0)
    if cfg["mlp"]:
        mlp_layer(0)
    if cfg["pool"]:
        pool_layer()
    if cfg["attn"]:
        attn_layer(1)
    if cfg["mlp"]:
        mlp_layer(1)

    with ExitStack() as ph:
        gfin = k.sb("gfin", [128, D], F32, ph)
        S.dma("sp", gfin[:], norm_final.to_broadcast([128, D]), (), [gfin.b()])
        yt = [k.sb("yt%d" % i, [128, D], F32, ph) for i in range(4)]
        sq = k.sb("sq_scr", [128, D], F32, ph)
        stat = [k.sb("stat%d" % i, [128, 4], F32, ph) for i in range(2)]
        for t in range(NT):
            b0 = bank2()
            for c in range(DC):
                bi = b0 + c // 4
                k.tr(psum[:, bi, (c % 4) * 128:(c % 4 + 1) * 128], xres[:, c, t * 128:(t + 1) * 128], ident_f[:],
                     [xres.b((c, t)), ident_f.b()], [pbuf(bi)])
            st = stat[t % 2]
            y = yt[t % 4]
            pin = psum[:, b0:b0 + 2, :].rearrange("p a n -> p (a n)")
            k.act(sq[:], pin, AF.Square, [pbuf(b0), pbuf(b0 + 1)], [sq.b(), st.b()], accum_out=st[:, 0:1])
            k.act(st[:, 1:2], st[:, 0:1], AF.Ln, [st.b()], [st.b()], bias=EPS, scale=1.0 / D)
            k.act(st[:, 2:3], st[:, 1:2], AF.Exp, [st.b()], [st.b()], scale=-0.5)
            k.stt(y[:], pin, st[:, 2:3], gfin[:], ALU.mult, ALU.mult,
                  [pbuf(b0), pbuf(b0 + 1), st.b(), gfin.b()], [y.b()])
            dst = y_p[t * 128:(t + 1) * 128, :] if t < 16 else y_s[:, :]
            S.dma("sp", dst, y[:], [y.b()], ())
        S.barrier()
    print("ops", S.nops, "waits", S.nwaits)
    return k


_CACHE = {}


def _get_program():
    if "k" not in _CACHE:
        _CACHE["k"] = build_program()
    return _CACHE["k"]


def kernel(**inputs):
    inp = {k_: np.asarray(v) for k_, v in inputs.items()}
    kk = _get_program()
    f = lambda a: np.ascontiguousarray(a, dtype=np.float32)
    shared = {
        "norm_mix": f(inp["norm_mix"]), "norm_xattn": f(inp["norm_xattn"]), "norm_mem": f(inp["norm_mem"]),
        "norm_mlp": f(inp["norm_mlp"]), "norm_final": f(inp["norm_final"].reshape(1, D)),
        "w_in": f(inp["w_in"][0]), "conv_w": f(inp["conv_w"][0]), "conv_b": f(inp["conv_b"].reshape(1, CONV_DIM)),
        "dt_bias": f(inp["dt_bias"].reshape(1, NH)), "a_log": f(inp["a_log"].reshape(1, NH)),
        "d_skip": f(inp["d_skip"].reshape(1, NH)), "norm_gated": f(inp["norm_gated"].reshape(1, DI)),
        "w_out": f(inp["w_out"][0]), "w_pool": f(inp["w_pool"][0]), "pool_scale": f(inp["pool_scale"].reshape(1, D)),
        "w_xq": f(inp["w_xq"]), "w_xk": f(inp["w_xk"]), "w_xv": f(inp["w_xv"]), "w_xo": f(inp["w_xo"]),
        "w_up": f(inp["w_up"]), "w_down": f(inp["w_down"]),
    }
    in_maps = []
    for c in range(NCORES):
        sl = slice(c * NSB, (c + 1) * NSB)
        m = dict(shared)
        m.update({
            "xp": f(inp["x_prompt"][c]),
            "xs": f(inp["x_sample"][sl].reshape(TS, D)),
            "ck": f(inp["cache_mem_k"][:, sl].reshape(2, NSB, NMEM, D)),
            "cv": f(inp["cache_mem_v"][:, sl].reshape(2, NSB, NMEM, D)),
            "ssm": f(inp["state_ssm"][0, sl].reshape(NSB, DI, NS)),
            "sconv": f(inp["state_conv"][0, sl].reshape(NSB * 3, CONV_DIM)),
            "spool": f(inp["state_pool"][0, sl].reshape(NSB * 15, D)),
            "mem": f(inp["mem_prompt"][c]),
        })
        in_maps.append(m)
    res = run_bass_kernel_spmd(kk.nc, in_maps, core_ids=list(range(NCORES)))
    R = res.results
    cat = lambda name, shp: np.concatenate([R[c][name].reshape(shp) for c in range(NCORES)], 0)
    y_prompt = np.stack([R[c]["y_p"] for c in range(NCORES)], 0)
    y_sample = cat("y_s", (NSB, DSEQ, D))
    mk = np.stack([R[c]["mk_p"].reshape(2, NMEM, XH, XD) for c in range(NCORES)], 1)
    mv = np.stack([R[c]["mv_p"].reshape(2, NMEM, XH, XD) for c in range(NCORES)], 1)
    ssm_p = np.stack([R[c]["ssm_p"].reshape(NH, HP, NS) for c in range(NCORES)], 0)[None]
    conv_p = np.stack([R[c]["conv_p"] for c in range(NCORES)], 0)[None]
    pool_p = np.stack([R[c]["pool_p"] for c in range(NCORES)], 0)[None]
    ssm_s = cat("ssm_s", (NSB, NH, HP, NS))[None]
    conv_s = cat("conv_s", (NSB, 3, CONV_DIM))[None]
    pool_s = cat("pool_s", (NSB, 15, D))[None]
    return (y_prompt, y_sample, mk, mv, ssm_p, conv_p, pool_p, ssm_s, conv_s, pool_s)
```

```python
import numpy as np
from contextlib import ExitStack
import concourse.bass as bass
import concourse.mybir as mybir
from concourse.bass_utils import run_bass_kernel_spmd

F32 = mybir.dt.float32
BF16 = mybir.dt.bfloat16
AF = mybir.ActivationFunctionType
ALU = mybir.AluOpType
AX = mybir.AxisListType

NCORES = 8
D = 1024
DC = 8
SEQ = 2048
NSB = 16
DSEQ = 8
TS = NSB * DSEQ
T = SEQ + TS
NT = T // 128
TBS = [(0, 512), (512, 512), (1024, 512), (1536, 512), (2048, 128)]
DI = 2048
NH = 32
HP = 64
NG = 8
NS = 128
CONV_DIM = 4096
IN_PROJ = 6176
NMEM = 256
XH = 4
XD = 256
DFF = 4096
EPS = 1e-5
POOL_W = (2, 4, 8, 16)

SAME_ENGINE_SYNC = False
NDS = 32


class Buf:
    __slots__ = ("w", "r", "excl")

    def __init__(self):
        self.w = None
        self.r = {}
        self.excl = False


class Tile:
    def __init__(self, t):
        self.t = t
        self.bufs = {}

    def b(self, key=None):
        v = self.bufs.get(key)
        if v is None:
            v = self.bufs[key] = Buf()
        return v

    def __getitem__(self, idx):
        return self.t[idx]


class _Eng:
    def __init__(self, h, sem):
        self.h = h
        self.sem = sem
        self.cnt = 0
        self.waited = {}


class Sched:
    def __init__(self, nc, es):
        self.nc = nc
        self.es = es
        self.eng = {}
        self.sems = {}
        for name, h in [("pe", nc.tensor), ("act", nc.scalar), ("dve", nc.vector),
                        ("pool", nc.gpsimd), ("sp", nc.sync)]:
            sem = es.enter_context(nc.semaphore("s_" + name))
            self.eng[name] = _Eng(h, sem)
            self.sems[name] = sem
        self.dcnt = [0] * NDS
        self.dnext = 0
        self.dnext_sw = 0
        for i in range(NDS):
            self.sems[("d", i)] = es.enter_context(nc.semaphore("sd%d" % i))
        self.nwaits = 0
        self.nops = 0

    def _wait(self, e, key, val):
        if e.waited.get(key, 0) >= val:
            return
        e.h.wait_ge(self.sems[key], val)
        e.waited[key] = val
        self.nwaits += 1

    @staticmethod
    def _deps(reads, writes, en=None):
        deps = {}
        for b in reads:
            if b.w is not None:
                k, v = b.w
                if deps.get(k, 0) < v:
                    deps[k] = v
            if b.excl:
                for k, v in b.r.items():
                    if k != en and deps.get(k, 0) < v:
                        deps[k] = v
        for b in writes:
            if b.w is not None:
                k, v = b.w
                if deps.get(k, 0) < v:
                    deps[k] = v
            for k, v in b.r.items():
                if deps.get(k, 0) < v:
                    deps[k] = v
        return deps

    @staticmethod
    def _mark(ev, reads, writes):
        k, v = ev
        for b in reads:
            b.r[k] = v
        for b in writes:
            b.w = ev
            b.r = {}

    def op(self, en, fn, reads=(), writes=()):
        e = self.eng[en]
        deps = self._deps(reads, writes, en)
        raw_self = 0
        if en != "pe":
            for b in reads:
                if b.w is not None and b.w[0] == en and b.w[1] > raw_self:
                    raw_self = b.w[1]
        for k, v in deps.items():
            if k == en:
                if en == "pe":
                    continue
                if not SAME_ENGINE_SYNC:
                    v = raw_self
                    if v == 0:
                        continue
            self._wait(e, k, v)
        ins = fn(e.h)
        e.cnt += 1
        ins.then_inc(e.sem, 1)
        self._mark((en, e.cnt), reads, writes)
        self.nops += 1
        return ins

    def dma(self, qn, out, in_, reads=(), writes=(), **kw):
        e = self.eng[qn]
        deps = self._deps(reads, writes)
        for k, v in deps.items():
            self._wait(e, k, v)
        half = NDS // 2
        if qn == "pool":
            i = half + self.dnext_sw
            self.dnext_sw = (self.dnext_sw + 1) % (NDS - half)
        else:
            i = self.dnext
            self.dnext = (i + 1) % half
        if self.dcnt[i] > 0:
            self._wait(e, ("d", i), 16 * self.dcnt[i])
        self.dcnt[i] += 1
        ins = e.h.dma_start(out=out, in_=in_, **kw)
        ins.then_inc(self.sems[("d", i)], 16)
        self._mark((("d", i), 16 * self.dcnt[i]), reads, writes)
        self.nops += 1
        return ins

    def barrier(self, engines=("pe", "act", "dve", "pool", "sp")):
        for en in engines:
            e = self.eng[en]
            for on, o in self.eng.items():
                if on != en and o.cnt > 0:
                    self._wait(e, on, o.cnt)
            for i in range(NDS):
                if self.dcnt[i]:
                    self._wait(e, ("d", i), 16 * self.dcnt[i])


class K:
    def __init__(self):
        self.nc = bass.Bass("TRN2", target_bir_lowering=False)
        self.es = ExitStack()
        self.S = Sched(self.nc, self.es)
        self.bank_rr = 0

    def sb(self, name, shape, dt, es=None):
        es = es or self.es
        self.uid = getattr(self, "uid", 0) + 1
        return Tile(es.enter_context(self.nc.sbuf_tensor("%s_u%d" % (name, self.uid), list(shape), dt)))

    def dram_in(self, name, shape, dt=F32):
        return self.nc.dram_tensor(name, list(shape), dt, kind="ExternalInput").ap()

    def dram_out(self, name, shape, dt=F32):
        return self.nc.dram_tensor(name, list(shape), dt, kind="ExternalOutput").ap()

    def bank(self, excl=()):
        i = self.bank_rr
        pe_ = getattr(self, "perm_excl", ())
        while i in excl or i in pe_:
            i = (i + 1) % 8
        self.bank_rr = (i + 1) % 8
        return i

    def mm(self, out, lhsT, rhs, start, stop, reads, writes):
        return self.S.op("pe", lambda h: h.matmul(out, lhsT=lhsT, rhs=rhs, start=start, stop=stop,
                                                  skip_group_check=True), reads, writes)

    def tr(self, out, in_, ident, reads, writes):
        return self.S.op("pe", lambda h: h.transpose(out, in_, ident), reads, writes)

    def act(self, out, in_, func, reads, writes, bias=None, scale=None, accum_out=None):
        kw = {}
        if bias is not None:
            kw["bias"] = bias
        if scale is not None:
            kw["scale"] = scale
        if accum_out is not None:
            kw["accum_out"] = accum_out
        return self.S.op("act", lambda h: h.activation(out, in_, func, **kw), reads, writes)

    def ts(self, en, out, in0, s1, op0, reads, writes, s2=None, op1=None):
        if op1 is None:
            if en == "pool" and op0 == ALU.mult:
                return self.S.op(en, lambda h: h.tensor_scalar(out, in0, s1, 0.0, ALU.mult, ALU.add), reads, writes)
            return self.S.op(en, lambda h: h.tensor_scalar(out, in0, s1, None, op0), reads, writes)
        return self.S.op(en, lambda h: h.tensor_scalar(out, in0, s1, s2, op0, op1), reads, writes)

    def stt(self, out, in0, scalar, in1, op0, op1, reads, writes):
        return self.S.op("dve", lambda h: h.scalar_tensor_tensor(out, in0, scalar, in1, op0, op1), reads, writes)

    def tt(self, en, out, in0, in1, op, reads, writes):
        return self.S.op(en, lambda h: h.tensor_tensor(out, in0, in1, op), reads, writes)

    def cp(self, en, out, in_, reads, writes):
        if en == "act":
            return self.S.op("act", lambda h: h.copy(out, in_), reads, writes)
        return self.S.op(en, lambda h: h.tensor_copy(out, in_), reads, writes)

    def memset(self, en, ap, val, writes):
        return self.S.op(en, lambda h: h.memset(ap, val), (), writes)


CFG = {"mamba": True, "pool": True, "attn": True, "mlp": True}
MBS = 9


def build_program():
    k = K()
    nc, S, es = k.nc, k.S, k.es
    cfg = CFG

    xp = k.dram_in("xp", [SEQ, D])
    xs = k.dram_in("xs", [TS, D])
    ck = k.dram_in("ck", [2, NSB, NMEM, D])
    cv = k.dram_in("cv", [2, NSB, NMEM, D])
    ssm = k.dram_in("ssm", [NSB, DI, NS])
    sconv = k.dram_in("sconv", [NSB * 3, CONV_DIM])
    spool = k.dram_in("spool", [NSB * 15, D])
    mem = k.dram_in("mem", [NMEM, D])
    norm_mix = k.dram_in("norm_mix", [2, D])
    norm_xattn = k.dram_in("norm_xattn", [2, D])
    norm_mem = k.dram_in("norm_mem", [2, D])
    norm_mlp = k.dram_in("norm_mlp", [2, D])
    norm_final = k.dram_in("norm_final", [1, D])
    w_in = k.dram_in("w_in", [D, IN_PROJ])
    conv_w = k.dram_in("conv_w", [4, CONV_DIM])
    conv_b = k.dram_in("conv_b", [1, CONV_DIM])
    dt_bias = k.dram_in("dt_bias", [1, NH])
    a_log = k.dram_in("a_log", [1, NH])
    d_skip = k.dram_in("d_skip", [1, NH])
    norm_gated = k.dram_in("norm_gated", [1, DI])
    w_out = k.dram_in("w_out", [DI, D])
    w_pool = k.dram_in("w_pool", [4, 256, 256])
    pool_scale = k.dram_in("pool_scale", [1, D])
    w_xq = k.dram_in("w_xq", [2, D, D])
    w_xk = k.dram_in("w_xk", [2, D, D])
    w_xv = k.dram_in("w_xv", [2, D, D])
    w_xo = k.dram_in("w_xo", [2, D, D])
    w_up = k.dram_in("w_up", [2, D, DFF])
    w_down = k.dram_in("w_down", [2, DFF, D])

    y_p = k.dram_out("y_p", [SEQ, D])
    y_s = k.dram_out("y_s", [TS, D])
    mk_p = k.dram_out("mk_p", [2, NMEM, D])
    mv_p = k.dram_out("mv_p", [2, NMEM, D])
    ssm_p = k.dram_out("ssm_p", [DI, NS])
    conv_p = k.dram_out("conv_p", [3, CONV_DIM])
    pool_p = k.dram_out("pool_p", [15, D])
    ssm_s = k.dram_out("ssm_s", [NSB, DI, NS])
    conv_s = k.dram_out("conv_s", [NSB * 3, CONV_DIM])
    pool_s = k.dram_out("pool_s", [NSB * 15, D])

    xres = k.sb("xres", [128, DC, T], F32)
    ident_f = k.sb("ident_f", [128, 128], F32)
    ident_b = k.sb("ident_b", [128, 128], BF16)
    ones_f = k.sb("ones_f", [128, 128], F32)
    ones_b = k.sb("ones_b", [128, 128], BF16)
    zeros_b = k.sb("zeros_b", [128, 512], BF16)
    colv = k.sb("colv", [128, 32, 12], F32)
    k.wsl = []
    psum = Tile(es.enter_context(nc.psum_tensor("psum", [128, 8, 512], F32)))
    k.slab_rr = 0

    def pbuf(i):
        b_ = psum.b(i)
        b_.excl = True
        return b_

    def next_slab():
        k.slab_rr = (k.slab_rr + 1) % len(k.wsl)
        return k.wsl[k.slab_rr]

    def bank2(excl=()):
        if k.bank_rr % 2:
            k.bank_rr = (k.bank_rr + 1) % 8
        b0 = k.bank_rr
        pe_ = getattr(k, "perm_excl", ())
        while b0 in excl or (b0 + 1) in excl or b0 in pe_ or (b0 + 1) in pe_:
            b0 = (b0 + 2) % 8
        k.bank_rr = (b0 + 2) % 8
        return b0

    def xb(c, t0, tn):
        return [xres.b((c, tt)) for tt in range(t0 // 128, (t0 + tn) // 128)]

    k.memset("pool", ones_f[:], 1.0, [ones_f.b()])
    S.op("pool", lambda h: h.affine_select(ident_f[:], ones_f[:], [[-1, 128]], ALU.is_equal, 0.0,
                                           base=0, channel_multiplier=1),
         [ones_f.b()], [ident_f.b()])
    k.cp("pool", ident_b[:], ident_f[:], [ident_f.b()], [ident_b.b()])
    k.cp("pool", ones_b[:], ones_f[:], [ones_f.b()], [ones_b.b()])
    k.memset("pool", zeros_b[:], 0.0, [zeros_b.b()])
    with ExitStack() as ph:
        vecrows = k.sb("vecrows", [16, 4096], F32, ph)
        k.memset("dve", vecrows[:], 0.0, [vecrows.b()])
        S.dma("sp", vecrows[0:4, :], conv_w[:, :], (), [vecrows.b()])
        S.dma("sp", vecrows[4:5, :], conv_b[:, :], (), [vecrows.b()])
        S.dma("sp", vecrows[5:7, 0:D], norm_mix[:, :], (), [vecrows.b()])
        S.dma("sp", vecrows[7:9, 0:D], norm_xattn[:, :], (), [vecrows.b()])
        S.dma("sp", vecrows[9:11, 0:D], norm_mlp[:, :], (), [vecrows.b()])
        S.dma("sp", vecrows[11:12, 0:D], pool_scale[:, :], (), [vecrows.b()])
        bi = k.bank()
        for c in range(32):
            k.tr(psum[:, bi, c * 12:(c + 1) * 12], vecrows[0:12, c * 128:(c + 1) * 128], ident_f[0:12, 0:12],
                 [vecrows.b(), ident_f.b()], [pbuf(bi)])
        k.cp("dve", colv[:], psum[:, bi, 0:384].rearrange("p (c r) -> p c r", r=12), [pbuf(bi)], [colv.b()])
        S.barrier()
    CV_CONVW, CV_CONVB, CV_MIX, CV_XATTN, CV_MLP, CV_PSCALE = 0, 4, 5, 7, 9, 11

    with ExitStack() as ph:
        xin = [k.sb("xin%d" % i, [128, D], F32, ph) for i in range(6)]
        for t in range(NT):
            xt = xin[t % 6]
            src = xp[t * 128:(t + 1) * 128, :] if t < 16 else xs[:, :]
            S.dma("sp", xt[:], src, (), [xt.b()])
            for half in range(2):
                bi = k.bank()
                for c4 in range(4):
                    c = half * 4 + c4
                    k.tr(psum[:, bi, c4 * 128:(c4 + 1) * 128], xt[:, c * 128:(c + 1) * 128], ident_f[:],
                         [xt.b(), ident_f.b()], [pbuf(bi)])
                eng = "dve" if half == 0 else "act"
                k.cp(eng, xres[:, half * 4:half * 4 + 4, t * 128:(t + 1) * 128],
                     psum[:, bi, :].rearrange("p (c n) -> p c n", c=4),
                     [pbuf(bi)], [xres.b((c, t)) for c in range(half * 4, half * 4 + 4)])
        S.barrier()

    def rmsnorm_fm(ph_tiles, gidx, out_fn, tbs):
        sqt, rst = ph_tiles
        for tbi, (t0, tn) in tbs:
            bi = k.bank()
            for c in range(DC):
                sq = sqt[c % 2]
                k.act(sq[:, 0:tn], xres[:, c, t0:t0 + tn], AF.Square, xb(c, t0, tn), [sq.b()])
                k.mm(psum[:, bi, 0:tn], ones_b[:], sq[:, 0:tn], c == 0, c == DC - 1,
                     [ones_b.b(), sq.b()], [pbuf(bi)])
            rs = rst[tbi % 2]
            k.act(rs[:, 0:tn], psum[:, bi, 0:tn], AF.Ln, [pbuf(bi)], [rs.b()], bias=EPS, scale=1.0 / D)
            k.act(rs[:, 0:tn], rs[:, 0:tn], AF.Exp, [rs.b()], [rs.b()], scale=-0.5)
            for c in range(DC):
                o_ap, o_bufs = out_fn(c, tbi, t0, tn)
                k.stt(o_ap, xres[:, c, t0:t0 + tn], colv[:, c, gidx:gidx + 1], rs[:, 0:tn], ALU.mult, ALU.mult,
                      xb(c, t0, tn) + [colv.b(), rs.b()], o_bufs)

    def linear_fm(W, KC, c0, ncols, rhs_fn, evac_fn, tbs):
        NW = 4096 // KC
        for s0 in range(0, ncols, NW):
            nw = min(NW, ncols - s0)
            slab = next_slab()
            view = slab[:, 0:KC * nw].rearrange("p (k n) -> p k n", k=KC)
            S.dma("pool", view, W[0:KC * 128, c0 + s0:c0 + s0 + nw].rearrange("(k p) n -> p k n", p=128),
                  (), [slab.b()])
            for m in range(nw // 128):
                for tbi, (t0, tn) in tbs:
                    bi = k.bank()
                    for kc in range(KC):
                        rhs, rreads = rhs_fn(kc, tbi, t0, tn)
                        k.mm(psum[:, bi, 0:tn], view[:, kc, m * 128:(m + 1) * 128], rhs, kc == 0, kc == KC - 1,
                             [slab.b()] + rreads, [pbuf(bi)])
                    evac_fn((s0 // 128) + m, tbi, t0, tn, psum[:, bi, 0:tn], pbuf(bi))

    ALL_TBS = list(enumerate(TBS))

    def evac_add_xres(m, tbi, t0, tn, ps, pb):
        k.tt("dve", xres[:, m, t0:t0 + tn], ps, xres[:, m, t0:t0 + tn], ALU.add,
             [pb] + xb(m, t0, tn), xb(m, t0, tn))

    def mlp_layer(li):
        with ExitStack() as ph:
            k.wsl = [k.sb("wsl%d" % i, [128, 4096], BF16, ph) for i in range(3)]
            h = k.sb("mlp_h", [128, DC, T], BF16, ph)
            a = k.sb("mlp_a", [128, DC, T], BF16, ph)
            sqt = [k.sb("mlp_sq%d" % i, [128, 512], BF16, ph) for i in range(2)]
            rst = [k.sb("mlp_rs%d" % i, [128, 512], F32, ph) for i in range(2)]
            rl = [k.sb("mlp_rl%d" % i, [128, 512], F32, ph) for i in range(3)]
            k.rl_rr = 0
            rmsnorm_fm((sqt, rst), CV_MLP + li,
                       lambda c, tbi, t0, tn: (h[:, c, t0:t0 + tn], [h.b((c, tbi))]), ALL_TBS)
            for j in range(4):
                def ev_up(m, tbi, t0, tn, ps, pb):
                    r = rl[k.rl_rr]
                    k.rl_rr = (k.rl_rr + 1) % 3
                    k.act(r[:, 0:tn], ps, AF.Relu, [pb], [r.b()])
                    k.tt("pool", a[:, m, t0:t0 + tn], r[:, 0:tn], r[:, 0:tn], ALU.mult, [r.b()], [a.b((m, tbi))])
                linear_fm(w_up[li], DC, j * 1024, 1024,
                          lambda kc, tbi, t0, tn: (h[:, kc, t0:t0 + tn], [h.b((kc, tbi))]), ev_up, ALL_TBS)
                linear_fm(w_down[li][j * 1024:(j + 1) * 1024, :], DC, 0, 1024,
                          lambda kc, tbi, t0, tn: (a[:, kc, t0:t0 + tn], [a.b((kc, tbi))]), evac_add_xres, ALL_TBS)
            S.barrier()

    def attn_layer(li):
        scale = float(XD) ** -0.5
        with ExitStack() as ph:
            wq = k.sb("at_wq", [128, DC, D], BF16, ph)
            wo = k.sb("at_wo", [128, DC, D], BF16, ph)
            sqt = [k.sb("at_sq%d" % i, [128, 512], BF16, ph) for i in range(2)]
            rst = [k.sb("at_rs%d" % i, [128, 512], F32, ph) for i in range(2)]
            hn = [k.sb("at_hn0", [128, DC, 512], BF16, ph)] * 2
            qt = [k.sb("at_q0", [128, DC, 512], BF16, ph)] * 2
            ot = hn
            kT = k.sb("at_kT", [128, DC, NMEM], BF16, ph)
            Vp = k.sb("at_V", [128, 2, D], BF16, ph)
            Pt = [k.sb("at_P%d" % i, [128, XH, NMEM], BF16, ph) for i in range(2)]
            Pn = Pt
            PT = [k.sb("at_PT%d" % i, [128, XH * 2, 128], BF16, ph) for i in range(2)]
            sst = [k.sb("at_st%d" % i, [128, 16], F32, ph) for i in range(2)]

            with ExitStack() as ph2:
                k.wsl = [k.sb("wsl%d" % i, [128, 4096], BF16, ph2) for i in range(2)]
                grow = k.sb("at_grow", [128, D], F32, ph2)
                memt = k.sb("at_mem", [128, D], F32, ph2)
                mn = k.sb("at_mn", [128, D], BF16, ph2)
                mnT = k.sb("at_mnT", [128, DC, NMEM], BF16, ph2)
                ktok = k.sb("at_ktok", [128, 2, D], F32, ph2)
                vtok = ktok
                sq = k.sb("at_sqscr", [128, D], BF16, ph2)
                st = k.sb("at_mst", [128, 4], F32, ph2)
                S.dma("sp", grow[:], norm_mem[li:li + 1, :].to_broadcast([128, D]), (), [grow.b()])
                for mt in range(2):
                    S.dma("sp", memt[:], mem[mt * 128:(mt + 1) * 128, :], (), [memt.b()])
                    k.act(sq[:], memt[:], AF.Square, [memt.b()], [sq.b(), st.b()], accum_out=st[:, 0:1])
                    k.act(st[:, 1:2], st[:, 0:1], AF.Ln, [st.b()], [st.b()], bias=EPS, scale=1.0 / D)
                    k.act(st[:, 2:3], st[:, 1:2], AF.Exp, [st.b()], [st.b()], scale=-0.5)
                    k.stt(mn[:], memt[:], st[:, 2:3], grow[:], ALU.mult, ALU.mult,
                          [memt.b(), st.b(), grow.b()], [mn.b()])
                    bi = k.bank()
                    pv = psum[:, bi, :].bitcast(BF16)
                    for c in range(DC):
                        k.tr(pv[:, c * 128:(c + 1) * 128], mn[:, c * 128:(c + 1) * 128], ident_b[:],
                             [mn.b(), ident_b.b()], [pbuf(bi)])
                    k.cp("dve", mnT[:, :, mt * 128:(mt + 1) * 128], pv.rearrange("p (c n) -> p c n", c=DC),
                         [pbuf(bi)], [mnT.b()])
                for which, (W, tok, outd) in enumerate(((w_xk[li], ktok, mk_p[li]), (w_xv[li], vtok, mv_p[li]))):
                    for ch in range(2):
                        slab = next_slab()
                        view = slab[:, 0:DC * 512].rearrange("p (k n) -> p k n", k=DC)
                        S.dma("pool", view, W[:, ch * 512:(ch + 1) * 512].rearrange("(k p) n -> p k n", p=128),
                              (), [slab.b()])
                        for mt in range(2):
                            bi = k.bank()
                            for kc in range(DC):
                                k.mm(psum[:, bi, :], mnT[:, kc, mt * 128:(mt + 1) * 128], view[:, kc, :],
                                     kc == 0, kc == DC - 1, [mnT.b(), slab.b()], [pbuf(bi)])
                            k.cp("act", tok[:, mt, ch * 512:(ch + 1) * 512], psum[:, bi, :], [pbuf(bi)], [tok.b()])
                    S.dma("sp", outd.rearrange("(a p) n -> p a n", p=128), tok[:], [tok.b()], ())
                    if which == 0:
                        for mt in range(2):
                            for c4 in range(2):
                                bi = k.bank()
                                for cc in range(4):
                                    c = c4 * 4 + cc
                                    k.tr(psum[:, bi, cc * 128:(cc + 1) * 128], ktok[:, mt, c * 128:(c + 1) * 128], ident_f[:],
                                         [ktok.b(), ident_f.b()], [pbuf(bi)])
                                k.cp("dve", kT[:, c4 * 4:c4 * 4 + 4, mt * 128:(mt + 1) * 128],
                                     psum[:, bi, :].rearrange("p (c n) -> p c n", c=4), [pbuf(bi)], [kT.b()])
                    else:
                        k.cp("act", Vp[:], vtok[:], [vtok.b()], [Vp.b()])
                    if which == 0:
                        S.dma("pool", wq[:], w_xq[li].rearrange("(k p) n -> p k n", p=128), (), [wq.b()])
                        S.dma("pool", wo[:], w_xo[li].rearrange("(k p) n -> p k n", p=128), (), [wo.b()])
                S.barrier()

            Kb = [k.sb("at_Kb%d" % i, [128, 2, D], BF16, ph) for i in range(2)]
            Vb = [k.sb("at_Vb%d" % i, [128, 2, D], BF16, ph) for i in range(2)]
            kTb = [k.sb("at_kTb%d" % i, [128, DC, NMEM], BF16, ph) for i in range(2)]
            Qz = [k.sb("at_Qz%d" % i, [128, DC, 128], BF16, ph) for i in range(2)]
            for i in range(2):
                k.memset("pool", Qz[i][:], 0.0, [Qz[i].b()])

            def sm1(b0, bufs):
                P, PTt, st = bufs
                sview = psum[:, b0:b0 + 2, :].rearrange("p a (h m) -> p (a h) m", h=2)
                S.op("dve", lambda h: h.tensor_reduce(st[:, 0:4], sview, AX.X, ALU.max),
                     [pbuf(b0), pbuf(b0 + 1)], [st.b()])
                k.ts("dve", st[:, 4:8], st[:, 0:4], -scale, ALU.mult, [st.b()], [st.b()])
                for hd in range(XH):
                    k.act(P[:, hd, :], sview[:, hd, :], AF.Exp, [pbuf(b0), pbuf(b0 + 1), st.b()], [P.b(), st.b()],
                          bias=st[:, 4 + hd:5 + hd], scale=scale, accum_out=st[:, 8 + hd:9 + hd])
                S.op("dve", lambda h: h.reciprocal(st[:, 12:16], st[:, 8:12]), [st.b()], [st.b()])
                k.tt("dve", P[:], P[:], st[:, 12:16].unsqueeze(2).to_broadcast([128, XH, NMEM]), ALU.mult,
                     [P.b(), st.b()], [P.b()])

            def sm2(bufs, excl=()):
                P, PTt, st = bufs
                bi = k.bank(excl=excl)
                pv = psum[:, bi, :].bitcast(BF16)
                for hd in range(XH):
                    for mc in range(2):
                        j = hd * 2 + mc
                        k.tr(pv[:, j * 128:(j + 1) * 128], P[:, hd, mc * 128:(mc + 1) * 128], ident_b[:],
                             [P.b(), ident_b.b()], [pbuf(bi)])
                k.cp("act", PTt[:], pv.rearrange("p (j n) -> p j n", j=XH * 2), [pbuf(bi)], [PTt.b()])
                return PTt

            def softmax_tile(b0, bufs, excl=()):
                sm1(b0, bufs)
                return sm2(bufs, excl)

            qs = k.sb("at_qs", [128, DC, 128], BF16, ph)
            os_ = k.sb("at_os", [128, DC, 128], BF16, ph)
            Ps = k.sb("at_Ps", [128, XH, NMEM], BF16, ph)
            PTs = k.sb("at_PTs", [128, XH * 2, 128], BF16, ph)
            sts = k.sb("at_sts", [128, 16], F32, ph)
            SB0 = 6
            t0s, tns = TBS[4]
            hnt = hn[0]
            rmsnorm_fm((sqt, rst), CV_XATTN + li,
                       lambda c, tbi_, t0_, tn_: (hnt[:, c, 0:tn_], [hnt.b()]), [(4, (t0s, tns))])
            for m in range(DC):
                bi = k.bank()
                for kc in range(DC):
                    k.mm(psum[:, bi, 0:tns], wq[:, kc, m * 128:(m + 1) * 128], hnt[:, kc, 0:tns], kc == 0, kc == DC - 1,
                         [wq.b(), hnt.b()], [pbuf(bi)])
                k.cp("act", qs[:, m, :], psum[:, bi, 0:tns], [pbuf(bi)], [qs.b()])
            k.perm_excl = (SB0, SB0 + 1)
            for bb in range(2):
                k.mm(psum[:, SB0 + bb, :], zeros_b[:, 0:128], zeros_b[:], True, True, [zeros_b.b()], [pbuf(SB0 + bb)])

            def sample_K(b, excl):
                Kt, kTt, Qzt = Kb[b % 2], kTb[b % 2], Qz[b % 2]
                S.dma("pool", Kt[:], ck[li, b].rearrange("(a p) n -> p a n", p=128), (), [Kt.b()])
                for c4 in range(2):
                    bi = k.bank(excl=excl)
                    pv = psum[:, bi, :].bitcast(BF16)
                    for cc in range(4):
                        for mc in range(2):
                            c = c4 * 4 + cc
                            j = cc * 2 + mc
                            k.tr(pv[:, j * 128:(j + 1) * 128], Kt[:, mc, c * 128:(c + 1) * 128], ident_b[:],
                                 [Kt.b(), ident_b.b()], [pbuf(bi)])
                    k.cp("dve" if c4 == 0 else "act", kTt[:, c4 * 4:c4 * 4 + 4, :],
                         pv.rearrange("p (c m) -> p c m", c=4), [pbuf(bi)], [kTt.b()])
                if b >= 2:
                    pb_ = b - 2
                    k.memset("pool", Qzt[:, :, pb_ * 8:pb_ * 8 + 8], 0.0, [Qzt.b()])
                k.cp("pool", Qzt[:, :, b * 8:b * 8 + 8], qs[:, :, b * 8:b * 8 + 8], [qs.b()], [Qzt.b()])
                for hd in range(XH):
                    for dc in range(2):
                        k.mm(psum[:, SB0 + hd // 2, (hd % 2) * 256:(hd % 2) * 256 + 256],
                             Qzt[:, hd * 2 + dc, :], kTt[:, hd * 2 + dc, :], False, (b == NSB - 1 and dc == 1),
                             [Qzt.b(), kTt.b()], [pbuf(SB0 + hd // 2)])

            def sample_V(b):
                Vt = Vb[b % 2]
                S.dma("pool", Vt[:], cv[li, b].rearrange("(a p) n -> p a n", p=128), (), [Vt.b()])
                for d8 in range(DC):
                    hd = d8 // 2
                    for mc in range(2):
                        k.mm(psum[:, SB0 + d8 // 4, (d8 % 4) * 128 + b * 8:(d8 % 4) * 128 + b * 8 + 8],
                             Vt[:, mc, d8 * 128:(d8 + 1) * 128], PTs[:, hd * 2 + mc, b * 8:b * 8 + 8],
                             mc == 0, mc == 1, [Vt.b(), PTs.b()], [pbuf(SB0 + d8 // 4)])

            tile_ctr = 0
            gt = 0
            for tbi, (t0, tn) in ALL_TBS[0:4]:
                hnt, qtt, ott = hn[tbi % 2], qt[tbi % 2], ot[tbi % 2]
                rmsnorm_fm((sqt, rst), CV_XATTN + li,
                           lambda c, tbi_, t0_, tn_: (hnt[:, c, 0:tn_], [hnt.b()]), [(tbi, (t0, tn))])
                for m in range(DC):
                    bi = k.bank()
                    for kc in range(DC):
                        k.mm(psum[:, bi, 0:tn], wq[:, kc, m * 128:(m + 1) * 128], hnt[:, kc, 0:tn], kc == 0, kc == DC - 1,
                             [wq.b(), hnt.b()], [pbuf(bi)])
                    k.cp("act", qtt[:, m, 0:tn], psum[:, bi, 0:tn], [pbuf(bi)], [qtt.b()])

                def scoresA(tt, excl=()):
                    lsl = slice(tt * 128, (tt + 1) * 128)
                    b0 = bank2(excl)
                    for hd in range(XH):
                        for dc in range(2):
                            k.mm(psum[:, b0 + hd // 2, (hd % 2) * 256:(hd % 2) * 256 + 256],
                                 qtt[:, hd * 2 + dc, lsl], kT[:, hd * 2 + dc, :], dc == 0, dc == 1,
                                 [qtt.b(), kT.b()], [pbuf(b0 + hd // 2)])
                    return b0

                def restB2(tt, slot, excl=()):
                    lsl = slice(tt * 128, (tt + 1) * 128)
                    PTt = sm2((Pt[slot], PT[slot], sst[slot]), excl)
                    bo = bank2(excl)
                    for d8 in range(DC):
                        hd = d8 // 2
                        for mc in range(2):
                            k.mm(psum[:, bo + d8 // 4, (d8 % 4) * 128:(d8 % 4 + 1) * 128],
                                 Vp[:, mc, d8 * 128:(d8 + 1) * 128], PTt[:, hd * 2 + mc, :], mc == 0, mc == 1,
                                 [Vp.b(), PTt.b()], [pbuf(bo + d8 // 4)])
                    k.cp("act", ott[:, :, lsl], psum[:, bo:bo + 2, :].rearrange("p a (c n) -> p (a c) n", c=4),
                         [pbuf(bo), pbuf(bo + 1)], [ott.b()])

                ntile = tn // 128
                sc = [None] * ntile
                slot0 = tile_ctr
                for j in range(ntile + 2):
                    if j < ntile:
                        ex_a = (sc[j - 1], sc[j - 1] + 1) if j >= 1 else ()
                        sc[j] = scoresA(j, excl=ex_a)
                    if 0 <= j - 1 < ntile:
                        sl_ = (slot0 + j - 1) % 2
                        sm1(sc[j - 1], (Pt[sl_], PT[sl_], sst[sl_]))
                    if 0 <= j - 2 < ntile:
                        live = (sc[j], sc[j] + 1) if j < ntile else ()
                        restB2(j - 2, (slot0 + j - 2) % 2, live)
                        if gt < 8:
                            for b in (2 * gt, 2 * gt + 1):
                                sample_K(b, live)
                            if gt == 7:
                                for i in range(2):
                                    k.memset("pool", Qz[i][:], 0.0, [Qz[i].b()])
                                softmax_tile(SB0, (Ps, PTs, sts), live)
                        else:
                            for b in (2 * (gt - 8), 2 * (gt - 8) + 1):
                                sample_V(b)
                        gt += 1
                tile_ctr += ntile
                for m in range(DC):
                    bi = k.bank()
                    for kc in range(DC):
                        k.mm(psum[:, bi, 0:tn], wo[:, kc, m * 128:(m + 1) * 128], ott[:, kc, 0:tn], kc == 0, kc == DC - 1,
                             [wo.b(), ott.b()], [pbuf(bi)])
                    evac_add_xres(m, tbi, t0, tn, psum[:, bi, 0:tn], pbuf(bi))
            k.cp("act", os_[:], psum[:, SB0:SB0 + 2, :].rearrange("p a (c n) -> p (a c) n", c=4),
                 [pbuf(SB0), pbuf(SB0 + 1)], [os_.b()])
            k.perm_excl = ()
            for m in range(DC):
                bi = k.bank()
                for kc in range(DC):
                    k.mm(psum[:, bi, 0:tns], wo[:, kc, m * 128:(m + 1) * 128], os_[:, kc, :], kc == 0, kc == DC - 1,
                         [wo.b(), os_.b()], [pbuf(bi)])
                evac_add_xres(m, 4, t0s, tns, psum[:, bi, 0:tns], pbuf(bi))
            S.barrier()

    def pool_layer():
        with ExitStack() as ph:
            HP_ = 16
            up_ = k.sb("pl_up", [128, DC, HP_ + SEQ], BF16, ph)
            us_ = k.sb("pl_us", [128, DC, NSB, 24], BF16, ph)
            pooled_s = k.sb("pl_pooled_s", [128, DC, TS], BF16, ph)
            sqt = [k.sb("pl_sq%d" % i, [128, 512], BF16, ph) for i in range(2)]
            rst = [k.sb("pl_rs%d" % i, [128, 512], F32, ph) for i in range(2)]
            wA = k.sb("pl_wA", [128, 2, 2048], BF16, ph)
            wB = k.sb("pl_wB", [128, 2, 2048], BF16, ph)
            wp = k.sb("pl_wp", [128, 4, 2, 256], BF16, ph)
            ptmp = [k.sb("pl_ptmp%d" % i, [128, 512], F32, ph) for i in range(2)]
            invc = k.sb("pl_invc", [128, 4, 16], F32, ph)
            iot = k.sb("pl_iota", [128, 16], F32, ph)
            ph3 = ExitStack()
            hist = k.sb("pl_hist", [128, 2, D], F32, ph3)

            S.dma("pool", wp[:], w_pool.rearrange("g (k p) n -> p g k n", p=128), (), [wp.b()])
            S.op("pool", lambda h: h.iota(iot[:], [[1, 16]], base=1, channel_multiplier=0, allow_small_or_imprecise_dtypes=True), (), [iot.b()])
            for g, w in enumerate(POOL_W):
                k.ts("dve", invc[:, g, :], iot[:], float(w), ALU.min, [iot.b()], [invc.b()])
            S.op("dve", lambda h: h.reciprocal(invc[:], invc[:]), [invc.b()], [invc.b()])

            k.memset("pool", up_[:, :, 0:HP_], 0.0, [up_.b("hist")])
            k.memset("pool", us_[:, :, :, 0:1], 0.0, [us_.b()])
            S.dma("sp", hist[:, 0, :], spool[0:128, :], (), [hist.b()])
            S.dma("sp", hist[0:112, 1, :], spool[128:240, :], (), [hist.b()])
            usf = us_[:].rearrange("p c b j -> p c (b j)")
            for c in range(DC):
                bi = k.bank()
                k.tr(psum[:, bi, 0:128], hist[:, 0, c * 128:(c + 1) * 128], ident_f[:],
                     [hist.b(), ident_f.b()], [pbuf(bi)])
                k.tr(psum[:, bi, 128:240], hist[0:112, 1, c * 128:(c + 1) * 128], ident_f[0:112, 0:112],
                     [hist.b(), ident_f.b()], [pbuf(bi)])
                k.cp("dve", us_[:, c, :, 1:16], psum[:, bi, 0:240].rearrange("p (b j) -> p b j", j=15),
                     [pbuf(bi)], [us_.b()])

            S.barrier()
            ph3.close()
            outp = k.sb("pl_outp", [128, D], F32, ph)
            outs = k.sb("pl_outs", [128, D], F32, ph)

            def norm_out(c, tbi, t0, tn):
                if tbi < 4:
                    return up_[:, c, HP_ + t0:HP_ + t0 + tn], [up_.b((c, tbi))]
                return us_[:, c, :, 16:24], [us_.b()]
            sq_, rs_ = sqt, rst
            for tbi, (t0, tn) in ALL_TBS:
                bi = k.bank()
                for c in range(DC):
                    sq = sq_[c % 2]
                    k.act(sq[:, 0:tn], xres[:, c, t0:t0 + tn], AF.Square, xb(c, t0, tn), [sq.b()])
                    k.mm(psum[:, bi, 0:tn], ones_b[:], sq[:, 0:tn], c == 0, c == DC - 1, [ones_b.b(), sq.b()], [pbuf(bi)])
                rs = rs_[tbi % 2]
                k.act(rs[:, 0:tn], psum[:, bi, 0:tn], AF.Ln, [pbuf(bi)], [rs.b()], bias=EPS, scale=1.0 / D)
                k.act(rs[:, 0:tn], rs[:, 0:tn], AF.Exp, [rs.b()], [rs.b()], scale=-0.5)
                for c in range(DC):
                    o_ap, o_bufs = norm_out(c, tbi, t0, tn)
                    xin_ = xres[:, c, t0:t0 + tn]
                    rin_ = rs[:, 0:tn]
                    if tbi == 4:
                        xin_ = xin_.rearrange("p (b j) -> p b j", j=8)
                        rin_ = rin_.rearrange("p (b j) -> p b j", j=8)
                    k.stt(o_ap, xin_, colv[:, c, CV_MIX + 1:CV_MIX + 2], rin_, ALU.mult, ALU.mult,
                          xb(c, t0, tn) + [colv.b(), rs.b()], o_bufs)

            b0 = bank2()
            pvb = [psum[:, b0 + i, :].bitcast(BF16) for i in range(2)]
            for c in range(DC):
                k.tr(pvb[0][:, c * 128:(c + 1) * 128], up_[:, c, HP_ + SEQ - 128:HP_ + SEQ], ident_b[:],
                     [up_.b((c, 3)), ident_b.b()], [pbuf(b0)])
            k.cp("dve", outp[:], pvb[0], [pbuf(b0)], [outp.b()])
            S.dma("sp", pool_p[:, :], outp[113:128, :], [outp.b()], ())
            usn = k.sb("pl_usn", [128, DC, 128], BF16, ph)
            k.cp("pool", usn[:].rearrange("p c (b j) -> p c b j", j=8), us_[:, :, :, 16:24], [us_.b()], [usn.b()])
            for c in range(DC):
                k.tr(pvb[1][:, c * 128:(c + 1) * 128], usn[:, c, :], ident_b[:], [usn.b(), ident_b.b()], [pbuf(b0 + 1)])
            k.cp("dve", outs[:], pvb[1], [pbuf(b0 + 1)], [outs.b()])
            for b in range(NSB):
                S.dma("sp", pool_s[b * 15 + 7:b * 15 + 15, :], outs[b * 8:b * 8 + 8, :], [outs.b()], ())
                S.dma("sp", pool_s[b * 15:b * 15 + 7, :], spool[b * 15 + 8:b * 15 + 15, :], (), ())

            for g, w in enumerate(POOL_W):
                cs = slice(2 * g, 2 * g + 2)
                nst = g + 1
                L = HP_ + SEQ
                src = up_
                src_b = [up_.b((c, tb)) for c in (2 * g, 2 * g + 1) for tb in range(4)] + [up_.b("hist")]
                cur = None
                sh = 1
                for s in range(nst):
                    dst = wA if s % 2 == 0 else wB
                    if s == 0:
                        k.tt("dve", dst[:, :, 0:SEQ], up_[:, cs, HP_:L], up_[:, cs, HP_ - 1:L - 1], ALU.add,
                             src_b, [dst.b()])
                    else:
                        k.tt("dve", dst[:, :, sh:SEQ], cur[:, :, sh:SEQ], cur[:, :, 0:SEQ - sh], ALU.add,
                             [cur.b()], [dst.b()])
                        k.cp("dve", dst[:, :, 0:sh], cur[:, :, 0:sh], [cur.b()], [dst.b()])
                    cur = dst
                    sh *= 2
                tmp16 = wB if cur is wA else wA
                t16 = k.sb("pl_t16_%d" % g, [128, 2, 16], F32, ph)
                k.tt("dve", tmp16[:, :, 0:16], cur[:, :, 0:16], invc[:, g:g + 1, :].to_broadcast([128, 2, 16]), ALU.mult,
                     [cur.b(), invc.b()], [tmp16.b()])
                k.tt("dve", t16[:], tmp16[:, :, 0:16], up_[:, cs, HP_:HP_ + 16], ALU.subtract,
                     [tmp16.b()] + src_b, [t16.b()])
                k.stt(up_[:, cs, HP_:L], cur[:, :, 0:SEQ], 1.0 / w, up_[:, cs, HP_:L], ALU.mult, ALU.subtract,
                      [cur.b()] + src_b, src_b)
                k.cp("dve", up_[:, cs, HP_:HP_ + 16], t16[:], [t16.b()] + src_b, src_b)
                sA = k.sb("pl_sA%d" % g, [128, 2, NSB, 8], F32, ph)
                k.tt("dve", sA[:], us_[:, cs, :, 16:24], us_[:, cs, :, 15:23], ALU.add, [us_.b()], [sA.b()])
                for j in range(2, w):
                    k.tt("dve", sA[:], sA[:], us_[:, cs, :, 16 - j:24 - j], ALU.add, [us_.b(), sA.b()], [sA.b()])
                k.stt(pooled_s[:, cs, :].rearrange("p c (b j) -> p c b j", j=8), sA[:], 1.0 / w, us_[:, cs, :, 16:24],
                      ALU.mult, ALU.subtract, [sA.b(), us_.b()], [pooled_s.b(g)])
                for mo in range(2):
                    m = 2 * g + mo
                    for tbi, (t0, tn) in ALL_TBS:
                        bi = k.bank()
                        for kc in range(2):
                            if tbi < 4:
                                rhs_ = up_[:, 2 * g + kc, HP_ + t0:HP_ + t0 + tn]
                                rb_ = [up_.b((2 * g + kc, tbi))]
                            else:
                                rhs_ = pooled_s[:, 2 * g + kc, :]
                                rb_ = [pooled_s.b(g)]
                            k.mm(psum[:, bi, 0:tn], wp[:, g, kc, mo * 128:(mo + 1) * 128], rhs_,
                                 kc == 0, kc == 1, [wp.b()] + rb_, [pbuf(bi)])
                        pt_ = ptmp[(mo * 5 + tbi) % 2]
                        k.act(pt_[:, 0:tn], psum[:, bi, 0:tn], AF.Copy, [pbuf(bi), colv.b()], [pt_.b()],
                              scale=colv[:, m, CV_PSCALE:CV_PSCALE + 1])
                        k.tt("pool", xres[:, m, t0:t0 + tn], xres[:, m, t0:t0 + tn], pt_[:, 0:tn], ALU.add,
                             [pt_.b()] + xb(m, t0, tn), xb(m, t0, tn))

            S.barrier()

    def mamba_layer():
        with ExitStack() as ph:
            h = k.sb("mb_h", [128, DC, T], BF16, ph)
            with ExitStack() as ph0:
                sqt = [k.sb("mb_sq%d" % i, [128, 512], BF16, ph0) for i in range(2)]
                rst = [k.sb("mb_rs%d" % i, [128, 512], F32, ph0) for i in range(2)]
                rmsnorm_fm((sqt, rst), CV_MIX + 0,
                           lambda c, tbi, t0, tn: (h[:, c, t0:t0 + tn], [h.b((c, tbi))]), ALL_TBS)
                S.barrier()

            Umat = k.sb("mb_U", [128, 128], F32, ph)
            SameB = k.sb("mb_SB", [128, 128], F32, ph)
            Ublk = k.sb("mb_Ub", [128, 128], F32, ph)
            S.op("pool", lambda hh: hh.affine_select(Umat[:], ones_f[:], [[1, 128]], ALU.is_ge, 0.0,
                                                     base=0, channel_multiplier=-1), [ones_f.b()], [Umat.b()])
            S.op("pool", lambda hh: hh.affine_select(SameB[:].rearrange("p (b j) -> p b j", j=8),
                                                     ones_f[:].rearrange("p (b j) -> p b j", j=8),
                                                     [[8, 16], [0, 8]], ALU.is_ge, 0.0, base=7, channel_multiplier=-1),
                 [ones_f.b()], [SameB.b()])
            S.op("pool", lambda hh: hh.affine_select(SameB[:].rearrange("p (b j) -> p b j", j=8),
                                                     SameB[:].rearrange("p (b j) -> p b j", j=8),
                                                     [[-8, 16], [0, 8]], ALU.is_ge, 0.0, base=0, channel_multiplier=1),
                 [SameB.b()], [SameB.b()])
            k.tt("pool", Ublk[:], SameB[:], Umat[:], ALU.mult, [SameB.b(), Umat.b()], [Ublk.b()])
            brow = k.sb("mb_brow", [128, 3, NH], F32, ph)
            S.dma("sp", brow[:, 0, :], dt_bias.to_broadcast([128, NH]), (), [brow.b()])
            S.dma("sp", brow[:, 1, :], a_log.to_broadcast([128, NH]), (), [brow.b()])
            S.dma("sp", brow[:, 2, :], d_skip.to_broadcast([128, NH]), (), [brow.b()])
            k.act(brow[:, 1, :], brow[:, 1, :], AF.Exp, [brow.b()], [brow.b()])
            k.ts("dve", brow[:, 1, :], brow[:, 1, :], -1.0, ALU.mult, [brow.b()], [brow.b()])
            wdt = k.sb("mb_wdt", [128, DC, NH], BF16, ph)
            S.dma("pool", wdt[:], w_in[:, 6144:6176].rearrange("(k p) n -> p k n", p=128), (), [wdt.b()])

            dt_a = k.sb("mb_dt", [128, NT, NH], F32, ph)
            cd_a = k.sb("mb_cd", [128, NT, NH], F32, ph)
            dtd_a = k.sb("mb_dtd", [128, NT, NH], F32, ph)
            eacs_a = k.sb("mb_eacs", [128, NT, NH], F32, ph)
            cdp2 = k.sb("mb_cdp2", [128, NSB, 16], F32, ph)
            nb_a = k.sb("mb_nb", [128, NT, NH], F32, ph)
            dtAh = k.sb("mb_dtAh", [128, NT, NH], BF16, ph)
            dtAl = k.sb("mb_dtAl", [128, NT, NH], BF16, ph)
            phT = ExitStack()
            dtA_a = k.sb("mb_dtA", [128, NT, NH], F32, phT)
            nacs_a = k.sb("mb_nacs", [128, NT, NH], F32, phT)
            tmp32 = k.sb("mb_tmp32", [128, NH], F32, phT)
            Xs = k.sb("mb_Xs", [128, NSB, NH], F32, phT)
            CDB = k.sb("mb_CDB", [128, NSB, NH], F32, phT)
            dtAf = k.sb("mb_dtAf", [128, NT, NH], F32, phT)
            for t in range(NT):
                Um = Umat if t < 16 else Ublk
                Jm = ones_f if t < 16 else SameB
                bi = k.bank()
                for kc in range(DC):
                    k.mm(psum[:, bi, 0:NH], h[:, kc, t * 128:(t + 1) * 128], wdt[:, kc, :], kc == 0, kc == DC - 1,
                         [h.b((kc, min(t // 4, 4))), wdt.b()], [pbuf(bi)])
                k.tt("dve", tmp32[:], psum[:, bi, 0:NH], brow[:, 0, :], ALU.add, [pbuf(bi), brow.b()], [tmp32.b()])
                k.act(tmp32[:], tmp32[:], AF.Exp, [tmp32.b()], [tmp32.b()])
                k.act(dt_a[:, t, :], tmp32[:], AF.Ln, [tmp32.b()], [dt_a.b(t)], bias=1.0, scale=1.0)
                k.tt("dve", dtA_a[:, t, :], dt_a[:, t, :], brow[:, 1, :], ALU.mult, [dt_a.b(t), brow.b()], [dtA_a.b(t)])
                bi = k.bank()
                k.mm(psum[:, bi, 0:NH], Um[:], dtA_a[:, t, :], True, True, [Um.b(), dtA_a.b(t)], [pbuf(bi)])
                k.mm(psum[:, bi, 64:64 + NH], Jm[:], dtA_a[:, t, :], True, True, [Jm.b(), dtA_a.b(t)], [pbuf(bi)])
                k.ts("dve", nacs_a[:, t, :], psum[:, bi, 0:NH], -1.0, ALU.mult, [pbuf(bi)], [nacs_a.b(t)])
                k.act(eacs_a[:, t, :], psum[:, bi, 0:NH], AF.Exp, [pbuf(bi)], [eacs_a.b(t)])
                k.act(cd_a[:, t, :], psum[:, bi, 64:64 + NH], AF.Exp, [pbuf(bi)], [cd_a.b(t)])
                k.tt("dve", tmp32[:], psum[:, bi, 64:64 + NH], nacs_a[:, t, :], ALU.add,
                     [pbuf(bi), nacs_a.b(t)], [tmp32.b()])
                k.act(tmp32[:], tmp32[:], AF.Exp, [tmp32.b()], [tmp32.b()])
                k.tt("dve", dtd_a[:, t, :], tmp32[:], dt_a[:, t, :], ALU.mult, [tmp32.b(), dt_a.b(t)], [dtd_a.b(t)])
                k.act(tmp32[:], dt_a[:, t, :], AF.Ln, [dt_a.b(t)], [tmp32.b()])
                k.tt("dve", nb_a[:, t, :], tmp32[:], nacs_a[:, t, :], ALU.add, [tmp32.b(), nacs_a.b(t)], [nb_a.b(t)])

            k.cp("dve", dtAh[:], dtA_a[:], [dtA_a.b(t_) for t_ in range(NT)], [dtAh.b()])
            k.cp("dve", dtAf[:], dtAh[:], [dtAh.b()], [dtAf.b()])
            k.tt("dve", dtAl[:], dtA_a[:], dtAf[:], ALU.subtract, [dtA_a.b(t_) for t_ in range(NT)] + [dtAf.b()], [dtAl.b()])
            k.tt("dve", Xs[:], dtA_a[:, 16:17, :].to_broadcast([128, NSB, NH]),
                 SameB[:, 0:128:8].unsqueeze(2).to_broadcast([128, NSB, NH]), ALU.mult,
                 [dtA_a.b(16), SameB.b()], [Xs.b()])
            bi = k.bank()
            k.mm(psum[:, bi, :], ones_f[:], Xs[:].rearrange("p b h -> p (b h)"), True, True,
                 [ones_f.b(), Xs.b()], [pbuf(bi)])
            k.act(CDB[:].rearrange("p b h -> p (b h)"), psum[:, bi, :], AF.Exp, [pbuf(bi)], [CDB.b()])
            k.cp("dve", cdp2[0:64, :, :], CDB[0:64, :, 0:NH:2], [CDB.b()], [cdp2.b()])
            k.cp("dve", cdp2[64:128, :, :], CDB[64:128, :, 1:NH:2], [CDB.b()], [cdp2.b()])

            S.barrier()
            phT.close()
            wz = k.sb("mb_wz", [128, DC, 256], BF16, ph)
            wx = [k.sb("mb_wx%d" % i, [128, DC, 512], BF16, ph) for i in range(2)]
            wog = [k.sb("mb_wog0", [128, 2, D], BF16, ph)] * 2
            dgw = k.sb("mb_dgw", [128, 4, 4, 128], BF16, ph)
            rawt = [k.sb("mb_raw%d" % i, [128, 4, 3 + 512], BF16, ph) for i in range(2)]
            carry = k.sb("mb_carry", [128, 4, 3], BF16, ph)
            raws = k.sb("mb_raws", [128, 4, NSB, 11], BF16, ph)
            scv = k.sb("mb_scv", [48, 4, 128], F32, ph)
            ncv = k.sb("mb_ncv", [128, 4, 51], F32, ph)
            ncvo = k.sb("mb_ncvo", [128, 4, 128], F32, ph)
            hout = ncvo
            xact = [k.sb("mb_xact%d" % i, [128, 4, 512], BF16, ph) for i in range(2)]
            tht = [k.sb("mb_th%d" % i, [128, 512], BF16, ph) for i in range(2)]
            vht = [k.sb("mb_vh%d" % i, [128, 512], BF16, ph) for i in range(2)]
            cbh = k.sb("mb_cbh", [128, 32], F32, ph)
            k.ts("dve", cbh[:], colv[:, :, 4], 0.5, ALU.mult, [colv.b()], [cbh.b()])
            ygT = k.sb("mb_ygT", [128, 2, 512], BF16, ph)
            ngrow = [k.sb("mb_ngrow%d" % i, [128, 256], F32, ph) for i in range(2)]
            zs4 = [k.sb("mb_zs4_%d" % i, [128, 4, 256], BF16, ph) for i in range(2)]
            xdts = [k.sb("mb_xdts%d" % i, [128, 4, HP], BF16, ph) for i in range(2)]
            xB = [k.sb("mb_xB%d" % i, [128, 384], BF16, ph) for i in range(2)]
            Dg = k.sb("mb_Dg", [128, 4, 128], BF16, ph)
            cbs = [k.sb("mb_cbs%d" % i, [128, 128], BF16, ph) for i in range(2)]
            U_b = [k.sb("mb_Ubf%d" % i, [128, 128], BF16, ph) for i in range(2)]
            k.cp("pool", U_b[0][:], Umat[:], [Umat.b()], [U_b[0].b()])
            k.cp("pool", U_b[1][:], Ublk[:], [Ublk.b()], [U_b[1].b()])
            dcy = [k.sb("mb_dcy%d" % i, [128, 128], BF16, ph) for i in range(4)]
            MT = [k.sb("mb_MT%d" % i, [128, 128], BF16, ph) for i in range(8)]
            Neg4 = [k.sb("mb_Neg%d" % i, [128, 128], BF16, ph) for i in range(2)]
            for i, Us in enumerate((Umat, Ublk)):
                k.ts("dve", Neg4[i][:], Us[:], -1.0, ALU.add, [Us.b()], [Neg4[i].b()], s2=30000.0, op1=ALU.mult)
            t1 = k.sb("mb_t1", [128, 4, HP], F32, ph)
            yg = k.sb("mb_yg", [128, 256], F32, ph)
            ygn = [k.sb("mb_ygn%d" % i, [128, 256], BF16, ph) for i in range(2)]
            yst = k.sb("mb_yst", [128, 4], F32, ph)
            mhalf = k.sb("mb_mhalf", [128, 1], F32, ph)
            k.memset("pool", mhalf[:], -0.5, [mhalf.b()])
            hTf = k.sb("mb_hTf", [128, 256], F32, ph)
            hTb = k.sb("mb_hTb", [128, 256], BF16, ph)
            h0s = [k.sb("mb_h0s%d" % i, [128, 2, 2, 128], F32, ph) for i in range(4)]
            h0T = k.sb("mb_h0T", [128, 2, 256], BF16, ph)
            CTz = [k.sb("mb_CTz%d" % i, [128, 128], BF16, ph) for i in range(2)]
            Bm = [k.sb("mb_Bm%d" % i, [128, 128], BF16, ph) for i in range(2)]
            for i in range(2):
                k.memset("pool", CTz[i][:], 0.0, [CTz[i].b()])

            def cglob_of(gg):
                return [2 * gg, 2 * gg + 1, 16 + gg, 24 + gg]

            def load_wx(gg):
                for (dst0, src0, n) in ((0, DI + gg * 256, 256), (256, 2 * DI + gg * 128, 128),
                                        (384, 2 * DI + 1024 + gg * 128, 128)):
                    S.dma("pool", wx[gg % 2][:, :, dst0:dst0 + n],
                          w_in[:, src0:src0 + n].rearrange("(k p) n -> p k n", p=128), (), [wx[gg % 2].b()])

            def load_h0(gg, e8):
                for b2 in range(2):
                    S.dma("sp", h0s[e8 % 4][:, b2, :, :], ssm[e8 * 2 + b2, gg * 256:(gg + 1) * 256, :]
                          .rearrange("(a p) n -> p a n", p=128), (), [h0s[e8 % 4].b()])

            def setup_early(gg):
                cg = cglob_of(gg)
                if gg == 0:
                    load_wx(0)
                S.dma("pool", wz[:], w_in[:, gg * 256:(gg + 1) * 256].rearrange("(k p) n -> p k n", p=128), (), [wz.b()])
                if gg + 1 < NG:
                    load_wx(gg + 1)
                S.dma("sp", ngrow[gg % 2][:], norm_gated[:, gg * 256:(gg + 1) * 256].to_broadcast([128, 256]), (),
                      [ngrow[gg % 2].b()])
                for ci in range(4):
                    S.dma("sp", scv[:, ci, :], sconv[:, cg[ci] * 128:(cg[ci] + 1) * 128], (), [scv.b()])
                for ci in range(4):
                    for tap in range(4):
                        k.ts("dve", dgw[:, ci, tap, :], ident_f[:], colv[:, cg[ci], tap:tap + 1], ALU.mult,
                             [ident_f.b(), colv.b()], [dgw.b()])
                k.memset("dve", carry[:], 0.0, [carry.b()])

            def setup_late(gg):
                S.dma("pool", wog[0][:], w_out[gg * 256:(gg + 1) * 256, :].rearrange("(k p) n -> p k n", p=128), (), [wog[0].b()])
                for r in range(4):
                    k.ts("dve", Dg[:, r, :], ident_f[:], brow[:, 2, 4 * gg + r:4 * gg + r + 1], ALU.mult,
                         [ident_f.b(), brow.b()], [Dg.b()])

            def P_units(gg, tbi, bset):
                t0, tn = TBS[tbi]
                cglob = cglob_of(gg)
                wxg = wx[gg % 2]
                rw, xa, zz = rawt[bset], xact[bset], zs4[bset]
                units = []

                def u_hist():
                    bi = k.bank()
                    for ci in range(4):
                        k.tr(psum[:, bi, ci * 48:(ci + 1) * 48], scv[:, ci, :], ident_f[0:48, 0:48],
                             [scv.b(), ident_f.b()], [pbuf(bi)])
                    k.cp("dve", raws[:, :, :, 0:3], psum[:, bi, 0:192].rearrange("p (c b j) -> p c b j", c=4, j=3),
                         [pbuf(bi)], [raws.b()])
                if tbi == 4:
                    units.append(u_hist)

                def u_in(ci):
                    if ci == 0 and tbi < 4:
                        k.cp("dve", rw[:, :, 0:3], carry[:], [carry.b()], [rw.b()])
                    bi = k.bank()
                    for kc in range(DC):
                        k.mm(psum[:, bi, 0:tn], wxg[:, kc, ci * 128:(ci + 1) * 128], h[:, kc, t0:t0 + tn],
                             kc == 0, kc == DC - 1, [wxg.b(), h.b((kc, tbi))], [pbuf(bi)])
                    if tbi < 4:
                        k.cp("act", rw[:, ci, 3:3 + tn], psum[:, bi, 0:tn], [pbuf(bi)], [rw.b()])
                        if tbi == 3:
                            k.cp("dve", ncv[:, ci, 48:51], psum[:, bi, 509:512], [pbuf(bi)], [ncv.b()])
                    else:
                        pvv = psum[:, bi, 0:128].rearrange("p (b j) -> p b j", j=8)
                        k.cp("act", raws[:, ci, :, 3:11], pvv, [pbuf(bi)], [raws.b()])
                        k.cp("dve", ncv[:, ci, 0:48].rearrange("p (b j) -> p b j", j=3), pvv[:, :, 5:8],
                             [pbuf(bi)], [ncv.b()])
                    if ci == 3 and tbi < 3:
                        k.cp("dve", carry[:], rw[:, :, 512:515], [rw.b()], [carry.b()])

                def u_cv(ci):
                    bi = k.bank()
                    for tap in range(4):
                        if tbi < 4:
                            rhs_ = rw[:, ci, tap:tap + tn]
                            rb_ = rw.b()
                        else:
                            rhs_ = raws[:, ci, :, tap:tap + 8]
                            rb_ = raws.b()
                        k.mm(psum[:, bi, 0:tn], dgw[:, ci, tap, :], rhs_, tap == 0, tap == 3,
                             [dgw.b(), rb_], [pbuf(bi)])
                    th_, vh_ = tht[ci % 2], vht[ci % 2]
                    k.act(th_[:, 0:tn], psum[:, bi, 0:tn], AF.Tanh, [pbuf(bi), cbh.b()], [th_.b()],
                          bias=cbh[:, cglob[ci]:cglob[ci] + 1], scale=0.5)
                    k.act(vh_[:, 0:tn], psum[:, bi, 0:tn], AF.Identity, [pbuf(bi), cbh.b()], [vh_.b()],
                          bias=cbh[:, cglob[ci]:cglob[ci] + 1], scale=0.5)
                    k.stt(xa[:, ci, 0:tn], th_[:, 0:tn], 1.0, vh_[:, 0:tn], ALU.add, ALU.mult,
                          [th_.b(), vh_.b()], [xa.b()])

                def u_z(tt):
                    t = t0 // 128 + tt
                    bi = k.bank()
                    for kc in range(DC):
                        k.mm(psum[:, bi, 0:256], h[:, kc, t * 128:(t + 1) * 128], wz[:, kc, :],
                             kc == 0, kc == DC - 1, [h.b((kc, tbi)), wz.b()], [pbuf(bi)])
                    th_, vh_ = tht[tt % 2], vht[tt % 2]
                    k.act(th_[:, 0:256], psum[:, bi, 0:256], AF.Tanh, [pbuf(bi)], [th_.b()], scale=0.5)
                    k.act(vh_[:, 0:256], psum[:, bi, 0:256], AF.Identity, [pbuf(bi)], [vh_.b()], scale=0.5)
                    k.stt(zz[:, tt, :], th_[:, 0:256], 1.0, vh_[:, 0:256], ALU.add, ALU.mult,
                          [th_.b(), vh_.b()], [zz.b(tt)])

                for ci in range(4):
                    units.append(lambda ci=ci: u_in(ci))
                for ci in range(4):
                    units.append(lambda ci=ci: u_cv(ci))
                for tt in range(tn // 128):
                    units.append(lambda tt=tt: u_z(tt))
                return units

            setup_early(0)
            for e8 in range(4):
                load_h0(0, e8)
            for u_ in P_units(0, 0, 0):
                u_()
            for g in range(NG):
                hd0 = 4 * g
                cglob = cglob_of(g)
                wo = wog[0]
                ngrow_c = ngrow[g % 2]
                setup_late(g)
                for tbi, (t0, tn) in ALL_TBS:
                    bidx = g * len(TBS) + tbi
                    xact_c, zs4_c = xact[bidx % 2], zs4[bidx % 2]
                    if tbi + 1 < len(TBS):
                        nxt_units = P_units(g, tbi + 1, (bidx + 1) % 2)
                    elif g + 1 < NG:
                        setup_early(g + 1)
                        nxt_units = P_units(g + 1, 0, (bidx + 1) % 2)
                    else:
                        nxt_units = []

                    def head(tt):
                        t = t0 // 128 + tt
                        sl = t % 2
                        lsl = slice(tt * 128, (tt + 1) * 128)
                        Ub_ = U_b[0] if t < 16 else U_b[1]
                        Ng = Neg4[0] if t < 16 else Neg4[1]
                        bt = k.bank()
                        pv = psum[:, bt, :].bitcast(BF16)
                        for ci in range(3):
                            k.tr(pv[:, ci * 128:(ci + 1) * 128], xact_c[:, ci, lsl], ident_b[:],
                                 [xact_c.b(), ident_b.b()], [pbuf(bt)])
                        xv = pv[:, 0:256].rearrange("p (r q) -> p r q", q=HP)
                        k.tt("dve", xdts[sl][:], xv, dtd_a[:, t, hd0:hd0 + 4].unsqueeze(2).to_broadcast([128, 4, HP]), ALU.mult,
                             [pbuf(bt), dtd_a.b(t)], [xdts[sl].b()])
                        k.cp("act", xB[sl][:], pv[:, 0:384], [pbuf(bt)], [xB[sl].b()])
                        bc = k.bank()
                        k.mm(psum[:, bc, 0:128], xact_c[:, 2, lsl], xact_c[:, 3, lsl], True, True, [xact_c.b()], [pbuf(bc)])
                        k.cp("act", cbs[sl][:], psum[:, bc, 0:128], [pbuf(bc)], [cbs[sl].b()])
                        br = k.bank()
                        for r in range(4):
                            k.mm(psum[:, br, r * 128:(r + 1) * 128], ident_b[:], Ng[:], r == 0, False,
                                 [ident_b.b(), Ng.b()], [pbuf(br)])
                        for r in range(4):
                            k.mm(psum[:, br, r * 128:(r + 1) * 128],
                                 dtAh[:, t, hd0 + r:hd0 + r + 1].to_broadcast([128, 128]), Ub_[:], False, False,
                                 [dtAh.b(), Ub_.b()], [pbuf(br)])
                            k.mm(psum[:, br, r * 128:(r + 1) * 128],
                                 dtAl[:, t, hd0 + r:hd0 + r + 1].to_broadcast([128, 128]), Ub_[:], False, r == 3,
                                 [dtAl.b(), Ub_.b()], [pbuf(br)])
                        for r in range(4):
                            dc_ = dcy[(t * 4 + r) % 4]
                            mt_ = MT[(t % 2) * 4 + r]
                            k.act(dc_[:], psum[:, br, r * 128:(r + 1) * 128], AF.Exp, [pbuf(br), nb_a.b(t)], [dc_.b()],
                                  bias=nb_a[:, t, hd0 + r:hd0 + r + 1], scale=1.0)
                            k.tt("pool", mt_[:], dc_[:], cbs[sl][:], ALU.mult, [dc_.b(), cbs[sl].b()], [mt_.b()])
                        return None

                    def tail(tt, banks):
                        t = t0 // 128 + tt
                        sl = t % 2
                        lsl = slice(tt * 128, (tt + 1) * 128)
                        has_off = True
                        by = k.bank()
                        for r in range(4):
                            mt_ = MT[(t % 2) * 4 + r]
                            k.mm(psum[:, by, r * HP:(r + 1) * HP], mt_[:], xB[sl][:, r * HP:(r + 1) * HP], True, False,
                                 [mt_.b(), xB[sl].b()], [pbuf(by)])
                            k.mm(psum[:, by, r * HP:(r + 1) * HP], Dg[:, r, :], xB[sl][:, r * HP:(r + 1) * HP], False, True,
                                 [Dg.b(), xB[sl].b()], [pbuf(by)])
                        bs_ = None
                        if t < 16:
                            bs_ = k.bank()
                            k.mm(psum[:, bs_, 0:256], xB[sl][:, 256:384], xdts[sl][:].rearrange("p r q -> p (r q)"), True, True,
                                 [xB[sl].b(), xdts[sl].b()], [pbuf(bs_)])
                        bo = k.bank()
                        held = (by, bo)
                        if t < 16:
                            if t == 0:
                                has_off = False
                            else:
                                k.mm(psum[:, bo, 0:256], xact_c[:, 3, lsl], hTb[:], True, True, [xact_c.b(), hTb.b()], [pbuf(bo)])
                        else:
                            k.perm_excl = held
                            for e8 in range(8):
                                hs_ = h0s[e8 % 4]
                                bh = k.bank(excl=held)
                                for b2 in range(2):
                                    for a in range(2):
                                        k.tr(psum[:, bh, (b2 * 2 + a) * 128:(b2 * 2 + a + 1) * 128],
                                             hs_[:, b2, a, :], ident_f[:], [hs_.b(), ident_f.b()], [pbuf(bh)])
                                k.cp("act" if e8 % 2 else "dve", h0T[:],
                                     psum[:, bh, :].rearrange("p (b q) -> p b q", b=2), [pbuf(bh)], [h0T.b()])
                                for b4 in range(2):
                                    b = e8 * 2 + b4
                                    cz = CTz[b % 2]
                                    if b >= 2:
                                        k.memset("pool", cz[:, (b - 2) * 8:(b - 2) * 8 + 8], 0.0, [cz.b()])
                                    k.cp("pool", cz[:, b * 8:b * 8 + 8], xact_c[:, 3, b * 8:b * 8 + 8], [xact_c.b()], [cz.b()])
                                    k.mm(psum[:, bo, 0:256], cz[:], h0T[:, b4, :], b == 0, b == NSB - 1,
                                         [cz.b(), h0T.b()], [pbuf(bo)])
                                    bmt = Bm[b % 2]
                                    k.ts("pool", bmt[:], xB[sl][:, 256:384], SameB[:, b * 8:b * 8 + 1], ALU.mult,
                                         [xB[sl].b(), SameB.b()], [bmt.b()])
                                    for a in range(2):
                                        bn = k.bank(excl=held)
                                        k.mm(psum[:, bn, 0:128], xdts[sl][:, 2 * a:2 * a + 2, :].rearrange("p r q -> p (r q)"),
                                             bmt[:], True, True, [xdts[sl].b(), bmt.b()], [pbuf(bn)])
                                        k.stt(hs_[:, b4, a, :], hs_[:, b4, a, :], cdp2[:, b, 2 * g + a:2 * g + a + 1],
                                              psum[:, bn, 0:128], ALU.mult, ALU.add, [hs_.b(), cdp2.b(), pbuf(bn)], [hs_.b()])
                                for b4 in range(2):
                                    S.dma("sp", ssm_s[e8 * 2 + b4, g * 256:(g + 1) * 256, :]
                                          .rearrange("(a p) n -> p a n", p=128), hs_[:, b4, :, :], [hs_.b()], ())
                                if e8 + 4 < 8:
                                    load_h0(g, e8 + 4)
                                elif g + 1 < NG:
                                    load_h0(g + 1, e8 - 4)
                                for _ in range(2):
                                    if ucur[0] < len(nxt_units):
                                        nxt_units[ucur[0]]()
                                        ucur[0] += 1
                            k.perm_excl = ()
                            for i in range(2):
                                k.memset("pool", CTz[i][:], 0.0, [CTz[i].b()])
                        if t < 16:
                            if t == 0:
                                k.cp("dve", hTf[:], psum[:, bs_, 0:256], [pbuf(bs_)], [hTf.b()])
                            else:
                                hv_ = hTf[:].rearrange("p (r q) -> p r q", q=HP)
                                k.tt("dve", hv_, hv_, cd_a[:, t, hd0:hd0 + 4].unsqueeze(2).to_broadcast([128, 4, HP]), ALU.mult,
                                     [hTf.b(), cd_a.b(t)], [hTf.b()])
                                k.tt("dve", hTf[:], hTf[:], psum[:, bs_, 0:256], ALU.add, [hTf.b(), pbuf(bs_)], [hTf.b()])
                            if t < 15:
                                k.cp("pool", hTb[:], hTf[:], [hTf.b()], [hTb.b()])
                            else:
                                bf_ = k.bank(excl=held)
                                for a in range(2):
                                    k.tr(psum[:, bf_, a * 128:(a + 1) * 128], hTf[:, a * 128:(a + 1) * 128], ident_f[:],
                                         [hTf.b(), ident_f.b()], [pbuf(bf_)])
                                k.cp("dve", hout[:, 0:2, :], psum[:, bf_, 0:256].rearrange("p (a n) -> p a n", a=2), [pbuf(bf_)], [hout.b()])
                                S.dma("sp", ssm_p[g * 256:(g + 1) * 256, :].rearrange("(a p) n -> p a n", p=128), hout[:, 0:2, :],
                                      [hout.b()], ())
                        yv = psum[:, by, 0:256]
                        if has_off:
                            k.tt("dve", t1[:], psum[:, bo, 0:256].rearrange("p (r q) -> p r q", q=HP),
                                 eacs_a[:, t, hd0:hd0 + 4].unsqueeze(2).to_broadcast([128, 4, HP]), ALU.mult,
                                 [pbuf(bo), eacs_a.b(t)], [t1.b()])
                            k.tt("dve", yg[:], yv, t1[:].rearrange("p r q -> p (r q)"), ALU.add, [pbuf(by), t1.b()], [yg.b()])
                            k.tt("dve", yg[:], yg[:], zs4_c[:, tt, :], ALU.mult, [yg.b(), zs4_c.b(tt)], [yg.b()])
                        else:
                            k.tt("dve", yg[:], yv, zs4_c[:, tt, :], ALU.mult, [pbuf(by), zs4_c.b(tt)], [yg.b()])
                        yn_ = ygn[t % 2]
                        k.act(yn_[:], yg[:], AF.Square, [yg.b()], [yn_.b(), yst.b()], accum_out=yst[:, 0:1])
                        k.ts("pool", yst[:, 1:2], yst[:, 0:1], 1.0 / 256, ALU.mult, [yst.b()], [yst.b()], s2=EPS, op1=ALU.add)
                        k.tt("pool", yst[:, 2:3], yst[:, 1:2], mhalf[:, 0:1], ALU.pow, [yst.b(), mhalf.b()], [yst.b()])
                        yn_ = ygn[t % 2]
                        k.stt(yn_[:], yg[:], yst[:, 2:3], ngrow_c[:], ALU.mult, ALU.mult, [yg.b(), yst.b(), ngrow_c.b()], [yn_.b()])

                    def fin(tt):
                        t = t0 // 128 + tt
                        lsl = slice(tt * 128, (tt + 1) * 128)
                        yn_ = ygn[t % 2]
                        bg = k.bank()
                        pg = psum[:, bg, :].bitcast(BF16)
                        for a in range(2):
                            k.tr(pg[:, a * 128:(a + 1) * 128], yn_[:, a * 128:(a + 1) * 128], ident_b[:],
                                 [yn_.b(), ident_b.b()], [pbuf(bg)])
                        k.cp("act", ygT[:, :, lsl], pg[:, 0:256].rearrange("p (a n) -> p a n", a=2), [pbuf(bg)], [ygT.b()])

                    ntile = tn // 128
                    per_step = -(-len(nxt_units) // ntile)
                    ucur = [0]
                    head(0)
                    for tt in range(ntile):
                        if tt + 1 < ntile:
                            head(tt + 1)
                        if tt >= 1:
                            fin(tt - 1)
                        tail(tt, None)
                        lim = min(len(nxt_units), (tt + 1) * per_step)
                        while ucur[0] < lim:
                            nxt_units[ucur[0]]()
                            ucur[0] += 1
                    fin(ntile - 1)
                    for m in range(DC):
                        bi = k.bank()
                        for kc in range(2):
                            k.mm(psum[:, bi, 0:tn], wo[:, kc, m * 128:(m + 1) * 128], ygT[:, kc, 0:tn], kc == 0, kc == 1,
                                 [wo.b(), ygT.b()], [pbuf(bi)])
                        evac_add_xres(m, tbi, t0, tn, psum[:, bi, 0:tn], pbuf(bi))
                bi = k.bank()
                for ci in range(4):
                    k.tr(psum[0:51, bi, ci * 128:(ci + 1) * 128], ncv[:, ci, :], ident_f[:], [ncv.b(), ident_f.b()], [pbuf(bi)])
                k.cp("dve", ncvo[0:51, :, :], psum[0:51, bi, :].rearrange("p (c n) -> p c n", c=4), [pbuf(bi)], [ncvo.b()])
                for ci in range(4):
                    cs_ = slice(cglob[ci] * 128, (cglob[ci] + 1) * 128)
                    S.dma("sp", conv_s[:, cs_], ncvo[0:48, ci, :], [ncvo.b()], ())
                    S.dma("sp", conv_p[:, cs_], ncvo[48:51, ci, :], [ncvo.b()], ())
            S.barrier()

    if cfg["mamba"]:
        mamba_layer()
    if cfg["attn"]:
        attn_layer(0)
    if cfg["mlp"]:
        mlp_layer(0)
    if cfg["pool"]:
        pool_layer()
    if cfg["attn"]:
        attn_layer(1)
    if cfg["mlp"]:
        mlp_layer(1)

    with ExitStack() as ph:
        gfin = k.sb("gfin", [128, D], F32, ph)
        S.dma("sp", gfin[:], norm_final.to_broadcast([128, D]), (), [gfin.b()])
        yt = [k.sb("yt%d" % i, [128, D], F32, ph) for i in range(4)]
        sq = k.sb("sq_scr", [128, D], F32, ph)
        stat = [k.sb("stat%d" % i, [128, 4], F32, ph) for i in range(2)]
        for t in range(NT):
            b0 = bank2()
            for c in range(DC):
                bi = b0 + c // 4
                k.tr(psum[:, bi, (c % 4) * 128:(c % 4 + 1) * 128], xres[:, c, t * 128:(t + 1) * 128], ident_f[:],
                     [xres.b((c, t)), ident_f.b()], [pbuf(bi)])
            st = stat[t % 2]
            y = yt[t % 4]
            pin = psum[:, b0:b0 + 2, :].rearrange("p a n -> p (a n)")
            k.act(sq[:], pin, AF.Square, [pbuf(b0), pbuf(b0 + 1)], [sq.b(), st.b()], accum_out=st[:, 0:1])
            k.act(st[:, 1:2], st[:, 0:1], AF.Ln, [st.b()], [st.b()], bias=EPS, scale=1.0 / D)
            k.act(st[:, 2:3], st[:, 1:2], AF.Exp, [st.b()], [st.b()], scale=-0.5)
            k.stt(y[:], pin, st[:, 2:3], gfin[:], ALU.mult, ALU.mult,
                  [pbuf(b0), pbuf(b0 + 1), st.b(), gfin.b()], [y.b()])
            dst = y_p[t * 128:(t + 1) * 128, :] if t < 16 else y_s[:, :]
            S.dma("sp", dst, y[:], [y.b()], ())
        S.barrier()
    print("ops", S.nops, "waits", S.nwaits)
    return k


_CACHE = {}


def _get_program():
    if "k" not in _CACHE:
        _CACHE["k"] = build_program()
    return _CACHE["k"]


def kernel(**inputs):
    inp = {k_: np.asarray(v) for k_, v in inputs.items()}
    kk = _get_program()
    f = lambda a: np.ascontiguousarray(a, dtype=np.float32)
    shared = {
        "norm_mix": f(inp["norm_mix"]), "norm_xattn": f(inp["norm_xattn"]), "norm_mem": f(inp["norm_mem"]),
        "norm_mlp": f(inp["norm_mlp"]), "norm_final": f(inp["norm_final"].reshape(1, D)),
        "w_in": f(inp["w_in"][0]), "conv_w": f(inp["conv_w"][0]), "conv_b": f(inp["conv_b"].reshape(1, CONV_DIM)),
        "dt_bias": f(inp["dt_bias"].reshape(1, NH)), "a_log": f(inp["a_log"].reshape(1, NH)),
        "d_skip": f(inp["d_skip"].reshape(1, NH)), "norm_gated": f(inp["norm_gated"].reshape(1, DI)),
        "w_out": f(inp["w_out"][0]), "w_pool": f(inp["w_pool"][0]), "pool_scale": f(inp["pool_scale"].reshape(1, D)),
        "w_xq": f(inp["w_xq"]), "w_xk": f(inp["w_xk"]), "w_xv": f(inp["w_xv"]), "w_xo": f(inp["w_xo"]),
        "w_up": f(inp["w_up"]), "w_down": f(inp["w_down"]),
    }
    in_maps = []
    for c in range(NCORES):
        sl = slice(c * NSB, (c + 1) * NSB)
        m = dict(shared)
        m.update({
            "xp": f(inp["x_prompt"][c]),
            "xs": f(inp["x_sample"][sl].reshape(TS, D)),
            "ck": f(inp["cache_mem_k"][:, sl].reshape(2, NSB, NMEM, D)),
            "cv": f(inp["cache_mem_v"][:, sl].reshape(2, NSB, NMEM, D)),
            "ssm": f(inp["state_ssm"][0, sl].reshape(NSB, DI, NS)),
            "sconv": f(inp["state_conv"][0, sl].reshape(NSB * 3, CONV_DIM)),
            "spool": f(inp["state_pool"][0, sl].reshape(NSB * 15, D)),
            "mem": f(inp["mem_prompt"][c]),
        })
        in_maps.append(m)
    res = run_bass_kernel_spmd(kk.nc, in_maps, core_ids=list(range(NCORES)))
    R = res.results
    cat = lambda name, shp: np.concatenate([R[c][name].reshape(shp) for c in range(NCORES)], 0)
    y_prompt = np.stack([R[c]["y_p"] for c in range(NCORES)], 0)
    y_sample = cat("y_s", (NSB, DSEQ, D))
    mk = np.stack([R[c]["mk_p"].reshape(2, NMEM, XH, XD) for c in range(NCORES)], 1)
    mv = np.stack([R[c]["mv_p"].reshape(2, NMEM, XH, XD) for c in range(NCORES)], 1)
    ssm_p = np.stack([R[c]["ssm_p"].reshape(NH, HP, NS) for c in range(NCORES)], 0)[None]
    conv_p = np.stack([R[c]["conv_p"] for c in range(NCORES)], 0)[None]
    pool_p = np.stack([R[c]["pool_p"] for c in range(NCORES)], 0)[None]
    ssm_s = cat("ssm_s", (NSB, NH, HP, NS))[None]
    conv_s = cat("conv_s", (NSB, 3, CONV_DIM))[None]
    pool_s = cat("pool_s", (NSB, 15, D))[None]
    return (y_prompt, y_sample, mk, mv, ssm_p, conv_p, pool_p, ssm_s, conv_s, pool_s)
```

```python
import numpy as np
from contextlib import ExitStack
import concourse.bass as bass
import concourse.mybir as mybir
from concourse.bass_utils import run_bass_kernel_spmd

F32 = mybir.dt.float32
BF16 = mybir.dt.bfloat16
AF = mybir.ActivationFunctionType
ALU = mybir.AluOpType
AX = mybir.AxisListType

NCORES = 8
D = 1024
DC = 8
SEQ = 2048
NSB = 16
DSEQ = 8
TS = NSB * DSEQ
T = SEQ + TS
NT = T // 128
TBS = [(0, 512), (512, 512), (1024, 512), (1536, 512), (2048, 128)]
DI = 2048
NH = 32
HP = 64
NG = 8
NS = 128
CONV_DIM = 4096
IN_PROJ = 6176
NMEM = 256
XH = 4
XD = 256
DFF = 4096
EPS = 1e-5
POOL_W = (2, 4, 8, 16)

SAME_ENGINE_SYNC = False
NDS = 32


class Buf:
    __slots__ = ("w", "r", "excl")

    def __init__(self):
        self.w = None
        self.r = {}
        self.excl = False


class Tile:
    def __init__(self, t):
        self.t = t
        self.bufs = {}

    def b(self, key=None):
        v = self.bufs.get(key)
        if v is None:
            v = self.bufs[key] = Buf()
        return v

    def __getitem__(self, idx):
        return self.t[idx]


class _Eng:
    def __init__(self, h, sem):
        self.h = h
        self.sem = sem
        self.cnt = 0
        self.waited = {}


class Sched:
    def __init__(self, nc, es):
        self.nc = nc
        self.es = es
        self.eng = {}
        self.sems = {}
        for name, h in [("pe", nc.tensor), ("act", nc.scalar), ("dve", nc.vector),
                        ("pool", nc.gpsimd), ("sp", nc.sync)]:
            sem = es.enter_context(nc.semaphore("s_" + name))
            self.eng[name] = _Eng(h, sem)
            self.sems[name] = sem
        self.dcnt = [0] * NDS
        self.dnext = 0
        self.dnext_sw = 0
        for i in range(NDS):
            self.sems[("d", i)] = es.enter_context(nc.semaphore("sd%d" % i))
        self.nwaits = 0
        self.nops = 0

    def _wait(self, e, key, val):
        if e.waited.get(key, 0) >= val:
            return
        e.h.wait_ge(self.sems[key], val)
        e.waited[key] = val
        self.nwaits += 1

    @staticmethod
    def _deps(reads, writes, en=None):
        deps = {}
        for b in reads:
            if b.w is not None:
                k, v = b.w
                if deps.get(k, 0) < v:
                    deps[k] = v
            if b.excl:
                for k, v in b.r.items():
                    if k != en and deps.get(k, 0) < v:
                        deps[k] = v
        for b in writes:
            if b.w is not None:
                k, v = b.w
                if deps.get(k, 0) < v:
                    deps[k] = v
            for k, v in b.r.items():
                if deps.get(k, 0) < v:
                    deps[k] = v
        return deps

    @staticmethod
    def _mark(ev, reads, writes):
        k, v = ev
        for b in reads:
            b.r[k] = v
        for b in writes:
            b.w = ev
            b.r = {}

    def op(self, en, fn, reads=(), writes=()):
        e = self.eng[en]
        deps = self._deps(reads, writes, en)
        raw_self = 0
        if en != "pe":
            for b in reads:
                if b.w is not None and b.w[0] == en and b.w[1] > raw_self:
                    raw_self = b.w[1]
        for k, v in deps.items():
            if k == en:
                if en == "pe":
                    continue
                if not SAME_ENGINE_SYNC:
                    v = raw_self
                    if v == 0:
                        continue
            self._wait(e, k, v)
        ins = fn(e.h)
        e.cnt += 1
        ins.then_inc(e.sem, 1)
        self._mark((en, e.cnt), reads, writes)
        self.nops += 1
        return ins

    def dma(self, qn, out, in_, reads=(), writes=(), **kw):
        e = self.eng[qn]
        deps = self._deps(reads, writes)
        for k, v in deps.items():
            self._wait(e, k, v)
        half = NDS // 2
        if qn == "pool":
            i = half + self.dnext_sw
            self.dnext_sw = (self.dnext_sw + 1) % (NDS - half)
        else:
            i = self.dnext
            self.dnext = (i + 1) % half
        if self.dcnt[i] > 0:
            self._wait(e, ("d", i), 16 * self.dcnt[i])
        self.dcnt[i] += 1
        ins = e.h.dma_start(out=out, in_=in_, **kw)
        ins.then_inc(self.sems[("d", i)], 16)
        self._mark((("d", i), 16 * self.dcnt[i]), reads, writes)
        self.nops += 1
        return ins

    def barrier(self, engines=("pe", "act", "dve", "pool", "sp")):
        for en in engines:
            e = self.eng[en]
            for on, o in self.eng.items():
                if on != en and o.cnt > 0:
                    self._wait(e, on, o.cnt)
            for i in range(NDS):
                if self.dcnt[i]:
                    self._wait(e, ("d", i), 16 * self.dcnt[i])


class K:
    def __init__(self):
        self.nc = bass.Bass("TRN2", target_bir_lowering=False)
        self.es = ExitStack()
        self.S = Sched(self.nc, self.es)
        self.bank_rr = 0

    def sb(self, name, shape, dt, es=None):
        es = es or self.es
        self.uid = getattr(self, "uid", 0) + 1
        return Tile(es.enter_context(self.nc.sbuf_tensor("%s_u%d" % (name, self.uid), list(shape), dt)))

    def dram_in(self, name, shape, dt=F32):
        return self.nc.dram_tensor(name, list(shape), dt, kind="ExternalInput").ap()

    def dram_out(self, name, shape, dt=F32):
        return self.nc.dram_tensor(name, list(shape), dt, kind="ExternalOutput").ap()

    def bank(self, excl=()):
        i = self.bank_rr
        pe_ = getattr(self, "perm_excl", ())
        while i in excl or i in pe_:
            i = (i + 1) % 8
        self.bank_rr = (i + 1) % 8
        return i

    def mm(self, out, lhsT, rhs, start, stop, reads, writes):
        return self.S.op("pe", lambda h: h.matmul(out, lhsT=lhsT, rhs=rhs, start=start, stop=stop,
                                                  skip_group_check=True), reads, writes)

    def tr(self, out, in_, ident, reads, writes):
        return self.S.op("pe", lambda h: h.transpose(out, in_, ident), reads, writes)

    def act(self, out, in_, func, reads, writes, bias=None, scale=None, accum_out=None):
        kw = {}
        if bias is not None:
            kw["bias"] = bias
        if scale is not None:
            kw["scale"] = scale
        if accum_out is not None:
            kw["accum_out"] = accum_out
        return self.S.op("act", lambda h: h.activation(out, in_, func, **kw), reads, writes)

    def ts(self, en, out, in0, s1, op0, reads, writes, s2=None, op1=None):
        if op1 is None:
            if en == "pool" and op0 == ALU.mult:
                return self.S.op(en, lambda h: h.tensor_scalar(out, in0, s1, 0.0, ALU.mult, ALU.add), reads, writes)
            return self.S.op(en, lambda h: h.tensor_scalar(out, in0, s1, None, op0), reads, writes)
        return self.S.op(en, lambda h: h.tensor_scalar(out, in0, s1, s2, op0, op1), reads, writes)

    def stt(self, out, in0, scalar, in1, op0, op1, reads, writes):
        return self.S.op("dve", lambda h: h.scalar_tensor_tensor(out, in0, scalar, in1, op0, op1), reads, writes)

    def tt(self, en, out, in0, in1, op, reads, writes):
        return self.S.op(en, lambda h: h.tensor_tensor(out, in0, in1, op), reads, writes)

    def cp(self, en, out, in_, reads, writes):
        if en == "act":
            return self.S.op("act", lambda h: h.copy(out, in_), reads, writes)
        return self.S.op(en, lambda h: h.tensor_copy(out, in_), reads, writes)

    def memset(self, en, ap, val, writes):
        return self.S.op(en, lambda h: h.memset(ap, val), (), writes)


CFG = {"mamba": True, "pool": True, "attn": True, "mlp": True}
MBS = 9


def build_program():
    k = K()
    nc, S, es = k.nc, k.S, k.es
    cfg = CFG

    xp = k.dram_in("xp", [SEQ, D])
    xs = k.dram_in("xs", [TS, D])
    ck = k.dram_in("ck", [2, NSB, NMEM, D])
    cv = k.dram_in("cv", [2, NSB, NMEM, D])
    ssm = k.dram_in("ssm", [NSB, DI, NS])
    sconv = k.dram_in("sconv", [NSB * 3, CONV_DIM])
    spool = k.dram_in("spool", [NSB * 15, D])
    mem = k.dram_in("mem", [NMEM, D])
    norm_mix = k.dram_in("norm_mix", [2, D])
    norm_xattn = k.dram_in("norm_xattn", [2, D])
    norm_mem = k.dram_in("norm_mem", [2, D])
    norm_mlp = k.dram_in("norm_mlp", [2, D])
    norm_final = k.dram_in("norm_final", [1, D])
    w_in = k.dram_in("w_in", [D, IN_PROJ])
    conv_w = k.dram_in("conv_w", [4, CONV_DIM])
    conv_b = k.dram_in("conv_b", [1, CONV_DIM])
    dt_bias = k.dram_in("dt_bias", [1, NH])
    a_log = k.dram_in("a_log", [1, NH])
    d_skip = k.dram_in("d_skip", [1, NH])
    norm_gated = k.dram_in("norm_gated", [1, DI])
    w_out = k.dram_in("w_out", [DI, D])
    w_pool = k.dram_in("w_pool", [4, 256, 256])
    pool_scale = k.dram_in("pool_scale", [1, D])
    w_xq = k.dram_in("w_xq", [2, D, D])
    w_xk = k.dram_in("w_xk", [2, D, D])
    w_xv = k.dram_in("w_xv", [2, D, D])
    w_xo = k.dram_in("w_xo", [2, D, D])
    w_up = k.dram_in("w_up", [2, D, DFF])
    w_down = k.dram_in("w_down", [2, DFF, D])

    y_p = k.dram_out("y_p", [SEQ, D])
    y_s = k.dram_out("y_s", [TS, D])
    mk_p = k.dram_out("mk_p", [2, NMEM, D])
    mv_p = k.dram_out("mv_p", [2, NMEM, D])
    ssm_p = k.dram_out("ssm_p", [DI, NS])
    conv_p = k.dram_out("conv_p", [3, CONV_DIM])
    pool_p = k.dram_out("pool_p", [15, D])
    ssm_s = k.dram_out("ssm_s", [NSB, DI, NS])
    conv_s = k.dram_out("conv_s", [NSB * 3, CONV_DIM])
    pool_s = k.dram_out("pool_s", [NSB * 15, D])

    xres = k.sb("xres", [128, DC, T], F32)
    ident_f = k.sb("ident_f", [128, 128], F32)
    ident_b = k.sb("ident_b", [128, 128], BF16)
    ones_f = k.sb("ones_f", [128, 128], F32)
    ones_b = k.sb("ones_b", [128, 128], BF16)
    zeros_b = k.sb("zeros_b", [128, 512], BF16)
    colv = k.sb("colv", [128, 32, 12], F32)
    k.wsl = []
    psum = Tile(es.enter_context(nc.psum_tensor("psum", [128, 8, 512], F32)))
    k.slab_rr = 0

    def pbuf(i):
        b_ = psum.b(i)
        b_.excl = True
        return b_

    def next_slab():
        k.slab_rr = (k.slab_rr + 1) % len(k.wsl)
        return k.wsl[k.slab_rr]

    def bank2(excl=()):
        if k.bank_rr % 2:
            k.bank_rr = (k.bank_rr + 1) % 8
        b0 = k.bank_rr
        pe_ = getattr(k, "perm_excl", ())
        while b0 in excl or (b0 + 1) in excl or b0 in pe_ or (b0 + 1) in pe_:
            b0 = (b0 + 2) % 8
        k.bank_rr = (b0 + 2) % 8
        return b0

    def xb(c, t0, tn):
        return [xres.b((c, tt)) for tt in range(t0 // 128, (t0 + tn) // 128)]

    k.memset("pool", ones_f[:], 1.0, [ones_f.b()])
    S.op("pool", lambda h: h.affine_select(ident_f[:], ones_f[:], [[-1, 128]], ALU.is_equal, 0.0,
                                           base=0, channel_multiplier=1),
         [ones_f.b()], [ident_f.b()])
    k.cp("pool", ident_b[:], ident_f[:], [ident_f.b()], [ident_b.b()])
    k.cp("pool", ones_b[:], ones_f[:], [ones_f.b()], [ones_b.b()])
    k.memset("pool", zeros_b[:], 0.0, [zeros_b.b()])
    with ExitStack() as ph:
        vecrows = k.sb("vecrows", [16, 4096], F32, ph)
        k.memset("dve", vecrows[:], 0.0, [vecrows.b()])
        S.dma("sp", vecrows[0:4, :], conv_w[:, :], (), [vecrows.b()])
        S.dma("sp", vecrows[4:5, :], conv_b[:, :], (), [vecrows.b()])
        S.dma("sp", vecrows[5:7, 0:D], norm_mix[:, :], (), [vecrows.b()])
        S.dma("sp", vecrows[7:9, 0:D], norm_xattn[:, :], (), [vecrows.b()])
        S.dma("sp", vecrows[9:11, 0:D], norm_mlp[:, :], (), [vecrows.b()])
        S.dma("sp", vecrows[11:12, 0:D], pool_scale[:, :], (), [vecrows.b()])
        bi = k.bank()
        for c in range(32):
            k.tr(psum[:, bi, c * 12:(c + 1) * 12], vecrows[0:12, c * 128:(c + 1) * 128], ident_f[0:12, 0:12],
                 [vecrows.b(), ident_f.b()], [pbuf(bi)])
        k.cp("dve", colv[:], psum[:, bi, 0:384].rearrange("p (c r) -> p c r", r=12), [pbuf(bi)], [colv.b()])
        S.barrier()
    CV_CONVW, CV_CONVB, CV_MIX, CV_XATTN, CV_MLP, CV_PSCALE = 0, 4, 5, 7, 9, 11

    with ExitStack() as ph:
        xin = [k.sb("xin%d" % i, [128, D], F32, ph) for i in range(6)]
        for t in range(NT):
            xt = xin[t % 6]
            src = xp[t * 128:(t + 1) * 128, :] if t < 16 else xs[:, :]
            S.dma("sp", xt[:], src, (), [xt.b()])
            for half in range(2):
                bi = k.bank()
                for c4 in range(4):
                    c = half * 4 + c4
                    k.tr(psum[:, bi, c4 * 128:(c4 + 1) * 128], xt[:, c * 128:(c + 1) * 128], ident_f[:],
                         [xt.b(), ident_f.b()], [pbuf(bi)])
                eng = "dve" if half == 0 else "act"
                k.cp(eng, xres[:, half * 4:half * 4 + 4, t * 128:(t + 1) * 128],
                     psum[:, bi, :].rearrange("p (c n) -> p c n", c=4),
                     [pbuf(bi)], [xres.b((c, t)) for c in range(half * 4, half * 4 + 4)])
        S.barrier()

    def rmsnorm_fm(ph_tiles, gidx, out_fn, tbs):
        sqt, rst = ph_tiles
        for tbi, (t0, tn) in tbs:
            bi = k.bank()
            for c in range(DC):
                sq = sqt[c % 2]
                k.act(sq[:, 0:tn], xres[:, c, t0:t0 + tn], AF.Square, xb(c, t0, tn), [sq.b()])
                k.mm(psum[:, bi, 0:tn], ones_b[:], sq[:, 0:tn], c == 0, c == DC - 1,
                     [ones_b.b(), sq.b()], [pbuf(bi)])
            rs = rst[tbi % 2]
            k.act(rs[:, 0:tn], psum[:, bi, 0:tn], AF.Ln, [pbuf(bi)], [rs.b()], bias=EPS, scale=1.0 / D)
            k.act(rs[:, 0:tn], rs[:, 0:tn], AF.Exp, [rs.b()], [rs.b()], scale=-0.5)
            for c in range(DC):
                o_ap, o_bufs = out_fn(c, tbi, t0, tn)
                k.stt(o_ap, xres[:, c, t0:t0 + tn], colv[:, c, gidx:gidx + 1], rs[:, 0:tn], ALU.mult, ALU.mult,
                      xb(c, t0, tn) + [colv.b(), rs.b()], o_bufs)

    def linear_fm(W, KC, c0, ncols, rhs_fn, evac_fn, tbs):
        NW = 4096 // KC
        for s0 in range(0, ncols, NW):
            nw = min(NW, ncols - s0)
            slab = next_slab()
            view = slab[:, 0:KC * nw].rearrange("p (k n) -> p k n", k=KC)
            S.dma("pool", view, W[0:KC * 128, c0 + s0:c0 + s0 + nw].rearrange("(k p) n -> p k n", p=128),
                  (), [slab.b()])
            for m in range(nw // 128):
                for tbi, (t0, tn) in tbs:
                    bi = k.bank()
                    for kc in range(KC):
                        rhs, rreads = rhs_fn(kc, tbi, t0, tn)
                        k.mm(psum[:, bi, 0:tn], view[:, kc, m * 128:(m + 1) * 128], rhs, kc == 0, kc == KC - 1,
                             [slab.b()] + rreads, [pbuf(bi)])
                    evac_fn((s0 // 128) + m, tbi, t0, tn, psum[:, bi, 0:tn], pbuf(bi))

    ALL_TBS = list(enumerate(TBS))

    def evac_add_xres(m, tbi, t0, tn, ps, pb):
        k.tt("dve", xres[:, m, t0:t0 + tn], ps, xres[:, m, t0:t0 + tn], ALU.add,
             [pb] + xb(m, t0, tn), xb(m, t0, tn))

    def mlp_layer(li):
        with ExitStack() as ph:
            k.wsl = [k.sb("wsl%d" % i, [128, 4096], BF16, ph) for i in range(3)]
            h = k.sb("mlp_h", [128, DC, T], BF16, ph)
            a = k.sb("mlp_a", [128, DC, T], BF16, ph)
            sqt = [k.sb("mlp_sq%d" % i, [128, 512], BF16, ph) for i in range(2)]
            rst = [k.sb("mlp_rs%d" % i, [128, 512], F32, ph) for i in range(2)]
            rl = [k.sb("mlp_rl%d" % i, [128, 512], F32, ph) for i in range(3)]
            k.rl_rr = 0
            rmsnorm_fm((sqt, rst), CV_MLP + li,
                       lambda c, tbi, t0, tn: (h[:, c, t0:t0 + tn], [h.b((c, tbi))]), ALL_TBS)
            for j in range(4):
                def ev_up(m, tbi, t0, tn, ps, pb):
                    r = rl[k.rl_rr]
                    k.rl_rr = (k.rl_rr + 1) % 3
                    k.act(r[:, 0:tn], ps, AF.Relu, [pb], [r.b()])
                    k.tt("pool", a[:, m, t0:t0 + tn], r[:, 0:tn], r[:, 0:tn], ALU.mult, [r.b()], [a.b((m, tbi))])
                linear_fm(w_up[li], DC, j * 1024, 1024,
                          lambda kc, tbi, t0, tn: (h[:, kc, t0:t0 + tn], [h.b((kc, tbi))]), ev_up, ALL_TBS)
                linear_fm(w_down[li][j * 1024:(j + 1) * 1024, :], DC, 0, 1024,
                          lambda kc, tbi, t0, tn: (a[:, kc, t0:t0 + tn], [a.b((kc, tbi))]), evac_add_xres, ALL_TBS)
            S.barrier()

    def attn_layer(li):
        scale = float(XD) ** -0.5
        with ExitStack() as ph:
            wq = k.sb("at_wq", [128, DC, D], BF16, ph)
            wo = k.sb("at_wo", [128, DC, D], BF16, ph)
            sqt = [k.sb("at_sq%d" % i, [128, 512], BF16, ph) for i in range(2)]
            rst = [k.sb("at_rs%d" % i, [128, 512], F32, ph) for i in range(2)]
            hn = [k.sb("at_hn0", [128, DC, 512], BF16, ph)] * 2
            qt = [k.sb("at_q%d" % i, [128, DC, 512], BF16, ph) for i in range(2)]
            ot = qt
            kT = k.sb("at_kT", [128, DC, NMEM], BF16, ph)
            Vp = k.sb("at_V", [128, 2, D], BF16, ph)
            Pt = [k.sb("at_P%d" % i, [128, XH, NMEM], BF16, ph) for i in range(2)]
            Pn = Pt
            PT = [k.sb("at_PT%d" % i, [128, XH * 2, 128], BF16, ph) for i in range(2)]
            sst = [k.sb("at_st%d" % i, [128, 16], F32, ph) for i in range(2)]

            with ExitStack() as ph2:
                k.wsl = [k.sb("wsl%d" % i, [128, 4096], BF16, ph2) for i in range(2)]
                grow = k.sb("at_grow", [128, D], F32, ph2)
                memt = k.sb("at_mem", [128, D], F32, ph2)
                mn = k.sb("at_mn", [128, D], BF16, ph2)
                mnT = k.sb("at_mnT", [128, DC, NMEM], BF16, ph2)
                ktok = k.sb("at_ktok", [128, 2, D], F32, ph2)
                vtok = ktok
                sq = k.sb("at_sqscr", [128, D], BF16, ph2)
                st = k.sb("at_mst", [128, 4], F32, ph2)
                S.dma("sp", grow[:], norm_mem[li:li + 1, :].to_broadcast([128, D]), (), [grow.b()])
                for mt in range(2):
                    S.dma("sp", memt[:], mem[mt * 128:(mt + 1) * 128, :], (), [memt.b()])
                    k.act(sq[:], memt[:], AF.Square, [memt.b()], [sq.b(), st.b()], accum_out=st[:, 0:1])
                    k.act(st[:, 1:2], st[:, 0:1], AF.Ln, [st.b()], [st.b()], bias=EPS, scale=1.0 / D)
                    k.act(st[:, 2:3], st[:, 1:2], AF.Exp, [st.b()], [st.b()], scale=-0.5)
                    k.stt(mn[:], memt[:], st[:, 2:3], grow[:], ALU.mult, ALU.mult,
                          [memt.b(), st.b(), grow.b()], [mn.b()])
                    bi = k.bank()
                    pv = psum[:, bi, :].bitcast(BF16)
                    for c in range(DC):
                        k.tr(pv[:, c * 128:(c + 1) * 128], mn[:, c * 128:(c + 1) * 128], ident_b[:],
                             [mn.b(), ident_b.b()], [pbuf(bi)])
                    k.cp("dve", mnT[:, :, mt * 128:(mt + 1) * 128], pv.rearrange("p (c n) -> p c n", c=DC),
                         [pbuf(bi)], [mnT.b()])
                for which, (W, tok, outd) in enumerate(((w_xk[li], ktok, mk_p[li]), (w_xv[li], vtok, mv_p[li]))):
                    for ch in range(2):
                        slab = next_slab()
                        view = slab[:, 0:DC * 512].rearrange("p (k n) -> p k n", k=DC)
                        S.dma("pool", view, W[:, ch * 512:(ch + 1) * 512].rearrange("(k p) n -> p k n", p=128),
                              (), [slab.b()])
                        for mt in range(2):
                            bi = k.bank()
                            for kc in range(DC):
                                k.mm(psum[:, bi, :], mnT[:, kc, mt * 128:(mt + 1) * 128], view[:, kc, :],
                                     kc == 0, kc == DC - 1, [mnT.b(), slab.b()], [pbuf(bi)])
                            k.cp("act", tok[:, mt, ch * 512:(ch + 1) * 512], psum[:, bi, :], [pbuf(bi)], [tok.b()])
                    S.dma("sp", outd.rearrange("(a p) n -> p a n", p=128), tok[:], [tok.b()], ())
                    if which == 0:
                        for mt in range(2):
                            for c4 in range(2):
                                bi = k.bank()
                                for cc in range(4):
                                    c = c4 * 4 + cc
                                    k.tr(psum[:, bi, cc * 128:(cc + 1) * 128], ktok[:, mt, c * 128:(c + 1) * 128], ident_f[:],
                                         [ktok.b(), ident_f.b()], [pbuf(bi)])
                                k.cp("dve", kT[:, c4 * 4:c4 * 4 + 4, mt * 128:(mt + 1) * 128],
                                     psum[:, bi, :].rearrange("p (c n) -> p c n", c=4), [pbuf(bi)], [kT.b()])
                    else:
                        k.cp("act", Vp[:], vtok[:], [vtok.b()], [Vp.b()])
                    if which == 0:
                        S.dma("pool", wq[:], w_xq[li].rearrange("(k p) n -> p k n", p=128), (), [wq.b()])
                        S.dma("pool", wo[:], w_xo[li].rearrange("(k p) n -> p k n", p=128), (), [wo.b()])
                S.barrier()

            Kb = [k.sb("at_Kb%d" % i, [128, 2, D], BF16, ph) for i in range(2)]
            Vb = [k.sb("at_Vb%d" % i, [128, 2, D], BF16, ph) for i in range(2)]
            kTb = [k.sb("at_kTb%d" % i, [128, DC, NMEM], BF16, ph) for i in range(2)]
            Qz = [k.sb("at_Qz%d" % i, [128, DC, 128], BF16, ph) for i in range(2)]
            for i in range(2):
                k.memset("pool", Qz[i][:], 0.0, [Qz[i].b()])

            def sm1(b0, bufs):
                P, PTt, st = bufs
                sview = psum[:, b0:b0 + 2, :].rearrange("p a (h m) -> p (a h) m", h=2)
                S.op("dve", lambda h: h.tensor_reduce(st[:, 0:4], sview, AX.X, ALU.max),
                     [pbuf(b0), pbuf(b0 + 1)], [st.b()])
                k.ts("dve", st[:, 4:8], st[:, 0:4], -scale, ALU.mult, [st.b()], [st.b()])
                for hd in range(XH):
                    k.act(P[:, hd, :], sview[:, hd, :], AF.Exp, [pbuf(b0), pbuf(b0 + 1), st.b()], [P.b(), st.b()],
                          bias=st[:, 4 + hd:5 + hd], scale=scale, accum_out=st[:, 8 + hd:9 + hd])
                S.op("dve", lambda h: h.reciprocal(st[:, 12:16], st[:, 8:12]), [st.b()], [st.b()])
                k.tt("dve", P[:], P[:], st[:, 12:16].unsqueeze(2).to_broadcast([128, XH, NMEM]), ALU.mult,
                     [P.b(), st.b()], [P.b()])

            def sm2(bufs, excl=()):
                P, PTt, st = bufs
                bi = k.bank(excl=excl)
                pv = psum[:, bi, :].bitcast(BF16)
                for hd in range(XH):
                    for mc in range(2):
                        j = hd * 2 + mc
                        k.tr(pv[:, j * 128:(j + 1) * 128], P[:, hd, mc * 128:(mc + 1) * 128], ident_b[:],
                             [P.b(), ident_b.b()], [pbuf(bi)])
                k.cp("act", PTt[:], pv.rearrange("p (j n) -> p j n", j=XH * 2), [pbuf(bi)], [PTt.b()])
                return PTt

            def softmax_tile(b0, bufs, excl=()):
                sm1(b0, bufs)
                return sm2(bufs, excl)

            qs = k.sb("at_qs", [128, DC, 128], BF16, ph)
            os_ = k.sb("at_os", [128, DC, 128], BF16, ph)
            Ps = k.sb("at_Ps", [128, XH, NMEM], BF16, ph)
            PTs = k.sb("at_PTs", [128, XH * 2, 128], BF16, ph)
            sts = k.sb("at_sts", [128, 16], F32, ph)
            SB0 = 6
            t0s, tns = TBS[4]
            hnt = hn[0]
            rmsnorm_fm((sqt, rst), CV_XATTN + li,
                       lambda c, tbi_, t0_, tn_: (hnt[:, c, 0:tn_], [hnt.b()]), [(4, (t0s, tns))])
            for m in range(DC):
                bi = k.bank()
                for kc in range(DC):
                    k.mm(psum[:, bi, 0:tns], wq[:, kc, m * 128:(m + 1) * 128], hnt[:, kc, 0:tns], kc == 0, kc == DC - 1,
                         [wq.b(), hnt.b()], [pbuf(bi)])
                k.cp("act", qs[:, m, :], psum[:, bi, 0:tns], [pbuf(bi)], [qs.b()])
            k.perm_excl = (SB0, SB0 + 1)
            for bb in range(2):
                k.mm(psum[:, SB0 + bb, :], zeros_b[:, 0:128], zeros_b[:], True, True, [zeros_b.b()], [pbuf(SB0 + bb)])

            def sample_K(b, excl):
                Kt, kTt, Qzt = Kb[b % 2], kTb[b % 2], Qz[b % 2]
                S.dma("pool", Kt[:], ck[li, b].rearrange("(a p) n -> p a n", p=128), (), [Kt.b()])
                for c4 in range(2):
                    bi = k.bank(excl=excl)
                    pv = psum[:, bi, :].bitcast(BF16)
                    for cc in range(4):
                        for mc in range(2):
                            c = c4 * 4 + cc
                            j = cc * 2 + mc
                            k.tr(pv[:, j * 128:(j + 1) * 128], Kt[:, mc, c * 128:(c + 1) * 128], ident_b[:],
                                 [Kt.b(), ident_b.b()], [pbuf(bi)])
                    k.cp("dve" if c4 == 0 else "act", kTt[:, c4 * 4:c4 * 4 + 4, :],
                         pv.rearrange("p (c m) -> p c m", c=4), [pbuf(bi)], [kTt.b()])
                if b >= 2:
                    pb_ = b - 2
                    k.memset("pool", Qzt[:, :, pb_ * 8:pb_ * 8 + 8], 0.0, [Qzt.b()])
                k.cp("pool", Qzt[:, :, b * 8:b * 8 + 8], qs[:, :, b * 8:b * 8 + 8], [qs.b()], [Qzt.b()])
                for hd in range(XH):
                    for dc in range(2):
                        k.mm(psum[:, SB0 + hd // 2, (hd % 2) * 256:(hd % 2) * 256 + 256],
                             Qzt[:, hd * 2 + dc, :], kTt[:, hd * 2 + dc, :], False, (b == NSB - 1 and dc == 1),
                             [Qzt.b(), kTt.b()], [pbuf(SB0 + hd // 2)])

            def sample_V(b):
                Vt = Vb[b % 2]
                S.dma("pool", Vt[:], cv[li, b].rearrange("(a p) n -> p a n", p=128), (), [Vt.b()])
                for d8 in range(DC):
                    hd = d8 // 2
                    for mc in range(2):
                        k.mm(psum[:, SB0 + d8 // 4, (d8 % 4) * 128 + b * 8:(d8 % 4) * 128 + b * 8 + 8],
                             Vt[:, mc, d8 * 128:(d8 + 1) * 128], PTs[:, hd * 2 + mc, b * 8:b * 8 + 8],
                             mc == 0, mc == 1, [Vt.b(), PTs.b()], [pbuf(SB0 + d8 // 4)])

            tile_ctr = 0
            gt = 0
            def nq_units(tbi):
                t0, tn = TBS[tbi]
                hnt, qtt = hn[0], qt[tbi % 2]

                def u_norm():
                    rmsnorm_fm((sqt, rst), CV_XATTN + li,
                               lambda c, tbi_, t0_, tn_: (hnt[:, c, 0:tn_], [hnt.b()]), [(tbi, (t0, tn))])

                def u_q(m):
                    bi = k.bank()
                    for kc in range(DC):
                        k.mm(psum[:, bi, 0:tn], wq[:, kc, m * 128:(m + 1) * 128], hnt[:, kc, 0:tn], kc == 0, kc == DC - 1,
                             [wq.b(), hnt.b()], [pbuf(bi)])
                    k.cp("act", qtt[:, m, 0:tn], psum[:, bi, 0:tn], [pbuf(bi)], [qtt.b()])
                return [u_norm] + [(lambda m=m: u_q(m)) for m in range(DC)]

            for u_ in nq_units(0):
                u_()
            for tbi, (t0, tn) in ALL_TBS[0:4]:
                hnt, qtt, ott = hn[0], qt[tbi % 2], ot[tbi % 2]
                nxt_units = nq_units(tbi + 1) if tbi + 1 < 4 else []
                ucur = 0

                def scoresA(tt, excl=()):
                    lsl = slice(tt * 128, (tt + 1) * 128)
                    b0 = bank2(excl)
                    for hd in range(XH):
                        for dc in range(2):
                            k.mm(psum[:, b0 + hd // 2, (hd % 2) * 256:(hd % 2) * 256 + 256],
                                 qtt[:, hd * 2 + dc, lsl], kT[:, hd * 2 + dc, :], dc == 0, dc == 1,
                                 [qtt.b(), kT.b()], [pbuf(b0 + hd // 2)])
                    return b0

                def restB2(tt, slot, excl=()):
                    lsl = slice(tt * 128, (tt + 1) * 128)
                    PTt = sm2((Pt[slot], PT[slot], sst[slot]), excl)
                    bo = bank2(excl)
                    for d8 in range(DC):
                        hd = d8 // 2
                        for mc in range(2):
                            k.mm(psum[:, bo + d8 // 4, (d8 % 4) * 128:(d8 % 4 + 1) * 128],
                                 Vp[:, mc, d8 * 128:(d8 + 1) * 128], PTt[:, hd * 2 + mc, :], mc == 0, mc == 1,
                                 [Vp.b(), PTt.b()], [pbuf(bo + d8 // 4)])
                    k.cp("act", ott[:, :, lsl], psum[:, bo:bo + 2, :].rearrange("p a (c n) -> p (a c) n", c=4),
                         [pbuf(bo), pbuf(bo + 1)], [ott.b()])

                ntile = tn // 128
                sc = [None] * ntile
                slot0 = tile_ctr
                for j in range(ntile + 2):
                    if j < ntile:
                        ex_a = (sc[j - 1], sc[j - 1] + 1) if j >= 1 else ()
                        sc[j] = scoresA(j, excl=ex_a)
                    if 0 <= j - 1 < ntile:
                        sl_ = (slot0 + j - 1) % 2
                        sm1(sc[j - 1], (Pt[sl_], PT[sl_], sst[sl_]))
                    if 0 <= j - 2 < ntile:
                        live = (sc[j], sc[j] + 1) if j < ntile else ()
                        restB2(j - 2, (slot0 + j - 2) % 2, live)
                        if gt < 8:
                            for b in (2 * gt, 2 * gt + 1):
                                sample_K(b, live)
                            if gt == 7:
                                for i in range(2):
                                    k.memset("pool", Qz[i][:], 0.0, [Qz[i].b()])
                                softmax_tile(SB0, (Ps, PTs, sts), live)
                        else:
                            for b in (2 * (gt - 8), 2 * (gt - 8) + 1):
                                sample_V(b)
                        gt += 1
                        k.perm_excl = (SB0, SB0 + 1) + tuple(live)
                        lim = min(len(nxt_units), -(-len(nxt_units) * (j - 1) // ntile))
                        while ucur < lim:
                            nxt_units[ucur]()
                            ucur += 1
                        k.perm_excl = (SB0, SB0 + 1)
                tile_ctr += ntile
                for m in range(DC):
                    bi = k.bank()
                    for kc in range(DC):
                        k.mm(psum[:, bi, 0:tn], wo[:, kc, m * 128:(m + 1) * 128], ott[:, kc, 0:tn], kc == 0, kc == DC - 1,
                             [wo.b(), ott.b()], [pbuf(bi)])
                    evac_add_xres(m, tbi, t0, tn, psum[:, bi, 0:tn], pbuf(bi))
            k.cp("act", os_[:], psum[:, SB0:SB0 + 2, :].rearrange("p a (c n) -> p (a c) n", c=4),
                 [pbuf(SB0), pbuf(SB0 + 1)], [os_.b()])
            k.perm_excl = ()
            for m in range(DC):
                bi = k.bank()
                for kc in range(DC):
                    k.mm(psum[:, bi, 0:tns], wo[:, kc, m * 128:(m + 1) * 128], os_[:, kc, :], kc == 0, kc == DC - 1,
                         [wo.b(), os_.b()], [pbuf(bi)])
                evac_add_xres(m, 4, t0s, tns, psum[:, bi, 0:tns], pbuf(bi))
            S.barrier()

    def pool_layer():
        with ExitStack() as ph:
            HP_ = 16
            up_ = k.sb("pl_up", [128, DC, HP_ + SEQ], BF16, ph)
            us_ = k.sb("pl_us", [128, DC, NSB, 24], BF16, ph)
            pooled_s = k.sb("pl_pooled_s", [128, DC, TS], BF16, ph)
            sqt = [k.sb("pl_sq%d" % i, [128, 512], BF16, ph) for i in range(2)]
            rst = [k.sb("pl_rs%d" % i, [128, 512], F32, ph) for i in range(2)]
            wA = k.sb("pl_wA", [128, 2, 2048], BF16, ph)
            wB = k.sb("pl_wB", [128, 2, 2048], BF16, ph)
            wp = k.sb("pl_wp", [128, 4, 2, 256], BF16, ph)
            ptmp = [k.sb("pl_ptmp%d" % i, [128, 512], F32, ph) for i in range(2)]
            invc = k.sb("pl_invc", [128, 4, 16], F32, ph)
            iot = k.sb("pl_iota", [128, 16], F32, ph)
            ph3 = ExitStack()
            hist = k.sb("pl_hist", [128, 2, D], F32, ph3)

            S.dma("pool", wp[:], w_pool.rearrange("g (k p) n -> p g k n", p=128), (), [wp.b()])
            S.op("pool", lambda h: h.iota(iot[:], [[1, 16]], base=1, channel_multiplier=0, allow_small_or_imprecise_dtypes=True), (), [iot.b()])
            for g, w in enumerate(POOL_W):
                k.ts("dve", invc[:, g, :], iot[:], float(w), ALU.min, [iot.b()], [invc.b()])
            S.op("dve", lambda h: h.reciprocal(invc[:], invc[:]), [invc.b()], [invc.b()])

            k.memset("pool", up_[:, :, 0:HP_], 0.0, [up_.b("hist")])
            k.memset("pool", us_[:, :, :, 0:1], 0.0, [us_.b()])
            S.dma("sp", hist[:, 0, :], spool[0:128, :], (), [hist.b()])
            S.dma("sp", hist[0:112, 1, :], spool[128:240, :], (), [hist.b()])
            usf = us_[:].rearrange("p c b j -> p c (b j)")
            for c in range(DC):
                bi = k.bank()
                k.tr(psum[:, bi, 0:128], hist[:, 0, c * 128:(c + 1) * 128], ident_f[:],
                     [hist.b(), ident_f.b()], [pbuf(bi)])
                k.tr(psum[:, bi, 128:240], hist[0:112, 1, c * 128:(c + 1) * 128], ident_f[0:112, 0:112],
                     [hist.b(), ident_f.b()], [pbuf(bi)])
                k.cp("dve", us_[:, c, :, 1:16], psum[:, bi, 0:240].rearrange("p (b j) -> p b j", j=15),
                     [pbuf(bi)], [us_.b()])

            S.barrier()
            ph3.close()
            outp = k.sb("pl_outp", [128, D], F32, ph)
            outs = k.sb("pl_outs", [128, D], F32, ph)

            def norm_out(c, tbi, t0, tn):
                if tbi < 4:
                    return up_[:, c, HP_ + t0:HP_ + t0 + tn], [up_.b((c, tbi))]
                return us_[:, c, :, 16:24], [us_.b()]
            sq_, rs_ = sqt, rst
            for tbi, (t0, tn) in ALL_TBS:
                bi = k.bank()
                for c in range(DC):
                    sq = sq_[c % 2]
                    k.act(sq[:, 0:tn], xres[:, c, t0:t0 + tn], AF.Square, xb(c, t0, tn), [sq.b()])
                    k.mm(psum[:, bi, 0:tn], ones_b[:], sq[:, 0:tn], c == 0, c == DC - 1, [ones_b.b(), sq.b()], [pbuf(bi)])
                rs = rs_[tbi % 2]
                k.act(rs[:, 0:tn], psum[:, bi, 0:tn], AF.Ln, [pbuf(bi)], [rs.b()], bias=EPS, scale=1.0 / D)
                k.act(rs[:, 0:tn], rs[:, 0:tn], AF.Exp, [rs.b()], [rs.b()], scale=-0.5)
                for c in range(DC):
                    o_ap, o_bufs = norm_out(c, tbi, t0, tn)
                    xin_ = xres[:, c, t0:t0 + tn]
                    rin_ = rs[:, 0:tn]
                    if tbi == 4:
                        xin_ = xin_.rearrange("p (b j) -> p b j", j=8)
                        rin_ = rin_.rearrange("p (b j) -> p b j", j=8)
                    k.stt(o_ap, xin_, colv[:, c, CV_MIX + 1:CV_MIX + 2], rin_, ALU.mult, ALU.mult,
                          xb(c, t0, tn) + [colv.b(), rs.b()], o_bufs)

            b0 = bank2()
            pvb = [psum[:, b0 + i, :].bitcast(BF16) for i in range(2)]
            for c in range(DC):
                k.tr(pvb[0][:, c * 128:(c + 1) * 128], up_[:, c, HP_ + SEQ - 128:HP_ + SEQ], ident_b[:],
                     [up_.b((c, 3)), ident_b.b()], [pbuf(b0)])
            k.cp("dve", outp[:], pvb[0], [pbuf(b0)], [outp.b()])
            S.dma("sp", pool_p[:, :], outp[113:128, :], [outp.b()], ())
            usn = k.sb("pl_usn", [128, DC, 128], BF16, ph)
            k.cp("pool", usn[:].rearrange("p c (b j) -> p c b j", j=8), us_[:, :, :, 16:24], [us_.b()], [usn.b()])
            for c in range(DC):
                k.tr(pvb[1][:, c * 128:(c + 1) * 128], usn[:, c, :], ident_b[:], [usn.b(), ident_b.b()], [pbuf(b0 + 1)])
            k.cp("dve", outs[:], pvb[1], [pbuf(b0 + 1)], [outs.b()])
            for b in range(NSB):
                S.dma("sp", pool_s[b * 15 + 7:b * 15 + 15, :], outs[b * 8:b * 8 + 8, :], [outs.b()], ())
                S.dma("sp", pool_s[b * 15:b * 15 + 7, :], spool[b * 15 + 8:b * 15 + 15, :], (), ())

            for g, w in enumerate(POOL_W):
                cs = slice(2 * g, 2 * g + 2)
                nst = g + 1
                L = HP_ + SEQ
                src = up_
                src_b = [up_.b((c, tb)) for c in (2 * g, 2 * g + 1) for tb in range(4)] + [up_.b("hist")]
                cur = None
                sh = 1
                for s in range(nst):
                    dst = wA if s % 2 == 0 else wB
                    if s == 0:
                        k.tt("dve", dst[:, :, 0:SEQ], up_[:, cs, HP_:L], up_[:, cs, HP_ - 1:L - 1], ALU.add,
                             src_b, [dst.b()])
                    else:
                        k.tt("dve", dst[:, :, sh:SEQ], cur[:, :, sh:SEQ], cur[:, :, 0:SEQ - sh], ALU.add,
                             [cur.b()], [dst.b()])
                        k.cp("dve", dst[:, :, 0:sh], cur[:, :, 0:sh], [cur.b()], [dst.b()])
                    cur = dst
                    sh *= 2
                tmp16 = wB if cur is wA else wA
                t16 = k.sb("pl_t16_%d" % g, [128, 2, 16], F32, ph)
                k.tt("dve", tmp16[:, :, 0:16], cur[:, :, 0:16], invc[:, g:g + 1, :].to_broadcast([128, 2, 16]), ALU.mult,
                     [cur.b(), invc.b()], [tmp16.b()])
                k.tt("dve", t16[:], tmp16[:, :, 0:16], up_[:, cs, HP_:HP_ + 16], ALU.subtract,
                     [tmp16.b()] + src_b, [t16.b()])
                k.stt(up_[:, cs, HP_:L], cur[:, :, 0:SEQ], 1.0 / w, up_[:, cs, HP_:L], ALU.mult, ALU.subtract,
                      [cur.b()] + src_b, src_b)
                k.cp("dve", up_[:, cs, HP_:HP_ + 16], t16[:], [t16.b()] + src_b, src_b)
                sA = k.sb("pl_sA%d" % g, [128, 2, NSB, 8], F32, ph)
                k.tt("dve", sA[:], us_[:, cs, :, 16:24], us_[:, cs, :, 15:23], ALU.add, [us_.b()], [sA.b()])
                for j in range(2, w):
                    k.tt("dve", sA[:], sA[:], us_[:, cs, :, 16 - j:24 - j], ALU.add, [us_.b(), sA.b()], [sA.b()])
                k.stt(pooled_s[:, cs, :].rearrange("p c (b j) -> p c b j", j=8), sA[:], 1.0 / w, us_[:, cs, :, 16:24],
                      ALU.mult, ALU.subtract, [sA.b(), us_.b()], [pooled_s.b(g)])
                for mo in range(2):
                    m = 2 * g + mo
                    for tbi, (t0, tn) in ALL_TBS:
                        bi = k.bank()
                        for kc in range(2):
                            if tbi < 4:
                                rhs_ = up_[:, 2 * g + kc, HP_ + t0:HP_ + t0 + tn]
                                rb_ = [up_.b((2 * g + kc, tbi))]
                            else:
                                rhs_ = pooled_s[:, 2 * g + kc, :]
                                rb_ = [pooled_s.b(g)]
                            k.mm(psum[:, bi, 0:tn], wp[:, g, kc, mo * 128:(mo + 1) * 128], rhs_,
                                 kc == 0, kc == 1, [wp.b()] + rb_, [pbuf(bi)])
                        pt_ = ptmp[(mo * 5 + tbi) % 2]
                        k.act(pt_[:, 0:tn], psum[:, bi, 0:tn], AF.Copy, [pbuf(bi), colv.b()], [pt_.b()],
                              scale=colv[:, m, CV_PSCALE:CV_PSCALE + 1])
                        k.tt("pool", xres[:, m, t0:t0 + tn], xres[:, m, t0:t0 + tn], pt_[:, 0:tn], ALU.add,
                             [pt_.b()] + xb(m, t0, tn), xb(m, t0, tn))

            S.barrier()

    def mamba_layer():
        with ExitStack() as ph:
            h = k.sb("mb_h", [128, DC, T], BF16, ph)
            with ExitStack() as ph0:
                sqt = [k.sb("mb_sq%d" % i, [128, 512], BF16, ph0) for i in range(2)]
                rst = [k.sb("mb_rs%d" % i, [128, 512], F32, ph0) for i in range(2)]
                rmsnorm_fm((sqt, rst), CV_MIX + 0,
                           lambda c, tbi, t0, tn: (h[:, c, t0:t0 + tn], [h.b((c, tbi))]), ALL_TBS)
                S.barrier()

            Umat = k.sb("mb_U", [128, 128], F32, ph)
            SameB = k.sb("mb_SB", [128, 128], F32, ph)
            Ublk = k.sb("mb_Ub", [128, 128], F32, ph)
            S.op("pool", lambda hh: hh.affine_select(Umat[:], ones_f[:], [[1, 128]], ALU.is_ge, 0.0,
                                                     base=0, channel_multiplier=-1), [ones_f.b()], [Umat.b()])
            S.op("pool", lambda hh: hh.affine_select(SameB[:].rearrange("p (b j) -> p b j", j=8),
                                                     ones_f[:].rearrange("p (b j) -> p b j", j=8),
                                                     [[8, 16], [0, 8]], ALU.is_ge, 0.0, base=7, channel_multiplier=-1),
                 [ones_f.b()], [SameB.b()])
            S.op("pool", lambda hh: hh.affine_select(SameB[:].rearrange("p (b j) -> p b j", j=8),
                                                     SameB[:].rearrange("p (b j) -> p b j", j=8),
                                                     [[-8, 16], [0, 8]], ALU.is_ge, 0.0, base=0, channel_multiplier=1),
                 [SameB.b()], [SameB.b()])
            k.tt("pool", Ublk[:], SameB[:], Umat[:], ALU.mult, [SameB.b(), Umat.b()], [Ublk.b()])
            brow = k.sb("mb_brow", [128, 3, NH], F32, ph)
            S.dma("sp", brow[:, 0, :], dt_bias.to_broadcast([128, NH]), (), [brow.b()])
            S.dma("sp", brow[:, 1, :], a_log.to_broadcast([128, NH]), (), [brow.b()])
            S.dma("sp", brow[:, 2, :], d_skip.to_broadcast([128, NH]), (), [brow.b()])
            k.act(brow[:, 1, :], brow[:, 1, :], AF.Exp, [brow.b()], [brow.b()])
            k.ts("dve", brow[:, 1, :], brow[:, 1, :], -1.0, ALU.mult, [brow.b()], [brow.b()])
            wdt = k.sb("mb_wdt", [128, DC, NH], BF16, ph)
            S.dma("pool", wdt[:], w_in[:, 6144:6176].rearrange("(k p) n -> p k n", p=128), (), [wdt.b()])

            dt_a = k.sb("mb_dt", [128, NT, NH], F32, ph)
            cd_a = k.sb("mb_cd", [128, NT, NH], F32, ph)
            dtd_a = k.sb("mb_dtd", [128, NT, NH], F32, ph)
            eacs_a = k.sb("mb_eacs", [128, NT, NH], F32, ph)
            cdp2 = k.sb("mb_cdp2", [128, NSB, 16], F32, ph)
            nb_a = k.sb("mb_nb", [128, NT, NH], F32, ph)
            dtAh = k.sb("mb_dtAh", [128, NT, NH], BF16, ph)
            dtAl = k.sb("mb_dtAl", [128, NT, NH], BF16, ph)
            phT = ExitStack()
            dtA_a = k.sb("mb_dtA", [128, NT, NH], F32, phT)
            nacs_a = k.sb("mb_nacs", [128, NT, NH], F32, phT)
            tmp32 = k.sb("mb_tmp32", [128, NH], F32, phT)
            Xs = k.sb("mb_Xs", [128, NSB, NH], F32, phT)
            CDB = k.sb("mb_CDB", [128, NSB, NH], F32, phT)
            dtAf = k.sb("mb_dtAf", [128, NT, NH], F32, phT)
            for t in range(NT):
                Um = Umat if t < 16 else Ublk
                Jm = ones_f if t < 16 else SameB
                bi = k.bank()
                for kc in range(DC):
                    k.mm(psum[:, bi, 0:NH], h[:, kc, t * 128:(t + 1) * 128], wdt[:, kc, :], kc == 0, kc == DC - 1,
                         [h.b((kc, min(t // 4, 4))), wdt.b()], [pbuf(bi)])
                k.tt("dve", tmp32[:], psum[:, bi, 0:NH], brow[:, 0, :], ALU.add, [pbuf(bi), brow.b()], [tmp32.b()])
                k.act(tmp32[:], tmp32[:], AF.Exp, [tmp32.b()], [tmp32.b()])
                k.act(dt_a[:, t, :], tmp32[:], AF.Ln, [tmp32.b()], [dt_a.b(t)], bias=1.0, scale=1.0)
                k.tt("dve", dtA_a[:, t, :], dt_a[:, t, :], brow[:, 1, :], ALU.mult, [dt_a.b(t), brow.b()], [dtA_a.b(t)])
                bi = k.bank()
                k.mm(psum[:, bi, 0:NH], Um[:], dtA_a[:, t, :], True, True, [Um.b(), dtA_a.b(t)], [pbuf(bi)])
                k.mm(psum[:, bi, 64:64 + NH], Jm[:], dtA_a[:, t, :], True, True, [Jm.b(), dtA_a.b(t)], [pbuf(bi)])
                k.ts("dve", nacs_a[:, t, :], psum[:, bi, 0:NH], -1.0, ALU.mult, [pbuf(bi)], [nacs_a.b(t)])
                k.act(eacs_a[:, t, :], psum[:, bi, 0:NH], AF.Exp, [pbuf(bi)], [eacs_a.b(t)])
                k.act(cd_a[:, t, :], psum[:, bi, 64:64 + NH], AF.Exp, [pbuf(bi)], [cd_a.b(t)])
                k.tt("dve", tmp32[:], psum[:, bi, 64:64 + NH], nacs_a[:, t, :], ALU.add,
                     [pbuf(bi), nacs_a.b(t)], [tmp32.b()])
                k.act(tmp32[:], tmp32[:], AF.Exp, [tmp32.b()], [tmp32.b()])
                k.tt("dve", dtd_a[:, t, :], tmp32[:], dt_a[:, t, :], ALU.mult, [tmp32.b(), dt_a.b(t)], [dtd_a.b(t)])
                k.act(tmp32[:], dt_a[:, t, :], AF.Ln, [dt_a.b(t)], [tmp32.b()])
                k.tt("dve", nb_a[:, t, :], tmp32[:], nacs_a[:, t, :], ALU.add, [tmp32.b(), nacs_a.b(t)], [nb_a.b(t)])

            k.cp("dve", dtAh[:], dtA_a[:], [dtA_a.b(t_) for t_ in range(NT)], [dtAh.b()])
            k.cp("dve", dtAf[:], dtAh[:], [dtAh.b()], [dtAf.b()])
            k.tt("dve", dtAl[:], dtA_a[:], dtAf[:], ALU.subtract, [dtA_a.b(t_) for t_ in range(NT)] + [dtAf.b()], [dtAl.b()])
            k.tt("dve", Xs[:], dtA_a[:, 16:17, :].to_broadcast([128, NSB, NH]),
                 SameB[:, 0:128:8].unsqueeze(2).to_broadcast([128, NSB, NH]), ALU.mult,
                 [dtA_a.b(16), SameB.b()], [Xs.b()])
            bi = k.bank()
            k.mm(psum[:, bi, :], ones_f[:], Xs[:].rearrange("p b h -> p (b h)"), True, True,
                 [ones_f.b(), Xs.b()], [pbuf(bi)])
            k.act(CDB[:].rearrange("p b h -> p (b h)"), psum[:, bi, :], AF.Exp, [pbuf(bi)], [CDB.b()])
            k.cp("dve", cdp2[0:64, :, :], CDB[0:64, :, 0:NH:2], [CDB.b()], [cdp2.b()])
            k.cp("dve", cdp2[64:128, :, :], CDB[64:128, :, 1:NH:2], [CDB.b()], [cdp2.b()])

            S.barrier()
            phT.close()
            wz = k.sb("mb_wz", [128, DC, 256], BF16, ph)
            wx = [k.sb("mb_wx%d" % i, [128, DC, 512], BF16, ph) for i in range(2)]
            wog = [k.sb("mb_wog0", [128, 2, D], BF16, ph)] * 2
            dgw = k.sb("mb_dgw", [128, 4, 4, 128], BF16, ph)
            rawt = [k.sb("mb_raw%d" % i, [128, 4, 3 + 512], BF16, ph) for i in range(2)]
            carry = k.sb("mb_carry", [128, 4, 3], BF16, ph)
            raws = k.sb("mb_raws", [128, 4, NSB, 11], BF16, ph)
            scv = k.sb("mb_scv", [48, 4, 128], F32, ph)
            ncv = k.sb("mb_ncv", [128, 4, 51], F32, ph)
            ncvo = k.sb("mb_ncvo", [128, 4, 128], F32, ph)
            hout = ncvo
            xact = [k.sb("mb_xact%d" % i, [128, 4, 512], BF16, ph) for i in range(2)]
            tht = [k.sb("mb_th%d" % i, [128, 512], BF16, ph) for i in range(2)]
            vht = [k.sb("mb_vh%d" % i, [128, 512], BF16, ph) for i in range(2)]
            cbh = k.sb("mb_cbh", [128, 32], F32, ph)
            k.ts("dve", cbh[:], colv[:, :, 4], 0.5, ALU.mult, [colv.b()], [cbh.b()])
            ygT = k.sb("mb_ygT", [128, 2, 512], BF16, ph)
            ngrow = [k.sb("mb_ngrow%d" % i, [128, 256], F32, ph) for i in range(2)]
            zs4 = [k.sb("mb_zs4_%d" % i, [128, 4, 256], BF16, ph) for i in range(2)]
            xdts = [k.sb("mb_xdts%d" % i, [128, 4, HP], BF16, ph) for i in range(2)]
            xB = [k.sb("mb_xB%d" % i, [128, 384], BF16, ph) for i in range(2)]
            Dg = k.sb("mb_Dg", [128, 4, 128], BF16, ph)
            cbs = [k.sb("mb_cbs%d" % i, [128, 128], BF16, ph) for i in range(2)]
            U_b = [k.sb("mb_Ubf%d" % i, [128, 128], BF16, ph) for i in range(2)]
            k.cp("pool", U_b[0][:], Umat[:], [Umat.b()], [U_b[0].b()])
            k.cp("pool", U_b[1][:], Ublk[:], [Ublk.b()], [U_b[1].b()])
            dcy = [k.sb("mb_dcy%d" % i, [128, 128], BF16, ph) for i in range(4)]
            MT = [k.sb("mb_MT%d" % i, [128, 128], BF16, ph) for i in range(8)]
            Neg4 = [k.sb("mb_Neg%d" % i, [128, 128], BF16, ph) for i in range(2)]
            for i, Us in enumerate((Umat, Ublk)):
                k.ts("dve", Neg4[i][:], Us[:], -1.0, ALU.add, [Us.b()], [Neg4[i].b()], s2=30000.0, op1=ALU.mult)
            t1 = k.sb("mb_t1", [128, 4, HP], F32, ph)
            yg = k.sb("mb_yg", [128, 256], F32, ph)
            ygn = [k.sb("mb_ygn%d" % i, [128, 256], BF16, ph) for i in range(2)]
            yst = k.sb("mb_yst", [128, 4], F32, ph)
            mhalf = k.sb("mb_mhalf", [128, 1], F32, ph)
            k.memset("pool", mhalf[:], -0.5, [mhalf.b()])
            hTf = k.sb("mb_hTf", [128, 256], F32, ph)
            hTb = k.sb("mb_hTb", [128, 256], BF16, ph)
            h0s = [k.sb("mb_h0s%d" % i, [128, 2, 2, 128], F32, ph) for i in range(4)]
            h0T = k.sb("mb_h0T", [128, 2, 256], BF16, ph)
            CTz = [k.sb("mb_CTz%d" % i, [128, 128], BF16, ph) for i in range(2)]
            Bm = [k.sb("mb_Bm%d" % i, [128, 128], BF16, ph) for i in range(2)]
            for i in range(2):
                k.memset("pool", CTz[i][:], 0.0, [CTz[i].b()])

            def cglob_of(gg):
                return [2 * gg, 2 * gg + 1, 16 + gg, 24 + gg]

            def load_wx(gg):
                for (dst0, src0, n) in ((0, DI + gg * 256, 256), (256, 2 * DI + gg * 128, 128),
                                        (384, 2 * DI + 1024 + gg * 128, 128)):
                    S.dma("pool", wx[gg % 2][:, :, dst0:dst0 + n],
                          w_in[:, src0:src0 + n].rearrange("(k p) n -> p k n", p=128), (), [wx[gg % 2].b()])

            def load_h0(gg, e8):
                for b2 in range(2):
                    S.dma("sp", h0s[e8 % 4][:, b2, :, :], ssm[e8 * 2 + b2, gg * 256:(gg + 1) * 256, :]
                          .rearrange("(a p) n -> p a n", p=128), (), [h0s[e8 % 4].b()])

            def setup_early(gg):
                cg = cglob_of(gg)
                if gg == 0:
                    load_wx(0)
                S.dma("pool", wz[:], w_in[:, gg * 256:(gg + 1) * 256].rearrange("(k p) n -> p k n", p=128), (), [wz.b()])
                if gg + 1 < NG:
                    load_wx(gg + 1)
                S.dma("sp", ngrow[gg % 2][:], norm_gated[:, gg * 256:(gg + 1) * 256].to_broadcast([128, 256]), (),
                      [ngrow[gg % 2].b()])
                for ci in range(4):
                    S.dma("sp", scv[:, ci, :], sconv[:, cg[ci] * 128:(cg[ci] + 1) * 128], (), [scv.b()])
                for ci in range(4):
                    for tap in range(4):
                        k.ts("dve", dgw[:, ci, tap, :], ident_f[:], colv[:, cg[ci], tap:tap + 1], ALU.mult,
                             [ident_f.b(), colv.b()], [dgw.b()])
                k.memset("dve", carry[:], 0.0, [carry.b()])

            def setup_late(gg):
                S.dma("pool", wog[0][:], w_out[gg * 256:(gg + 1) * 256, :].rearrange("(k p) n -> p k n", p=128), (), [wog[0].b()])
                for r in range(4):
                    k.ts("dve", Dg[:, r, :], ident_f[:], brow[:, 2, 4 * gg + r:4 * gg + r + 1], ALU.mult,
                         [ident_f.b(), brow.b()], [Dg.b()])

            def P_units(gg, tbi, bset):
                t0, tn = TBS[tbi]
                cglob = cglob_of(gg)
                wxg = wx[gg % 2]
                rw, xa, zz = rawt[bset], xact[bset], zs4[bset]
                units = []

                def u_hist():
                    bi = k.bank()
                    for ci in range(4):
                        k.tr(psum[:, bi, ci * 48:(ci + 1) * 48], scv[:, ci, :], ident_f[0:48, 0:48],
                             [scv.b(), ident_f.b()], [pbuf(bi)])
                    k.cp("dve", raws[:, :, :, 0:3], psum[:, bi, 0:192].rearrange("p (c b j) -> p c b j", c=4, j=3),
                         [pbuf(bi)], [raws.b()])
                if tbi == 4:
                    units.append(u_hist)

                def u_in(ci):
                    if ci == 0 and tbi < 4:
                        k.cp("dve", rw[:, :, 0:3], carry[:], [carry.b()], [rw.b()])
                    bi = k.bank()
                    for kc in range(DC):
                        k.mm(psum[:, bi, 0:tn], wxg[:, kc, ci * 128:(ci + 1) * 128], h[:, kc, t0:t0 + tn],
                             kc == 0, kc == DC - 1, [wxg.b(), h.b((kc, tbi))], [pbuf(bi)])
                    if tbi < 4:
                        k.cp("act", rw[:, ci, 3:3 + tn], psum[:, bi, 0:tn], [pbuf(bi)], [rw.b()])
                        if tbi == 3:
                            k.cp("dve", ncv[:, ci, 48:51], psum[:, bi, 509:512], [pbuf(bi)], [ncv.b()])
                    else:
                        pvv = psum[:, bi, 0:128].rearrange("p (b j) -> p b j", j=8)
                        k.cp("act", raws[:, ci, :, 3:11], pvv, [pbuf(bi)], [raws.b()])
                        k.cp("dve", ncv[:, ci, 0:48].rearrange("p (b j) -> p b j", j=3), pvv[:, :, 5:8],
                             [pbuf(bi)], [ncv.b()])
                    if ci == 3 and tbi < 3:
                        k.cp("dve", carry[:], rw[:, :, 512:515], [rw.b()], [carry.b()])

                def u_cv(ci):
                    bi = k.bank()
                    for tap in range(4):
                        if tbi < 4:
                            rhs_ = rw[:, ci, tap:tap + tn]
                            rb_ = rw.b()
                        else:
                            rhs_ = raws[:, ci, :, tap:tap + 8]
                            rb_ = raws.b()
                        k.mm(psum[:, bi, 0:tn], dgw[:, ci, tap, :], rhs_, tap == 0, tap == 3,
                             [dgw.b(), rb_], [pbuf(bi)])
                    th_, vh_ = tht[ci % 2], vht[ci % 2]
                    k.act(th_[:, 0:tn], psum[:, bi, 0:tn], AF.Tanh, [pbuf(bi), cbh.b()], [th_.b()],
                          bias=cbh[:, cglob[ci]:cglob[ci] + 1], scale=0.5)
                    k.act(vh_[:, 0:tn], psum[:, bi, 0:tn], AF.Identity, [pbuf(bi), cbh.b()], [vh_.b()],
                          bias=cbh[:, cglob[ci]:cglob[ci] + 1], scale=0.5)
                    k.stt(xa[:, ci, 0:tn], th_[:, 0:tn], 1.0, vh_[:, 0:tn], ALU.add, ALU.mult,
                          [th_.b(), vh_.b()], [xa.b()])

                def u_z(tt):
                    t = t0 // 128 + tt
                    bi = k.bank()
                    for kc in range(DC):
                        k.mm(psum[:, bi, 0:256], h[:, kc, t * 128:(t + 1) * 128], wz[:, kc, :],
                             kc == 0, kc == DC - 1, [h.b((kc, tbi)), wz.b()], [pbuf(bi)])
                    th_, vh_ = tht[tt % 2], vht[tt % 2]
                    k.act(th_[:, 0:256], psum[:, bi, 0:256], AF.Tanh, [pbuf(bi)], [th_.b()], scale=0.5)
                    k.act(vh_[:, 0:256], psum[:, bi, 0:256], AF.Identity, [pbuf(bi)], [vh_.b()], scale=0.5)
                    k.stt(zz[:, tt, :], th_[:, 0:256], 1.0, vh_[:, 0:256], ALU.add, ALU.mult,
                          [th_.b(), vh_.b()], [zz.b(tt)])

                for ci in range(4):
                    units.append(lambda ci=ci: u_in(ci))
                for ci in range(4):
                    units.append(lambda ci=ci: u_cv(ci))
                for tt in range(tn // 128):
                    units.append(lambda tt=tt: u_z(tt))
                return units

            setup_early(0)
            for e8 in range(4):
                load_h0(0, e8)
            for u_ in P_units(0, 0, 0):
                u_()
            for g in range(NG):
                hd0 = 4 * g
                cglob = cglob_of(g)
                wo = wog[0]
                ngrow_c = ngrow[g % 2]
                setup_late(g)
                for tbi, (t0, tn) in ALL_TBS:
                    bidx = g * len(TBS) + tbi
                    xact_c, zs4_c = xact[bidx % 2], zs4[bidx % 2]
                    if tbi + 1 < len(TBS):
                        nxt_units = P_units(g, tbi + 1, (bidx + 1) % 2)
                    elif g + 1 < NG:
                        setup_early(g + 1)
                        nxt_units = P_units(g + 1, 0, (bidx + 1) % 2)
                    else:
                        nxt_units = []

                    def head(tt):
                        t = t0 // 128 + tt
                        sl = t % 2
                        lsl = slice(tt * 128, (tt + 1) * 128)
                        Ub_ = U_b[0] if t < 16 else U_b[1]
                        Ng = Neg4[0] if t < 16 else Neg4[1]
                        bt = k.bank()
                        pv = psum[:, bt, :].bitcast(BF16)
                        for ci in range(3):
                            k.tr(pv[:, ci * 128:(ci + 1) * 128], xact_c[:, ci, lsl], ident_b[:],
                                 [xact_c.b(), ident_b.b()], [pbuf(bt)])
                        xv = pv[:, 0:256].rearrange("p (r q) -> p r q", q=HP)
                        k.tt("dve", xdts[sl][:], xv, dtd_a[:, t, hd0:hd0 + 4].unsqueeze(2).to_broadcast([128, 4, HP]), ALU.mult,
                             [pbuf(bt), dtd_a.b(t)], [xdts[sl].b()])
                        k.cp("act", xB[sl][:], pv[:, 0:384], [pbuf(bt)], [xB[sl].b()])
                        bc = k.bank()
                        k.mm(psum[:, bc, 0:128], xact_c[:, 2, lsl], xact_c[:, 3, lsl], True, True, [xact_c.b()], [pbuf(bc)])
                        k.cp("act", cbs[sl][:], psum[:, bc, 0:128], [pbuf(bc)], [cbs[sl].b()])
                        br = k.bank()
                        for r in range(4):
                            k.mm(psum[:, br, r * 128:(r + 1) * 128], ident_b[:], Ng[:], r == 0, False,
                                 [ident_b.b(), Ng.b()], [pbuf(br)])
                        for r in range(4):
                            k.mm(psum[:, br, r * 128:(r + 1) * 128],
                                 dtAh[:, t, hd0 + r:hd0 + r + 1].to_broadcast([128, 128]), Ub_[:], False, False,
                                 [dtAh.b(), Ub_.b()], [pbuf(br)])
                            k.mm(psum[:, br, r * 128:(r + 1) * 128],
                                 dtAl[:, t, hd0 + r:hd0 + r + 1].to_broadcast([128, 128]), Ub_[:], False, r == 3,
                                 [dtAl.b(), Ub_.b()], [pbuf(br)])
                        for r in range(4):
                            dc_ = dcy[(t * 4 + r) % 4]
                            mt_ = MT[(t % 2) * 4 + r]
                            k.act(dc_[:], psum[:, br, r * 128:(r + 1) * 128], AF.Exp, [pbuf(br), nb_a.b(t)], [dc_.b()],
                                  bias=nb_a[:, t, hd0 + r:hd0 + r + 1], scale=1.0)
                            k.tt("pool", mt_[:], dc_[:], cbs[sl][:], ALU.mult, [dc_.b(), cbs[sl].b()], [mt_.b()])
                        return None

                    def tail(tt, banks):
                        t = t0 // 128 + tt
                        sl = t % 2
                        lsl = slice(tt * 128, (tt + 1) * 128)
                        has_off = True
                        by = k.bank()
                        for r in range(4):
                            mt_ = MT[(t % 2) * 4 + r]
                            k.mm(psum[:, by, r * HP:(r + 1) * HP], mt_[:], xB[sl][:, r * HP:(r + 1) * HP], True, False,
                                 [mt_.b(), xB[sl].b()], [pbuf(by)])
                            k.mm(psum[:, by, r * HP:(r + 1) * HP], Dg[:, r, :], xB[sl][:, r * HP:(r + 1) * HP], False, True,
                                 [Dg.b(), xB[sl].b()], [pbuf(by)])
                        bs_ = None
                        if t < 16:
                            bs_ = k.bank()
                            k.mm(psum[:, bs_, 0:256], xB[sl][:, 256:384], xdts[sl][:].rearrange("p r q -> p (r q)"), True, True,
                                 [xB[sl].b(), xdts[sl].b()], [pbuf(bs_)])
                        bo = k.bank()
                        held = (by, bo)
                        if t < 16:
                            if t == 0:
                                has_off = False
                            else:
                                k.mm(psum[:, bo, 0:256], xact_c[:, 3, lsl], hTb[:], True, True, [xact_c.b(), hTb.b()], [pbuf(bo)])
                        else:
                            k.perm_excl = held
                            for e8 in range(8):
                                hs_ = h0s[e8 % 4]
                                bh = k.bank(excl=held)
                                for b2 in range(2):
                                    for a in range(2):
                                        k.tr(psum[:, bh, (b2 * 2 + a) * 128:(b2 * 2 + a + 1) * 128],
                                             hs_[:, b2, a, :], ident_f[:], [hs_.b(), ident_f.b()], [pbuf(bh)])
                                k.cp("act" if e8 % 2 else "dve", h0T[:],
                                     psum[:, bh, :].rearrange("p (b q) -> p b q", b=2), [pbuf(bh)], [h0T.b()])
                                for b4 in range(2):
                                    b = e8 * 2 + b4
                                    cz = CTz[b % 2]
                                    if b >= 2:
                                        k.memset("pool", cz[:, (b - 2) * 8:(b - 2) * 8 + 8], 0.0, [cz.b()])
                                    k.cp("pool", cz[:, b * 8:b * 8 + 8], xact_c[:, 3, b * 8:b * 8 + 8], [xact_c.b()], [cz.b()])
                                    k.mm(psum[:, bo, 0:256], cz[:], h0T[:, b4, :], b == 0, b == NSB - 1,
                                         [cz.b(), h0T.b()], [pbuf(bo)])
                                    bmt = Bm[b % 2]
                                    k.ts("pool", bmt[:], xB[sl][:, 256:384], SameB[:, b * 8:b * 8 + 1], ALU.mult,
                                         [xB[sl].b(), SameB.b()], [bmt.b()])
                                    for a in range(2):
                                        bn = k.bank(excl=held)
                                        k.mm(psum[:, bn, 0:128], xdts[sl][:, 2 * a:2 * a + 2, :].rearrange("p r q -> p (r q)"),
                                             bmt[:], True, True, [xdts[sl].b(), bmt.b()], [pbuf(bn)])
                                        k.stt(hs_[:, b4, a, :], hs_[:, b4, a, :], cdp2[:, b, 2 * g + a:2 * g + a + 1],
                                              psum[:, bn, 0:128], ALU.mult, ALU.add, [hs_.b(), cdp2.b(), pbuf(bn)], [hs_.b()])
                                for b4 in range(2):
                                    S.dma("sp", ssm_s[e8 * 2 + b4, g * 256:(g + 1) * 256, :]
                                          .rearrange("(a p) n -> p a n", p=128), hs_[:, b4, :, :], [hs_.b()], ())
                                if e8 + 4 < 8:
                                    load_h0(g, e8 + 4)
                                elif g + 1 < NG:
                                    load_h0(g + 1, e8 - 4)
                                for _ in range(2):
                                    if ucur[0] < len(nxt_units):
                                        nxt_units[ucur[0]]()
                                        ucur[0] += 1
                            k.perm_excl = ()
                            for i in range(2):
                                k.memset("pool", CTz[i][:], 0.0, [CTz[i].b()])
                        if t < 16:
                            if t == 0:
                                k.cp("dve", hTf[:], psum[:, bs_, 0:256], [pbuf(bs_)], [hTf.b()])
                            else:
                                hv_ = hTf[:].rearrange("p (r q) -> p r q", q=HP)
                                k.tt("dve", hv_, hv_, cd_a[:, t, hd0:hd0 + 4].unsqueeze(2).to_broadcast([128, 4, HP]), ALU.mult,
                                     [hTf.b(), cd_a.b(t)], [hTf.b()])
                                k.tt("dve", hTf[:], hTf[:], psum[:, bs_, 0:256], ALU.add, [hTf.b(), pbuf(bs_)], [hTf.b()])
                            if t < 15:
                                k.cp("pool", hTb[:], hTf[:], [hTf.b()], [hTb.b()])
                            else:
                                bf_ = k.bank(excl=held)
                                for a in range(2):
                                    k.tr(psum[:, bf_, a * 128:(a + 1) * 128], hTf[:, a * 128:(a + 1) * 128], ident_f[:],
                                         [hTf.b(), ident_f.b()], [pbuf(bf_)])
                                k.cp("dve", hout[:, 0:2, :], psum[:, bf_, 0:256].rearrange("p (a n) -> p a n", a=2), [pbuf(bf_)], [hout.b()])
                                S.dma("sp", ssm_p[g * 256:(g + 1) * 256, :].rearrange("(a p) n -> p a n", p=128), hout[:, 0:2, :],
                                      [hout.b()], ())
                        yv = psum[:, by, 0:256]
                        if has_off:
                            k.tt("dve", t1[:], psum[:, bo, 0:256].rearrange("p (r q) -> p r q", q=HP),
                                 eacs_a[:, t, hd0:hd0 + 4].unsqueeze(2).to_broadcast([128, 4, HP]), ALU.mult,
                                 [pbuf(bo), eacs_a.b(t)], [t1.b()])
                            k.tt("dve", yg[:], yv, t1[:].rearrange("p r q -> p (r q)"), ALU.add, [pbuf(by), t1.b()], [yg.b()])
                            k.tt("dve", yg[:], yg[:], zs4_c[:, tt, :], ALU.mult, [yg.b(), zs4_c.b(tt)], [yg.b()])
                        else:
                            k.tt("dve", yg[:], yv, zs4_c[:, tt, :], ALU.mult, [pbuf(by), zs4_c.b(tt)], [yg.b()])
                        yn_ = ygn[t % 2]
                        k.act(yn_[:], yg[:], AF.Square, [yg.b()], [yn_.b(), yst.b()], accum_out=yst[:, 0:1])
                        k.ts("pool", yst[:, 1:2], yst[:, 0:1], 1.0 / 256, ALU.mult, [yst.b()], [yst.b()], s2=EPS, op1=ALU.add)
                        k.tt("pool", yst[:, 2:3], yst[:, 1:2], mhalf[:, 0:1], ALU.pow, [yst.b(), mhalf.b()], [yst.b()])
                        yn_ = ygn[t % 2]
                        k.stt(yn_[:], yg[:], yst[:, 2:3], ngrow_c[:], ALU.mult, ALU.mult, [yg.b(), yst.b(), ngrow_c.b()], [yn_.b()])

                    def fin(tt):
                        t = t0 // 128 + tt
                        lsl = slice(tt * 128, (tt + 1) * 128)
                        yn_ = ygn[t % 2]
                        bg = k.bank()
                        pg = psum[:, bg, :].bitcast(BF16)
                        for a in range(2):
                            k.tr(pg[:, a * 128:(a + 1) * 128], yn_[:, a * 128:(a + 1) * 128], ident_b[:],
                                 [yn_.b(), ident_b.b()], [pbuf(bg)])
                        k.cp("act", ygT[:, :, lsl], pg[:, 0:256].rearrange("p (a n) -> p a n", a=2), [pbuf(bg)], [ygT.b()])

                    ntile = tn // 128
                    per_step = -(-len(nxt_units) // ntile)
                    ucur = [0]
                    head(0)
                    for tt in range(ntile):
                        if tt + 1 < ntile:
                            head(tt + 1)
                        if tt >= 1:
                            fin(tt - 1)
                        tail(tt, None)
                        lim = min(len(nxt_units), (tt + 1) * per_step)
                        while ucur[0] < lim:
                            nxt_units[ucur[0]]()
                            ucur[0] += 1
                    fin(ntile - 1)
                    for m in range(DC):
                        bi = k.bank()
                        for kc in range(2):
                            k.mm(psum[:, bi, 0:tn], wo[:, kc, m * 128:(m + 1) * 128], ygT[:, kc, 0:tn], kc == 0, kc == 1,
                                 [wo.b(), ygT.b()], [pbuf(bi)])
                        evac_add_xres(m, tbi, t0, tn, psum[:, bi, 0:tn], pbuf(bi))
                bi = k.bank()
                for ci in range(4):
                    k.tr(psum[0:51, bi, ci * 128:(ci + 1) * 128], ncv[:, ci, :], ident_f[:], [ncv.b(), ident_f.b()], [pbuf(bi)])
                k.cp("dve", ncvo[0:51, :, :], psum[0:51, bi, :].rearrange("p (c n) -> p c n", c=4), [pbuf(bi)], [ncvo.b()])
                for ci in range(4):
                    cs_ = slice(cglob[ci] * 128, (cglob[ci] + 1) * 128)
                    S.dma("sp", conv_s[:, cs_], ncvo[0:48, ci, :], [ncvo.b()], ())
                    S.dma("sp", conv_p[:, cs_], ncvo[48:51, ci, :], [ncvo.b()], ())
            S.barrier()

    if cfg["mamba"]:
        mamba_layer()
    if cfg["attn"]:
        attn_layer(0)
    if cfg["mlp"]:
        mlp_layer(0)
    if cfg["pool"]:
        pool_layer()
    if cfg["attn"]:
        attn_layer(1)
    if cfg["mlp"]:
        mlp_layer(1)

    with ExitStack() as ph:
        gfin = k.sb("gfin", [128, D], F32, ph)
        S.dma("sp", gfin[:], norm_final.to_broadcast([128, D]), (), [gfin.b()])
        yt = [k.sb("yt%d" % i, [128, D], F32, ph) for i in range(4)]
        sq = k.sb("sq_scr", [128, D], F32, ph)
        stat = [k.sb("stat%d" % i, [128, 4], F32, ph) for i in range(2)]
        for t in range(NT):
            b0 = bank2()
            for c in range(DC):
                bi = b0 + c // 4
                k.tr(psum[:, bi, (c % 4) * 128:(c % 4 + 1) * 128], xres[:, c, t * 128:(t + 1) * 128], ident_f[:],
                     [xres.b((c, t)), ident_f.b()], [pbuf(bi)])
            st = stat[t % 2]
            y = yt[t % 4]
            pin = psum[:, b0:b0 + 2, :].rearrange("p a n -> p (a n)")
            k.act(sq[:], pin, AF.Square, [pbuf(b0), pbuf(b0 + 1)], [sq.b(), st.b()], accum_out=st[:, 0:1])
            k.act(st[:, 1:2], st[:, 0:1], AF.Ln, [st.b()], [st.b()], bias=EPS, scale=1.0 / D)
            k.act(st[:, 2:3], st[:, 1:2], AF.Exp, [st.b()], [st.b()], scale=-0.5)
            k.stt(y[:], pin, st[:, 2:3], gfin[:], ALU.mult, ALU.mult,
                  [pbuf(b0), pbuf(b0 + 1), st.b(), gfin.b()], [y.b()])
            dst = y_p[t * 128:(t + 1) * 128, :] if t < 16 else y_s[:, :]
            S.dma("sp", dst, y[:], [y.b()], ())
        S.barrier()
    print("ops", S.nops, "waits", S.nwaits)
    return k


_CACHE = {}


def _get_program():
    if "k" not in _CACHE:
        _CACHE["k"] = build_program()
    return _CACHE["k"]


def kernel(**inputs):
    inp = {k_: np.asarray(v) for k_, v in inputs.items()}
    kk = _get_program()
    f = lambda a: np.ascontiguousarray(a, dtype=np.float32)
    shared = {
        "norm_mix": f(inp["norm_mix"]), "norm_xattn": f(inp["norm_xattn"]), "norm_mem": f(inp["norm_mem"]),
        "norm_mlp": f(inp["norm_mlp"]), "norm_final": f(inp["norm_final"].reshape(1, D)),
        "w_in": f(inp["w_in"][0]), "conv_w": f(inp["conv_w"][0]), "conv_b": f(inp["conv_b"].reshape(1, CONV_DIM)),
        "dt_bias": f(inp["dt_bias"].reshape(1, NH)), "a_log": f(inp["a_log"].reshape(1, NH)),
        "d_skip": f(inp["d_skip"].reshape(1, NH)), "norm_gated": f(inp["norm_gated"].reshape(1, DI)),
        "w_out": f(inp["w_out"][0]), "w_pool": f(inp["w_pool"][0]), "pool_scale": f(inp["pool_scale"].reshape(1, D)),
        "w_xq": f(inp["w_xq"]), "w_xk": f(inp["w_xk"]), "w_xv": f(inp["w_xv"]), "w_xo": f(inp["w_xo"]),
        "w_up": f(inp["w_up"]), "w_down": f(inp["w_down"]),
    }
    in_maps = []
    for c in range(NCORES):
        sl = slice(c * NSB, (c + 1) * NSB)
        m = dict(shared)
        m.update({
            "xp": f(inp["x_prompt"][c]),
            "xs": f(inp["x_sample"][sl].reshape(TS, D)),
            "ck": f(inp["cache_mem_k"][:, sl].reshape(2, NSB, NMEM, D)),
            "cv": f(inp["cache_mem_v"][:, sl].reshape(2, NSB, NMEM, D)),
            "ssm": f(inp["state_ssm"][0, sl].reshape(NSB, DI, NS)),
            "sconv": f(inp["state_conv"][0, sl].reshape(NSB * 3, CONV_DIM)),
            "spool": f(inp["state_pool"][0, sl].reshape(NSB * 15, D)),
            "mem": f(inp["mem_prompt"][c]),
        })
        in_maps.append(m)
    res = run_bass_kernel_spmd(kk.nc, in_maps, core_ids=list(range(NCORES)))
    R = res.results
    cat = lambda name, shp: np.concatenate([R[c][name].reshape(shp) for c in range(NCORES)], 0)
    y_prompt = np.stack([R[c]["y_p"] for c in range(NCORES)], 0)
    y_sample = cat("y_s", (NSB, DSEQ, D))
    mk = np.stack([R[c]["mk_p"].reshape(2, NMEM, XH, XD) for c in range(NCORES)], 1)
    mv = np.stack([R[c]["mv_p"].reshape(2, NMEM, XH, XD) for c in range(NCORES)], 1)
    ssm_p = np.stack([R[c]["ssm_p"].reshape(NH, HP, NS) for c in range(NCORES)], 0)[None]
    conv_p = np.stack([R[c]["conv_p"] for c in range(NCORES)], 0)[None]
    pool_p = np.stack([R[c]["pool_p"] for c in range(NCORES)], 0)[None]
    ssm_s = cat("ssm_s", (NSB, NH, HP, NS))[None]
    conv_s = cat("conv_s", (NSB, 3, CONV_DIM))[None]
    pool_s = cat("pool_s", (NSB, 15, D))[None]
    return (y_prompt, y_sample, mk, mv, ssm_p, conv_p, pool_p, ssm_s, conv_s, pool_s)
```

```python
import numpy as np
from contextlib import ExitStack
import concourse.bass as bass
import concourse.mybir as mybir
from concourse.bass_utils import run_bass_kernel_spmd

F32 = mybir.dt.float32
BF16 = mybir.dt.bfloat16
AF = mybir.ActivationFunctionType
ALU = mybir.AluOpType
AX = mybir.AxisListType

NCORES = 8
D = 1024
DC = 8
SEQ = 2048
NSB = 16
DSEQ = 8
TS = NSB * DSEQ
T = SEQ + TS
NT = T // 128
TBS = [(0, 512), (512, 512), (1024, 512), (1536, 512), (2048, 128)]
DI = 2048
NH = 32
HP = 64
NG = 8
NS = 128
CONV_DIM = 4096
IN_PROJ = 6176
NMEM = 256
XH = 4
XD = 256
DFF = 4096
EPS = 1e-5
POOL_W = (2, 4, 8, 16)

SAME_ENGINE_SYNC = False
NDS = 32


class Buf:
    __slots__ = ("w", "r", "excl")

    def __init__(self):
        self.w = None
        self.r = {}
        self.excl = False


class Tile:
    def __init__(self, t):
        self.t = t
        self.bufs = {}

    def b(self, key=None):
        v = self.bufs.get(key)
        if v is None:
            v = self.bufs[key] = Buf()
        return v

    def __getitem__(self, idx):
        return self.t[idx]


class _Eng:
    def __init__(self, h, sem):
        self.h = h
        self.sem = sem
        self.cnt = 0
        self.waited = {}


class Sched:
    def __init__(self, nc, es):
        self.nc = nc
        self.es = es
        self.eng = {}
        self.sems = {}
        for name, h in [("pe", nc.tensor), ("act", nc.scalar), ("dve", nc.vector),
                        ("pool", nc.gpsimd), ("sp", nc.sync)]:
            sem = es.enter_context(nc.semaphore("s_" + name))
            self.eng[name] = _Eng(h, sem)
            self.sems[name] = sem
        self.dcnt = [0] * NDS
        self.dnext = 0
        self.dnext_sw = 0
        for i in range(NDS):
            self.sems[("d", i)] = es.enter_context(nc.semaphore("sd%d" % i))
        self.nwaits = 0
        self.nops = 0

    def _wait(self, e, key, val):
        if e.waited.get(key, 0) >= val:
            return
        e.h.wait_ge(self.sems[key], val)
        e.waited[key] = val
        self.nwaits += 1

    @staticmethod
    def _deps(reads, writes, en=None):
        deps = {}
        for b in reads:
            if b.w is not None:
                k, v = b.w
                if deps.get(k, 0) < v:
                    deps[k] = v
            if b.excl:
                for k, v in b.r.items():
                    if k != en and deps.get(k, 0) < v:
                        deps[k] = v
        for b in writes:
            if b.w is not None:
                k, v = b.w
                if deps.get(k, 0) < v:
                    deps[k] = v
            for k, v in b.r.items():
                if deps.get(k, 0) < v:
                    deps[k] = v
        return deps

    @staticmethod
    def _mark(ev, reads, writes):
        k, v = ev
        for b in reads:
            b.r[k] = v
        for b in writes:
            b.w = ev
            b.r = {}

    def op(self, en, fn, reads=(), writes=()):
        e = self.eng[en]
        deps = self._deps(reads, writes, en)
        raw_self = 0
        if en != "pe":
            for b in reads:
                if b.w is not None and b.w[0] == en and b.w[1] > raw_self:
                    raw_self = b.w[1]
        for k, v in deps.items():
            if k == en:
                if en == "pe":
                    continue
                if not SAME_ENGINE_SYNC:
                    v = raw_self
                    if v == 0:
                        continue
            self._wait(e, k, v)
        ins = fn(e.h)
        e.cnt += 1
        ins.then_inc(e.sem, 1)
        self._mark((en, e.cnt), reads, writes)
        self.nops += 1
        return ins

    def dma(self, qn, out, in_, reads=(), writes=(), **kw):
        e = self.eng[qn]
        deps = self._deps(reads, writes)
        for k, v in deps.items():
            self._wait(e, k, v)
        half = NDS // 2
        if qn == "pool":
            i = half + self.dnext_sw
            self.dnext_sw = (self.dnext_sw + 1) % (NDS - half)
        else:
            i = self.dnext
            self.dnext = (i + 1) % half
        if self.dcnt[i] > 0:
            self._wait(e, ("d", i), 16 * self.dcnt[i])
        self.dcnt[i] += 1
        ins = e.h.dma_start(out=out, in_=in_, **kw)
        ins.then_inc(self.sems[("d", i)], 16)
        self._mark((("d", i), 16 * self.dcnt[i]), reads, writes)
        self.nops += 1
        return ins

    def barrier(self, engines=("pe", "act", "dve", "pool", "sp")):
        for en in engines:
            e = self.eng[en]
            for on, o in self.eng.items():
                if on != en and o.cnt > 0:
                    self._wait(e, on, o.cnt)
            for i in range(NDS):
                if self.dcnt[i]:
                    self._wait(e, ("d", i), 16 * self.dcnt[i])


class K:
    def __init__(self):
        self.nc = bass.Bass("TRN2", target_bir_lowering=False)
        self.es = ExitStack()
        self.S = Sched(self.nc, self.es)
        self.bank_rr = 0

    def sb(self, name, shape, dt, es=None):
        es = es or self.es
        self.uid = getattr(self, "uid", 0) + 1
        return Tile(es.enter_context(self.nc.sbuf_tensor("%s_u%d" % (name, self.uid), list(shape), dt)))

    def dram_in(self, name, shape, dt=F32):
        return self.nc.dram_tensor(name, list(shape), dt, kind="ExternalInput").ap()

    def dram_out(self, name, shape, dt=F32):
        return self.nc.dram_tensor(name, list(shape), dt, kind="ExternalOutput").ap()

    def bank(self, excl=()):
        i = self.bank_rr
        pe_ = getattr(self, "perm_excl", ())
        while i in excl or i in pe_:
            i = (i + 1) % 8
        self.bank_rr = (i + 1) % 8
        return i

    def mm(self, out, lhsT, rhs, start, stop, reads, writes):
        return self.S.op("pe", lambda h: h.matmul(out, lhsT=lhsT, rhs=rhs, start=start, stop=stop,
                                                  skip_group_check=True), reads, writes)

    def tr(self, out, in_, ident, reads, writes):
        return self.S.op("pe", lambda h: h.transpose(out, in_, ident), reads, writes)

    def act(self, out, in_, func, reads, writes, bias=None, scale=None, accum_out=None):
        kw = {}
        if bias is not None:
            kw["bias"] = bias
        if scale is not None:
            kw["scale"] = scale
        if accum_out is not None:
            kw["accum_out"] = accum_out
        return self.S.op("act", lambda h: h.activation(out, in_, func, **kw), reads, writes)

    def ts(self, en, out, in0, s1, op0, reads, writes, s2=None, op1=None):
        if op1 is None:
            if en == "pool" and op0 == ALU.mult:
                return self.S.op(en, lambda h: h.tensor_scalar(out, in0, s1, 0.0, ALU.mult, ALU.add), reads, writes)
            return self.S.op(en, lambda h: h.tensor_scalar(out, in0, s1, None, op0), reads, writes)
        return self.S.op(en, lambda h: h.tensor_scalar(out, in0, s1, s2, op0, op1), reads, writes)

    def stt(self, out, in0, scalar, in1, op0, op1, reads, writes):
        return self.S.op("dve", lambda h: h.scalar_tensor_tensor(out, in0, scalar, in1, op0, op1), reads, writes)

    def tt(self, en, out, in0, in1, op, reads, writes):
        return self.S.op(en, lambda h: h.tensor_tensor(out, in0, in1, op), reads, writes)

    def cp(self, en, out, in_, reads, writes):
        if en == "act":
            return self.S.op("act", lambda h: h.copy(out, in_), reads, writes)
        return self.S.op(en, lambda h: h.tensor_copy(out, in_), reads, writes)

    def memset(self, en, ap, val, writes):
        return self.S.op(en, lambda h: h.memset(ap, val), (), writes)


CFG = {"mamba": True, "pool": True, "attn": True, "mlp": True}
MBS = 9


def build_program():
    k = K()
    nc, S, es = k.nc, k.S, k.es
    cfg = CFG

    xp = k.dram_in("xp", [SEQ, D])
    xs = k.dram_in("xs", [TS, D])
    ck = k.dram_in("ck", [2, NSB, NMEM, D])
    cv = k.dram_in("cv", [2, NSB, NMEM, D])
    ssm = k.dram_in("ssm", [NSB, DI, NS])
    sconv = k.dram_in("sconv", [NSB * 3, CONV_DIM])
    spool = k.dram_in("spool", [NSB * 15, D])
    mem = k.dram_in("mem", [NMEM, D])
    norm_mix = k.dram_in("norm_mix", [2, D])
    norm_xattn = k.dram_in("norm_xattn", [2, D])
    norm_mem = k.dram_in("norm_mem", [2, D])
    norm_mlp = k.dram_in("norm_mlp", [2, D])
    norm_final = k.dram_in("norm_final", [1, D])
    w_in = k.dram_in("w_in", [D, IN_PROJ])
    conv_w = k.dram_in("conv_w", [4, CONV_DIM])
    conv_b = k.dram_in("conv_b", [1, CONV_DIM])
    dt_bias = k.dram_in("dt_bias", [1, NH])
    a_log = k.dram_in("a_log", [1, NH])
    d_skip = k.dram_in("d_skip", [1, NH])
    norm_gated = k.dram_in("norm_gated", [1, DI])
    w_out = k.dram_in("w_out", [DI, D])
    w_pool = k.dram_in("w_pool", [4, 256, 256])
    pool_scale = k.dram_in("pool_scale", [1, D])
    w_xq = k.dram_in("w_xq", [2, D, D])
    w_xk = k.dram_in("w_xk", [2, D, D])
    w_xv = k.dram_in("w_xv", [2, D, D])
    w_xo = k.dram_in("w_xo", [2, D, D])
    w_up = k.dram_in("w_up", [2, D, DFF])
    w_down = k.dram_in("w_down", [2, DFF, D])

    y_p = k.dram_out("y_p", [SEQ, D])
    y_s = k.dram_out("y_s", [TS, D])
    mk_p = k.dram_out("mk_p", [2, NMEM, D])
    mv_p = k.dram_out("mv_p", [2, NMEM, D])
    ssm_p = k.dram_out("ssm_p", [DI, NS])
    conv_p = k.dram_out("conv_p", [3, CONV_DIM])
    pool_p = k.dram_out("pool_p", [15, D])
    ssm_s = k.dram_out("ssm_s", [NSB, DI, NS])
    conv_s = k.dram_out("conv_s", [NSB * 3, CONV_DIM])
    pool_s = k.dram_out("pool_s", [NSB * 15, D])

    xres = k.sb("xres", [128, DC, T], F32)
    ident_f = k.sb("ident_f", [128, 128], F32)
    ident_b = k.sb("ident_b", [128, 128], BF16)
    ones_f = k.sb("ones_f", [128, 128], F32)
    ones_b = k.sb("ones_b", [128, 128], BF16)
    zeros_b = k.sb("zeros_b", [128, 512], BF16)
    colv = k.sb("colv", [128, 32, 12], F32)
    k.wsl = []
    psum = Tile(es.enter_context(nc.psum_tensor("psum", [128, 8, 512], F32)))
    k.slab_rr = 0

    def pbuf(i):
        b_ = psum.b(i)
        b_.excl = True
        return b_

    def next_slab():
        k.slab_rr = (k.slab_rr + 1) % len(k.wsl)
        return k.wsl[k.slab_rr]

    def bank2(excl=()):
        if k.bank_rr % 2:
            k.bank_rr = (k.bank_rr + 1) % 8
        b0 = k.bank_rr
        pe_ = getattr(k, "perm_excl", ())
        while b0 in excl or (b0 + 1) in excl or b0 in pe_ or (b0 + 1) in pe_:
            b0 = (b0 + 2) % 8
        k.bank_rr = (b0 + 2) % 8
        return b0

    def xb(c, t0, tn):
        return [xres.b((c, tt)) for tt in range(t0 // 128, (t0 + tn) // 128)]

    k.memset("pool", ones_f[:], 1.0, [ones_f.b()])
    S.op("pool", lambda h: h.affine_select(ident_f[:], ones_f[:], [[-1, 128]], ALU.is_equal, 0.0,
                                           base=0, channel_multiplier=1),
         [ones_f.b()], [ident_f.b()])
    k.cp("pool", ident_b[:], ident_f[:], [ident_f.b()], [ident_b.b()])
    k.cp("pool", ones_b[:], ones_f[:], [ones_f.b()], [ones_b.b()])
    k.memset("pool", zeros_b[:], 0.0, [zeros_b.b()])
    with ExitStack() as ph:
        vecrows = k.sb("vecrows", [16, 4096], F32, ph)
        k.memset("dve", vecrows[:], 0.0, [vecrows.b()])
        S.dma("sp", vecrows[0:4, :], conv_w[:, :], (), [vecrows.b()])
        S.dma("sp", vecrows[4:5, :], conv_b[:, :], (), [vecrows.b()])
        S.dma("sp", vecrows[5:7, 0:D], norm_mix[:, :], (), [vecrows.b()])
        S.dma("sp", vecrows[7:9, 0:D], norm_xattn[:, :], (), [vecrows.b()])
        S.dma("sp", vecrows[9:11, 0:D], norm_mlp[:, :], (), [vecrows.b()])
        S.dma("sp", vecrows[11:12, 0:D], pool_scale[:, :], (), [vecrows.b()])
        bi = k.bank()
        for c in range(32):
            k.tr(psum[:, bi, c * 12:(c + 1) * 12], vecrows[0:12, c * 128:(c + 1) * 128], ident_f[0:12, 0:12],
                 [vecrows.b(), ident_f.b()], [pbuf(bi)])
        k.cp("dve", colv[:], psum[:, bi, 0:384].rearrange("p (c r) -> p c r", r=12), [pbuf(bi)], [colv.b()])
        S.barrier()
    CV_CONVW, CV_CONVB, CV_MIX, CV_XATTN, CV_MLP, CV_PSCALE = 0, 4, 5, 7, 9, 11

    with ExitStack() as ph:
        xin = [k.sb("xin%d" % i, [128, D], F32, ph) for i in range(6)]
        for t in range(NT):
            xt = xin[t % 6]
            src = xp[t * 128:(t + 1) * 128, :] if t < 16 else xs[:, :]
            S.dma("sp", xt[:], src, (), [xt.b()])
            for half in range(2):
                bi = k.bank()
                for c4 in range(4):
                    c = half * 4 + c4
                    k.tr(psum[:, bi, c4 * 128:(c4 + 1) * 128], xt[:, c * 128:(c + 1) * 128], ident_f[:],
                         [xt.b(), ident_f.b()], [pbuf(bi)])
                eng = "dve" if half == 0 else "act"
                k.cp(eng, xres[:, half * 4:half * 4 + 4, t * 128:(t + 1) * 128],
                     psum[:, bi, :].rearrange("p (c n) -> p c n", c=4),
                     [pbuf(bi)], [xres.b((c, t)) for c in range(half * 4, half * 4 + 4)])
        S.barrier()

    def rmsnorm_fm(ph_tiles, gidx, out_fn, tbs):
        sqt, rst = ph_tiles
        for tbi, (t0, tn) in tbs:
            bi = k.bank()
            for c in range(DC):
                sq = sqt[c % 2]
                k.act(sq[:, 0:tn], xres[:, c, t0:t0 + tn], AF.Square, xb(c, t0, tn), [sq.b()])
                k.mm(psum[:, bi, 0:tn], ones_b[:], sq[:, 0:tn], c == 0, c == DC - 1,
                     [ones_b.b(), sq.b()], [pbuf(bi)])
            rs = rst[tbi % 2]
            k.act(rs[:, 0:tn], psum[:, bi, 0:tn], AF.Ln, [pbuf(bi)], [rs.b()], bias=EPS, scale=1.0 / D)
            k.act(rs[:, 0:tn], rs[:, 0:tn], AF.Exp, [rs.b()], [rs.b()], scale=-0.5)
            for c in range(DC):
                o_ap, o_bufs = out_fn(c, tbi, t0, tn)
                k.stt(o_ap, xres[:, c, t0:t0 + tn], colv[:, c, gidx:gidx + 1], rs[:, 0:tn], ALU.mult, ALU.mult,
                      xb(c, t0, tn) + [colv.b(), rs.b()], o_bufs)

    def linear_fm(W, KC, c0, ncols, rhs_fn, evac_fn, tbs):
        NW = 4096 // KC
        for s0 in range(0, ncols, NW):
            nw = min(NW, ncols - s0)
            slab = next_slab()
            view = slab[:, 0:KC * nw].rearrange("p (k n) -> p k n", k=KC)
            S.dma("pool", view, W[0:KC * 128, c0 + s0:c0 + s0 + nw].rearrange("(k p) n -> p k n", p=128),
                  (), [slab.b()])
            for m in range(nw // 128):
                for tbi, (t0, tn) in tbs:
                    bi = k.bank()
                    for kc in range(KC):
                        rhs, rreads = rhs_fn(kc, tbi, t0, tn)
                        k.mm(psum[:, bi, 0:tn], view[:, kc, m * 128:(m + 1) * 128], rhs, kc == 0, kc == KC - 1,
                             [slab.b()] + rreads, [pbuf(bi)])
                    evac_fn((s0 // 128) + m, tbi, t0, tn, psum[:, bi, 0:tn], pbuf(bi))

    ALL_TBS = list(enumerate(TBS))

    def evac_add_xres(m, tbi, t0, tn, ps, pb):
        k.tt("dve", xres[:, m, t0:t0 + tn], ps, xres[:, m, t0:t0 + tn], ALU.add,
             [pb] + xb(m, t0, tn), xb(m, t0, tn))

    def mlp_layer(li):
        with ExitStack() as ph:
            k.wsl = [k.sb("wsl%d" % i, [128, 4096], BF16, ph) for i in range(3)]
            h = k.sb("mlp_h", [128, DC, T], BF16, ph)
            a = k.sb("mlp_a", [128, DC, T], BF16, ph)
            sqt = [k.sb("mlp_sq%d" % i, [128, 512], BF16, ph) for i in range(2)]
            rst = [k.sb("mlp_rs%d" % i, [128, 512], F32, ph) for i in range(2)]
            rl = [k.sb("mlp_rl%d" % i, [128, 512], F32, ph) for i in range(3)]
            k.rl_rr = 0
            rmsnorm_fm((sqt, rst), CV_MLP + li,
                       lambda c, tbi, t0, tn: (h[:, c, t0:t0 + tn], [h.b((c, tbi))]), ALL_TBS)
            for j in range(4):
                def ev_up(m, tbi, t0, tn, ps, pb):
                    r = rl[k.rl_rr]
                    k.rl_rr = (k.rl_rr + 1) % 3
                    k.act(r[:, 0:tn], ps, AF.Relu, [pb], [r.b()])
                    k.tt("pool", a[:, m, t0:t0 + tn], r[:, 0:tn], r[:, 0:tn], ALU.mult, [r.b()], [a.b((m, tbi))])
                linear_fm(w_up[li], DC, j * 1024, 1024,
                          lambda kc, tbi, t0, tn: (h[:, kc, t0:t0 + tn], [h.b((kc, tbi))]), ev_up, ALL_TBS)
                linear_fm(w_down[li][j * 1024:(j + 1) * 1024, :], DC, 0, 1024,
                          lambda kc, tbi, t0, tn: (a[:, kc, t0:t0 + tn], [a.b((kc, tbi))]), evac_add_xres, ALL_TBS)
            S.barrier()

    def attn_layer(li):
        scale = float(XD) ** -0.5
        with ExitStack() as ph:
            wq = k.sb("at_wq", [128, DC, D], BF16, ph)
            wo = k.sb("at_wo", [128, DC, D], BF16, ph)
            sqt = [k.sb("at_sq%d" % i, [128, 512], BF16, ph) for i in range(2)]
            rst = [k.sb("at_rs%d" % i, [128, 512], F32, ph) for i in range(2)]
            hn = [k.sb("at_hn0", [128, DC, 512], BF16, ph)] * 2
            qt = [k.sb("at_q0", [128, DC, 512], BF16, ph)] * 2
            ot = hn
            kT = k.sb("at_kT", [128, DC, NMEM], BF16, ph)
            Vp = k.sb("at_V", [128, 2, D], BF16, ph)
            Pt = [k.sb("at_P%d" % i, [128, XH, NMEM], BF16, ph) for i in range(2)]
            Pn = Pt
            PT = [k.sb("at_PT%d" % i, [128, XH * 2, 128], BF16, ph) for i in range(2)]
            sst = [k.sb("at_st%d" % i, [128, 16], F32, ph) for i in range(2)]

            with ExitStack() as ph2:
                k.wsl = [k.sb("wsl%d" % i, [128, 4096], BF16, ph2) for i in range(2)]
                grow = k.sb("at_grow", [128, D], F32, ph2)
                memt = k.sb("at_mem", [128, D], F32, ph2)
                mn = k.sb("at_mn", [128, D], BF16, ph2)
                mnT = k.sb("at_mnT", [128, DC, NMEM], BF16, ph2)
                ktok = k.sb("at_ktok", [128, 2, D], F32, ph2)
                vtok = ktok
                sq = k.sb("at_sqscr", [128, D], BF16, ph2)
                st = k.sb("at_mst", [128, 4], F32, ph2)
                S.dma("sp", grow[:], norm_mem[li:li + 1, :].to_broadcast([128, D]), (), [grow.b()])
                for mt in range(2):
                    S.dma("sp", memt[:], mem[mt * 128:(mt + 1) * 128, :], (), [memt.b()])
                    k.act(sq[:], memt[:], AF.Square, [memt.b()], [sq.b(), st.b()], accum_out=st[:, 0:1])
                    k.act(st[:, 1:2], st[:, 0:1], AF.Ln, [st.b()], [st.b()], bias=EPS, scale=1.0 / D)
                    k.act(st[:, 2:3], st[:, 1:2], AF.Exp, [st.b()], [st.b()], scale=-0.5)
                    k.stt(mn[:], memt[:], st[:, 2:3], grow[:], ALU.mult, ALU.mult,
                          [memt.b(), st.b(), grow.b()], [mn.b()])
                    bi = k.bank()
                    pv = psum[:, bi, :].bitcast(BF16)
                    for c in range(DC):
                        k.tr(pv[:, c * 128:(c + 1) * 128], mn[:, c * 128:(c + 1) * 128], ident_b[:],
                             [mn.b(), ident_b.b()], [pbuf(bi)])
                    k.cp("dve", mnT[:, :, mt * 128:(mt + 1) * 128], pv.rearrange("p (c n) -> p c n", c=DC),
                         [pbuf(bi)], [mnT.b()])
                for which, (W, tok, outd) in enumerate(((w_xk[li], ktok, mk_p[li]), (w_xv[li], vtok, mv_p[li]))):
                    for ch in range(2):
                        slab = next_slab()
                        view = slab[:, 0:DC * 512].rearrange("p (k n) -> p k n", k=DC)
                        S.dma("pool", view, W[:, ch * 512:(ch + 1) * 512].rearrange("(k p) n -> p k n", p=128),
                              (), [slab.b()])
                        for mt in range(2):
                            bi = k.bank()
                            for kc in range(DC):
                                k.mm(psum[:, bi, :], mnT[:, kc, mt * 128:(mt + 1) * 128], view[:, kc, :],
                                     kc == 0, kc == DC - 1, [mnT.b(), slab.b()], [pbuf(bi)])
                            k.cp("act", tok[:, mt, ch * 512:(ch + 1) * 512], psum[:, bi, :], [pbuf(bi)], [tok.b()])
                    S.dma("sp", outd.rearrange("(a p) n -> p a n", p=128), tok[:], [tok.b()], ())
                    if which == 0:
                        for mt in range(2):
                            for c4 in range(2):
                                bi = k.bank()
                                for cc in range(4):
                                    c = c4 * 4 + cc
                                    k.tr(psum[:, bi, cc * 128:(cc + 1) * 128], ktok[:, mt, c * 128:(c + 1) * 128], ident_f[:],
                                         [ktok.b(), ident_f.b()], [pbuf(bi)])
                                k.cp("dve", kT[:, c4 * 4:c4 * 4 + 4, mt * 128:(mt + 1) * 128],
                                     psum[:, bi, :].rearrange("p (c n) -> p c n", c=4), [pbuf(bi)], [kT.b()])
                    else:
                        k.cp("act", Vp[:], vtok[:], [vtok.b()], [Vp.b()])
                    if which == 0:
                        S.dma("pool", wq[:], w_xq[li].rearrange("(k p) n -> p k n", p=128), (), [wq.b()])
                        S.dma("pool", wo[:], w_xo[li].rearrange("(k p) n -> p k n", p=128), (), [wo.b()])
                S.barrier()

            Kb = [k.sb("at_Kb%d" % i, [128, 2, D], BF16, ph) for i in range(2)]
            Vb = [k.sb("at_Vb%d" % i, [128, 2, D], BF16, ph) for i in range(2)]
            kTb = [k.sb("at_kTb%d" % i, [128, DC, NMEM], BF16, ph) for i in range(2)]
            Qz = [k.sb("at_Qz%d" % i, [128, DC, 128], BF16, ph) for i in range(2)]
            for i in range(2):
                k.memset("pool", Qz[i][:], 0.0, [Qz[i].b()])

            def sm1(b0, bufs):
                P, PTt, st = bufs
                sview = psum[:, b0:b0 + 2, :].rearrange("p a (h m) -> p (a h) m", h=2)
                S.op("dve", lambda h: h.tensor_reduce(st[:, 0:4], sview, AX.X, ALU.max),
                     [pbuf(b0), pbuf(b0 + 1)], [st.b()])
                k.ts("dve", st[:, 4:8], st[:, 0:4], -scale, ALU.mult, [st.b()], [st.b()])
                for hd in range(XH):
                    k.act(P[:, hd, :], sview[:, hd, :], AF.Exp, [pbuf(b0), pbuf(b0 + 1), st.b()], [P.b(), st.b()],
                          bias=st[:, 4 + hd:5 + hd], scale=scale, accum_out=st[:, 8 + hd:9 + hd])
                S.op("dve", lambda h: h.reciprocal(st[:, 12:16], st[:, 8:12]), [st.b()], [st.b()])
                k.tt("dve", P[:], P[:], st[:, 12:16].unsqueeze(2).to_broadcast([128, XH, NMEM]), ALU.mult,
                     [P.b(), st.b()], [P.b()])

            def sm2(bufs, excl=()):
                P, PTt, st = bufs
                bi = k.bank(excl=excl)
                pv = psum[:, bi, :].bitcast(BF16)
                for hd in range(XH):
                    for mc in range(2):
                        j = hd * 2 + mc
                        k.tr(pv[:, j * 128:(j + 1) * 128], P[:, hd, mc * 128:(mc + 1) * 128], ident_b[:],
                             [P.b(), ident_b.b()], [pbuf(bi)])
                k.cp("act", PTt[:], pv.rearrange("p (j n) -> p j n", j=XH * 2), [pbuf(bi)], [PTt.b()])
                return PTt

            def softmax_tile(b0, bufs, excl=()):
                sm1(b0, bufs)
                return sm2(bufs, excl)

            qs = k.sb("at_qs", [128, DC, 128], BF16, ph)
            os_ = k.sb("at_os", [128, DC, 128], BF16, ph)
            Ps = k.sb("at_Ps", [128, XH, NMEM], BF16, ph)
            PTs = k.sb("at_PTs", [128, XH * 2, 128], BF16, ph)
            sts = k.sb("at_sts", [128, 16], F32, ph)
            SB0 = 6
            t0s, tns = TBS[4]
            hnt = hn[0]
            rmsnorm_fm((sqt, rst), CV_XATTN + li,
                       lambda c, tbi_, t0_, tn_: (hnt[:, c, 0:tn_], [hnt.b()]), [(4, (t0s, tns))])
            for m in range(DC):
                bi = k.bank()
                for kc in range(DC):
                    k.mm(psum[:, bi, 0:tns], wq[:, kc, m * 128:(m + 1) * 128], hnt[:, kc, 0:tns], kc == 0, kc == DC - 1,
                         [wq.b(), hnt.b()], [pbuf(bi)])
                k.cp("act", qs[:, m, :], psum[:, bi, 0:tns], [pbuf(bi)], [qs.b()])
            k.perm_excl = (SB0, SB0 + 1)
            for bb in range(2):
                k.mm(psum[:, SB0 + bb, :], zeros_b[:, 0:128], zeros_b[:], True, True, [zeros_b.b()], [pbuf(SB0 + bb)])

            def sample_K(b, excl):
                Kt, kTt, Qzt = Kb[b % 2], kTb[b % 2], Qz[b % 2]
                S.dma("pool", Kt[:], ck[li, b].rearrange("(a p) n -> p a n", p=128), (), [Kt.b()])
                for c4 in range(2):
                    bi = k.bank(excl=excl)
                    pv = psum[:, bi, :].bitcast(BF16)
                    for cc in range(4):
                        for mc in range(2):
                            c = c4 * 4 + cc
                            j = cc * 2 + mc
                            k.tr(pv[:, j * 128:(j + 1) * 128], Kt[:, mc, c * 128:(c + 1) * 128], ident_b[:],
                                 [Kt.b(), ident_b.b()], [pbuf(bi)])
                    k.cp("dve" if c4 == 0 else "act", kTt[:, c4 * 4:c4 * 4 + 4, :],
                         pv.rearrange("p (c m) -> p c m", c=4), [pbuf(bi)], [kTt.b()])
                if b >= 2:
                    pb_ = b - 2
                    k.memset("pool", Qzt[:, :, pb_ * 8:pb_ * 8 + 8], 0.0, [Qzt.b()])
                k.cp("pool", Qzt[:, :, b * 8:b * 8 + 8], qs[:, :, b * 8:b * 8 + 8], [qs.b()], [Qzt.b()])
                for hd in range(XH):
                    for dc in range(2):
                        k.mm(psum[:, SB0 + hd // 2, (hd % 2) * 256:(hd % 2) * 256 + 256],
                             Qzt[:, hd * 2 + dc, :], kTt[:, hd * 2 + dc, :], False, (b == NSB - 1 and dc == 1),
                             [Qzt.b(), kTt.b()], [pbuf(SB0 + hd // 2)])

            def sample_V(b):
                Vt = Vb[b % 2]
                S.dma("pool", Vt[:], cv[li, b].rearrange("(a p) n -> p a n", p=128), (), [Vt.b()])
                for d8 in range(DC):
                    hd = d8 // 2
                    for mc in range(2):
                        k.mm(psum[:, SB0 + d8 // 4, (d8 % 4) * 128 + b * 8:(d8 % 4) * 128 + b * 8 + 8],
                             Vt[:, mc, d8 * 128:(d8 + 1) * 128], PTs[:, hd * 2 + mc, b * 8:b * 8 + 8],
                             mc == 0, mc == 1, [Vt.b(), PTs.b()], [pbuf(SB0 + d8 // 4)])

            tile_ctr = 0
            gt = 0
            for tbi, (t0, tn) in ALL_TBS[0:4]:
                hnt, qtt, ott = hn[tbi % 2], qt[tbi % 2], ot[tbi % 2]
                rmsnorm_fm((sqt, rst), CV_XATTN + li,
                           lambda c, tbi_, t0_, tn_: (hnt[:, c, 0:tn_], [hnt.b()]), [(tbi, (t0, tn))])
                for m in range(DC):
                    bi = k.bank()
                    for kc in range(DC):
                        k.mm(psum[:, bi, 0:tn], wq[:, kc, m * 128:(m + 1) * 128], hnt[:, kc, 0:tn], kc == 0, kc == DC - 1,
                             [wq.b(), hnt.b()], [pbuf(bi)])
                    k.cp("act", qtt[:, m, 0:tn], psum[:, bi, 0:tn], [pbuf(bi)], [qtt.b()])

                def scoresA(tt, excl=()):
                    lsl = slice(tt * 128, (tt + 1) * 128)
                    b0 = bank2(excl)
                    for hd in range(XH):
                        for dc in range(2):
                            k.mm(psum[:, b0 + hd // 2, (hd % 2) * 256:(hd % 2) * 256 + 256],
                                 qtt[:, hd * 2 + dc, lsl], kT[:, hd * 2 + dc, :], dc == 0, dc == 1,
                                 [qtt.b(), kT.b()], [pbuf(b0 + hd // 2)])
                    return b0

                def restB2(tt, slot, excl=()):
                    lsl = slice(tt * 128, (tt + 1) * 128)
                    PTt = sm2((Pt[slot], PT[slot], sst[slot]), excl)
                    bo = bank2(excl)
                    for d8 in range(DC):
                        hd = d8 // 2
                        for mc in range(2):
                            k.mm(psum[:, bo + d8 // 4, (d8 % 4) * 128:(d8 % 4 + 1) * 128],
                                 Vp[:, mc, d8 * 128:(d8 + 1) * 128], PTt[:, hd * 2 + mc, :], mc == 0, mc == 1,
                                 [Vp.b(), PTt.b()], [pbuf(bo + d8 // 4)])
                    k.cp("act", ott[:, :, lsl], psum[:, bo:bo + 2, :].rearrange("p a (c n) -> p (a c) n", c=4),
                         [pbuf(bo), pbuf(bo + 1)], [ott.b()])

                ntile = tn // 128
                sc = [None] * ntile
                slot0 = tile_ctr
                for j in range(ntile + 2):
                    if j < ntile:
                        ex_a = (sc[j - 1], sc[j - 1] + 1) if j >= 1 else ()
                        sc[j] = scoresA(j, excl=ex_a)
                    if 0 <= j - 1 < ntile:
                        sl_ = (slot0 + j - 1) % 2
                        sm1(sc[j - 1], (Pt[sl_], PT[sl_], sst[sl_]))
                    if 0 <= j - 2 < ntile:
                        live = (sc[j], sc[j] + 1) if j < ntile else ()
                        restB2(j - 2, (slot0 + j - 2) % 2, live)
                        if gt < 8:
                            for b in (2 * gt, 2 * gt + 1):
                                sample_K(b, live)
                            if gt == 7:
                                for i in range(2):
                                    k.memset("pool", Qz[i][:], 0.0, [Qz[i].b()])
                                softmax_tile(SB0, (Ps, PTs, sts), live)
                        else:
                            for b in (2 * (gt - 8), 2 * (gt - 8) + 1):
                                sample_V(b)
                        gt += 1
                tile_ctr += ntile
                for m in range(DC):
                    bi = k.bank()
                    for kc in range(DC):
                        k.mm(psum[:, bi, 0:tn], wo[:, kc, m * 128:(m + 1) * 128], ott[:, kc, 0:tn], kc == 0, kc == DC - 1,
                             [wo.b(), ott.b()], [pbuf(bi)])
                    evac_add_xres(m, tbi, t0, tn, psum[:, bi, 0:tn], pbuf(bi))
            k.cp("act", os_[:], psum[:, SB0:SB0 + 2, :].rearrange("p a (c n) -> p (a c) n", c=4),
                 [pbuf(SB0), pbuf(SB0 + 1)], [os_.b()])
            k.perm_excl = ()
            for m in range(DC):
                bi = k.bank()
                for kc in range(DC):
                    k.mm(psum[:, bi, 0:tns], wo[:, kc, m * 128:(m + 1) * 128], os_[:, kc, :], kc == 0, kc == DC - 1,
                         [wo.b(), os_.b()], [pbuf(bi)])
                evac_add_xres(m, 4, t0s, tns, psum[:, bi, 0:tns], pbuf(bi))
            S.barrier()

    def pool_layer():
        with ExitStack() as ph:
            HP_ = 16
            up_ = k.sb("pl_up", [128, DC, HP_ + SEQ], BF16, ph)
            us_ = k.sb("pl_us", [128, DC, NSB, 24], BF16, ph)
            pooled_s = k.sb("pl_pooled_s", [128, DC, TS], BF16, ph)
            sqt = [k.sb("pl_sq%d" % i, [128, 512], BF16, ph) for i in range(2)]
            rst = [k.sb("pl_rs%d" % i, [128, 512], F32, ph) for i in range(2)]
            wA = k.sb("pl_wA", [128, 2, 2048], BF16, ph)
            wB = k.sb("pl_wB", [128, 2, 2048], BF16, ph)
            wp = k.sb("pl_wp", [128, 4, 2, 256], BF16, ph)
            ptmp = [k.sb("pl_ptmp%d" % i, [128, 512], F32, ph) for i in range(2)]
            invc = k.sb("pl_invc", [128, 4, 16], F32, ph)
            iot = k.sb("pl_iota", [128, 16], F32, ph)
            ph3 = ExitStack()
            hist = k.sb("pl_hist", [128, 2, D], F32, ph3)

            S.dma("pool", wp[:], w_pool.rearrange("g (k p) n -> p g k n", p=128), (), [wp.b()])
            S.op("pool", lambda h: h.iota(iot[:], [[1, 16]], base=1, channel_multiplier=0, allow_small_or_imprecise_dtypes=True), (), [iot.b()])
            for g, w in enumerate(POOL_W):
                k.ts("dve", invc[:, g, :], iot[:], float(w), ALU.min, [iot.b()], [invc.b()])
            S.op("dve", lambda h: h.reciprocal(invc[:], invc[:]), [invc.b()], [invc.b()])

            k.memset("pool", up_[:, :, 0:HP_], 0.0, [up_.b("hist")])
            k.memset("pool", us_[:, :, :, 0:1], 0.0, [us_.b()])
            S.dma("sp", hist[:, 0, :], spool[0:128, :], (), [hist.b()])
            S.dma("sp", hist[0:112, 1, :], spool[128:240, :], (), [hist.b()])
            usf = us_[:].rearrange("p c b j -> p c (b j)")
            for c in range(DC):
                bi = k.bank()
                k.tr(psum[:, bi, 0:128], hist[:, 0, c * 128:(c + 1) * 128], ident_f[:],
                     [hist.b(), ident_f.b()], [pbuf(bi)])
                k.tr(psum[:, bi, 128:240], hist[0:112, 1, c * 128:(c + 1) * 128], ident_f[0:112, 0:112],
                     [hist.b(), ident_f.b()], [pbuf(bi)])
                k.cp("dve", us_[:, c, :, 1:16], psum[:, bi, 0:240].rearrange("p (b j) -> p b j", j=15),
                     [pbuf(bi)], [us_.b()])

            S.barrier()
            ph3.close()
            outp = k.sb("pl_outp", [128, D], F32, ph)
            outs = k.sb("pl_outs", [128, D], F32, ph)

            def norm_out(c, tbi, t0, tn):
                if tbi < 4:
                    return up_[:, c, HP_ + t0:HP_ + t0 + tn], [up_.b((c, tbi))]
                return us_[:, c, :, 16:24], [us_.b()]
            sq_, rs_ = sqt, rst
            for tbi, (t0, tn) in ALL_TBS:
                bi = k.bank()
                for c in range(DC):
                    sq = sq_[c % 2]
                    k.act(sq[:, 0:tn], xres[:, c, t0:t0 + tn], AF.Square, xb(c, t0, tn), [sq.b()])
                    k.mm(psum[:, bi, 0:tn], ones_b[:], sq[:, 0:tn], c == 0, c == DC - 1, [ones_b.b(), sq.b()], [pbuf(bi)])
                rs = rs_[tbi % 2]
                k.act(rs[:, 0:tn], psum[:, bi, 0:tn], AF.Ln, [pbuf(bi)], [rs.b()], bias=EPS, scale=1.0 / D)
                k.act(rs[:, 0:tn], rs[:, 0:tn], AF.Exp, [rs.b()], [rs.b()], scale=-0.5)
                for c in range(DC):
                    o_ap, o_bufs = norm_out(c, tbi, t0, tn)
                    xin_ = xres[:, c, t0:t0 + tn]
                    rin_ = rs[:, 0:tn]
                    if tbi == 4:
                        xin_ = xin_.rearrange("p (b j) -> p b j", j=8)
                        rin_ = rin_.rearrange("p (b j) -> p b j", j=8)
                    k.stt(o_ap, xin_, colv[:, c, CV_MIX + 1:CV_MIX + 2], rin_, ALU.mult, ALU.mult,
                          xb(c, t0, tn) + [colv.b(), rs.b()], o_bufs)

            b0 = bank2()
            pvb = [psum[:, b0 + i, :].bitcast(BF16) for i in range(2)]
            for c in range(DC):
                k.tr(pvb[0][:, c * 128:(c + 1) * 128], up_[:, c, HP_ + SEQ - 128:HP_ + SEQ], ident_b[:],
                     [up_.b((c, 3)), ident_b.b()], [pbuf(b0)])
            k.cp("dve", outp[:], pvb[0], [pbuf(b0)], [outp.b()])
            S.dma("sp", pool_p[:, :], outp[113:128, :], [outp.b()], ())
            usn = k.sb("pl_usn", [128, DC, 128], BF16, ph)
            k.cp("pool", usn[:].rearrange("p c (b j) -> p c b j", j=8), us_[:, :, :, 16:24], [us_.b()], [usn.b()])
            for c in range(DC):
                k.tr(pvb[1][:, c * 128:(c + 1) * 128], usn[:, c, :], ident_b[:], [usn.b(), ident_b.b()], [pbuf(b0 + 1)])
            k.cp("dve", outs[:], pvb[1], [pbuf(b0 + 1)], [outs.b()])
            for b in range(NSB):
                S.dma("sp", pool_s[b * 15 + 7:b * 15 + 15, :], outs[b * 8:b * 8 + 8, :], [outs.b()], ())
                S.dma("sp", pool_s[b * 15:b * 15 + 7, :], spool[b * 15 + 8:b * 15 + 15, :], (), ())

            for g, w in enumerate(POOL_W):
                cs = slice(2 * g, 2 * g + 2)
                nst = g + 1
                L = HP_ + SEQ
                src = up_
                src_b = [up_.b((c, tb)) for c in (2 * g, 2 * g + 1) for tb in range(4)] + [up_.b("hist")]
                cur = None
                sh = 1
                for s in range(nst):
                    dst = wA if s % 2 == 0 else wB
                    if s == 0:
                        k.tt("dve", dst[:, :, 0:SEQ], up_[:, cs, HP_:L], up_[:, cs, HP_ - 1:L - 1], ALU.add,
                             src_b, [dst.b()])
                    else:
                        k.tt("dve", dst[:, :, sh:SEQ], cur[:, :, sh:SEQ], cur[:, :, 0:SEQ - sh], ALU.add,
                             [cur.b()], [dst.b()])
                        k.cp("dve", dst[:, :, 0:sh], cur[:, :, 0:sh], [cur.b()], [dst.b()])
                    cur = dst
                    sh *= 2
                tmp16 = wB if cur is wA else wA
                t16 = k.sb("pl_t16_%d" % g, [128, 2, 16], F32, ph)
                k.tt("dve", tmp16[:, :, 0:16], cur[:, :, 0:16], invc[:, g:g + 1, :].to_broadcast([128, 2, 16]), ALU.mult,
                     [cur.b(), invc.b()], [tmp16.b()])
                k.tt("dve", t16[:], tmp16[:, :, 0:16], up_[:, cs, HP_:HP_ + 16], ALU.subtract,
                     [tmp16.b()] + src_b, [t16.b()])
                k.stt(up_[:, cs, HP_:L], cur[:, :, 0:SEQ], 1.0 / w, up_[:, cs, HP_:L], ALU.mult, ALU.subtract,
                      [cur.b()] + src_b, src_b)
                k.cp("dve", up_[:, cs, HP_:HP_ + 16], t16[:], [t16.b()] + src_b, src_b)
                sA = k.sb("pl_sA%d" % g, [128, 2, NSB, 8], F32, ph)
                k.tt("dve", sA[:], us_[:, cs, :, 16:24], us_[:, cs, :, 15:23], ALU.add, [us_.b()], [sA.b()])
                for j in range(2, w):
                    k.tt("dve", sA[:], sA[:], us_[:, cs, :, 16 - j:24 - j], ALU.add, [us_.b(), sA.b()], [sA.b()])
                k.stt(pooled_s[:, cs, :].rearrange("p c (b j) -> p c b j", j=8), sA[:], 1.0 / w, us_[:, cs, :, 16:24],
                      ALU.mult, ALU.subtract, [sA.b(), us_.b()], [pooled_s.b(g)])
                for mo in range(2):
                    m = 2 * g + mo
                    for tbi, (t0, tn) in ALL_TBS:
                        bi = k.bank()
                        for kc in range(2):
                            if tbi < 4:
                                rhs_ = up_[:, 2 * g + kc, HP_ + t0:HP_ + t0 + tn]
                                rb_ = [up_.b((2 * g + kc, tbi))]
                            else:
                                rhs_ = pooled_s[:, 2 * g + kc, :]
                                rb_ = [pooled_s.b(g)]
                            k.mm(psum[:, bi, 0:tn], wp[:, g, kc, mo * 128:(mo + 1) * 128], rhs_,
                                 kc == 0, kc == 1, [wp.b()] + rb_, [pbuf(bi)])
                        pt_ = ptmp[(mo * 5 + tbi) % 2]
                        k.act(pt_[:, 0:tn], psum[:, bi, 0:tn], AF.Copy, [pbuf(bi), colv.b()], [pt_.b()],
                              scale=colv[:, m, CV_PSCALE:CV_PSCALE + 1])
                        k.tt("pool", xres[:, m, t0:t0 + tn], xres[:, m, t0:t0 + tn], pt_[:, 0:tn], ALU.add,
                             [pt_.b()] + xb(m, t0, tn), xb(m, t0, tn))

            S.barrier()

    def mamba_layer():
        with ExitStack() as ph:
            h = k.sb("mb_h", [128, DC, T], BF16, ph)
            with ExitStack() as ph0:
                sqt = [k.sb("mb_sq%d" % i, [128, 512], BF16, ph0) for i in range(2)]
                rst = [k.sb("mb_rs%d" % i, [128, 512], F32, ph0) for i in range(2)]
                rmsnorm_fm((sqt, rst), CV_MIX + 0,
                           lambda c, tbi, t0, tn: (h[:, c, t0:t0 + tn], [h.b((c, tbi))]), ALL_TBS)
                S.barrier()

            Umat = k.sb("mb_U", [128, 128], F32, ph)
            SameB = k.sb("mb_SB", [128, 128], F32, ph)
            Ublk = k.sb("mb_Ub", [128, 128], F32, ph)
            S.op("pool", lambda hh: hh.affine_select(Umat[:], ones_f[:], [[1, 128]], ALU.is_ge, 0.0,
                                                     base=0, channel_multiplier=-1), [ones_f.b()], [Umat.b()])
            S.op("pool", lambda hh: hh.affine_select(SameB[:].rearrange("p (b j) -> p b j", j=8),
                                                     ones_f[:].rearrange("p (b j) -> p b j", j=8),
                                                     [[8, 16], [0, 8]], ALU.is_ge, 0.0, base=7, channel_multiplier=-1),
                 [ones_f.b()], [SameB.b()])
            S.op("pool", lambda hh: hh.affine_select(SameB[:].rearrange("p (b j) -> p b j", j=8),
                                                     SameB[:].rearrange("p (b j) -> p b j", j=8),
                                                     [[-8, 16], [0, 8]], ALU.is_ge, 0.0, base=0, channel_multiplier=1),
                 [SameB.b()], [SameB.b()])
            k.tt("pool", Ublk[:], SameB[:], Umat[:], ALU.mult, [SameB.b(), Umat.b()], [Ublk.b()])
            brow = k.sb("mb_brow", [128, 3, NH], F32, ph)
            S.dma("sp", brow[:, 0, :], dt_bias.to_broadcast([128, NH]), (), [brow.b()])
            S.dma("sp", brow[:, 1, :], a_log.to_broadcast([128, NH]), (), [brow.b()])
            S.dma("sp", brow[:, 2, :], d_skip.to_broadcast([128, NH]), (), [brow.b()])
            k.act(brow[:, 1, :], brow[:, 1, :], AF.Exp, [brow.b()], [brow.b()])
            k.ts("dve", brow[:, 1, :], brow[:, 1, :], -1.0, ALU.mult, [brow.b()], [brow.b()])
            wdt = k.sb("mb_wdt", [128, DC, NH], BF16, ph)
            S.dma("pool", wdt[:], w_in[:, 6144:6176].rearrange("(k p) n -> p k n", p=128), (), [wdt.b()])

            dt_a = k.sb("mb_dt", [128, NT, NH], F32, ph)
            cd_a = k.sb("mb_cd", [128, NT, NH], F32, ph)
            dtd_a = k.sb("mb_dtd", [128, NT, NH], F32, ph)
            eacs_a = k.sb("mb_eacs", [128, NT, NH], F32, ph)
            cdp2 = k.sb("mb_cdp2", [128, NSB, 16], F32, ph)
            nb_a = k.sb("mb_nb", [128, NT, NH], F32, ph)
            dtAh = k.sb("mb_dtAh", [128, NT, NH], BF16, ph)
            dtAl = k.sb("mb_dtAl", [128, NT, NH], BF16, ph)
            phT = ExitStack()
            dtA_a = k.sb("mb_dtA", [128, NT, NH], F32, phT)
            nacs_a = k.sb("mb_nacs", [128, NT, NH], F32, phT)
            tmpA = k.sb("mb_tmpA", [128, NT, NH], F32, phT)
            Xs = k.sb("mb_Xs", [128, NSB, NH], F32, phT)
            CDB = k.sb("mb_CDB", [128, NSB, NH], F32, phT)
            dtAf = k.sb("mb_dtAf", [128, NT, NH], F32, phT)
            bA = k.bank()
            bB = k.bank()
            for t in range(NT):
                dst = psum[:, bA, t * NH:(t + 1) * NH] if t < 16 else psum[:, bB, 0:NH]
                for kc in range(DC):
                    k.mm(dst, h[:, kc, t * 128:(t + 1) * 128], wdt[:, kc, :], kc == 0, kc == DC - 1,
                         [h.b((kc, min(t // 4, 4))), wdt.b()], [pbuf(bA if t < 16 else bB)])
            allk = lambda tl: [tl.b(t_) for t_ in range(NT)]
            k.tt("dve", tmpA[:, 0:16, :], psum[:, bA, :].rearrange("p (t h) -> p t h", h=NH),
                 brow[:, 0:1, :].to_broadcast([128, 16, NH]), ALU.add, [pbuf(bA), brow.b()], [tmpA.b()])
            k.tt("dve", tmpA[:, 16, :], psum[:, bB, 0:NH], brow[:, 0, :], ALU.add, [pbuf(bB), brow.b()], [tmpA.b()])
            k.act(tmpA[:], tmpA[:], AF.Exp, [tmpA.b()], [tmpA.b()])
            k.act(dt_a[:], tmpA[:], AF.Ln, [tmpA.b()], allk(dt_a), bias=1.0, scale=1.0)
            k.tt("dve", dtA_a[:], dt_a[:], brow[:, 1:2, :].to_broadcast([128, NT, NH]), ALU.mult,
                 allk(dt_a) + [brow.b()], allk(dtA_a))
            bC = k.bank()
            bD = k.bank()
            bE = k.bank()
            dflat = dtA_a[:, 0:16, :].rearrange("p t h -> p (t h)")
            k.mm(psum[:, bC, :], Umat[:], dflat, True, True, [Umat.b()] + allk(dtA_a), [pbuf(bC)])
            k.mm(psum[:, bD, :], ones_f[:], dflat, True, True, [ones_f.b()] + allk(dtA_a), [pbuf(bD)])
            k.mm(psum[:, bE, 0:NH], Ublk[:], dtA_a[:, 16, :], True, True, [Ublk.b()] + allk(dtA_a), [pbuf(bE)])
            k.mm(psum[:, bE, 64:64 + NH], SameB[:], dtA_a[:, 16, :], True, True, [SameB.b()] + allk(dtA_a), [pbuf(bE)])
            for (tsl, acs_ps, tot_ps, pbs) in (
                    (slice(0, 16), psum[:, bC, :].rearrange("p (t h) -> p t h", h=NH),
                     psum[:, bD, :].rearrange("p (t h) -> p t h", h=NH), [pbuf(bC), pbuf(bD)]),
                    (slice(16, 17), psum[:, bE, 0:NH].unsqueeze(1), psum[:, bE, 64:64 + NH].unsqueeze(1), [pbuf(bE)])):
                k.ts("dve", nacs_a[:, tsl, :], acs_ps, -1.0, ALU.mult, pbs, allk(nacs_a))
                k.act(eacs_a[:, tsl, :], acs_ps, AF.Exp, pbs, allk(eacs_a))
                k.act(cd_a[:, tsl, :], tot_ps, AF.Exp, pbs, allk(cd_a))
                k.tt("dve", tmpA[:, tsl, :], tot_ps, nacs_a[:, tsl, :], ALU.add, pbs + allk(nacs_a), [tmpA.b()])
            k.act(tmpA[:], tmpA[:], AF.Exp, [tmpA.b()], [tmpA.b()])
            k.tt("dve", dtd_a[:], tmpA[:], dt_a[:], ALU.mult, [tmpA.b()] + allk(dt_a), allk(dtd_a))
            k.act(tmpA[:], dt_a[:], AF.Ln, allk(dt_a), [tmpA.b()])
            k.tt("dve", nb_a[:], tmpA[:], nacs_a[:], ALU.add, [tmpA.b()] + allk(nacs_a), allk(nb_a))

            k.cp("dve", dtAh[:], dtA_a[:], [dtA_a.b(t_) for t_ in range(NT)], [dtAh.b()])
            k.cp("dve", dtAf[:], dtAh[:], [dtAh.b()], [dtAf.b()])
            k.tt("dve", dtAl[:], dtA_a[:], dtAf[:], ALU.subtract, [dtA_a.b(t_) for t_ in range(NT)] + [dtAf.b()], [dtAl.b()])
            k.tt("dve", Xs[:], dtA_a[:, 16:17, :].to_broadcast([128, NSB, NH]),
                 SameB[:, 0:128:8].unsqueeze(2).to_broadcast([128, NSB, NH]), ALU.mult,
                 [dtA_a.b(16), SameB.b()], [Xs.b()])
            bi = k.bank()
            k.mm(psum[:, bi, :], ones_f[:], Xs[:].rearrange("p b h -> p (b h)"), True, True,
                 [ones_f.b(), Xs.b()], [pbuf(bi)])
            k.act(CDB[:].rearrange("p b h -> p (b h)"), psum[:, bi, :], AF.Exp, [pbuf(bi)], [CDB.b()])
            k.cp("dve", cdp2[0:64, :, :], CDB[0:64, :, 0:NH:2], [CDB.b()], [cdp2.b()])
            k.cp("dve", cdp2[64:128, :, :], CDB[64:128, :, 1:NH:2], [CDB.b()], [cdp2.b()])

            S.barrier()
            phT.close()
            wz = k.sb("mb_wz", [128, DC, 256], BF16, ph)
            wx = [k.sb("mb_wx%d" % i, [128, DC, 512], BF16, ph) for i in range(2)]
            wog = [k.sb("mb_wog0", [128, 2, D], BF16, ph)] * 2
            dgw = k.sb("mb_dgw", [128, 4, 4, 128], BF16, ph)
            rawt = [k.sb("mb_raw%d" % i, [128, 4, 3 + 512], BF16, ph) for i in range(2)]
            carry = k.sb("mb_carry", [128, 4, 3], BF16, ph)
            raws = k.sb("mb_raws", [128, 4, NSB, 11], BF16, ph)
            scv = k.sb("mb_scv", [48, 4, 128], F32, ph)
            ncv = k.sb("mb_ncv", [128, 4, 51], F32, ph)
            ncvo = k.sb("mb_ncvo", [128, 4, 128], F32, ph)
            hout = ncvo
            xact = [k.sb("mb_xact%d" % i, [128, 4, 512], BF16, ph) for i in range(2)]
            tht = [k.sb("mb_th%d" % i, [128, 512], BF16, ph) for i in range(2)]
            vht = [k.sb("mb_vh%d" % i, [128, 512], BF16, ph) for i in range(2)]
            cbh = k.sb("mb_cbh", [128, 32], F32, ph)
            k.ts("dve", cbh[:], colv[:, :, 4], 0.5, ALU.mult, [colv.b()], [cbh.b()])
            ygT = k.sb("mb_ygT", [128, 2, 512], BF16, ph)
            ngrow = [k.sb("mb_ngrow%d" % i, [128, 256], F32, ph) for i in range(2)]
            zs4 = [k.sb("mb_zs4_%d" % i, [128, 4, 256], BF16, ph) for i in range(2)]
            xdts = [k.sb("mb_xdts%d" % i, [128, 4, HP], BF16, ph) for i in range(2)]
            xB = [k.sb("mb_xB%d" % i, [128, 384], BF16, ph) for i in range(2)]
            Dg = k.sb("mb_Dg", [128, 4, 128], BF16, ph)
            cbs = [k.sb("mb_cbs%d" % i, [128, 128], BF16, ph) for i in range(2)]
            U_b = [k.sb("mb_Ubf%d" % i, [128, 128], BF16, ph) for i in range(2)]
            k.cp("pool", U_b[0][:], Umat[:], [Umat.b()], [U_b[0].b()])
            k.cp("pool", U_b[1][:], Ublk[:], [Ublk.b()], [U_b[1].b()])
            dcy = [k.sb("mb_dcy%d" % i, [128, 128], BF16, ph) for i in range(4)]
            MT = [k.sb("mb_MT%d" % i, [128, 128], BF16, ph) for i in range(8)]
            Neg4 = [k.sb("mb_Neg%d" % i, [128, 128], BF16, ph) for i in range(2)]
            for i, Us in enumerate((Umat, Ublk)):
                k.ts("dve", Neg4[i][:], Us[:], -1.0, ALU.add, [Us.b()], [Neg4[i].b()], s2=30000.0, op1=ALU.mult)
            t1 = k.sb("mb_t1", [128, 4, HP], F32, ph)
            yg = k.sb("mb_yg", [128, 256], F32, ph)
            ygn = [k.sb("mb_ygn%d" % i, [128, 256], BF16, ph) for i in range(2)]
            yst = k.sb("mb_yst", [128, 4], F32, ph)
            mhalf = k.sb("mb_mhalf", [128, 1], F32, ph)
            k.memset("pool", mhalf[:], -0.5, [mhalf.b()])
            hTf = k.sb("mb_hTf", [128, 256], F32, ph)
            hTb = k.sb("mb_hTb", [128, 256], BF16, ph)
            h0s = [k.sb("mb_h0s%d" % i, [128, 2, 2, 128], F32, ph) for i in range(4)]
            h0T = k.sb("mb_h0T", [128, 2, 256], BF16, ph)
            CTz = [k.sb("mb_CTz%d" % i, [128, 128], BF16, ph) for i in range(2)]
            Bm = [k.sb("mb_Bm%d" % i, [128, 128], BF16, ph) for i in range(2)]
            for i in range(2):
                k.memset("pool", CTz[i][:], 0.0, [CTz[i].b()])

            def cglob_of(gg):
                return [2 * gg, 2 * gg + 1, 16 + gg, 24 + gg]

            def load_wx(gg):
                for (dst0, src0, n) in ((0, DI + gg * 256, 256), (256, 2 * DI + gg * 128, 128),
                                        (384, 2 * DI + 1024 + gg * 128, 128)):
                    S.dma("pool", wx[gg % 2][:, :, dst0:dst0 + n],
                          w_in[:, src0:src0 + n].rearrange("(k p) n -> p k n", p=128), (), [wx[gg % 2].b()])

            def load_h0(gg, e8):
                for b2 in range(2):
                    S.dma("sp", h0s[e8 % 4][:, b2, :, :], ssm[e8 * 2 + b2, gg * 256:(gg + 1) * 256, :]
                          .rearrange("(a p) n -> p a n", p=128), (), [h0s[e8 % 4].b()])

            def setup_early(gg):
                cg = cglob_of(gg)
                if gg == 0:
                    load_wx(0)
                S.dma("pool", wz[:], w_in[:, gg * 256:(gg + 1) * 256].rearrange("(k p) n -> p k n", p=128), (), [wz.b()])
                if gg + 1 < NG:
                    load_wx(gg + 1)
                S.dma("sp", ngrow[gg % 2][:], norm_gated[:, gg * 256:(gg + 1) * 256].to_broadcast([128, 256]), (),
                      [ngrow[gg % 2].b()])
                for ci in range(4):
                    S.dma("sp", scv[:, ci, :], sconv[:, cg[ci] * 128:(cg[ci] + 1) * 128], (), [scv.b()])
                for ci in range(4):
                    for tap in range(4):
                        k.ts("dve", dgw[:, ci, tap, :], ident_f[:], colv[:, cg[ci], tap:tap + 1], ALU.mult,
                             [ident_f.b(), colv.b()], [dgw.b()])
                k.memset("dve", carry[:], 0.0, [carry.b()])

            def setup_late(gg):
                S.dma("pool", wog[0][:], w_out[gg * 256:(gg + 1) * 256, :].rearrange("(k p) n -> p k n", p=128), (), [wog[0].b()])
                for r in range(4):
                    k.ts("dve", Dg[:, r, :], ident_f[:], brow[:, 2, 4 * gg + r:4 * gg + r + 1], ALU.mult,
                         [ident_f.b(), brow.b()], [Dg.b()])

            def P_units(gg, tbi, bset):
                t0, tn = TBS[tbi]
                cglob = cglob_of(gg)
                wxg = wx[gg % 2]
                rw, xa, zz = rawt[bset], xact[bset], zs4[bset]
                units = []

                def u_hist():
                    bi = k.bank()
                    for ci in range(4):
                        k.tr(psum[:, bi, ci * 48:(ci + 1) * 48], scv[:, ci, :], ident_f[0:48, 0:48],
                             [scv.b(), ident_f.b()], [pbuf(bi)])
                    k.cp("dve", raws[:, :, :, 0:3], psum[:, bi, 0:192].rearrange("p (c b j) -> p c b j", c=4, j=3),
                         [pbuf(bi)], [raws.b()])
                if tbi == 4:
                    units.append(u_hist)

                def u_in(ci):
                    if ci == 0 and tbi < 4:
                        k.cp("dve", rw[:, :, 0:3], carry[:], [carry.b()], [rw.b()])
                    bi = k.bank()
                    for kc in range(DC):
                        k.mm(psum[:, bi, 0:tn], wxg[:, kc, ci * 128:(ci + 1) * 128], h[:, kc, t0:t0 + tn],
                             kc == 0, kc == DC - 1, [wxg.b(), h.b((kc, tbi))], [pbuf(bi)])
                    if tbi < 4:
                        k.cp("act", rw[:, ci, 3:3 + tn], psum[:, bi, 0:tn], [pbuf(bi)], [rw.b()])
                        if tbi == 3:
                            k.cp("dve", ncv[:, ci, 48:51], psum[:, bi, 509:512], [pbuf(bi)], [ncv.b()])
                    else:
                        pvv = psum[:, bi, 0:128].rearrange("p (b j) -> p b j", j=8)
                        k.cp("act", raws[:, ci, :, 3:11], pvv, [pbuf(bi)], [raws.b()])
                        k.cp("dve", ncv[:, ci, 0:48].rearrange("p (b j) -> p b j", j=3), pvv[:, :, 5:8],
                             [pbuf(bi)], [ncv.b()])
                    if ci == 3 and tbi < 3:
                        k.cp("dve", carry[:], rw[:, :, 512:515], [rw.b()], [carry.b()])

                def u_cv(ci):
                    bi = k.bank()
                    for tap in range(4):
                        if tbi < 4:
                            rhs_ = rw[:, ci, tap:tap + tn]
                            rb_ = rw.b()
                        else:
                            rhs_ = raws[:, ci, :, tap:tap + 8]
                            rb_ = raws.b()
                        k.mm(psum[:, bi, 0:tn], dgw[:, ci, tap, :], rhs_, tap == 0, tap == 3,
                             [dgw.b(), rb_], [pbuf(bi)])
                    th_, vh_ = tht[ci % 2], vht[ci % 2]
                    k.act(th_[:, 0:tn], psum[:, bi, 0:tn], AF.Tanh, [pbuf(bi), cbh.b()], [th_.b()],
                          bias=cbh[:, cglob[ci]:cglob[ci] + 1], scale=0.5)
                    k.act(vh_[:, 0:tn], psum[:, bi, 0:tn], AF.Identity, [pbuf(bi), cbh.b()], [vh_.b()],
                          bias=cbh[:, cglob[ci]:cglob[ci] + 1], scale=0.5)
                    k.stt(xa[:, ci, 0:tn], th_[:, 0:tn], 1.0, vh_[:, 0:tn], ALU.add, ALU.mult,
                          [th_.b(), vh_.b()], [xa.b()])

                def u_z(tt):
                    t = t0 // 128 + tt
                    bi = k.bank()
                    for kc in range(DC):
                        k.mm(psum[:, bi, 0:256], h[:, kc, t * 128:(t + 1) * 128], wz[:, kc, :],
                             kc == 0, kc == DC - 1, [h.b((kc, tbi)), wz.b()], [pbuf(bi)])
                    th_, vh_ = tht[tt % 2], vht[tt % 2]
                    k.act(th_[:, 0:256], psum[:, bi, 0:256], AF.Tanh, [pbuf(bi)], [th_.b()], scale=0.5)
                    k.act(vh_[:, 0:256], psum[:, bi, 0:256], AF.Identity, [pbuf(bi)], [vh_.b()], scale=0.5)
                    k.stt(zz[:, tt, :], th_[:, 0:256], 1.0, vh_[:, 0:256], ALU.add, ALU.mult,
                          [th_.b(), vh_.b()], [zz.b(tt)])

                for ci in range(4):
                    units.append(lambda ci=ci: u_in(ci))
                for ci in range(4):
                    units.append(lambda ci=ci: u_cv(ci))
                for tt in range(tn // 128):
                    units.append(lambda tt=tt: u_z(tt))
                return units

            setup_early(0)
            for e8 in range(4):
                load_h0(0, e8)
            for u_ in P_units(0, 0, 0):
                u_()
            for g in range(NG):
                hd0 = 4 * g
                cglob = cglob_of(g)
                wo = wog[0]
                ngrow_c = ngrow[g % 2]
                setup_late(g)
                for tbi, (t0, tn) in ALL_TBS:
                    bidx = g * len(TBS) + tbi
                    xact_c, zs4_c = xact[bidx % 2], zs4[bidx % 2]
                    if tbi + 1 < len(TBS):
                        nxt_units = P_units(g, tbi + 1, (bidx + 1) % 2)
                    elif g + 1 < NG:
                        setup_early(g + 1)
                        nxt_units = P_units(g + 1, 0, (bidx + 1) % 2)
                    else:
                        nxt_units = []

                    def head(tt):
                        t = t0 // 128 + tt
                        sl = t % 2
                        lsl = slice(tt * 128, (tt + 1) * 128)
                        Ub_ = U_b[0] if t < 16 else U_b[1]
                        Ng = Neg4[0] if t < 16 else Neg4[1]
                        bt = k.bank()
                        pv = psum[:, bt, :].bitcast(BF16)
                        for ci in range(3):
                            k.tr(pv[:, ci * 128:(ci + 1) * 128], xact_c[:, ci, lsl], ident_b[:],
                                 [xact_c.b(), ident_b.b()], [pbuf(bt)])
                        xv = pv[:, 0:256].rearrange("p (r q) -> p r q", q=HP)
                        k.tt("dve", xdts[sl][:], xv, dtd_a[:, t, hd0:hd0 + 4].unsqueeze(2).to_broadcast([128, 4, HP]), ALU.mult,
                             [pbuf(bt), dtd_a.b(t)], [xdts[sl].b()])
                        k.cp("act", xB[sl][:], pv[:, 0:384], [pbuf(bt)], [xB[sl].b()])
                        bc = k.bank()
                        k.mm(psum[:, bc, 0:128], xact_c[:, 2, lsl], xact_c[:, 3, lsl], True, True, [xact_c.b()], [pbuf(bc)])
                        k.cp("act", cbs[sl][:], psum[:, bc, 0:128], [pbuf(bc)], [cbs[sl].b()])
                        br = k.bank()
                        for r in range(4):
                            k.mm(psum[:, br, r * 128:(r + 1) * 128], ident_b[:], Ng[:], r == 0, False,
                                 [ident_b.b(), Ng.b()], [pbuf(br)])
                        for r in range(4):
                            k.mm(psum[:, br, r * 128:(r + 1) * 128],
                                 dtAh[:, t, hd0 + r:hd0 + r + 1].to_broadcast([128, 128]), Ub_[:], False, False,
                                 [dtAh.b(), Ub_.b()], [pbuf(br)])
                            k.mm(psum[:, br, r * 128:(r + 1) * 128],
                                 dtAl[:, t, hd0 + r:hd0 + r + 1].to_broadcast([128, 128]), Ub_[:], False, r == 3,
                                 [dtAl.b(), Ub_.b()], [pbuf(br)])
                        for r in range(4):
                            dc_ = dcy[(t * 4 + r) % 4]
                            mt_ = MT[(t % 2) * 4 + r]
                            k.act(dc_[:], psum[:, br, r * 128:(r + 1) * 128], AF.Exp, [pbuf(br), nb_a.b(t)], [dc_.b()],
                                  bias=nb_a[:, t, hd0 + r:hd0 + r + 1], scale=1.0)
                            k.tt("pool", mt_[:], dc_[:], cbs[sl][:], ALU.mult, [dc_.b(), cbs[sl].b()], [mt_.b()])
                        return None

                    def tail(tt, banks):
                        t = t0 // 128 + tt
                        sl = t % 2
                        lsl = slice(tt * 128, (tt + 1) * 128)
                        has_off = True
                        by = k.bank()
                        for r in range(4):
                            mt_ = MT[(t % 2) * 4 + r]
                            k.mm(psum[:, by, r * HP:(r + 1) * HP], mt_[:], xB[sl][:, r * HP:(r + 1) * HP], True, False,
                                 [mt_.b(), xB[sl].b()], [pbuf(by)])
                            k.mm(psum[:, by, r * HP:(r + 1) * HP], Dg[:, r, :], xB[sl][:, r * HP:(r + 1) * HP], False, True,
                                 [Dg.b(), xB[sl].b()], [pbuf(by)])
                        bs_ = None
                        if t < 16:
                            bs_ = k.bank()
                            k.mm(psum[:, bs_, 0:256], xB[sl][:, 256:384], xdts[sl][:].rearrange("p r q -> p (r q)"), True, True,
                                 [xB[sl].b(), xdts[sl].b()], [pbuf(bs_)])
                        bo = k.bank()
                        held = (by, bo)
                        if t < 16:
                            if t == 0:
                                has_off = False
                            else:
                                k.mm(psum[:, bo, 0:256], xact_c[:, 3, lsl], hTb[:], True, True, [xact_c.b(), hTb.b()], [pbuf(bo)])
                        else:
                            k.perm_excl = held
                            for e8 in range(8):
                                hs_ = h0s[e8 % 4]
                                bh = k.bank(excl=held)
                                for b2 in range(2):
                                    for a in range(2):
                                        k.tr(psum[:, bh, (b2 * 2 + a) * 128:(b2 * 2 + a + 1) * 128],
                                             hs_[:, b2, a, :], ident_f[:], [hs_.b(), ident_f.b()], [pbuf(bh)])
                                k.cp("act" if e8 % 2 else "dve", h0T[:],
                                     psum[:, bh, :].rearrange("p (b q) -> p b q", b=2), [pbuf(bh)], [h0T.b()])
                                for b4 in range(2):
                                    b = e8 * 2 + b4
                                    cz = CTz[b % 2]
                                    if b >= 2:
                                        k.memset("pool", cz[:, (b - 2) * 8:(b - 2) * 8 + 8], 0.0, [cz.b()])
                                    k.cp("pool", cz[:, b * 8:b * 8 + 8], xact_c[:, 3, b * 8:b * 8 + 8], [xact_c.b()], [cz.b()])
                                    k.mm(psum[:, bo, 0:256], cz[:], h0T[:, b4, :], b == 0, b == NSB - 1,
                                         [cz.b(), h0T.b()], [pbuf(bo)])
                                    bmt = Bm[b % 2]
                                    k.ts("pool", bmt[:], xB[sl][:, 256:384], SameB[:, b * 8:b * 8 + 1], ALU.mult,
                                         [xB[sl].b(), SameB.b()], [bmt.b()])
                                    for a in range(2):
                                        bn = k.bank(excl=held)
                                        k.mm(psum[:, bn, 0:128], xdts[sl][:, 2 * a:2 * a + 2, :].rearrange("p r q -> p (r q)"),
                                             bmt[:], True, True, [xdts[sl].b(), bmt.b()], [pbuf(bn)])
                                        k.stt(hs_[:, b4, a, :], hs_[:, b4, a, :], cdp2[:, b, 2 * g + a:2 * g + a + 1],
                                              psum[:, bn, 0:128], ALU.mult, ALU.add, [hs_.b(), cdp2.b(), pbuf(bn)], [hs_.b()])
                                for b4 in range(2):
                                    S.dma("sp", ssm_s[e8 * 2 + b4, g * 256:(g + 1) * 256, :]
                                          .rearrange("(a p) n -> p a n", p=128), hs_[:, b4, :, :], [hs_.b()], ())
                                if e8 + 4 < 8:
                                    load_h0(g, e8 + 4)
                                elif g + 1 < NG:
                                    load_h0(g + 1, e8 - 4)
                                for _ in range(2):
                                    if ucur[0] < len(nxt_units):
                                        nxt_units[ucur[0]]()
                                        ucur[0] += 1
                            k.perm_excl = ()
                            for i in range(2):
                                k.memset("pool", CTz[i][:], 0.0, [CTz[i].b()])
                        if t < 16:
                            if t == 0:
                                k.cp("dve", hTf[:], psum[:, bs_, 0:256], [pbuf(bs_)], [hTf.b()])
                            else:
                                hv_ = hTf[:].rearrange("p (r q) -> p r q", q=HP)
                                k.tt("dve", hv_, hv_, cd_a[:, t, hd0:hd0 + 4].unsqueeze(2).to_broadcast([128, 4, HP]), ALU.mult,
                                     [hTf.b(), cd_a.b(t)], [hTf.b()])
                                k.tt("dve", hTf[:], hTf[:], psum[:, bs_, 0:256], ALU.add, [hTf.b(), pbuf(bs_)], [hTf.b()])
                            if t < 15:
                                k.cp("pool", hTb[:], hTf[:], [hTf.b()], [hTb.b()])
                            else:
                                bf_ = k.bank(excl=held)
                                for a in range(2):
                                    k.tr(psum[:, bf_, a * 128:(a + 1) * 128], hTf[:, a * 128:(a + 1) * 128], ident_f[:],
                                         [hTf.b(), ident_f.b()], [pbuf(bf_)])
                                k.cp("dve", hout[:, 0:2, :], psum[:, bf_, 0:256].rearrange("p (a n) -> p a n", a=2), [pbuf(bf_)], [hout.b()])
                                S.dma("sp", ssm_p[g * 256:(g + 1) * 256, :].rearrange("(a p) n -> p a n", p=128), hout[:, 0:2, :],
                                      [hout.b()], ())
                        yv = psum[:, by, 0:256]
                        if has_off:
                            k.tt("dve", t1[:], psum[:, bo, 0:256].rearrange("p (r q) -> p r q", q=HP),
                                 eacs_a[:, t, hd0:hd0 + 4].unsqueeze(2).to_broadcast([128, 4, HP]), ALU.mult,
                                 [pbuf(bo), eacs_a.b(t)], [t1.b()])
                            k.tt("dve", yg[:], yv, t1[:].rearrange("p r q -> p (r q)"), ALU.add, [pbuf(by), t1.b()], [yg.b()])
                            k.tt("dve", yg[:], yg[:], zs4_c[:, tt, :], ALU.mult, [yg.b(), zs4_c.b(tt)], [yg.b()])
                        else:
                            k.tt("dve", yg[:], yv, zs4_c[:, tt, :], ALU.mult, [pbuf(by), zs4_c.b(tt)], [yg.b()])
                        yn_ = ygn[t % 2]
                        k.act(yn_[:], yg[:], AF.Square, [yg.b()], [yn_.b(), yst.b()], accum_out=yst[:, 0:1])
                        k.ts("pool", yst[:, 1:2], yst[:, 0:1], 1.0 / 256, ALU.mult, [yst.b()], [yst.b()], s2=EPS, op1=ALU.add)
                        k.tt("pool", yst[:, 2:3], yst[:, 1:2], mhalf[:, 0:1], ALU.pow, [yst.b(), mhalf.b()], [yst.b()])
                        yn_ = ygn[t % 2]
                        k.stt(yn_[:], yg[:], yst[:, 2:3], ngrow_c[:], ALU.mult, ALU.mult, [yg.b(), yst.b(), ngrow_c.b()], [yn_.b()])

                    def fin(tt):
                        t = t0 // 128 + tt
                        lsl = slice(tt * 128, (tt + 1) * 128)
                        yn_ = ygn[t % 2]
                        bg = k.bank()
                        pg = psum[:, bg, :].bitcast(BF16)
                        for a in range(2):
                            k.tr(pg[:, a * 128:(a + 1) * 128], yn_[:, a * 128:(a + 1) * 128], ident_b[:],
                                 [yn_.b(), ident_b.b()], [pbuf(bg)])
                        k.cp("act", ygT[:, :, lsl], pg[:, 0:256].rearrange("p (a n) -> p a n", a=2), [pbuf(bg)], [ygT.b()])

                    ntile = tn // 128
                    per_step = -(-len(nxt_units) // ntile)
                    ucur = [0]
                    head(0)
                    for tt in range(ntile):
                        if tt + 1 < ntile:
                            head(tt + 1)
                        if tt >= 1:
                            fin(tt - 1)
                        tail(tt, None)
                        lim = min(len(nxt_units), (tt + 1) * per_step)
                        while ucur[0] < lim:
                            nxt_units[ucur[0]]()
                            ucur[0] += 1
                    fin(ntile - 1)
                    for m in range(DC):
                        bi = k.bank()
                        for kc in range(2):
                            k.mm(psum[:, bi, 0:tn], wo[:, kc, m * 128:(m + 1) * 128], ygT[:, kc, 0:tn], kc == 0, kc == 1,
                                 [wo.b(), ygT.b()], [pbuf(bi)])
                        evac_add_xres(m, tbi, t0, tn, psum[:, bi, 0:tn], pbuf(bi))
                bi = k.bank()
                for ci in range(4):
                    k.tr(psum[0:51, bi, ci * 128:(ci + 1) * 128], ncv[:, ci, :], ident_f[:], [ncv.b(), ident_f.b()], [pbuf(bi)])
                k.cp("dve", ncvo[0:51, :, :], psum[0:51, bi, :].rearrange("p (c n) -> p c n", c=4), [pbuf(bi)], [ncvo.b()])
                for ci in range(4):
                    cs_ = slice(cglob[ci] * 128, (cglob[ci] + 1) * 128)
                    S.dma("sp", conv_s[:, cs_], ncvo[0:48, ci, :], [ncvo.b()], ())
                    S.dma("sp", conv_p[:, cs_], ncvo[48:51, ci, :], [ncvo.b()], ())
            S.barrier()

    if cfg["mamba"]:
        mamba_layer()
    if cfg["attn"]:
        attn_layer(0)
    if cfg["mlp"]:
        mlp_layer(0)
    if cfg["pool"]:
        pool_layer()
    if cfg["attn"]:
        attn_layer(1)
    if cfg["mlp"]:
        mlp_layer(1)

    with ExitStack() as ph:
        gfin = k.sb("gfin", [128, D], F32, ph)
        S.dma("sp", gfin[:], norm_final.to_broadcast([128, D]), (), [gfin.b()])
        yt = [k.sb("yt%d" % i, [128, D], F32, ph) for i in range(4)]
        sq = k.sb("sq_scr", [128, D], F32, ph)
        stat = [k.sb("stat%d" % i, [128, 4], F32, ph) for i in range(2)]
        for t in range(NT):
            b0 = bank2()
            for c in range(DC):
                bi = b0 + c // 4
                k.tr(psum[:, bi, (c % 4) * 128:(c % 4 + 1) * 128], xres[:, c, t * 128:(t + 1) * 128], ident_f[:],
                     [xres.b((c, t)), ident_f.b()], [pbuf(bi)])
            st = stat[t % 2]
            y = yt[t % 4]
            pin = psum[:, b0:b0 + 2, :].rearrange("p a n -> p (a n)")
            k.act(sq[:], pin, AF.Square, [pbuf(b0), pbuf(b0 + 1)], [sq.b(), st.b()], accum_out=st[:, 0:1])
            k.act(st[:, 1:2], st[:, 0:1], AF.Ln, [st.b()], [st.b()], bias=EPS, scale=1.0 / D)
            k.act(st[:, 2:3], st[:, 1:2], AF.Exp, [st.b()], [st.b()], scale=-0.5)
            k.stt(y[:], pin, st[:, 2:3], gfin[:], ALU.mult, ALU.mult,
                  [pbuf(b0), pbuf(b0 + 1), st.b(), gfin.b()], [y.b()])
            dst = y_p[t * 128:(t + 1) * 128, :] if t < 16 else y_s[:, :]
            S.dma("sp", dst, y[:], [y.b()], ())
        S.barrier()
    print("ops", S.nops, "waits", S.nwaits)
    return k


_CACHE = {}


def _get_program():
    if "k" not in _CACHE:
        _CACHE["k"] = build_program()
    return _CACHE["k"]


def kernel(**inputs):
    inp = {k_: np.asarray(v) for k_, v in inputs.items()}
    kk = _get_program()
    f = lambda a: np.ascontiguousarray(a, dtype=np.float32)
    shared = {
        "norm_mix": f(inp["norm_mix"]), "norm_xattn": f(inp["norm_xattn"]), "norm_mem": f(inp["norm_mem"]),
        "norm_mlp": f(inp["norm_mlp"]), "norm_final": f(inp["norm_final"].reshape(1, D)),
        "w_in": f(inp["w_in"][0]), "conv_w": f(inp["conv_w"][0]), "conv_b": f(inp["conv_b"].reshape(1, CONV_DIM)),
        "dt_bias": f(inp["dt_bias"].reshape(1, NH)), "a_log": f(inp["a_log"].reshape(1, NH)),
        "d_skip": f(inp["d_skip"].reshape(1, NH)), "norm_gated": f(inp["norm_gated"].reshape(1, DI)),
        "w_out": f(inp["w_out"][0]), "w_pool": f(inp["w_pool"][0]), "pool_scale": f(inp["pool_scale"].reshape(1, D)),
        "w_xq": f(inp["w_xq"]), "w_xk": f(inp["w_xk"]), "w_xv": f(inp["w_xv"]), "w_xo": f(inp["w_xo"]),
        "w_up": f(inp["w_up"]), "w_down": f(inp["w_down"]),
    }
    in_maps = []
    for c in range(NCORES):
        sl = slice(c * NSB, (c + 1) * NSB)
        m = dict(shared)
        m.update({
            "xp": f(inp["x_prompt"][c]),
            "xs": f(inp["x_sample"][sl].reshape(TS, D)),
            "ck": f(inp["cache_mem_k"][:, sl].reshape(2, NSB, NMEM, D)),
            "cv": f(inp["cache_mem_v"][:, sl].reshape(2, NSB, NMEM, D)),
            "ssm": f(inp["state_ssm"][0, sl].reshape(NSB, DI, NS)),
            "sconv": f(inp["state_conv"][0, sl].reshape(NSB * 3, CONV_DIM)),
            "spool": f(inp["state_pool"][0, sl].reshape(NSB * 15, D)),
            "mem": f(inp["mem_prompt"][c]),
        })
        in_maps.append(m)
    res = run_bass_kernel_spmd(kk.nc, in_maps, core_ids=list(range(NCORES)))
    R = res.results
    cat = lambda name, shp: np.concatenate([R[c][name].reshape(shp) for c in range(NCORES)], 0)
    y_prompt = np.stack([R[c]["y_p"] for c in range(NCORES)], 0)
    y_sample = cat("y_s", (NSB, DSEQ, D))
    mk = np.stack([R[c]["mk_p"].reshape(2, NMEM, XH, XD) for c in range(NCORES)], 1)
    mv = np.stack([R[c]["mv_p"].reshape(2, NMEM, XH, XD) for c in range(NCORES)], 1)
    ssm_p = np.stack([R[c]["ssm_p"].reshape(NH, HP, NS) for c in range(NCORES)], 0)[None]
    conv_p = np.stack([R[c]["conv_p"] for c in range(NCORES)], 0)[None]
    pool_p = np.stack([R[c]["pool_p"] for c in range(NCORES)], 0)[None]
    ssm_s = cat("ssm_s", (NSB, NH, HP, NS))[None]
    conv_s = cat("conv_s", (NSB, 3, CONV_DIM))[None]
    pool_s = cat("pool_s", (NSB, 15, D))[None]
    return (y_prompt, y_sample, mk, mv, ssm_p, conv_p, pool_p, ssm_s, conv_s, pool_s)
```
